# Optimizing a Trainium2 kernel written in Bass

```python
import jax, jax.numpy as jnp
from jax import lax
import numpy as np

D_MODEL = 1024
BATCH = 8
SEQ = 4096
DEPTH = 1

D_RNN = D_MODEL // 2
RNN_BLOCKS = 8
RNN_BLOCK = D_RNN // RNN_BLOCKS
CONV_WIDTH = 4
LRU_C = 8.0
N_HEADS = 8
HEAD_DIM = 64
N_KV = 2
REP = N_HEADS // N_KV
D_ATT = N_HEADS * HEAD_DIM
CMP_LEN = 32
CMP_STRIDE = 16
CMP_HIDDEN = 256
SLC_BLOCK = 64
SLC_TOPK = 16
WINDOW = 512
Q_BLOCK = 64
D_FF = 4 * D_MODEL
EPS = 1e-6
NEG = -1e30
FORCED = 1e4

kernel_name = 'hybrid_rglru_nsa_block'


def rms_norm(x, g):
    xf = x.astype(jnp.float32)
    y = xf * lax.rsqrt(jnp.mean(xf * xf, axis=-1, keepdims=True) + EPS)
    return (y * g.astype(jnp.float32)).astype(x.dtype)


def modulate(h, shift, scale):
    return h * (1.0 + scale[:, None, :]) + shift[:, None, :]


def alibi_slopes():
    return jnp.asarray(2.0 ** (-8.0 * np.arange(1, N_HEADS + 1) / N_HEADS), jnp.float32)


def causal_depthwise_conv(x, w, b):
    y = lax.conv_general_dilated(x, w[:, None, :], window_strides=(1,), padding=[(CONV_WIDTH - 1, 0)],
                                 dimension_numbers=('NWC', 'WIO', 'NWC'), feature_group_count=x.shape[-1])
    return y + b


def block_diag(x, w, b):
    xb = x.reshape(x.shape[:-1] + (RNN_BLOCKS, RNN_BLOCK))
    return jnp.einsum('bsni,nij->bsnj', xb, w.astype(jnp.float32)).reshape(x.shape) + b.astype(jnp.float32)


def rg_lru(x, w_a, b_a, w_x, b_x, lam):
    xf = x.astype(jnp.float32)
    r = jax.nn.sigmoid(block_diag(xf, w_a, b_a))
    i = jax.nn.sigmoid(block_diag(xf, w_x, b_x))
    log_a = -LRU_C * r * jax.nn.softplus(-lam.astype(jnp.float32))
    a = jnp.exp(log_a)
    u = jnp.sqrt(-jnp.expm1(2.0 * log_a)) * (i * xf)

    def combine(e1, e2):
        a1, b1 = e1
        a2, b2 = e2
        return a1 * a2, a2 * b1 + b2

    _, h = lax.associative_scan(combine, (a, u), axis=1)
    return h.astype(x.dtype)


def to_heads(t):
    B, S, _ = t.shape
    return t.reshape(B, S, N_KV, HEAD_DIM).transpose(0, 2, 1, 3)


def compress_blocks(k, pos_emb, w1, w2):
    B, G, S, D = k.shape
    n_cmp = (S - CMP_LEN) // CMP_STRIDE + 1
    idx = CMP_STRIDE * jnp.arange(n_cmp)[:, None] + jnp.arange(CMP_LEN)[None, :]
    blocks = (k[:, :, idx] + pos_emb).reshape(B, G, n_cmp, CMP_LEN * D)
    return jax.nn.gelu(blocks @ w1) @ w2


def nsa_attention(q, k_c, v_c, k_s, v_s, k_w, v_w, gate_logits,
                  cmp_pos_k, cmp_w1_k, cmp_w2_k, cmp_pos_v, cmp_w1_v, cmp_w2_v):
    B, S, _ = q.shape
    f32 = jnp.float32
    n_cmp = (S - CMP_LEN) // CMP_STRIDE + 1
    n_slc = S // SLC_BLOCK
    n_sel = min(SLC_TOPK, n_slc)
    n_qb = S // Q_BLOCK
    qh = q.reshape(B, S, N_KV, REP, HEAD_DIM).transpose(0, 2, 3, 1, 4) * (HEAD_DIM ** -0.5)
    kc = compress_blocks(to_heads(k_c), cmp_pos_k, cmp_w1_k, cmp_w2_k)
    vc = compress_blocks(to_heads(v_c), cmp_pos_v, cmp_w1_v, cmp_w2_v).astype(f32)
    ks = to_heads(k_s).reshape(B, N_KV, n_slc, SLC_BLOCK, HEAD_DIM)
    vs = to_heads(v_s).reshape(B, N_KV, n_slc, SLC_BLOCK, HEAD_DIM)
    pad = ((0, 0), (0, 0), (WINDOW, 0), (0, 0))
    kw = jnp.pad(to_heads(k_w), pad)
    vw = jnp.pad(to_heads(v_w), pad)
    gates = jax.nn.sigmoid(gate_logits.astype(f32)).reshape(B, S, 3, N_KV, REP).transpose(2, 0, 3, 4, 1)
    slopes = alibi_slopes().reshape(1, N_KV, REP, 1, 1)
    c_start = CMP_STRIDE * jnp.arange(n_cmp)
    cmp_end = c_start + (CMP_LEN - 1)
    s_start = SLC_BLOCK * jnp.arange(n_slc)
    overlap = ((c_start[:, None] < s_start[None, :] + SLC_BLOCK) &
               (c_start[:, None] + CMP_LEN > s_start[None, :])).astype(f32)
    blk_ids = jnp.arange(n_slc)
    gather = jax.vmap(jax.vmap(lambda kb, ix: kb[ix]))

    def one_block(i):
        q0 = i * Q_BLOCK
        qb = lax.dynamic_slice_in_dim(qh, q0, Q_BLOCK, axis=3)
        t = q0 + jnp.arange(Q_BLOCK)
        d_c = (t[:, None] - cmp_end[None, :]).astype(f32)
        sc = jnp.einsum('bgrqd,bgcd->bgrqc', qb, kc).astype(f32) - slopes * d_c
        sc = jnp.where(d_c >= 0, sc, NEG)
        p_c = jax.nn.softmax(sc, axis=-1) * (t >= CMP_LEN - 1).astype(f32)[:, None]
        o_c = jnp.einsum('bgrqc,bgcd->bgrqd', p_c, vc)
        imp = jnp.einsum('bgrqc,cj->bgqj', p_c, overlap)
        cur = t // SLC_BLOCK
        forced = (blk_ids[None, :] == 0) | (blk_ids[None, :] == cur[:, None]) | (blk_ids[None, :] == cur[:, None] - 1)
        visible = s_start[None, :] <= t[:, None]
        imp = jnp.where(forced, FORCED, jnp.where(visible, imp, NEG))
        _, sel = lax.top_k(imp, n_sel)
        k_sel = gather(ks, sel)
        v_sel = gather(vs, sel).astype(f32)
        pos = sel[..., None] * SLC_BLOCK + jnp.arange(SLC_BLOCK)
        d_s = (t[:, None, None] - pos).astype(f32)[:, :, None]
        ss = jnp.einsum('bgrqd,bgqnkd->bgrqnk', qb, k_sel).astype(f32) - slopes[..., None] * d_s
        ss = jnp.where(d_s >= 0, ss, NEG)
        p_s = jax.nn.softmax(ss.reshape(ss.shape[:4] + (-1,)), axis=-1).reshape(ss.shape)
        o_s = jnp.einsum('bgrqnk,bgqnkd->bgrqd', p_s, v_sel)
        kwb = lax.dynamic_slice_in_dim(kw, q0, WINDOW + Q_BLOCK, axis=2)
        vwb = lax.dynamic_slice_in_dim(vw, q0, WINDOW + Q_BLOCK, axis=2).astype(f32)
        s_pos = q0 - WINDOW + jnp.arange(WINDOW + Q_BLOCK)
        d_w = (t[:, None] - s_pos[None, :]).astype(f32)
        ok = (d_w >= 0) & (d_w < WINDOW) & (s_pos[None, :] >= 0)
        sw = jnp.einsum('bgrqd,bgkd->bgrqk', qb, kwb).astype(f32) - slopes * d_w
        sw = jnp.where(ok, sw, NEG)
        o_w = jnp.einsum('bgrqk,bgkd->bgrqd', jax.nn.softmax(sw, axis=-1), vwb)
        gb = lax.dynamic_slice_in_dim(gates, q0, Q_BLOCK, axis=4)[..., None]
        return gb[0] * o_c + gb[1] * o_s + gb[2] * o_w

    out = lax.map(one_block, jnp.arange(n_qb))
    return out.transpose(1, 0, 4, 2, 3, 5).reshape(B, S, D_ATT).astype(q.dtype)


def hybrid_mixer(h, w_in, conv_w, conv_b, lru_wa, lru_ba, lru_wx, lru_bx, lru_lambda,
                 cmp_pos_k, cmp_w1_k, cmp_w2_k, cmp_pos_v, cmp_w1_v, cmp_w2_v,
                 norm_rnn_out, norm_att_out, w_out):
    sizes = [D_RNN, D_RNN, D_ATT] + [N_KV * HEAD_DIM] * 6 + [3 * N_HEADS]
    splits = np.cumsum(sizes)[:-1].tolist()
    z = h @ w_in
    g_rnn, x_rnn, q, k_c, v_c, k_s, v_s, k_w, v_w, gate_logits = jnp.split(z, splits, axis=-1)
    rnn = jax.nn.gelu(g_rnn) * rg_lru(causal_depthwise_conv(x_rnn, conv_w, conv_b),
                                      lru_wa, lru_ba, lru_wx, lru_bx, lru_lambda)
    att = nsa_attention(q, k_c, v_c, k_s, v_s, k_w, v_w, gate_logits,
                        cmp_pos_k, cmp_w1_k, cmp_w2_k, cmp_pos_v, cmp_w1_v, cmp_w2_v)
    y = jnp.concatenate([rms_norm(rnn, norm_rnn_out), rms_norm(att, norm_att_out)], axis=-1)
    return y @ w_out


def setup_inputs(seed: int = 0) -> dict:
    key = jax.random.key(seed)
    ks = jax.random.split(key, 32)
    L = DEPTH
    f32 = jnp.float32

    def nrm(k, shape, scale):
        return jax.random.normal(k, shape, f32) * scale

    def gain(k, n):
        return 1.0 + 0.05 * jax.random.normal(k, (L, n), f32)

    d_in = 2 * D_RNN + D_ATT + 6 * N_KV * HEAD_DIM + 3 * N_HEADS
    u = jax.random.uniform(ks[31], (L, D_RNN), f32, minval=0.9, maxval=0.999)
    a0 = u ** (1.0 / LRU_C)
    lam = jnp.log(a0) - jnp.log1p(-a0)
    return {
        'x': nrm(ks[0], (BATCH, SEQ, D_MODEL), 1.0),
        'c': nrm(ks[1], (BATCH, D_MODEL), 1.0),
        'ada_w': nrm(ks[2], (L, D_MODEL, 6 * D_MODEL), 0.1 * D_MODEL ** -0.5),
        'ada_b': nrm(ks[3], (L, 6 * D_MODEL), 0.01),
        'pre_norm_mix': gain(ks[4], D_MODEL),
        'w_in': nrm(ks[5], (L, D_MODEL, d_in), D_MODEL ** -0.5),
        'conv_w': nrm(ks[6], (L, CONV_WIDTH, D_RNN), CONV_WIDTH ** -0.5),
        'conv_b': nrm(ks[7], (L, D_RNN), 0.01),
        'lru_wa': nrm(ks[8], (L, RNN_BLOCKS, RNN_BLOCK, RNN_BLOCK), RNN_BLOCK ** -0.5),
        'lru_ba': nrm(ks[9], (L, D_RNN), 0.01),
        'lru_wx': nrm(ks[10], (L, RNN_BLOCKS, RNN_BLOCK, RNN_BLOCK), RNN_BLOCK ** -0.5),
        'lru_bx': nrm(ks[11], (L, D_RNN), 0.01),
        'lru_lambda': lam,
        'cmp_pos_k': nrm(ks[12], (L, CMP_LEN, HEAD_DIM), 0.1),
        'cmp_w1_k': nrm(ks[13], (L, CMP_LEN * HEAD_DIM, CMP_HIDDEN), (CMP_LEN * HEAD_DIM) ** -0.5),
        'cmp_w2_k': nrm(ks[14], (L, CMP_HIDDEN, HEAD_DIM), CMP_HIDDEN ** -0.5),
        'cmp_pos_v': nrm(ks[15], (L, CMP_LEN, HEAD_DIM), 0.1),
        'cmp_w1_v': nrm(ks[16], (L, CMP_LEN * HEAD_DIM, CMP_HIDDEN), (CMP_LEN * HEAD_DIM) ** -0.5),
        'cmp_w2_v': nrm(ks[17], (L, CMP_HIDDEN, HEAD_DIM), CMP_HIDDEN ** -0.5),
        'norm_rnn_out': gain(ks[18], D_RNN),
        'norm_att_out': gain(ks[19], D_ATT),
        'w_out': nrm(ks[20], (L, D_RNN + D_ATT, D_MODEL), (D_RNN + D_ATT) ** -0.5),
        'post_norm_mix': gain(ks[21], D_MODEL),
        'pre_norm_mlp': gain(ks[22], D_MODEL),
        'w_ff1': nrm(ks[23], (L, D_MODEL, D_FF), D_MODEL ** -0.5),
        'w_ff2': nrm(ks[24], (L, D_FF, D_MODEL), D_FF ** -0.5),
        'post_norm_mlp': gain(ks[25], D_MODEL),
    }


def reference(x, c, ada_w, ada_b, pre_norm_mix, w_in, conv_w, conv_b, lru_wa, lru_ba, lru_wx, lru_bx,
              lru_lambda, cmp_pos_k, cmp_w1_k, cmp_w2_k, cmp_pos_v, cmp_w1_v, cmp_w2_v,
              norm_rnn_out, norm_att_out, w_out, post_norm_mix, pre_norm_mlp, w_ff1, w_ff2, post_norm_mlp):
    for l in range(DEPTH):
        mod = jax.nn.silu(c) @ ada_w[l] + ada_b[l]
        sh1, sc1, g1, sh2, sc2, g2 = jnp.split(mod, 6, axis=-1)
        h = modulate(rms_norm(x, pre_norm_mix[l]), sh1, sc1)
        y = hybrid_mixer(h, w_in[l], conv_w[l], conv_b[l], lru_wa[l], lru_ba[l], lru_wx[l], lru_bx[l],
                         lru_lambda[l], cmp_pos_k[l], cmp_w1_k[l], cmp_w2_k[l], cmp_pos_v[l], cmp_w1_v[l],
                         cmp_w2_v[l], norm_rnn_out[l], norm_att_out[l], w_out[l])
        x = x + (1.0 + g1[:, None, :]) * rms_norm(y, post_norm_mix[l])
        h = modulate(rms_norm(x, pre_norm_mlp[l]), sh2, sc2)
        y = jnp.square(jax.nn.relu(h @ w_ff1[l])) @ w_ff2[l]
        x = x + (1.0 + g2[:, None, :]) * rms_norm(y, post_norm_mlp[l])
    return x
```

```python
import numpy as np
import ml_dtypes
from contextlib import ExitStack
import concourse.bass as bass
import concourse.mybir as mybir
from concourse.bass_utils import run_bass_kernel_spmd

F32 = mybir.dt.float32
BF16 = mybir.dt.bfloat16
AF = mybir.ActivationFunctionType
ALU = mybir.AluOpType

S = 4096
D = 1024
NT = S // 128
NCH = S // 512
DIN = 2328
NEGM = -30000.0
EPS = 1e-6
import os
EVAC = os.environ.get('KDBG_EVAC', '')


class Buf:
    __slots__ = ("name", "lw", "rd")

    def __init__(self, name):
        self.name = name
        self.lw = None
        self.rd = {}


class DSem:
    __slots__ = ("sem", "cnt")

    def __init__(self, sem):
        self.sem = sem
        self.cnt = 0


class Eng:
    def __init__(self, name, eng, self_sync):
        self.name = name
        self.eng = eng
        self.self_sync = self_sync
        self.cur = None
        self.cnt = 0
        self.own = set()
        self.waited = {}


class K:
    EPOCH = 4000

    def __init__(self, nc, es):
        self.nc = nc
        self.es = es
        self.nsem = 0
        self.pe = Eng("pe", nc.tensor, False)
        self.act = Eng("act", nc.scalar, True)
        self.dve = Eng("dve", nc.vector, True)
        self.pool = Eng("pool", nc.gpsimd, True)
        self.sp = Eng("sp", nc.sync, True)
        self.dsems = []
        self.ninst = 0

    def new_sem(self, name):
        self.nsem += 1
        return self.es.enter_context(self.nc.semaphore(f"{name}_{self.nsem}"))

    def dsem(self, name="d"):
        d = DSem(self.new_sem(name))
        self.dsems.append(d)
        return d

    def _wait(self, E, tok):
        sem, val = tok
        key = id(sem)
        if (not E.self_sync) and key in E.own:
            return
        if E.waited.get(key, 0) >= val:
            return
        E.eng.wait_ge(sem, val)
        E.waited[key] = val

    def _deps(self, E, reads, writes):
        for b in reads:
            if b.lw is not None:
                self._wait(E, b.lw)
        for b in writes:
            if b.lw is not None:
                self._wait(E, b.lw)
            for tok in b.rd.values():
                self._wait(E, tok)

    def _commit(self, tok, reads, writes):
        key = id(tok[0])
        for b in reads:
            old = b.rd.get(key)
            if old is None or old[1] < tok[1]:
                b.rd[key] = tok
        for b in writes:
            b.lw = tok
            b.rd = {}

    def op(self, E, fn, reads=(), writes=()):
        self._deps(E, reads, writes)
        inst = fn()
        if E.cur is None or E.cnt >= self.EPOCH:
            E.cur = self.new_sem(E.name)
            E.own.add(id(E.cur))
            E.cnt = 0
        E.cnt += 1
        inst.then_inc(E.cur, 1)
        tok = (E.cur, E.cnt)
        self._commit(tok, reads, writes)
        self.ninst += 1
        return tok

    def dma(self, Q, ds, out, in_, reads=(), writes=(), **kw):
        self._deps(Q, reads, writes)
        if ds.cnt:
            self._wait(Q, (ds.sem, ds.cnt))
        inst = Q.eng.dma_start(out=out, in_=in_, **kw)
        ds.cnt += 16
        inst.then_inc(ds.sem, 16)
        tok = (ds.sem, ds.cnt)
        self._commit(tok, reads, writes)
        self.ninst += 1
        return tok

    def drain(self):
        for E in (self.pe, self.act, self.dve, self.pool):
            if E.cur is not None:
                self._wait(self.sp, (E.cur, E.cnt))
        for d in self.dsems:
            if d.cnt:
                self._wait(self.sp, (d.sem, d.cnt))

    def barrier(self):
        engs = (self.pe, self.act, self.dve, self.pool, self.sp)
        for E in engs:
            for E2 in engs:
                if E2 is not E and E2.cur is not None and E2.cnt:
                    self._wait(E, (E2.cur, E2.cnt))
            for d in self.dsems:
                if d.cnt:
                    self._wait(E, (d.sem, d.cnt))

    def finish(self, bufs):
        for b in bufs:
            if b.lw is not None:
                self._wait(self.sp, b.lw)


class Rot:
    def __init__(self, k, es, nc, name, n, shape, dt, with_dsem=True):
        self.n = n
        self.i = 0
        self.slots = []
        for j in range(n):
            t = es.enter_context(nc.sbuf_tensor(f"{name}{j}", shape, dt))
            self.slots.append((t, Buf(f"{name}{j}"), k.dsem(name) if with_dsem else None))

    def next(self):
        s = self.slots[self.i % self.n]
        self.i += 1
        return s


class _Stop(Exception):
    pass


def build(debug=False, upto=99):
    nc = bass.Bass("TRN2", target_bir_lowering=False)

    def din(name, shape, dt=F32):
        return nc.dram_tensor(name, list(shape), dt, kind="ExternalInput").ap()

    def dscr(name, shape, dt):
        return nc.dram_tensor(name, list(shape), dt, kind="Internal").ap()

    x_d = din("x", [S, D])
    c_d = din("c", [128, 8])
    adaw_d = din("ada_w", [D, 6 * D])
    adab_d = din("ada_b", [1, 6 * D])
    gpre1_d = din("g_pre1", [128, 8])
    gpre2_d = din("g_pre2", [128, 8])
    gpost1_d = din("g_post1", [128, D])
    gpost2_d = din("g_post2", [128, D])
    win_d = din("w_in", [D, DIN])
    convw_d = din("conv_w", [128, 4, 4])
    convb_d = din("conv_b", [128, 4])
    wa_d = din("lru_wa", [8, 64, 64])
    wx_d = din("lru_wx", [8, 64, 64])
    ba_d = din("lru_ba", [128, 4])
    bx_d = din("lru_bx", [128, 4])
    lam_d = din("lru_lam", [128, 4])
    posk_d = din("pos_k", [64, 32])
    posv_d = din("pos_v", [64, 32])
    w1k_d = din("w1_k", [2048, 256])
    w1v_d = din("w1_v", [2048, 256])
    w2k_d = din("w2_k", [256, 64])
    w2v_d = din("w2_v", [256, 64])
    grnn_d = din("g_rnn", [128, 4])
    gatt_d = din("g_att", [128, 4])
    wout_d = din("w_out", [D, D])
    wff1_d = din("w_ff1", [D, 4 * D])
    wff2_d = din("w_ff2", [4 * D, D])
    ident_d = din("k_ident", [128, 128], BF16)
    ce01_d = din("k_ce01", [64, 256], BF16)
    cllo_d = din("k_cllo", [128, 2, 8])
    cmask_d = din("k_cmask", [128, 2, S], BF16)
    e01_d = din("k_e01", [64, S], BF16)
    tab_d = din("k_tab", [64, S], BF16)
    sllo_d = din("k_sllo", [128, 8])
    wm01_d = din("k_wm01", [128, 8, 512], BF16)
    addm_d = din("k_addm", [128, NT, 64])
    ovl_d = din("k_ovl", [128, 2, 63], BF16)
    out_d = nc.dram_tensor("out", [S, D], F32, kind="ExternalOutput").ap()
    zr_d = dscr("zr_s", [1024, S], F32)
    q_d = dscr("q_s", [512, S], BF16)
    kc_d = dscr("kc_s", [128, S], BF16)
    vc_d = dscr("vc_s", [128, S], BF16)
    ks_d = dscr("ks_s", [128, S], BF16)
    kw_d = dscr("kw_s", [128, S], BF16)
    rnn_d = dscr("rnn_s", [512, S], BF16)
    x1_d = dscr("x1_s", [S, D], F32)
    dbg = {}
    if debug:
        for nm, shp in [("d_mod", [1, 6 * D]), ("d_att", [S, 512]), ("d_y", [S, D])]:
            dbg[nm] = nc.dram_tensor(nm, shp, F32, kind="ExternalOutput").ap()

    with ExitStack() as es:
        k = K(nc, es)
        PE, ACT, DVE, POOL, SP = k.pe, k.act, k.dve, k.pool, k.sp

        def sb(name, shape, dt, stack=es):
            return stack.enter_context(nc.sbuf_tensor(name, list(shape), dt))

        def pst(name, shape, dt, stack=es):
            return stack.enter_context(nc.psum_tensor(name, list(shape), dt))

        dbuf = {}

        def DB(key):
            if key not in dbuf:
                dbuf[key] = Buf(str(key))
            return dbuf[key]

        ident = sb("ident", [128, 128], BF16)
        ident_b = Buf("ident")
        ld0 = k.dsem("ld0")
        k.dma(SP, ld0, ident[:], ident_d, writes=[ident_b])
        ones_bf = sb("ones_bf", [128, 128], BF16)
        ones_b = Buf("ones")
        k.op(DVE, lambda: nc.vector.memset(ones_bf[:], 1.0), writes=[ones_b])
        A1 = sb("A1", [128, 8], F32); B1 = sb("B1", [128, 8], F32)
        A2 = sb("A2", [128, 8], F32); B2 = sb("B2", [128, 8], F32)
        C1row = sb("C1row", [128, D], F32); C2row = sb("C2row", [128, D], F32)
        A1_b, B1_b, A2_b, B2_b, C1_b, C2_b = (Buf(n) for n in ("A1", "B1", "A2", "B2", "C1", "C2"))
        mid = es.enter_context(ExitStack())
        vs_aug = sb("vs_aug", [128, NT, 2, 65], BF16, mid)
        vw_aug = sb("vw_aug", [128, NT, 2, 65], BF16, mid)
        gates = sb("gates", [128, NT, 24], F32, mid)
        vtok_b = [Buf(f"vtok{t}") for t in range(NT)]
        vones_b = Buf("vones")
        k.op(DVE, lambda: nc.vector.memset(vs_aug[:, :, :, 64:65], 1.0), writes=[vones_b])
        k.op(DVE, lambda: nc.vector.memset(vw_aug[:, :, :, 64:65], 1.0), writes=[vones_b])
        ssr = sb("ssr", [128, NT], F32, mid)
        ssq = sb("ssq", [128, NT], F32, mid)
        ssr_b = Buf("ssr")

        with ExitStack() as p0:
            csb = sb("csb", [128, 8], F32, p0)
            scb = sb("scb", [128, 8], BF16, p0)
            c_b = Buf("c")
            k.dma(SP, k.dsem("c"), csb[:], c_d, writes=[c_b])
            k.op(ACT, lambda: nc.scalar.activation(out=scb[:], in_=csb[:], func=AF.Silu), reads=[c_b], writes=[c_b])
            adab = sb("adab", [1, 6 * D], F32, p0)
            adab_b = Buf("adab")
            k.dma(SP, k.dsem("adab"), adab[:], adab_d, writes=[adab_b])
            modrow = sb("modrow", [1, 6 * D], F32, p0)
            modrow_b = Buf("modrow")
            modcol = sb("modcol", [128, 48], F32, p0)
            modcol_b = Buf("modcol")
            gp1 = sb("gp1", [128, 8], F32, p0); gp2 = sb("gp2", [128, 8], F32, p0)
            gq1 = sb("gq1", [128, D], F32, p0); gq2 = sb("gq2", [128, D], F32, p0)
            g_b = Buf("gvecs")
            k.dma(SP, ld0, gp1[:], gpre1_d, writes=[g_b])
            k.dma(SP, ld0, gp2[:], gpre2_d, writes=[g_b])
            k.dma(SP, ld0, gq1[:], gpost1_d, writes=[g_b])
            k.dma(SP, ld0, gq2[:], gpost2_d, writes=[g_b])
            adaw = Rot(k, p0, nc, "adaw", 2, [128, 8, 512], BF16)
            ps_row = pst("ps_row", [128, 512], F32, p0)
            ps_row_b = Buf("ps_row")
            ps_col = pst("ps_col", [128, 512], F32, p0)
            ps_col_b = Buf("ps_col")
            one11 = sb("one11", [1, 128], F32, p0)
            one11_b = Buf("one11")
            k.op(DVE, lambda: nc.vector.memset(one11[:], 1.0), writes=[one11_b])
            for pc in range(12):
                t, tb, ds = adaw.next()
                k.dma(POOL, ds, t[:], adaw_d[:, pc * 512:(pc + 1) * 512].rearrange("(k p) n -> p k n", p=128), writes=[tb])
                for kk in range(8):
                    k.op(PE, lambda kk=kk, t=t: nc.tensor.matmul(ps_row[0:1, :], lhsT=scb[:, kk:kk + 1], rhs=t[:, kk, :],
                                                                start=(kk == 0), stop=(kk == 7)),
                         reads=[tb, c_b], writes=[ps_row_b])
                k.op(DVE, lambda pc=pc: nc.vector.tensor_tensor(out=modrow[0:1, pc * 512:(pc + 1) * 512], in0=ps_row[0:1, :],
                                                                in1=adab[0:1, pc * 512:(pc + 1) * 512], op=ALU.add),
                     reads=[ps_row_b, adab_b], writes=[modrow_b])
            if debug:
                k.dma(SP, ld0, dbg["d_mod"], modrow[:], reads=[modrow_b], writes=[DB("d_mod")])
            for j in range(48):
                k.op(PE, lambda j=j: nc.tensor.matmul(ps_col[:, j:j + 1], lhsT=modrow[0:1, j * 128:(j + 1) * 128], rhs=one11[0:1, 0:1],
                                                      start=True, stop=True),
                     reads=[modrow_b, one11_b], writes=[ps_col_b])
            k.op(DVE, lambda: nc.vector.tensor_copy(out=modcol[:], in_=ps_col[:, 0:48]), reads=[ps_col_b], writes=[modcol_b])
            k.op(DVE, lambda: nc.vector.scalar_tensor_tensor(out=A1[:], in0=modcol[:, 8:16], scalar=1.0, in1=gp1[:], op0=ALU.add, op1=ALU.mult),
                 reads=[modcol_b, g_b], writes=[A1_b])
            k.op(DVE, lambda: nc.vector.tensor_copy(out=B1[:], in_=modcol[:, 0:8]), reads=[modcol_b], writes=[B1_b])
            k.op(DVE, lambda: nc.vector.scalar_tensor_tensor(out=A2[:], in0=modcol[:, 32:40], scalar=1.0, in1=gp2[:], op0=ALU.add, op1=ALU.mult),
                 reads=[modcol_b, g_b], writes=[A2_b])
            k.op(DVE, lambda: nc.vector.tensor_copy(out=B2[:], in_=modcol[:, 24:32]), reads=[modcol_b], writes=[B2_b])
            for (base, crow, cb, gq) in ((2048, C1row, C1_b, gq1), (5120, C2row, C2_b, gq2)):
                for hh in range(2):
                    k.op(PE, lambda base=base, hh=hh: nc.tensor.matmul(ps_row[:, :], lhsT=one11[0:1, :],
                                                                      rhs=modrow[0:1, base + hh * 512: base + (hh + 1) * 512],
                                                                      start=True, stop=True),
                         reads=[modrow_b, one11_b], writes=[ps_row_b])
                    k.op(DVE, lambda hh=hh, crow=crow, gq=gq: nc.vector.scalar_tensor_tensor(
                        out=crow[:, hh * 512:(hh + 1) * 512], in0=ps_row[:, :], scalar=1.0, in1=gq[:, hh * 512:(hh + 1) * 512],
                        op0=ALU.add, op1=ALU.mult), reads=[ps_row_b, g_b], writes=[cb])

        if upto <= 0:
            k.drain()
            return nc
        k.barrier()
        with ExitStack() as p1:
            hT = sb("hT", [128, 8, S], BF16, p1)
            hT_b = [[Buf(f"hT{t}_{j}") for j in range(8)] for t in range(NT)]
            win = sb("win", [128, 8, DIN], BF16, p1)
            win_b = Buf("win")
            wds = k.dsem("win")
            k.dma(POOL, wds, win[:], win_d.rearrange("(k p) n -> p k n", p=128), writes=[win_b])
            xt = Rot(k, p1, nc, "xt", 3, [128, D], F32)
            junk = sb("junk", [128, D], BF16, p1)
            junk_b = Buf("junk")
            xn = Rot(k, p1, nc, "xn", 2, [128, D], BF16, with_dsem=False)
            pT = [pst(f"pT{i}", [128, 1024], BF16, p1) for i in range(2)]
            pT_b = [Buf(f"pT{i}") for i in range(2)]
            sm1 = sb("sm1", [128, NT * 4], F32, p1)
            sm1_b = [Buf(f"sm1_{t}") for t in range(NT)]
            pz = [pst(f"pz{i}", [128, 512], F32, p1) for i in range(4)]
            pz_b = [Buf(f"pz{i}") for i in range(4)]
            st32 = Rot(k, p1, nc, "st32", 3, [128, 512], F32)
            st16 = Rot(k, p1, nc, "st16", 3, [128, 512], BF16)
            zi = 0
            fm_tiles = []
            for ct in range(8):
                fm_tiles.append((ct * 128, zr_d, ct * 128, "f32", 1.0))
            for ct in range(4):
                fm_tiles.append((1024 + ct * 128, q_d, ct * 128, "bf", 0.125))
            fm_tiles.append((1536, kc_d, 0, "bf", 1.0))
            fm_tiles.append((1664, vc_d, 0, "bf", 1.0))
            fm_tiles.append((1792, ks_d, 0, "bf", 1.0))
            fm_tiles.append((2048, kw_d, 0, "bf", 1.0))

            def norm_a(tt):
                t, tb, ds = xt.next()
                k.dma(SP, ds, t[:], x_d[tt * 128:(tt + 1) * 128, :], writes=[tb])
                s4 = sm1[:, tt * 4:tt * 4 + 4]; s4b = sm1_b[tt]
                k.op(ACT, lambda: nc.scalar.activation(out=junk[:], in_=t[:], func=AF.Square, accum_out=s4[:, 0:1]), reads=[tb], writes=[junk_b, s4b])
                k.op(ACT, lambda: nc.scalar.activation(out=s4[:, 1:2], in_=s4[:, 0:1], func=AF.Ln, scale=1.0 / D, bias=EPS), reads=[s4b], writes=[s4b])
                k.op(ACT, lambda: nc.scalar.activation(out=s4[:, 2:3], in_=s4[:, 1:2], func=AF.Exp, scale=-0.5), reads=[s4b], writes=[s4b])
                n, nb, _ = xn.next()
                k.op(DVE, lambda: nc.vector.tensor_scalar(out=n[:], in0=t[:], scalar1=s4[:, 2:3], scalar2=None, op0=ALU.mult), reads=[tb, s4b], writes=[nb])
                return tt, n, nb

            def norm_b(tt, n, nb):
                pp = pT[tt % 2]; ppb = pT_b[tt % 2]
                for j in range(8):
                    k.op(PE, lambda j=j: nc.tensor.transpose(out=pp[:, j * 128:(j + 1) * 128], in_=n[:, j * 128:(j + 1) * 128], identity=ident[:]),
                         reads=[nb, ident_b], writes=[ppb])
                for j in range(8):
                    if tt % 2 == 0:
                        k.op(DVE, lambda j=j: nc.vector.tensor_scalar(out=hT[:, j, tt * 128:(tt + 1) * 128], in0=pp[:, j * 128:(j + 1) * 128],
                                                                      scalar1=A1[:, j:j + 1], scalar2=B1[:, j:j + 1], op0=ALU.mult, op1=ALU.add),
                             reads=[ppb, A1_b, B1_b], writes=[hT_b[tt][j]])
                    else:
                        k.op(ACT, lambda j=j: nc.scalar.activation(out=hT[:, j, tt * 128:(tt + 1) * 128], in_=pp[:, j * 128:(j + 1) * 128],
                                                                   func=AF.Identity, scale=A1[:, j:j + 1], bias=B1[:, j:j + 1]),
                             reads=[ppb, A1_b, B1_b], writes=[hT_b[tt][j]])

            npend = [norm_a(0)]

            def norm_tile(_tt_unused=None):
                tt, n, nb = npend[0]
                norm_b(tt, n, nb)
                if tt + 1 < NT:
                    npend[0] = norm_a(tt + 1)

            for tl in range(4):
                norm_tile(tl)
            for ch in range(NCH):
                for ti, (c0, dst, r0, kind, scl) in enumerate(fm_tiles):
                    if ch + 1 < NCH and ti in (2, 6, 10, 14):
                        norm_tile((ch + 1) * 4 + (ti - 2) // 4)
                    pzz = pz[zi % 4]; pzb = pz_b[zi % 4]
                    for kk in range(8):
                        k.op(PE, lambda kk=kk, c0=c0, ch=ch, pzz=pzz: nc.tensor.matmul(pzz[:, :], lhsT=win[:, kk, c0:c0 + 128], rhs=hT[:, kk, ch * 512:(ch + 1) * 512],
                                                                                      start=(kk == 0), stop=(kk == 7)),
                             reads=[win_b] + [hT_b[t4][kk] for t4 in range(ch * 4, ch * 4 + 4)], writes=[pzb])
                    t, tb, ds = (st32 if kind == "f32" else st16).next()
                    if zi % 2 == 0:
                        k.op(ACT, lambda t=t, pzz=pzz, scl=scl: nc.scalar.activation(out=t[:], in_=pzz[:, :], func=AF.Copy, scale=scl), reads=[pzb], writes=[tb])
                    else:
                        k.op(DVE, lambda t=t, pzz=pzz, scl=scl: nc.vector.tensor_scalar(out=t[:], in0=pzz[:, :], scalar1=scl, scalar2=None, op0=ALU.mult),
                             reads=[pzb], writes=[tb])
                    k.dma(POOL, ds, dst[r0:r0 + 128, ch * 512:(ch + 1) * 512], t[:], reads=[tb], writes=[DB((id(dst), r0, ch))])
                    zi += 1
                for tl in range(4):
                    tt = ch * 4 + tl
                    pzz = pz[zi % 4]; pzb = pz_b[zi % 4]
                    use_dve = (zi % 2 == 1)
                    zi += 1
                    for (c0, n, o0) in ((1920, 128, 0), (2176, 152, 128)):
                        for kk in range(8):
                            k.op(PE, lambda kk=kk, c0=c0, n=n, o0=o0, tt=tt, pzz=pzz: nc.tensor.matmul(
                                pzz[:, o0:o0 + n], lhsT=hT[:, kk, tt * 128:(tt + 1) * 128], rhs=win[:, kk, c0:c0 + n],
                                start=(kk == 0 and o0 == 0), stop=(kk == 7 and o0 == 128)),
                                reads=[win_b, hT_b[tt][kk]], writes=[pzb])
                    if use_dve:
                        k.op(DVE, lambda tt=tt, pzz=pzz: nc.vector.tensor_copy(out=vs_aug[:, tt, :, 0:64], in_=pzz[:, 0:128].rearrange("p (g d) -> p g d", g=2)), reads=[pzb], writes=[vtok_b[tt]])
                        k.op(DVE, lambda tt=tt, pzz=pzz: nc.vector.tensor_copy(out=vw_aug[:, tt, :, 0:64], in_=pzz[:, 128:256].rearrange("p (g d) -> p g d", g=2)), reads=[pzb], writes=[vtok_b[tt]])
                        k.op(DVE, lambda tt=tt, pzz=pzz: nc.vector.tensor_copy(out=gates[:, tt, :], in_=pzz[:, 256:280]), reads=[pzb], writes=[vtok_b[tt]])
                    else:
                        k.op(ACT, lambda tt=tt, pzz=pzz: nc.scalar.activation(out=vs_aug[:, tt, :, 0:64], in_=pzz[:, 0:128].rearrange("p (g d) -> p g d", g=2), func=AF.Copy), reads=[pzb], writes=[vtok_b[tt]])
                        k.op(ACT, lambda tt=tt, pzz=pzz: nc.scalar.activation(out=vw_aug[:, tt, :, 0:64], in_=pzz[:, 128:256].rearrange("p (g d) -> p g d", g=2), func=AF.Copy), reads=[pzb], writes=[vtok_b[tt]])
                        k.op(ACT, lambda tt=tt, pzz=pzz: nc.scalar.activation(out=gates[:, tt, :], in_=pzz[:, 256:280], func=AF.Copy), reads=[pzb], writes=[vtok_b[tt]])
            k.op(ACT, lambda: nc.scalar.activation(out=gates[:], in_=gates[:], func=AF.Sigmoid), reads=vtok_b, writes=vtok_b)

        if upto <= 1:
            k.drain()
            return nc
        k.barrier()
        with ExitStack() as p2:
            cw = sb("cw", [128, 4, 4], F32, p2); cb_ = sb("cb", [128, 4], F32, p2)
            bat = sb("bat", [128, 4], F32, p2); bxt = sb("bxt", [128, 4], F32, p2)
            lam = sb("lam", [128, 4], F32, p2); cc1 = sb("cc1", [128, 4], F32, p2); cc2 = sb("cc2", [128, 4], F32, p2)
            prm_b = Buf("rnnprm")
            pds = k.dsem("rp")
            for (t, d) in ((cw, convw_d), (cb_, convb_d), (bat, ba_d), (bxt, bx_d), (lam, lam_d)):
                k.dma(SP, pds, t[:], d, writes=[prm_b])
            k.op(ACT, lambda: nc.scalar.activation(out=cc1[:], in_=lam[:], func=AF.Exp, scale=-1.0), reads=[prm_b], writes=[prm_b])
            k.op(ACT, lambda: nc.scalar.activation(out=cc1[:], in_=cc1[:], func=AF.Ln, bias=1.0), reads=[prm_b], writes=[prm_b])
            k.op(DVE, lambda: nc.vector.tensor_scalar(out=cc2[:], in0=cc1[:], scalar1=-16.0, scalar2=None, op0=ALU.mult), reads=[prm_b], writes=[prm_b])
            k.op(DVE, lambda: nc.vector.tensor_scalar(out=cc1[:], in0=cc1[:], scalar1=-8.0, scalar2=None, op0=ALU.mult), reads=[prm_b], writes=[prm_b])
            wbd_a = sb("wbd_a", [128, 4, 128], BF16, p2); wbd_x = sb("wbd_x", [128, 4, 128], BF16, p2)
            wbd_b = Buf("wbd")
            k.op(DVE, lambda: nc.vector.memset(wbd_a[:], 0.0), writes=[wbd_b])
            k.op(DVE, lambda: nc.vector.memset(wbd_x[:], 0.0), writes=[wbd_b])
            for (t, d) in ((wbd_a, wa_d), (wbd_x, wx_d)):
                dv = d.rearrange("(f two) i j -> two i f j", two=2)
                for hh in range(2):
                    k.dma(POOL, k.dsem("wbd"), t[hh * 64:(hh + 1) * 64, :, hh * 64:(hh + 1) * 64], dv[hh], writes=[wbd_b])
            NP = 4
            PW = S // NP
            xrR = Rot(k, p2, nc, "xr", 2, [128, S], F32)
            ggR = Rot(k, p2, nc, "gg", 2, [128, S], F32)
            xc = sb("xc", [128, S], F32, p2); xcb = sb("xcb", [128, S], BF16, p2)
            rr = sb("rr", [128, S], F32, p2); ii = sb("ii", [128, S], F32, p2); a2 = sb("a2", [128, S], F32, p2)
            hh_ = sb("hh", [128, S], F32, p2)
            rnb = sb("rnb", [128, S], BF16, p2); sqb = sb("sqb", [128, S], BF16, p2)
            xc_b, xcb_b, rr_b, ii_b, a2_b, hh_b, rnb_b, sqb_b = ([Buf(f"{n}{p}") for p in range(NP)] for n in ("xc", "xcb", "rr", "ii", "a2", "hh", "rnb", "sqb"))
            rn_ds = [k.dsem("rn") for _ in range(2)]
            pg = [pst(f"pg{i}", [128, 512], F32, p2) for i in range(3)]
            pg_b = [Buf(f"pg{i}") for i in range(3)]
            pstat = pst("pstat", [128, 512], F32, p2)
            pstat_b = Buf("pstat")
            gi = [0]
            gg_parts = [[Buf(f"ggs{sl}_{p}") for p in range(NP)] for sl in range(2)]

            def load_ft(ft):
                xr, xr_b, xr_ds = xrR.next()
                gg, _, gg_ds = ggR.next()
                gg_bp = gg_parts[ft % 2]
                k.dma(SP, xr_ds, xr[:], zr_d[512 + ft * 128:512 + (ft + 1) * 128, :], reads=[DB((id(zr_d), 512 + ft * 128, ch)) for ch in range(NCH)], writes=[xr_b])
                k.dma(SP, gg_ds, gg[:], zr_d[ft * 128:(ft + 1) * 128, :], reads=[DB((id(zr_d), ft * 128, ch)) for ch in range(NCH)], writes=gg_bp)
                return xr, xr_b, gg, gg_bp

            def cs(p):
                return p * PW, (p + 1) * PW

            def front(ft, xr, xr_b, gg, gg_b):
                for p in range(NP):
                    c0, c1 = cs(p)
                    k.op(DVE, lambda c0=c0, c1=c1: nc.vector.tensor_scalar(out=xc[:, c0:c1], in0=xr[:, c0:c1], scalar1=cw[:, ft, 3:4], scalar2=cb_[:, ft:ft + 1], op0=ALU.mult, op1=ALU.add),
                         reads=[xr_b, prm_b], writes=[xc_b[p]])
                    for sh in (1, 2, 3):
                        lo = max(c0, sh)
                        k.op(DVE, lambda lo=lo, c1=c1, sh=sh: nc.vector.scalar_tensor_tensor(out=xc[:, lo:c1], in0=xr[:, lo - sh:c1 - sh], scalar=cw[:, ft, 3 - sh:4 - sh],
                                                                                            in1=xc[:, lo:c1], op0=ALU.mult, op1=ALU.add),
                             reads=[xr_b, prm_b, xc_b[p]], writes=[xc_b[p]])
                    k.op(ACT, lambda c0=c0, c1=c1: nc.scalar.activation(out=xcb[:, c0:c1], in_=xc[:, c0:c1], func=AF.Copy), reads=[xc_b[p]], writes=[xcb_b[p]])
                    k.op(ACT, lambda c0=c0, c1=c1: nc.scalar.activation(out=gg[:, c0:c1], in_=gg[:, c0:c1], func=AF.Gelu_apprx_tanh), reads=[gg_b[p]], writes=[gg_b[p]])

            def rnn_gates(ft):
                for p in range(NP):
                    c0, c1 = cs(p)
                    for (wt, bt, dst, dstb) in ((wbd_a, bat, rr, rr_b), (wbd_x, bxt, ii, ii_b)):
                        for cc in range(c0, c1, 512):
                            pgg = pg[gi[0] % 3]; pgb = pg_b[gi[0] % 3]; gi[0] += 1
                            k.op(PE, lambda wt=wt, cc=cc, pgg=pgg: nc.tensor.matmul(pgg[:, :], lhsT=wt[:, ft, :], rhs=xcb[:, cc:cc + 512], start=True, stop=True),
                                 reads=[wbd_b, xcb_b[p]], writes=[pgb])
                            k.op(ACT, lambda bt=bt, cc=cc, pgg=pgg, dst=dst: nc.scalar.activation(out=dst[:, cc:cc + 512], in_=pgg[:, :], func=AF.Sigmoid, bias=bt[:, ft:ft + 1]),
                                 reads=[pgb, prm_b], writes=[dstb[p]])

            def exps_m(ft):
                for p in range(NP):
                    c0, c1 = cs(p)
                    k.op(ACT, lambda c0=c0, c1=c1: nc.scalar.activation(out=a2[:, c0:c1], in_=rr[:, c0:c1], func=AF.Exp, scale=cc2[:, ft:ft + 1]), reads=[rr_b[p], prm_b], writes=[a2_b[p]])
                    k.op(ACT, lambda c0=c0, c1=c1: nc.scalar.activation(out=rr[:, c0:c1], in_=rr[:, c0:c1], func=AF.Exp, scale=cc1[:, ft:ft + 1]), reads=[rr_b[p], prm_b], writes=[rr_b[p]])
                    k.op(DVE, lambda c0=c0, c1=c1: nc.vector.tensor_tensor(out=ii[:, c0:c1], in0=ii[:, c0:c1], in1=xc[:, c0:c1], op=ALU.mult), reads=[ii_b[p], xc_b[p]], writes=[ii_b[p]])

            def sqrts(ft):
                for p in range(NP):
                    c0, c1 = cs(p)
                    k.op(ACT, lambda c0=c0, c1=c1: nc.scalar.activation(out=a2[:, c0:c1], in_=a2[:, c0:c1], func=AF.Sqrt, scale=-1.0, bias=1.0), reads=[a2_b[p]], writes=[a2_b[p]])

            def tail(ft, gg, gg_b):
                for p in range(NP):
                    c0, c1 = cs(p)
                    k.op(DVE, lambda c0=c0, c1=c1: nc.vector.tensor_tensor(out=a2[:, c0:c1], in0=a2[:, c0:c1], in1=ii[:, c0:c1], op=ALU.mult), reads=[a2_b[p], ii_b[p]], writes=[a2_b[p]])
                    init = 0.0 if p == 0 else hh_[:, c0 - 1:c0]
                    k.op(DVE, lambda c0=c0, c1=c1, init=init: nc.vector.tensor_tensor_scan(out=hh_[:, c0:c1], data0=rr[:, c0:c1], data1=a2[:, c0:c1], initial=init, op0=ALU.mult, op1=ALU.add),
                         reads=[rr_b[p], a2_b[p]] + ([hh_b[p - 1]] if p else []), writes=[hh_b[p]])
                    k.op(DVE, lambda c0=c0, c1=c1: nc.vector.tensor_tensor(out=rnb[:, c0:c1], in0=gg[:, c0:c1], in1=hh_[:, c0:c1], op=ALU.mult), reads=[gg_b[p], hh_b[p]], writes=[rnb_b[p]])
                    k.op(DVE, lambda c0=c0, c1=c1: nc.vector.tensor_tensor(out=sqb[:, c0:c1], in0=rnb[:, c0:c1], in1=rnb[:, c0:c1], op=ALU.mult), reads=[rnb_b[p]], writes=[sqb_b[p]])
                    for cc in range(c0, c1, 512):
                        ch = cc // 512
                        k.dma(POOL, rn_ds[ch % 2], rnn_d[ft * 128:(ft + 1) * 128, cc:cc + 512], rnb[:, cc:cc + 512], reads=[rnb_b[p]], writes=[DB(("rnn", ch))])
                    for tt in range(c0 // 128, c1 // 128):
                        k.op(PE, lambda tt=tt: nc.tensor.matmul(pstat[:, tt * 4 + ft: tt * 4 + ft + 1], lhsT=sqb[:, tt * 128:(tt + 1) * 128], rhs=ones_bf[:, 0:1], start=True, stop=True),
                             reads=[sqb_b[p], ones_b], writes=[pstat_b])

            cur_ft = load_ft(0)
            front(0, *cur_ft)
            for ft in range(4):
                xr, xr_b, gg, gg_b = cur_ft
                if ft + 1 < 4:
                    nxt_ft = load_ft(ft + 1)
                rnn_gates(ft)
                exps_m(ft)
                sqrts(ft)
                if ft + 1 < 4:
                    front(ft + 1, *nxt_ft)
                tail(ft, gg, gg_b)
                if ft + 1 < 4:
                    cur_ft = nxt_ft
            k.op(DVE, lambda: nc.vector.tensor_reduce(out=ssr[:], in_=pstat[:, 0:NT * 4].rearrange("p (t f) -> p t f", f=4), axis=mybir.AxisListType.X, op=ALU.add),
                 reads=[pstat_b], writes=[ssr_b])
            k.op(ACT, lambda: nc.scalar.activation(out=ssq[:], in_=ssr[:], func=AF.Sqrt, scale=1.0 / 512, bias=EPS), reads=[ssr_b], writes=[ssr_b])
            k.op(DVE, lambda: nc.vector.reciprocal(out=ssr[:], in_=ssq[:]), reads=[ssr_b], writes=[ssr_b])

        if upto <= 2:
            k.drain()
            return nc
        k.barrier()
        with ExitStack() as p4:
            kcT = [sb(f"kcT{g}", [128, 256], BF16, p4) for g in range(2)]
            cllo = sb("cllo", [128, 2, 8], F32, p4)
            vc_aug = [sb(f"vca{g}", [128, 2, 128], BF16, p4) for g in range(2)]
            cmp_b = Buf("cmp")
            cds = k.dsem("cmpc")
            k.dma(SP, cds, cllo[:], cllo_d, writes=[cmp_b])
            for g in range(2):
                k.dma(SP, cds, kcT[g][64:128, :], ce01_d, writes=[cmp_b])
                k.dma(SP, cds, vc_aug[g][:, :, 65:128], ovl_d, writes=[cmp_b])
                k.op(DVE, lambda g=g: nc.vector.memset(vc_aug[g][:, :, 64:65], 1.0), writes=[cmp_b])
                k.op(DVE, lambda g=g: nc.vector.memset(vc_aug[g][:, :, 0:64], 0.0), writes=[cmp_b])
            kTs = [sb(f"kTs{g}", [128, S], BF16, p4) for g in range(2)]
            kTw = [sb(f"kTw{g}", [128, S], BF16, p4) for g in range(2)]
            kds_shared = k.dsem("kT")
            cds_shared = k.dsem("cst")
            kTs_b = [Buf(f"kTs{g}") for g in range(2)]
            kTw_b = [Buf(f"kTw{g}") for g in range(2)]
            for g in range(2):
                for (t, tb_, src) in ((kTs[g], kTs_b[g], ks_d), (kTw[g], kTw_b[g], kw_d)):
                    kds = kds_shared
                    k.dma(SP, kds, t[0:64, :], src[g * 64:(g + 1) * 64, :], reads=[DB((id(src), 0, ch)) for ch in range(NCH)], writes=[tb_])
                    k.dma(SP, kds, t[64:128, :], e01_d, writes=[tb_])
            cmask = sb("cmask", [128, 2, S], BF16, p4)
            tab = sb("tab", [128, S], BF16, p4)
            sllo = sb("sllo", [128, 8], F32, p4)
            wm01 = sb("wm01", [128, 8, 512], BF16, p4)
            addm = sb("addm", [128, NT, 64], F32, p4)
            cmask_b, tab_b, sllo_b, wm01_b, addm_b = (Buf(n) for n in ("cmask", "tab", "sllo", "wm01", "addm"))
            k.dma(SP, cds_shared, cmask[:], cmask_d, writes=[cmask_b])
            k.dma(SP, cds_shared, tab[64:128, :], tab_d, writes=[tab_b])
            k.dma(SP, cds_shared, sllo[:], sllo_d, writes=[sllo_b])
            k.dma(SP, cds_shared, wm01[:], wm01_d, writes=[wm01_b])
            k.dma(SP, cds_shared, addm[:], addm_d, writes=[addm_b])
            wout = sb("wout", [128, 8, D], BF16, p4)
            wout_b = Buf("wout")
            gcol = sb("gcol", [128, 8], F32, p4)
            gcol_b = Buf("gcol")
            gds = k.dsem("gcol")
            k.dma(SP, gds, gcol[:, 0:4], grnn_d, writes=[gcol_b])
            k.dma(SP, gds, gcol[:, 4:8], gatt_d, writes=[gcol_b])
            pw = ExitStack()
            if True:
                wst = Rot(k, pw, nc, "wst", 2, [128, D], F32)
                for kk in range(8):
                    t, tb, ds = wst.next()
                    k.dma(SP, ds, t[:], wout_d[kk * 128:(kk + 1) * 128, :], writes=[tb])
                    k.op(DVE, lambda kk=kk, t=t: nc.vector.tensor_scalar(out=wout[:, kk, :], in0=t[:], scalar1=gcol[:, kk:kk + 1], scalar2=None, op0=ALU.mult),
                         reads=[tb, gcol_b], writes=[wout_b])
            with ExitStack() as p3:
                w1 = [sb(f"w1_{i}", [64, 32, 256], BF16, p3) for i in range(2)]
                w2 = [sb(f"w2_{i}", [128, 2, 64], BF16, p3) for i in range(2)]
                pos = [sb(f"pos_{i}", [64, 32], BF16, p3) for i in range(2)]
                posf = [sb(f"posf_{i}", [64, 32], F32, p3) for i in range(2)]
                cw_b = Buf("cmpw")
                cw1_b = [[Buf(f"cw1_{i}_{q}") for q in range(4)] for i in range(2)]
                cwd = k.dsem("cmpw")
                cwd2 = k.dsem("cmpp")
                for i, (w1d, w2d, pd) in enumerate(((w1k_d, w2k_d, posk_d), (w1v_d, w2v_d, posv_d))):
                    w1v = w1d.rearrange("(l d) h -> d l h", d=64)
                    for l0 in range(0, 32, 8):
                        k.dma(POOL, k.dsem("w1"), w1[i][:, l0:l0 + 8, :], w1v[:, l0:l0 + 8, :], writes=[cw1_b[i][l0 // 8]])
                    k.dma(POOL, cwd, w2[i][:], w2d.rearrange("(t p) d -> p t d", p=128), writes=[cw_b])
                    k.dma(SP, cwd2, posf[i][:], pd, writes=[cw_b])
                    k.op(DVE, lambda i=i: nc.vector.tensor_copy(out=pos[i][:], in_=posf[i][:]), reads=[cw_b], writes=[cw_b])
                cin = [[sb(f"cin{i}{g}", [64, S], BF16, p3) for g in range(2)] for i in range(2)]
                cin_bs = [[Buf(f"cin{i}{g}") for g in range(2)] for i in range(2)]
                for i, src in enumerate((kc_d, vc_d)):
                    cin_ds = k.dsem("cin")
                    for g in range(2):
                        k.dma(SP, cin_ds, cin[i][g][:], src[g * 64:(g + 1) * 64, :], reads=[DB((id(src), 0, ch)) for ch in range(NCH)], writes=[cin_bs[i][g]])
                hid = sb("hid", [128, 2, 256], BF16, p3)
                hid_b = Buf("hid")
                cbias = sb("cbias", [128, 2, 2], F32, p3)
                cbias_b = Buf("cbias")
                pc_ = [pst(f"pc{i}", [128, 512], F32, p3) for i in range(3)]
                pc_b = [Buf(f"pc{i}") for i in range(3)]
                ci = 0
                for i in range(2):
                    for ht in range(2):
                        pcc = pc_[ci % 3]; pcb = pc_b[ci % 3]; ci += 1
                        for l in range(32):
                            k.op(PE, lambda i=i, ht=ht, l=l, pcc=pcc: nc.tensor.matmul(pcc[:, 0:1], lhsT=w1[i][:, l, ht * 128:(ht + 1) * 128], rhs=pos[i][:, l:l + 1],
                                                                                      start=(l == 0), stop=(l == 31)), reads=[cw_b, cw1_b[i][l // 8]], writes=[pcb])
                        k.op(DVE, lambda i=i, ht=ht, pcc=pcc: nc.vector.tensor_copy(out=cbias[:, i, ht:ht + 1], in_=pcc[:, 0:1]), reads=[pcb], writes=[cbias_b])
                for i in range(2):
                    for g in range(2):
                        cv = cin[i][g][:].rearrange("p (c r) -> p c r", r=16)
                        for ht in range(2):
                            pcc = pc_[ci % 3]; pcb = pc_b[ci % 3]; ci += 1
                            for l in range(32):
                                rhs = cv[:, 0:255, l] if l < 16 else cv[:, 1:256, l - 16]
                                k.op(PE, lambda i=i, ht=ht, l=l, pcc=pcc, rhs=rhs: nc.tensor.matmul(pcc[:, 0:255], lhsT=w1[i][:, l, ht * 128:(ht + 1) * 128], rhs=rhs,
                                                                                                   start=(l == 0), stop=(l == 31)), reads=[cw_b, cw1_b[i][l // 8], cin_bs[i][g]], writes=[pcb])
                            k.op(ACT, lambda i=i, ht=ht, pcc=pcc: nc.scalar.activation(out=hid[:, ht, 0:255], in_=pcc[:, 0:255], func=AF.Gelu_apprx_tanh, bias=cbias[:, i, ht:ht + 1]),
                                 reads=[pcb, cbias_b], writes=[hid_b])
                        if i == 0:
                            pcc = pc_[ci % 3]; pcb = pc_b[ci % 3]; ci += 1
                            for ht in range(2):
                                k.op(PE, lambda ht=ht, pcc=pcc: nc.tensor.matmul(pcc[0:64, 0:255], lhsT=w2[0][:, ht, :], rhs=hid[:, ht, 0:255], start=(ht == 0), stop=(ht == 1)),
                                     reads=[cw_b, hid_b], writes=[pcb])
                            k.op(DVE, lambda g=g, pcc=pcc: nc.vector.tensor_copy(out=kcT[g][0:64, 0:255], in_=pcc[0:64, 0:255]), reads=[pcb], writes=[cmp_b])
                        else:
                            for ct in range(2):
                                nr = 128 if ct == 0 else 127
                                pcc = pc_[ci % 3]; pcb = pc_b[ci % 3]; ci += 1
                                for ht in range(2):
                                    k.op(PE, lambda ht=ht, ct=ct, nr=nr, pcc=pcc: nc.tensor.matmul(pcc[0:nr, 0:64], lhsT=hid[:, ht, ct * 128:ct * 128 + nr], rhs=w2[1][:, ht, :],
                                                                                                  start=(ht == 0), stop=(ht == 1)), reads=[cw_b, hid_b], writes=[pcb])
                                k.op(DVE, lambda g=g, ct=ct, nr=nr, pcc=pcc: nc.vector.tensor_copy(out=vc_aug[g][0:nr, ct, 0:64], in_=pcc[0:nr, 0:64]), reads=[pcb], writes=[cmp_b])

            if upto <= 3:
                pw.close()
                k.drain()
                return nc
            k.barrier()
            pw.close()
            qs = Rot(k, p4, nc, "qs", 1, [128, 8, 512], BF16)
            qw = Rot(k, p4, nc, "qw", 2, [128, 8, 512], BF16)
            SLOPES = [2.0 ** (-(h + 1)) for h in range(8)]
            PT = Rot(k, p4, nc, "PT", 6, [128, 512], BF16, with_dsem=False)
            PTc = Rot(k, p4, nc, "PTc", 2, [128, 512], BF16, with_dsem=False)
            att = sb("att", [128, 4, 512], F32, p4)
            att_b = [Buf(f"att{i}") for i in range(4)]
            attb = sb("attb", [128, 4, 512], BF16, p4)
            attb_b = Buf("attb")
            attT = sb("attT", [128, 4, 512], BF16, p4)
            attT_b = Buf("attT")
            rnT = Rot(k, p4, nc, "rnT", 3, [128, 4, 512], BF16)
            imp = sb("imp", [128, 4, 2, 64], F32, p4)
            imp_b = [Buf(f"imp{i}") for i in range(4)]
            for i_ in range(4):
                k.op(DVE, lambda i_=i_: nc.vector.memset(imp[:, i_, :, :], 0.0), writes=[imp_b[i_]])
            selbT = [sb(f"selbT{g}", [128, 512], BF16, p4) for g in range(2)]
            selbT_b = [Buf(f"selbT{g}") for g in range(2)]
            sm = sb("sm", [128, 256], F32, p4)
            sm_b = [Buf(f"sm{i}") for i in range(16)]
            smi = [0]
            tk_a = sb("tk_a", [128, 8, 64], F32, p4); tk_b_ = sb("tk_b", [128, 8, 64], F32, p4)
            tk8 = sb("tk8", [128, 8, 16], F32, p4); tksel = sb("tksel", [128, 8, 64], BF16, p4)
            tk_bufs = [Buf(f"tk{u}") for u in range(8)]
            ssa = sb("ssa", [128, 8], F32, p4)
            ssa_b = Buf("ssa")
            ysb = Rot(k, p4, nc, "ysb", 2, [128, D], F32)
            xres = Rot(k, p4, nc, "xres", 2, [128, D], F32)
            junk4 = sb("junk4", [128, D], BF16, p4)
            junk4_b = Buf("junk4")
            print("P4 sbuf bytes remaining:", nc.sbuf_bytes_remaining)
            k.barrier()
            pS = [pst(f"pS{i}", [128, 512], F32, p4) for i in range(3)]
            pS_b = [Buf(f"pS{i}") for i in range(3)]
            pA = [pst(f"pA{i}", [128, 512], F32, p4) for i in range(2)]
            pA_b = [Buf(f"pA{i}") for i in range(2)]
            pTT = pst("pTT", [128, 1024], BF16, p4)
            pTT_b = Buf("pTT")
            pW = [pst(f"pW{i}", [128, 512], F32, p4) for i in range(1)]
            pW_b = [Buf(f"pW{i}") for i in range(1)]
            pC = pst("pC", [128, 512], F32, p4)
            pC_b = Buf("pC")
            cnt = {"s": 0, "a": 0}

            def nextS():
                i = cnt["s"] % 3; cnt["s"] += 1
                return pS[i], pS_b[i]

            def nextA():
                i = cnt["a"] % 2; cnt["a"] += 1
                return pA[i], pA_b[i]

            def small():
                i = smi[0] % 16; smi[0] += 1
                return sm[:, 16 * i:16 * i + 16], sm_b[i]

            att_written = set()

            def evac_group(pa, pab, stride, n, h, tl0, tt0, gate_idx, first, with_imp=False, g=None, first_in_group=False):
                s4, s4b = small()
                sums = pa[:, 64:64 + (n - 1) * stride + 1:stride]
                k.op(DVE, lambda: nc.vector.tensor_scalar(out=s4[:, 0:n], in0=sums, scalar1=1e-30, scalar2=None, op0=ALU.max), reads=[pab], writes=[s4b])
                k.op(DVE, lambda: nc.vector.reciprocal(out=s4[:, 4:4 + n], in_=s4[:, 0:n]), reads=[s4b], writes=[s4b])
                k.op(DVE, lambda: nc.vector.tensor_tensor(out=s4[:, 8:8 + n], in0=s4[:, 4:4 + n], in1=gates[:, tt0:tt0 + n, gate_idx], op=ALU.mult),
                     reads=[s4b] + [vtok_b[tt0 + i] for i in range(n)], writes=[s4b])
                for i in range(n):
                    tl = tl0 + i
                    dst = att[:, tl, h * 64:(h + 1) * 64]
                    src = pa[:, i * stride:i * stride + 64]
                    first = (h, tl) not in att_written
                    att_written.add((h, tl))
                    if first:
                        k.op(DVE, lambda dst=dst, src=src, i=i: nc.vector.tensor_scalar(out=dst, in0=src, scalar1=s4[:, 8 + i:9 + i], scalar2=None, op0=ALU.mult),
                             reads=[pab, s4b], writes=[att_b[tl]])
                    else:
                        k.op(DVE, lambda dst=dst, src=src, i=i: nc.vector.scalar_tensor_tensor(out=dst, in0=src, scalar=s4[:, 8 + i:9 + i], in1=dst, op0=ALU.mult, op1=ALU.add),
                             reads=[pab, s4b, att_b[tl]], writes=[att_b[tl]])
                    if with_imp:
                        idst = imp[:, tl, g, 1:64]
                        isrc = pa[:, i * stride + 65:i * stride + 128]
                        if first_in_group:
                            k.op(DVE, lambda idst=idst, isrc=isrc, i=i: nc.vector.tensor_scalar(out=idst, in0=isrc, scalar1=s4[:, 4 + i:5 + i], scalar2=None, op0=ALU.mult),
                                 reads=[pab, s4b], writes=[imp_b[tl]])
                        else:
                            k.op(DVE, lambda idst=idst, isrc=isrc, i=i: nc.vector.scalar_tensor_tensor(out=idst, in0=isrc, scalar=s4[:, 4 + i:5 + i], in1=idst, op0=ALU.mult, op1=ALU.add),
                                 reads=[pab, s4b, imp_b[tl]], writes=[imp_b[tl]])

            dbg_ds = k.dsem("dbg")

            qview = q_d.rearrange("(h d) t -> d h t", d=64)

            def load_chunk(jn):
                tn = jn * 512
                rt_, rtb_, rds_ = rnT.next()
                k.dma(SP, rds_, rt_[:], rnn_d[:, tn:tn + 512].rearrange("(f p) t -> p f t", p=128), reads=[DB(("rnn", jn))], writes=[rtb_])
                qw_, qwb_, qwd_ = qw.next()
                k.dma(SP, qwd_, qw_[0:64, :, :], qview[:, :, tn:tn + 512], reads=[DB((id(q_d), c4 * 128, jn)) for c4 in range(4)], writes=[qwb_])
                for h_ in range(8):
                    k.op(DVE, lambda h_=h_, qw_=qw_: nc.vector.tensor_scalar(out=qw_[64:128, h_, :], in0=tab[64:128, tn:tn + 512], scalar1=SLOPES[h_], scalar2=None, op0=ALU.mult),
                         reads=[tab_b], writes=[qwb_])
                return rt_, rtb_, qw_, qwb_

            def load_qs(jn):
                tn = jn * 512
                qs_, qsb_, qsd_ = qs.next()
                k.dma(SP, qsd_, qs_[0:64, :, :], qview[:, :, tn:tn + 512], reads=[DB((id(q_d), c4 * 128, jn)) for c4 in range(4)], writes=[qsb_])
                return qs_, qsb_

            nxt = load_chunk(0)
            nxt_qs = load_qs(0)
            wcnt = [0]

            def nextW():
                return pW[0], pW_b[0]

            def make_cback(jc, rt, rtb):
                units = []

                def u_tr(half):
                    for tl2 in range(2):
                        tl = half * 2 + tl2
                        for f in range(4):
                            k.op(PE, lambda tl=tl, tl2=tl2, f=f: nc.tensor.transpose(out=pTT[:, (tl2 * 4 + f) * 128:(tl2 * 4 + f + 1) * 128], in_=attb[:, tl, f * 128:(f + 1) * 128], identity=ident[:]),
                                 reads=[attb_b, ident_b], writes=[pTT_b])
                    for tl2 in range(2):
                        tl = half * 2 + tl2
                        k.op(DVE, lambda tl=tl, tl2=tl2: nc.vector.tensor_copy(out=attT[:, :, tl * 128:(tl + 1) * 128],
                                                                             in_=pTT[:, tl2 * 512:(tl2 + 1) * 512].rearrange("p (f t) -> p f t", f=4)),
                             reads=[pTT_b], writes=[attT_b])
                units.append(lambda: u_tr(0))
                units.append(lambda: u_tr(1))
                st = {}

                def u_mm(tl, half, part):
                    tt = jc * 4 + tl
                    if half == 0 and part == 0:
                        yt, ytb, yds = ysb.next()
                        xr_, xrb, xds = xres.next()
                        k.dma(SP, xds, xr_[:], x_d[tt * 128:(tt + 1) * 128, :], writes=[xrb])
                        st[tl] = (yt, ytb, yds, xr_, xrb)
                    yt, ytb, yds, xr_, xrb = st[tl]
                    if part == 0:
                        st[(tl, half)] = nextW()
                    pw, pwb = st[(tl, half)]
                    if part == 0:
                        for f in range(4):
                            k.op(PE, lambda f=f: nc.tensor.matmul(pw[:, :], lhsT=rt[:, f, tl * 128:(tl + 1) * 128], rhs=wout[:, f, half * 512:(half + 1) * 512],
                                                                  start=(f == 0), stop=False), reads=[rtb, wout_b], writes=[pwb])
                    else:
                        for f in range(4):
                            k.op(PE, lambda f=f: nc.tensor.matmul(pw[:, :], lhsT=attT[:, f, tl * 128:(tl + 1) * 128], rhs=wout[:, 4 + f, half * 512:(half + 1) * 512],
                                                                  start=False, stop=(f == 3)), reads=[attT_b, wout_b], writes=[pwb])
                        k.op(DVE, lambda: nc.vector.tensor_scalar(out=yt[:, half * 512:(half + 1) * 512], in0=pw[:, :], scalar1=ssr[:, tt:tt + 1], scalar2=None, op0=ALU.mult),
                             reads=[pwb, ssr_b], writes=[ytb])

                def u_epi(tl):
                    tt = jc * 4 + tl
                    yt, ytb, yds, xr_, xrb = st[tl]
                    if debug:
                        k.dma(POOL, dbg_ds, dbg["d_y"][tt * 128:(tt + 1) * 128, :], yt[:], reads=[ytb], writes=[DB(("dy", tt))])
                    s4, s4b = small()
                    k.op(ACT, lambda: nc.scalar.activation(out=junk4[:], in_=yt[:], func=AF.Square, accum_out=s4[:, 0:1]), reads=[ytb], writes=[junk4_b, s4b])
                    k.op(ACT, lambda: nc.scalar.activation(out=s4[:, 1:2], in_=s4[:, 0:1], func=AF.Ln, scale=1.0 / D, bias=EPS), reads=[s4b], writes=[s4b])
                    k.op(ACT, lambda: nc.scalar.activation(out=s4[:, 2:3], in_=s4[:, 1:2], func=AF.Exp, scale=-0.5), reads=[s4b], writes=[s4b])
                    k.op(DVE, lambda: nc.vector.scalar_tensor_tensor(out=yt[:], in0=yt[:], scalar=s4[:, 2:3], in1=C1row[:], op0=ALU.mult, op1=ALU.mult),
                         reads=[ytb, s4b, C1_b], writes=[ytb])
                    k.op(DVE, lambda: nc.vector.tensor_tensor(out=yt[:], in0=yt[:], in1=xr_[:], op=ALU.add), reads=[ytb, xrb], writes=[ytb])
                    k.dma(POOL, yds, x1_d[tt * 128:(tt + 1) * 128, :], yt[:], reads=[ytb], writes=[DB(("x1", tt))])

                for tl in range(4):
                    for half in range(2):
                        for part in range(2):
                            units.append(lambda tl=tl, half=half, part=part: u_mm(tl, half, part))
                    units.append(lambda tl=tl: u_epi(tl))
                return units

            cback = []
            for j in range(NCH):
                t0 = j * 512
                att_written.clear()
                rt, rtb, qwt, qwb = nxt
                qst, qsb = nxt_qs
                if j + 1 < NCH:
                    nxt = load_chunk(j + 1)
                ncts = [ct for ct in range(2) if 16 * (ct * 128) + 31 <= t0 + 511]

                def cmp_S(h):
                    g = h // 4
                    pts = []
                    for ct in ncts:
                        nr = 128 if ct == 0 else 127
                        ps, psb = nextS()
                        k.op(PE, lambda ps=ps, ct=ct, nr=nr, g=g, h=h: nc.tensor.matmul(ps[0:nr, :], lhsT=kcT[g][:, ct * 128:ct * 128 + nr], rhs=qwt[:, h, :], start=True, stop=False),
                             reads=[cmp_b, qwb], writes=[psb])
                        k.op(PE, lambda ps=ps, ct=ct, nr=nr: nc.tensor.matmul(ps[0:nr, :], lhsT=ident[:, 0:nr], rhs=cmask[:, ct, t0:t0 + 512], start=False, stop=True),
                             reads=[ident_b, cmask_b], writes=[psb])
                        pt, ptb, _ = PTc.next()
                        k.op(ACT, lambda ps=ps, pt=pt, nr=nr, ct=ct, h=h: nc.scalar.activation(out=pt[0:nr, :], in_=ps[0:nr, :], func=AF.Exp, bias=cllo[0:nr, ct, h:h + 1]),
                             reads=[psb, cmp_b], writes=[ptb])
                        pts.append((pt, ptb, ct, nr))
                    return pts

                def cmp_PV(h, pts):
                    g = h // 4
                    nmm = 4 * len(pts)
                    mi = 0
                    for tl in range(4):
                        for (pt, ptb, ct, nr) in pts:
                            k.op(PE, lambda pt=pt, tl=tl, nr=nr, ct=ct, g=g, mi=mi, nmm=nmm: nc.tensor.matmul(
                                pC[:, tl * 128:(tl + 1) * 128], lhsT=pt[0:nr, tl * 128:(tl + 1) * 128], rhs=vc_aug[g][0:nr, ct, :],
                                start=(mi == 0), stop=(mi == nmm - 1)),
                                reads=[ptb, cmp_b], writes=[pC_b])
                            mi += 1
                    evac_group(pC, pC_b, 128, 4, h, 0, j * 4, 0 * 8 + h, True, with_imp=True, g=g, first_in_group=(h % 4 == 0))

                pre = []
                cst = {}

                def u_cS(h):
                    cst[h] = cmp_S(h)

                def u_cPV(h):
                    cmp_PV(h, cst[h])
                for h in range(8):
                    pre.append(lambda h=h: u_cS(h))
                    pre.append(lambda h=h: u_cPV(h))
                units8 = [(tl, g) for tl in range(4) for g in range(2)]

                def tk_stage(sidx):
                    for u, (tl, g) in enumerate(units8):
                        tt = j * 4 + tl
                        tb = tk_bufs[u]
                        if sidx == 0:
                            k.op(DVE, lambda u=u, tl=tl, g=g, tt=tt: nc.vector.tensor_tensor(out=tk_a[:, u, :], in0=imp[:, tl, g, :], in1=addm[:, tt, :], op=ALU.add),
                                 reads=[imp_b[tl], addm_b], writes=[tb])
                        elif sidx == 1:
                            k.op(DVE, lambda u=u: nc.vector.max(out=tk8[:, u, 0:8], in_=tk_a[:, u, :]), reads=[tb], writes=[tb])
                        elif sidx == 2:
                            k.op(DVE, lambda u=u: nc.vector.match_replace(out=tk_b_[:, u, :], in_to_replace=tk8[:, u, 0:8], in_values=tk_a[:, u, :], imm_value=-3.0e38), reads=[tb], writes=[tb])
                        elif sidx == 3:
                            k.op(DVE, lambda u=u: nc.vector.max(out=tk8[:, u, 8:16], in_=tk_b_[:, u, :]), reads=[tb], writes=[tb])
                        elif sidx == 4:
                            k.op(DVE, lambda u=u: nc.vector.tensor_scalar(out=tksel[:, u, :], in0=tk_a[:, u, :], scalar1=tk8[:, u, 15:16], scalar2=NEGM, op0=ALU.is_lt, op1=ALU.mult),
                                 reads=[tb], writes=[tb])
                        elif sidx == 5:
                            k.op(PE, lambda u=u, g=g, tl=tl: nc.tensor.transpose(out=pTT[0:64, (g * 4 + tl) * 128:(g * 4 + tl + 1) * 128], in_=tksel[:, u, :], identity=ident[:]),
                                 reads=[tb, ident_b], writes=[pTT_b])
                    if sidx == 6:
                        for g in range(2):
                            if os.environ.get("KDBG_S6") == "act":
                                k.op(ACT, lambda g=g: nc.scalar.activation(out=selbT[g][64:128, :], in_=pTT[0:64, g * 512:(g + 1) * 512], func=AF.Copy), reads=[pTT_b], writes=[selbT_b[g]])
                            else:
                                k.op(DVE, lambda g=g: nc.vector.tensor_copy(out=selbT[g][64:128, :], in_=pTT[0:64, g * 512:(g + 1) * 512]), reads=[pTT_b], writes=[selbT_b[g]])
                    if sidx == 7:
                        for h_ in range(8):
                            k.op(DVE, lambda h_=h_: nc.vector.tensor_tensor(out=qst[64:128, h_, :], in0=qwt[64:128, h_, :], in1=selbT[h_ // 4][64:128, :], op=ALU.add),
                                 reads=[qwb, selbT_b[h_ // 4]], writes=[qsb])
                for sidx in range(8):
                    pre.append(lambda sidx=sidx: tk_stage(sidx))

                tasks = []
                for br in (2, 1):
                    for h in range(8):
                        kts = list(range(0, 4 * j + 4)) if br == 1 else list(range(max(0, 4 * j - 4), 4 * j + 4))
                        grp = {"h": h, "br": br, "g": h // 4, "pa": None, "npv": 0, "done": 0}
                        for kt in kts:
                            tls = [tl for tl in range(4) if kt <= 4 * j + tl and (br == 1 or kt >= 4 * j + tl - 4)]
                            grp["npv"] += len(tls)
                            tasks.append({"grp": grp, "kt": kt, "tls": tls})
                n_win = sum(1 for tk in tasks if tk["grp"]["br"] == 2)

                def emit_S(tk):
                    grp = tk["grp"]; h = grp["h"]; br = grp["br"]; g = grp["g"]; kt = tk["kt"]
                    kT = kTs[g] if br == 1 else kTw[g]
                    qq, qqb = (qst, qsb) if br == 1 else (qwt, qwb)
                    ps, psb = nextS()
                    m = (kt - (4 * j - 4)) if br == 2 else (4 + kt - 4 * j)
                    use_msk = (br == 2) or (m >= 4)
                    c0, c1 = 0, 512
                    if use_msk:
                        if m < 4:
                            c1 = 128 * (m + 1)
                        else:
                            c0 = 128 * (m - 4)
                    k.op(PE, lambda: nc.tensor.matmul(ps[:, c0:c1], lhsT=kT[:, kt * 128:(kt + 1) * 128], rhs=qq[:, h, c0:c1], start=True, stop=True),
                         reads=[(kTs_b[g] if br == 1 else kTw_b[g]), qqb], writes=[psb])
                    pt, ptb, _ = PT.next()
                    k.op(ACT, lambda: nc.scalar.activation(out=pt[:, c0:c1], in_=ps[:, c0:c1], func=AF.Exp, bias=sllo[:, h:h + 1]), reads=[psb, sllo_b], writes=[ptb])
                    if use_msk:
                        k.op(DVE, lambda: nc.vector.tensor_tensor(out=pt[:, c0:c1], in0=pt[:, c0:c1], in1=wm01[:, m, c0:c1], op=ALU.mult), reads=[ptb, wm01_b], writes=[ptb])
                    tk["pt"] = pt; tk["ptb"] = ptb

                def emit_PV(tk):
                    grp = tk["grp"]; h = grp["h"]; br = grp["br"]; g = grp["g"]; kt = tk["kt"]
                    vA = vs_aug if br == 1 else vw_aug
                    if grp["pa"] is None:
                        grp["pa"] = nextA()
                    pa, pab = grp["pa"]
                    pt = tk["pt"]; ptb = tk["ptb"]
                    for tl in tk["tls"]:
                        fm = (grp["done"] == 0)
                        grp["done"] += 1
                        last = (grp["done"] == grp["npv"])
                        k.op(PE, lambda tl=tl, fm=fm, last=last: nc.tensor.matmul(pa[:, tl * 65:(tl + 1) * 65], lhsT=pt[:, tl * 128:(tl + 1) * 128], rhs=vA[:, kt, g, :], start=fm, stop=last),
                             reads=[ptb, vtok_b[kt], vones_b], writes=[pab])
                    if grp["done"] == grp["npv"]:
                        evac_group(pa, pab, 65, 4, h, 0, j * 4, br * 8 + h, False)

                LOOK = 3
                n_sel = len(tasks) - n_win
                pre_every = max(1, n_win // (len(pre) + 1))
                cb_every = max(1, (n_sel - 2) // (len(cback) + 1)) if cback else 1
                for i in range(len(tasks) + LOOK):
                    if i < len(tasks):
                        if i == n_win:
                            while pre:
                                pre.pop(0)()
                        emit_S(tasks[i])
                        if i < n_win:
                            if pre and (i % pre_every == pre_every - 1):
                                pre.pop(0)()
                        else:
                            if cback and ((i - n_win) % cb_every == cb_every - 1):
                                cback.pop(0)()
                    if i - LOOK >= 0:
                        emit_PV(tasks[i - LOOK])
                while cback:
                    cback.pop(0)()
                if debug:
                    for tl in range(4):
                        k.dma(POOL, dbg_ds, dbg["d_att"][(j * 4 + tl) * 128:(j * 4 + tl + 1) * 128, :], att[:, tl, :], reads=[att_b[tl]], writes=[DB(("datt", j, tl))])
                for tl in range(4):
                    k.op(ACT, lambda tl=tl: nc.scalar.activation(out=junk4[:, 0:512], in_=att[:, tl, :], func=AF.Square, accum_out=ssa[:, tl:tl + 1]),
                         reads=[att_b[tl]], writes=[junk4_b, ssa_b])
                k.op(ACT, lambda: nc.scalar.activation(out=ssa[:, 4:8], in_=ssa[:, 0:4], func=AF.Ln, scale=1.0 / 512, bias=EPS), reads=[ssa_b], writes=[ssa_b])
                k.op(ACT, lambda: nc.scalar.activation(out=ssa[:, 4:8], in_=ssa[:, 4:8], func=AF.Exp, scale=-0.5), reads=[ssa_b], writes=[ssa_b])
                k.op(DVE, lambda: nc.vector.tensor_tensor(out=ssa[:, 4:8], in0=ssa[:, 4:8], in1=ssq[:, j * 4:(j + 1) * 4], op=ALU.mult), reads=[ssa_b, ssr_b], writes=[ssa_b])
                for tl in range(4):
                    k.op(DVE, lambda tl=tl: nc.vector.tensor_scalar(out=attb[:, tl, :], in0=att[:, tl, :], scalar1=ssa[:, 4 + tl:5 + tl], scalar2=None, op0=ALU.mult),
                         reads=[att_b[tl], ssa_b], writes=[attb_b])
                cback = make_cback(j, rt, rtb)
                if os.environ.get("KDBG_CB") == "now":
                    while cback:
                        cback.pop(0)()
                if j + 1 < NCH:
                    nxt_qs = load_qs(j + 1)
            while cback:
                cback.pop(0)()

        mid.close()
        if upto <= 4:
            k.drain()
            return nc
        k.barrier()
        with ExitStack() as p5:
            wf1 = sb("wf1", [128, 8, 4 * D], BF16, p5)
            wf2 = sb("wf2", [128, 32, D], BF16, p5)
            wf1_b = [Buf(f"wf1_{i}") for i in range(8)]; wf2_b = [Buf(f"wf2_{i}") for i in range(4)]
            wf1v = wff1_d.rearrange("(k p) n -> p k n", p=128)
            wf2v = wff2_d.rearrange("(k p) n -> p k n", p=128)
            for cb in range(8):
                k.dma(POOL, k.dsem("wf1"), wf1[:, :, cb * 512:(cb + 1) * 512], wf1v[:, :, cb * 512:(cb + 1) * 512], writes=[wf1_b[cb]])
            for hh in range(4):
                k.dma(POOL, k.dsem("wf2"), wf2[:, hh * 8:(hh + 1) * 8, :], wf2v[:, hh * 8:(hh + 1) * 8, :], writes=[wf2_b[hh]])
            CH = 256
            xin = Rot(k, p5, nc, "xin", 4, [128, D], F32)
            xnb = Rot(k, p5, nc, "xnb", 2, [128, D], BF16, with_dsem=False)
            hT2 = Rot(k, p5, nc, "hT2", 2, [128, 8, CH], BF16, with_dsem=False)
            aT = Rot(k, p5, nc, "aT", 1, [128, 32, CH], BF16, with_dsem=False)
            r32 = Rot(k, p5, nc, "r32", 3, [128, CH], F32, with_dsem=False)
            y2 = Rot(k, p5, nc, "y2", 2, [128, D], F32, with_dsem=False)
            ot = Rot(k, p5, nc, "ot", 2, [128, D], F32)
            junk5 = sb("junk5", [128, D], BF16, p5)
            junk5_b = Buf("junk5")
            sm5 = sb("sm5", [128, 64], F32, p5)
            sm5_b = [Buf(f"sm5_{i}") for i in range(16)]
            s5i = [0]
            pT5 = [pst(f"pT5_{i}", [128, 1024], BF16, p5) for i in range(2)]
            pT5_b = [Buf(f"pT5_{i}") for i in range(2)]
            pF = [pst(f"pF{i}", [128, 512], F32, p5) for i in range(2)]
            pF_b = [Buf(f"pF{i}") for i in range(2)]
            pY = [pst(f"pY{i}", [128, 512], F32, p5) for i in range(4)]
            pY_b = [Buf(f"pY{i}") for i in range(4)]
            fi = [0]
            out_bufs = []

            def prologue_a(cj):
                xtiles = []
                nts = []
                for tl in range(2):
                    tt = cj * 2 + tl
                    xt_, xtb, xds = xin.next()
                    k.dma(SP, xds, xt_[:], x1_d[tt * 128:(tt + 1) * 128, :], reads=[DB(("x1", tt))], writes=[xtb])
                    xtiles.append((xt_, xtb))
                    i5 = s5i[0] % 16; s5i[0] += 1
                    s4 = sm5[:, 4 * i5:4 * i5 + 4]; s4b = sm5_b[i5]
                    k.op(ACT, lambda xt_=xt_, s4=s4: nc.scalar.activation(out=junk5[:], in_=xt_[:], func=AF.Square, accum_out=s4[:, 0:1]), reads=[xtb], writes=[junk5_b, s4b])
                    k.op(ACT, lambda s4=s4: nc.scalar.activation(out=s4[:, 1:2], in_=s4[:, 0:1], func=AF.Sqrt, scale=1.0 / D, bias=EPS), reads=[s4b], writes=[s4b])
                    k.op(DVE, lambda s4=s4: nc.vector.reciprocal(out=s4[:, 2:3], in_=s4[:, 1:2]), reads=[s4b], writes=[s4b])
                    n, nb, _ = xnb.next()
                    k.op(DVE, lambda xt_=xt_, n=n, s4=s4: nc.vector.tensor_scalar(out=n[:], in0=xt_[:], scalar1=s4[:, 2:3], scalar2=None, op0=ALU.mult), reads=[xtb, s4b], writes=[nb])
                    nts.append((n, nb))
                return xtiles, nts

            def prologue_b(cj, nts):
                ht2, ht2b, _ = hT2.next()
                for tl in range(2):
                    tt = cj * 2 + tl
                    n, nb = nts[tl]
                    pp = pT5[tt % 2]; ppb = pT5_b[tt % 2]
                    for jj in range(8):
                        k.op(PE, lambda jj=jj, n=n, pp=pp: nc.tensor.transpose(out=pp[:, jj * 128:(jj + 1) * 128], in_=n[:, jj * 128:(jj + 1) * 128], identity=ident[:]),
                             reads=[nb, ident_b], writes=[ppb])
                    for jj in range(8):
                        if tt % 2 == 0:
                            k.op(DVE, lambda jj=jj, pp=pp, tl=tl, ht2=ht2: nc.vector.tensor_scalar(out=ht2[:, jj, tl * 128:(tl + 1) * 128], in0=pp[:, jj * 128:(jj + 1) * 128],
                                                                                                   scalar1=A2[:, jj:jj + 1], scalar2=B2[:, jj:jj + 1], op0=ALU.mult, op1=ALU.add),
                                 reads=[ppb, A2_b, B2_b], writes=[ht2b])
                        else:
                            k.op(ACT, lambda jj=jj, pp=pp, tl=tl, ht2=ht2: nc.scalar.activation(out=ht2[:, jj, tl * 128:(tl + 1) * 128], in_=pp[:, jj * 128:(jj + 1) * 128],
                                                                                                func=AF.Identity, scale=A2[:, jj:jj + 1], bias=B2[:, jj:jj + 1]),
                                 reads=[ppb, A2_b, B2_b], writes=[ht2b])
                return ht2, ht2b

            def ff1(ht2, ht2b, mid_cb=None):
                at, atb, _ = aT.next()
                res = None
                for f in range(32):
                    if f == 10 and mid_cb is not None:
                        res = mid_cb()
                    pf = pF[fi[0] % 2]; pfb = pF_b[fi[0] % 2]; fi[0] += 1
                    for kk in range(8):
                        k.op(PE, lambda kk=kk, f=f, pf=pf: nc.tensor.matmul(pf[:, 0:CH], lhsT=wf1[:, kk, f * 128:(f + 1) * 128], rhs=ht2[:, kk, :], start=(kk == 0), stop=(kk == 7)),
                             reads=[wf1_b[f // 4], ht2b], writes=[pfb])
                    r, rb, _ = r32.next()
                    k.op(ACT, lambda pf=pf, r=r: nc.scalar.activation(out=r[:], in_=pf[:, 0:CH], func=AF.Relu), reads=[pfb], writes=[rb])
                    k.op(DVE, lambda r=r, f=f: nc.vector.tensor_tensor(out=at[:, f, :], in0=r[:], in1=r[:], op=ALU.mult), reads=[rb], writes=[atb])
                return at, atb, res

            def ff2(cj, at, atb, xtiles):
                for tl in range(2):
                    tt = cj * 2 + tl
                    yy, yyb, _ = y2.next()
                    for half in range(2):
                        py = pY[(tl * 2 + half) % 4]; pyb = pY_b[(tl * 2 + half) % 4]
                        for f in range(32):
                            k.op(PE, lambda f=f, tl=tl, half=half, py=py: nc.tensor.matmul(py[:, :], lhsT=at[:, f, tl * 128:(tl + 1) * 128], rhs=wf2[:, f, half * 512:(half + 1) * 512],
                                                                                          start=(f == 0), stop=(f == 31)), reads=[atb, wf2_b[f // 8]], writes=[pyb])
                        k.op(ACT, lambda half=half, yy=yy, py=py: nc.scalar.activation(out=yy[:, half * 512:(half + 1) * 512], in_=py[:, :], func=AF.Copy), reads=[pyb], writes=[yyb])
                    i5 = s5i[0] % 16; s5i[0] += 1
                    s4 = sm5[:, 4 * i5:4 * i5 + 4]; s4b = sm5_b[i5]
                    k.op(ACT, lambda yy=yy, s4=s4: nc.scalar.activation(out=junk5[:], in_=yy[:], func=AF.Square, accum_out=s4[:, 0:1]), reads=[yyb], writes=[junk5_b, s4b])
                    k.op(ACT, lambda s4=s4: nc.scalar.activation(out=s4[:, 1:2], in_=s4[:, 0:1], func=AF.Sqrt, scale=1.0 / D, bias=EPS), reads=[s4b], writes=[s4b])
                    k.op(DVE, lambda s4=s4: nc.vector.reciprocal(out=s4[:, 2:3], in_=s4[:, 1:2]), reads=[s4b], writes=[s4b])
                    k.op(DVE, lambda yy=yy, s4=s4: nc.vector.scalar_tensor_tensor(out=yy[:], in0=yy[:], scalar=s4[:, 2:3], in1=C2row[:], op0=ALU.mult, op1=ALU.mult),
                         reads=[yyb, s4b, C2_b], writes=[yyb])
                    o, ob, ods = ot.next()
                    xt_, xtb = xtiles[tl]
                    k.op(DVE, lambda yy=yy, o=o, xt_=xt_: nc.vector.tensor_tensor(out=o[:], in0=yy[:], in1=xt_[:], op=ALU.add), reads=[yyb, xtb], writes=[ob])
                    db = DB(("out", tt))
                    k.dma(POOL, ods, out_d[tt * 128:(tt + 1) * 128, :], o[:], reads=[ob], writes=[db])
                    out_bufs.append(db)

            NCJ = S // CH
            xtiles, nts = prologue_a(0)
            ht2, ht2b = prologue_b(0, nts)
            for cj in range(NCJ):
                at, atb, res = ff1(ht2, ht2b, (lambda cj=cj: prologue_a(cj + 1)) if cj + 1 < NCJ else None)
                if cj + 1 < NCJ:
                    xtiles_n, nts_n = res
                    ht2_n, ht2b_n = prologue_b(cj + 1, nts_n)
                ff2(cj, at, atb, xtiles)
                if cj + 1 < NCJ:
                    xtiles, ht2, ht2b = xtiles_n, ht2_n, ht2b_n
            k.drain()
            k.finish(out_bufs + [b for kk_, b in dbuf.items() if isinstance(kk_, tuple) and kk_ and kk_[0] in ("datt", "dy")] + ([DB("d_mod")] if debug else []))
        print("bass instructions:", k.ninst, "semaphores:", k.nsem)
    return nc


def _consts():
    bf = ml_dtypes.bfloat16
    t = np.arange(S)
    slopes = 2.0 ** (-np.arange(1, 9, dtype=np.float64))
    c = np.arange(256)
    ce = 16 * c + 31
    ce01 = ((ce[None, :] // 64) == np.arange(64)[:, None]).astype(np.float32)
    ce01[:, 255] = 0.0
    cidx = 16 * (np.arange(2)[None, :] * 128 + np.arange(128)[:, None]) + 31
    cllo = (slopes[None, None, :] * (cidx[:, :, None] % 64)).astype(np.float32)
    cend = (16 * (np.arange(2)[None, :, None] * 128 + np.arange(128)[:, None, None]) + 31)
    cmask = np.where(cend <= t[None, None, :], 0.0, NEGM).astype(np.float32)
    e01 = ((t[None, :] // 64) == np.arange(64)[:, None]).astype(np.float32)
    tab = (64.0 * (np.arange(64)[:, None] - (t[None, :] // 64))).astype(np.float32)
    sllo = (slopes[None, :] * (np.arange(128)[:, None] % 64)).astype(np.float32)
    m = np.arange(8)[None, :, None]; kk = np.arange(128)[:, None, None]; tl = np.arange(512)[None, None, :]
    dd = (512 + tl) - (128 * m + kk)
    wm01 = ((dd >= 0) & (dd < 512)).astype(np.float32)
    tok = (np.arange(NT)[None, :, None] * 128 + np.arange(128)[:, None, None])
    cur = tok // 64
    jb = np.arange(64)[None, None, :]
    forced = (jb == 0) | (jb == cur) | (jb == cur - 1)
    addm = np.where(forced, 1.0e4, np.where(jb <= cur, 0.0, -1.0e30)).astype(np.float32)
    cc = (np.arange(2)[None, :, None] * 128 + np.arange(128)[:, None, None])
    ovl = ((cc >= 4 * jb - 1) & (cc <= 4 * jb + 3) & (cc < 255)).astype(np.float32)
    return {
        "k_ident": np.eye(128, dtype=np.float32).astype(bf),
        "k_ce01": ce01.astype(bf), "k_cllo": cllo,
        "k_cmask": cmask.astype(bf), "k_e01": e01.astype(bf), "k_tab": tab.astype(bf), "k_sllo": sllo, "k_wm01": wm01.astype(bf),
        "k_addm": addm, "k_ovl": np.ascontiguousarray(ovl[:, :, 1:]).astype(bf),
    }


def _col(v, n):
    return np.ascontiguousarray(np.asarray(v, np.float32).reshape(n, 128).T)


def _shared_inputs(inp):
    L = 0
    f = lambda a: np.ascontiguousarray(np.asarray(a, np.float32))
    d = {
        "ada_w": f(inp["ada_w"][L]), "ada_b": f(inp["ada_b"][L]).reshape(1, -1),
        "g_pre1": _col(inp["pre_norm_mix"][L], 8), "g_pre2": _col(inp["pre_norm_mlp"][L], 8),
        "g_post1": np.ascontiguousarray(np.broadcast_to(f(inp["post_norm_mix"][L])[None, :], (128, D))),
        "g_post2": np.ascontiguousarray(np.broadcast_to(f(inp["post_norm_mlp"][L])[None, :], (128, D))),
        "w_in": f(inp["w_in"][L]),
        "conv_w": np.ascontiguousarray(f(inp["conv_w"][L]).T.reshape(4, 128, 4).transpose(1, 0, 2)),
        "conv_b": _col(inp["conv_b"][L], 4),
        "lru_wa": f(inp["lru_wa"][L]), "lru_wx": f(inp["lru_wx"][L]),
        "lru_ba": _col(inp["lru_ba"][L], 4), "lru_bx": _col(inp["lru_bx"][L], 4), "lru_lam": _col(inp["lru_lambda"][L], 4),
        "pos_k": np.ascontiguousarray(f(inp["cmp_pos_k"][L]).T), "pos_v": np.ascontiguousarray(f(inp["cmp_pos_v"][L]).T),
        "w1_k": f(inp["cmp_w1_k"][L]), "w1_v": f(inp["cmp_w1_v"][L]),
        "w2_k": f(inp["cmp_w2_k"][L]), "w2_v": f(inp["cmp_w2_v"][L]),
        "g_rnn": _col(inp["norm_rnn_out"][L], 4), "g_att": _col(inp["norm_att_out"][L], 4),
        "w_out": f(inp["w_out"][L]), "w_ff1": f(inp["w_ff1"][L]), "w_ff2": f(inp["w_ff2"][L]),
    }
    d.update(_consts())
    return d


def kernel(**inputs):
    debug = bool(inputs.pop("_debug", False))
    upto = inputs.pop("_upto", 99)
    cores = inputs.pop("_cores", None)
    x = np.asarray(inputs["x"], np.float32)
    c = np.asarray(inputs["c"], np.float32)
    B = x.shape[0]
    shared = _shared_inputs(inputs)
    nc = build(debug=debug, upto=upto)
    bs = list(range(B)) if cores is None else list(cores)
    in_maps = []
    for b in bs:
        m = dict(shared)
        m["x"] = np.ascontiguousarray(x[b])
        m["c"] = _col(c[b], 8)
        in_maps.append(m)
    res = run_bass_kernel_spmd(nc, in_maps, core_ids=list(range(len(bs))))
    if debug:
        return res.results
    return np.stack([np.asarray(r["out"], np.float32) for r in res.results], axis=0)
```

```python
import numpy as np
import ml_dtypes
from contextlib import ExitStack
import concourse.bass as bass
import concourse.mybir as mybir
from concourse.bass_utils import run_bass_kernel_spmd

F32 = mybir.dt.float32
BF16 = mybir.dt.bfloat16
AF = mybir.ActivationFunctionType
ALU = mybir.AluOpType

S = 4096
D = 1024
NT = S // 128
NCH = S // 512
DIN = 2328
NEGM = -30000.0
EPS = 1e-6
import os
EVAC = os.environ.get('KDBG_EVAC', '')


class Buf:
    __slots__ = ("name", "lw", "rd")

    def __init__(self, name):
        self.name = name
        self.lw = None
        self.rd = {}


class DSem:
    __slots__ = ("sem", "cnt")

    def __init__(self, sem):
        self.sem = sem
        self.cnt = 0


class Eng:
    def __init__(self, name, eng, self_sync):
        self.name = name
        self.eng = eng
        self.self_sync = self_sync
        self.cur = None
        self.cnt = 0
        self.own = set()
        self.waited = {}


class K:
    EPOCH = 4000

    def __init__(self, nc, es):
        self.nc = nc
        self.es = es
        self.nsem = 0
        self.pe = Eng("pe", nc.tensor, False)
        self.act = Eng("act", nc.scalar, True)
        self.dve = Eng("dve", nc.vector, True)
        self.pool = Eng("pool", nc.gpsimd, True)
        self.sp = Eng("sp", nc.sync, True)
        self.dsems = []
        self.ninst = 0

    def new_sem(self, name):
        self.nsem += 1
        return self.es.enter_context(self.nc.semaphore(f"{name}_{self.nsem}"))

    def dsem(self, name="d"):
        d = DSem(self.new_sem(name))
        self.dsems.append(d)
        return d

    def _wait(self, E, tok):
        sem, val = tok
        key = id(sem)
        if (not E.self_sync) and key in E.own:
            return
        if E.waited.get(key, 0) >= val:
            return
        E.eng.wait_ge(sem, val)
        E.waited[key] = val

    def _deps(self, E, reads, writes):
        for b in reads:
            if b.lw is not None:
                self._wait(E, b.lw)
        for b in writes:
            if b.lw is not None:
                self._wait(E, b.lw)
            for tok in b.rd.values():
                self._wait(E, tok)

    def _commit(self, tok, reads, writes):
        key = id(tok[0])
        for b in reads:
            old = b.rd.get(key)
            if old is None or old[1] < tok[1]:
                b.rd[key] = tok
        for b in writes:
            b.lw = tok
            b.rd = {}

    def op(self, E, fn, reads=(), writes=()):
        self._deps(E, reads, writes)
        inst = fn()
        if E.cur is None or E.cnt >= self.EPOCH:
            E.cur = self.new_sem(E.name)
            E.own.add(id(E.cur))
            E.cnt = 0
        E.cnt += 1
        inst.then_inc(E.cur, 1)
        tok = (E.cur, E.cnt)
        self._commit(tok, reads, writes)
        self.ninst += 1
        return tok

    def dma(self, Q, ds, out, in_, reads=(), writes=(), **kw):
        self._deps(Q, reads, writes)
        if ds.cnt:
            self._wait(Q, (ds.sem, ds.cnt))
        inst = Q.eng.dma_start(out=out, in_=in_, **kw)
        ds.cnt += 16
        inst.then_inc(ds.sem, 16)
        tok = (ds.sem, ds.cnt)
        self._commit(tok, reads, writes)
        self.ninst += 1
        return tok

    def drain(self):
        for E in (self.pe, self.act, self.dve, self.pool):
            if E.cur is not None:
                self._wait(self.sp, (E.cur, E.cnt))
        for d in self.dsems:
            if d.cnt:
                self._wait(self.sp, (d.sem, d.cnt))

    def barrier(self):
        engs = (self.pe, self.act, self.dve, self.pool, self.sp)
        for E in engs:
            for E2 in engs:
                if E2 is not E and E2.cur is not None and E2.cnt:
                    self._wait(E, (E2.cur, E2.cnt))
            for d in self.dsems:
                if d.cnt:
                    self._wait(E, (d.sem, d.cnt))

    def finish(self, bufs):
        for b in bufs:
            if b.lw is not None:
                self._wait(self.sp, b.lw)


class Rot:
    def __init__(self, k, es, nc, name, n, shape, dt, with_dsem=True):
        self.n = n
        self.i = 0
        self.slots = []
        for j in range(n):
            t = es.enter_context(nc.sbuf_tensor(f"{name}{j}", shape, dt))
            self.slots.append((t, Buf(f"{name}{j}"), k.dsem(name) if with_dsem else None))

    def next(self):
        s = self.slots[self.i % self.n]
        self.i += 1
        return s


class _Stop(Exception):
    pass


def build(debug=False, upto=99):
    nc = bass.Bass("TRN2", target_bir_lowering=False)

    def din(name, shape, dt=F32):
        return nc.dram_tensor(name, list(shape), dt, kind="ExternalInput").ap()

    def dscr(name, shape, dt):
        return nc.dram_tensor(name, list(shape), dt, kind="Internal").ap()

    x_d = din("x", [S, D])
    c_d = din("c", [128, 8])
    adaw_d = din("ada_w", [D, 6 * D])
    adab_d = din("ada_b", [1, 6 * D])
    gpre1_d = din("g_pre1", [128, 8])
    gpre2_d = din("g_pre2", [128, 8])
    gpost1_d = din("g_post1", [128, D])
    gpost2_d = din("g_post2", [128, D])
    win_d = din("w_in", [D, DIN])
    convw_d = din("conv_w", [128, 4, 4])
    convb_d = din("conv_b", [128, 4])
    wa_d = din("lru_wa", [8, 64, 64])
    wx_d = din("lru_wx", [8, 64, 64])
    ba_d = din("lru_ba", [128, 4])
    bx_d = din("lru_bx", [128, 4])
    lam_d = din("lru_lam", [128, 4])
    posk_d = din("pos_k", [64, 32])
    posv_d = din("pos_v", [64, 32])
    w1k_d = din("w1_k", [2048, 256])
    w1v_d = din("w1_v", [2048, 256])
    w2k_d = din("w2_k", [256, 64])
    w2v_d = din("w2_v", [256, 64])
    grnn_d = din("g_rnn", [128, 4])
    gatt_d = din("g_att", [128, 4])
    wout_d = din("w_out", [D, D])
    wff1_d = din("w_ff1", [D, 4 * D])
    wff2_d = din("w_ff2", [4 * D, D])
    ident_d = din("k_ident", [128, 128], BF16)
    ce01_d = din("k_ce01", [64, 256], BF16)
    cllo_d = din("k_cllo", [128, 2, 8])
    cmask_d = din("k_cmask", [128, 2, S], BF16)
    e01_d = din("k_e01", [64, S], BF16)
    tab_d = din("k_tab", [64, S], BF16)
    sllo_d = din("k_sllo", [128, 8])
    wm01_d = din("k_wm01", [128, 8, 512], BF16)
    addm_d = din("k_addm", [128, NT, 64])
    ovl_d = din("k_ovl", [128, 2, 63], BF16)
    out_d = nc.dram_tensor("out", [S, D], F32, kind="ExternalOutput").ap()
    zr_d = dscr("zr_s", [1024, S], F32)
    q_d = dscr("q_s", [512, S], BF16)
    kc_d = dscr("kc_s", [128, S], BF16)
    vc_d = dscr("vc_s", [128, S], BF16)
    ks_d = dscr("ks_s", [128, S], BF16)
    kw_d = dscr("kw_s", [128, S], BF16)
    rnn_d = dscr("rnn_s", [512, S], BF16)
    x1_d = dscr("x1_s", [S, D], F32)
    dbg = {}
    if debug:
        for nm, shp in [("d_mod", [1, 6 * D]), ("d_att", [S, 512]), ("d_y", [S, D])]:
            dbg[nm] = nc.dram_tensor(nm, shp, F32, kind="ExternalOutput").ap()

    with ExitStack() as es:
        k = K(nc, es)
        PE, ACT, DVE, POOL, SP = k.pe, k.act, k.dve, k.pool, k.sp

        def sb(name, shape, dt, stack=es):
            return stack.enter_context(nc.sbuf_tensor(name, list(shape), dt))

        def pst(name, shape, dt, stack=es):
            return stack.enter_context(nc.psum_tensor(name, list(shape), dt))

        dbuf = {}

        def DB(key):
            if key not in dbuf:
                dbuf[key] = Buf(str(key))
            return dbuf[key]

        ident = sb("ident", [128, 128], BF16)
        ident_b = Buf("ident")
        ld0 = k.dsem("ld0")
        k.dma(SP, ld0, ident[:], ident_d, writes=[ident_b])
        ones_bf = sb("ones_bf", [128, 128], BF16)
        ones_b = Buf("ones")
        k.op(DVE, lambda: nc.vector.memset(ones_bf[:], 1.0), writes=[ones_b])
        A1 = sb("A1", [128, 8], F32); B1 = sb("B1", [128, 8], F32)
        A2 = sb("A2", [128, 8], F32); B2 = sb("B2", [128, 8], F32)
        C1row = sb("C1row", [128, D], F32); C2row = sb("C2row", [128, D], F32)
        A1_b, B1_b, A2_b, B2_b, C1_b, C2_b = (Buf(n) for n in ("A1", "B1", "A2", "B2", "C1", "C2"))
        mid = es.enter_context(ExitStack())
        vs_aug = sb("vs_aug", [128, NT, 2, 65], BF16, mid)
        vw_aug = sb("vw_aug", [128, NT, 2, 65], BF16, mid)
        gates = sb("gates", [128, NT, 24], F32, mid)
        vtok_b = [Buf(f"vtok{t}") for t in range(NT)]
        vones_b = Buf("vones")
        k.op(DVE, lambda: nc.vector.memset(vs_aug[:, :, :, 64:65], 1.0), writes=[vones_b])
        k.op(DVE, lambda: nc.vector.memset(vw_aug[:, :, :, 64:65], 1.0), writes=[vones_b])
        ssr = sb("ssr", [128, NT], F32, mid)
        ssq = sb("ssq", [128, NT], F32, mid)
        ssr_b = Buf("ssr")

        with ExitStack() as p0:
            csb = sb("csb", [128, 8], F32, p0)
            scb = sb("scb", [128, 8], BF16, p0)
            c_b = Buf("c")
            k.dma(SP, k.dsem("c"), csb[:], c_d, writes=[c_b])
            k.op(ACT, lambda: nc.scalar.activation(out=scb[:], in_=csb[:], func=AF.Silu), reads=[c_b], writes=[c_b])
            adab = sb("adab", [1, 6 * D], F32, p0)
            adab_b = Buf("adab")
            k.dma(SP, k.dsem("adab"), adab[:], adab_d, writes=[adab_b])
            modrow = sb("modrow", [1, 6 * D], F32, p0)
            modrow_b = Buf("modrow")
            modcol = sb("modcol", [128, 48], F32, p0)
            modcol_b = Buf("modcol")
            gp1 = sb("gp1", [128, 8], F32, p0); gp2 = sb("gp2", [128, 8], F32, p0)
            gq1 = sb("gq1", [128, D], F32, p0); gq2 = sb("gq2", [128, D], F32, p0)
            g_b = Buf("gvecs")
            k.dma(SP, ld0, gp1[:], gpre1_d, writes=[g_b])
            k.dma(SP, ld0, gp2[:], gpre2_d, writes=[g_b])
            k.dma(SP, ld0, gq1[:], gpost1_d, writes=[g_b])
            k.dma(SP, ld0, gq2[:], gpost2_d, writes=[g_b])
            adaw = Rot(k, p0, nc, "adaw", 2, [128, 8, 512], BF16)
            ps_row = pst("ps_row", [128, 512], F32, p0)
            ps_row_b = Buf("ps_row")
            ps_col = pst("ps_col", [128, 512], F32, p0)
            ps_col_b = Buf("ps_col")
            one11 = sb("one11", [1, 128], F32, p0)
            one11_b = Buf("one11")
            k.op(DVE, lambda: nc.vector.memset(one11[:], 1.0), writes=[one11_b])
            for pc in range(12):
                t, tb, ds = adaw.next()
                k.dma(POOL, ds, t[:], adaw_d[:, pc * 512:(pc + 1) * 512].rearrange("(k p) n -> p k n", p=128), writes=[tb])
                for kk in range(8):
                    k.op(PE, lambda kk=kk, t=t: nc.tensor.matmul(ps_row[0:1, :], lhsT=scb[:, kk:kk + 1], rhs=t[:, kk, :],
                                                                start=(kk == 0), stop=(kk == 7)),
                         reads=[tb, c_b], writes=[ps_row_b])
                k.op(DVE, lambda pc=pc: nc.vector.tensor_tensor(out=modrow[0:1, pc * 512:(pc + 1) * 512], in0=ps_row[0:1, :],
                                                                in1=adab[0:1, pc * 512:(pc + 1) * 512], op=ALU.add),
                     reads=[ps_row_b, adab_b], writes=[modrow_b])
            if debug:
                k.dma(SP, ld0, dbg["d_mod"], modrow[:], reads=[modrow_b], writes=[DB("d_mod")])
            for j in range(48):
                k.op(PE, lambda j=j: nc.tensor.matmul(ps_col[:, j:j + 1], lhsT=modrow[0:1, j * 128:(j + 1) * 128], rhs=one11[0:1, 0:1],
                                                      start=True, stop=True),
                     reads=[modrow_b, one11_b], writes=[ps_col_b])
            k.op(DVE, lambda: nc.vector.tensor_copy(out=modcol[:], in_=ps_col[:, 0:48]), reads=[ps_col_b], writes=[modcol_b])
            k.op(DVE, lambda: nc.vector.scalar_tensor_tensor(out=A1[:], in0=modcol[:, 8:16], scalar=1.0, in1=gp1[:], op0=ALU.add, op1=ALU.mult),
                 reads=[modcol_b, g_b], writes=[A1_b])
            k.op(DVE, lambda: nc.vector.tensor_copy(out=B1[:], in_=modcol[:, 0:8]), reads=[modcol_b], writes=[B1_b])
            k.op(DVE, lambda: nc.vector.scalar_tensor_tensor(out=A2[:], in0=modcol[:, 32:40], scalar=1.0, in1=gp2[:], op0=ALU.add, op1=ALU.mult),
                 reads=[modcol_b, g_b], writes=[A2_b])
            k.op(DVE, lambda: nc.vector.tensor_copy(out=B2[:], in_=modcol[:, 24:32]), reads=[modcol_b], writes=[B2_b])
            for (base, crow, cb, gq) in ((2048, C1row, C1_b, gq1), (5120, C2row, C2_b, gq2)):
                for hh in range(2):
                    k.op(PE, lambda base=base, hh=hh: nc.tensor.matmul(ps_row[:, :], lhsT=one11[0:1, :],
                                                                      rhs=modrow[0:1, base + hh * 512: base + (hh + 1) * 512],
                                                                      start=True, stop=True),
                         reads=[modrow_b, one11_b], writes=[ps_row_b])
                    k.op(DVE, lambda hh=hh, crow=crow, gq=gq: nc.vector.scalar_tensor_tensor(
                        out=crow[:, hh * 512:(hh + 1) * 512], in0=ps_row[:, :], scalar=1.0, in1=gq[:, hh * 512:(hh + 1) * 512],
                        op0=ALU.add, op1=ALU.mult), reads=[ps_row_b, g_b], writes=[cb])

        if upto <= 0:
            k.drain()
            return nc
        k.barrier()
        with ExitStack() as p1:
            hT = sb("hT", [128, 8, S], BF16, p1)
            hT_b = [[Buf(f"hT{t}_{j}") for j in range(8)] for t in range(NT)]
            win = sb("win", [128, 8, DIN], BF16, p1)
            win_b = Buf("win")
            wds = k.dsem("win")
            k.dma(POOL, wds, win[:], win_d.rearrange("(k p) n -> p k n", p=128), writes=[win_b])
            xt = Rot(k, p1, nc, "xt", 3, [128, D], F32)
            junk = sb("junk", [128, D], BF16, p1)
            junk_b = Buf("junk")
            xn = Rot(k, p1, nc, "xn", 2, [128, D], BF16, with_dsem=False)
            pT = [pst(f"pT{i}", [128, 1024], BF16, p1) for i in range(2)]
            pT_b = [Buf(f"pT{i}") for i in range(2)]
            sm1 = sb("sm1", [128, NT * 4], F32, p1)
            sm1_b = [Buf(f"sm1_{t}") for t in range(NT)]
            pz = [pst(f"pz{i}", [128, 512], F32, p1) for i in range(4)]
            pz_b = [Buf(f"pz{i}") for i in range(4)]
            st32 = Rot(k, p1, nc, "st32", 3, [128, 512], F32)
            st16 = Rot(k, p1, nc, "st16", 3, [128, 512], BF16)
            zi = 0
            fm_tiles = []
            for ct in range(8):
                fm_tiles.append((ct * 128, zr_d, ct * 128, "f32", 1.0))
            for ct in range(4):
                fm_tiles.append((1024 + ct * 128, q_d, ct * 128, "bf", 0.125))
            fm_tiles.append((1536, kc_d, 0, "bf", 1.0))
            fm_tiles.append((1664, vc_d, 0, "bf", 1.0))
            fm_tiles.append((1792, ks_d, 0, "bf", 1.0))
            fm_tiles.append((2048, kw_d, 0, "bf", 1.0))

            def norm_a(tt):
                t, tb, ds = xt.next()
                k.dma(SP, ds, t[:], x_d[tt * 128:(tt + 1) * 128, :], writes=[tb])
                s4 = sm1[:, tt * 4:tt * 4 + 4]; s4b = sm1_b[tt]
                k.op(ACT, lambda: nc.scalar.activation(out=junk[:], in_=t[:], func=AF.Square, accum_out=s4[:, 0:1]), reads=[tb], writes=[junk_b, s4b])
                k.op(ACT, lambda: nc.scalar.activation(out=s4[:, 1:2], in_=s4[:, 0:1], func=AF.Ln, scale=1.0 / D, bias=EPS), reads=[s4b], writes=[s4b])
                k.op(ACT, lambda: nc.scalar.activation(out=s4[:, 2:3], in_=s4[:, 1:2], func=AF.Exp, scale=-0.5), reads=[s4b], writes=[s4b])
                n, nb, _ = xn.next()
                k.op(DVE, lambda: nc.vector.tensor_scalar(out=n[:], in0=t[:], scalar1=s4[:, 2:3], scalar2=None, op0=ALU.mult), reads=[tb, s4b], writes=[nb])
                return tt, n, nb

            def norm_b(tt, n, nb):
                pp = pT[tt % 2]; ppb = pT_b[tt % 2]
                for j in range(8):
                    k.op(PE, lambda j=j: nc.tensor.transpose(out=pp[:, j * 128:(j + 1) * 128], in_=n[:, j * 128:(j + 1) * 128], identity=ident[:]),
                         reads=[nb, ident_b], writes=[ppb])
                for j in range(8):
                    if tt % 2 == 0:
                        k.op(DVE, lambda j=j: nc.vector.tensor_scalar(out=hT[:, j, tt * 128:(tt + 1) * 128], in0=pp[:, j * 128:(j + 1) * 128],
                                                                      scalar1=A1[:, j:j + 1], scalar2=B1[:, j:j + 1], op0=ALU.mult, op1=ALU.add),
                             reads=[ppb, A1_b, B1_b], writes=[hT_b[tt][j]])
                    else:
                        k.op(ACT, lambda j=j: nc.scalar.activation(out=hT[:, j, tt * 128:(tt + 1) * 128], in_=pp[:, j * 128:(j + 1) * 128],
                                                                   func=AF.Identity, scale=A1[:, j:j + 1], bias=B1[:, j:j + 1]),
                             reads=[ppb, A1_b, B1_b], writes=[hT_b[tt][j]])

            npend = [norm_a(0)]

            def norm_tile(_tt_unused=None):
                tt, n, nb = npend[0]
                norm_b(tt, n, nb)
                if tt + 1 < NT:
                    npend[0] = norm_a(tt + 1)

            for tl in range(4):
                norm_tile(tl)
            for ch in range(NCH):
                for ti, (c0, dst, r0, kind, scl) in enumerate(fm_tiles):
                    if ch + 1 < NCH and ti in (2, 6, 10, 14):
                        norm_tile((ch + 1) * 4 + (ti - 2) // 4)
                    pzz = pz[zi % 4]; pzb = pz_b[zi % 4]
                    for kk in range(8):
                        k.op(PE, lambda kk=kk, c0=c0, ch=ch, pzz=pzz: nc.tensor.matmul(pzz[:, :], lhsT=win[:, kk, c0:c0 + 128], rhs=hT[:, kk, ch * 512:(ch + 1) * 512],
                                                                                      start=(kk == 0), stop=(kk == 7)),
                             reads=[win_b] + [hT_b[t4][kk] for t4 in range(ch * 4, ch * 4 + 4)], writes=[pzb])
                    t, tb, ds = (st32 if kind == "f32" else st16).next()
                    if zi % 2 == 0:
                        k.op(ACT, lambda t=t, pzz=pzz, scl=scl: nc.scalar.activation(out=t[:], in_=pzz[:, :], func=AF.Copy, scale=scl), reads=[pzb], writes=[tb])
                    else:
                        k.op(DVE, lambda t=t, pzz=pzz, scl=scl: nc.vector.tensor_scalar(out=t[:], in0=pzz[:, :], scalar1=scl, scalar2=None, op0=ALU.mult),
                             reads=[pzb], writes=[tb])
                    k.dma(POOL, ds, dst[r0:r0 + 128, ch * 512:(ch + 1) * 512], t[:], reads=[tb], writes=[DB((id(dst), r0, ch))])
                    zi += 1
                for tl in range(4):
                    tt = ch * 4 + tl
                    pzz = pz[zi % 4]; pzb = pz_b[zi % 4]
                    use_dve = (zi % 2 == 1)
                    zi += 1
                    for (c0, n, o0) in ((1920, 128, 0), (2176, 152, 128)):
                        for kk in range(8):
                            k.op(PE, lambda kk=kk, c0=c0, n=n, o0=o0, tt=tt, pzz=pzz: nc.tensor.matmul(
                                pzz[:, o0:o0 + n], lhsT=hT[:, kk, tt * 128:(tt + 1) * 128], rhs=win[:, kk, c0:c0 + n],
                                start=(kk == 0 and o0 == 0), stop=(kk == 7 and o0 == 128)),
                                reads=[win_b, hT_b[tt][kk]], writes=[pzb])
                    if use_dve:
                        k.op(DVE, lambda tt=tt, pzz=pzz: nc.vector.tensor_copy(out=vs_aug[:, tt, :, 0:64], in_=pzz[:, 0:128].rearrange("p (g d) -> p g d", g=2)), reads=[pzb], writes=[vtok_b[tt]])
                        k.op(DVE, lambda tt=tt, pzz=pzz: nc.vector.tensor_copy(out=vw_aug[:, tt, :, 0:64], in_=pzz[:, 128:256].rearrange("p (g d) -> p g d", g=2)), reads=[pzb], writes=[vtok_b[tt]])
                        k.op(DVE, lambda tt=tt, pzz=pzz: nc.vector.tensor_copy(out=gates[:, tt, :], in_=pzz[:, 256:280]), reads=[pzb], writes=[vtok_b[tt]])
                    else:
                        k.op(ACT, lambda tt=tt, pzz=pzz: nc.scalar.activation(out=vs_aug[:, tt, :, 0:64], in_=pzz[:, 0:128].rearrange("p (g d) -> p g d", g=2), func=AF.Copy), reads=[pzb], writes=[vtok_b[tt]])
                        k.op(ACT, lambda tt=tt, pzz=pzz: nc.scalar.activation(out=vw_aug[:, tt, :, 0:64], in_=pzz[:, 128:256].rearrange("p (g d) -> p g d", g=2), func=AF.Copy), reads=[pzb], writes=[vtok_b[tt]])
                        k.op(ACT, lambda tt=tt, pzz=pzz: nc.scalar.activation(out=gates[:, tt, :], in_=pzz[:, 256:280], func=AF.Copy), reads=[pzb], writes=[vtok_b[tt]])
            k.op(ACT, lambda: nc.scalar.activation(out=gates[:], in_=gates[:], func=AF.Sigmoid), reads=vtok_b, writes=vtok_b)

        if upto <= 1:
            k.drain()
            return nc
        k.barrier()
        with ExitStack() as p2:
            cw = sb("cw", [128, 4, 4], F32, p2); cb_ = sb("cb", [128, 4], F32, p2)
            bat = sb("bat", [128, 4], F32, p2); bxt = sb("bxt", [128, 4], F32, p2)
            lam = sb("lam", [128, 4], F32, p2); cc1 = sb("cc1", [128, 4], F32, p2); cc2 = sb("cc2", [128, 4], F32, p2)
            prm_b = Buf("rnnprm")
            pds = k.dsem("rp")
            for (t, d) in ((cw, convw_d), (cb_, convb_d), (bat, ba_d), (bxt, bx_d), (lam, lam_d)):
                k.dma(SP, pds, t[:], d, writes=[prm_b])
            k.op(ACT, lambda: nc.scalar.activation(out=cc1[:], in_=lam[:], func=AF.Exp, scale=-1.0), reads=[prm_b], writes=[prm_b])
            k.op(ACT, lambda: nc.scalar.activation(out=cc1[:], in_=cc1[:], func=AF.Ln, bias=1.0), reads=[prm_b], writes=[prm_b])
            k.op(DVE, lambda: nc.vector.tensor_scalar(out=cc2[:], in0=cc1[:], scalar1=-16.0, scalar2=None, op0=ALU.mult), reads=[prm_b], writes=[prm_b])
            k.op(DVE, lambda: nc.vector.tensor_scalar(out=cc1[:], in0=cc1[:], scalar1=-8.0, scalar2=None, op0=ALU.mult), reads=[prm_b], writes=[prm_b])
            wbd_a = sb("wbd_a", [128, 4, 128], BF16, p2); wbd_x = sb("wbd_x", [128, 4, 128], BF16, p2)
            wbd_b = Buf("wbd")
            k.op(DVE, lambda: nc.vector.memset(wbd_a[:], 0.0), writes=[wbd_b])
            k.op(DVE, lambda: nc.vector.memset(wbd_x[:], 0.0), writes=[wbd_b])
            for (t, d) in ((wbd_a, wa_d), (wbd_x, wx_d)):
                dv = d.rearrange("(f two) i j -> two i f j", two=2)
                for hh in range(2):
                    k.dma(POOL, k.dsem("wbd"), t[hh * 64:(hh + 1) * 64, :, hh * 64:(hh + 1) * 64], dv[hh], writes=[wbd_b])
            NP = 4
            PW = S // NP
            xrR = Rot(k, p2, nc, "xr", 2, [128, S], F32)
            ggR = Rot(k, p2, nc, "gg", 2, [128, S], F32)
            xc = sb("xc", [128, S], F32, p2); xcb = sb("xcb", [128, S], BF16, p2)
            rr = sb("rr", [128, S], F32, p2); ii = sb("ii", [128, S], F32, p2); a2 = sb("a2", [128, S], F32, p2)
            hh_ = sb("hh", [128, S], F32, p2)
            rnb = sb("rnb", [128, S], BF16, p2); sqb = sb("sqb", [128, S], BF16, p2)
            xc_b, xcb_b, rr_b, ii_b, a2_b, hh_b, rnb_b, sqb_b = ([Buf(f"{n}{p}") for p in range(NP)] for n in ("xc", "xcb", "rr", "ii", "a2", "hh", "rnb", "sqb"))
            rn_ds = [k.dsem("rn") for _ in range(2)]
            pg = [pst(f"pg{i}", [128, 512], F32, p2) for i in range(3)]
            pg_b = [Buf(f"pg{i}") for i in range(3)]
            pstat = pst("pstat", [128, 512], F32, p2)
            pstat_b = Buf("pstat")
            gi = [0]
            gg_parts = [[Buf(f"ggs{sl}_{p}") for p in range(NP)] for sl in range(2)]

            def load_ft(ft):
                xr, xr_b, xr_ds = xrR.next()
                gg, _, gg_ds = ggR.next()
                gg_bp = gg_parts[ft % 2]
                k.dma(SP, xr_ds, xr[:], zr_d[512 + ft * 128:512 + (ft + 1) * 128, :], reads=[DB((id(zr_d), 512 + ft * 128, ch)) for ch in range(NCH)], writes=[xr_b])
                k.dma(SP, gg_ds, gg[:], zr_d[ft * 128:(ft + 1) * 128, :], reads=[DB((id(zr_d), ft * 128, ch)) for ch in range(NCH)], writes=gg_bp)
                return xr, xr_b, gg, gg_bp

            def cs(p):
                return p * PW, (p + 1) * PW

            def front(ft, xr, xr_b, gg, gg_b):
                for p in range(NP):
                    c0, c1 = cs(p)
                    k.op(DVE, lambda c0=c0, c1=c1: nc.vector.tensor_scalar(out=xc[:, c0:c1], in0=xr[:, c0:c1], scalar1=cw[:, ft, 3:4], scalar2=cb_[:, ft:ft + 1], op0=ALU.mult, op1=ALU.add),
                         reads=[xr_b, prm_b], writes=[xc_b[p]])
                    for sh in (1, 2, 3):
                        lo = max(c0, sh)
                        k.op(DVE, lambda lo=lo, c1=c1, sh=sh: nc.vector.scalar_tensor_tensor(out=xc[:, lo:c1], in0=xr[:, lo - sh:c1 - sh], scalar=cw[:, ft, 3 - sh:4 - sh],
                                                                                            in1=xc[:, lo:c1], op0=ALU.mult, op1=ALU.add),
                             reads=[xr_b, prm_b, xc_b[p]], writes=[xc_b[p]])
                    k.op(ACT, lambda c0=c0, c1=c1: nc.scalar.activation(out=xcb[:, c0:c1], in_=xc[:, c0:c1], func=AF.Copy), reads=[xc_b[p]], writes=[xcb_b[p]])
                    k.op(ACT, lambda c0=c0, c1=c1: nc.scalar.activation(out=gg[:, c0:c1], in_=gg[:, c0:c1], func=AF.Gelu_apprx_tanh), reads=[gg_b[p]], writes=[gg_b[p]])

            def rnn_gates(ft):
                for p in range(NP):
                    c0, c1 = cs(p)
                    for (wt, bt, dst, dstb) in ((wbd_a, bat, rr, rr_b), (wbd_x, bxt, ii, ii_b)):
                        for cc in range(c0, c1, 512):
                            pgg = pg[gi[0] % 3]; pgb = pg_b[gi[0] % 3]; gi[0] += 1
                            k.op(PE, lambda wt=wt, cc=cc, pgg=pgg: nc.tensor.matmul(pgg[:, :], lhsT=wt[:, ft, :], rhs=xcb[:, cc:cc + 512], start=True, stop=True),
                                 reads=[wbd_b, xcb_b[p]], writes=[pgb])
                            k.op(ACT, lambda bt=bt, cc=cc, pgg=pgg, dst=dst: nc.scalar.activation(out=dst[:, cc:cc + 512], in_=pgg[:, :], func=AF.Sigmoid, bias=bt[:, ft:ft + 1]),
                                 reads=[pgb, prm_b], writes=[dstb[p]])

            def exps_m(ft):
                for p in range(NP):
                    c0, c1 = cs(p)
                    k.op(ACT, lambda c0=c0, c1=c1: nc.scalar.activation(out=a2[:, c0:c1], in_=rr[:, c0:c1], func=AF.Exp, scale=cc2[:, ft:ft + 1]), reads=[rr_b[p], prm_b], writes=[a2_b[p]])
                    k.op(ACT, lambda c0=c0, c1=c1: nc.scalar.activation(out=rr[:, c0:c1], in_=rr[:, c0:c1], func=AF.Exp, scale=cc1[:, ft:ft + 1]), reads=[rr_b[p], prm_b], writes=[rr_b[p]])
                    k.op(DVE, lambda c0=c0, c1=c1: nc.vector.tensor_tensor(out=ii[:, c0:c1], in0=ii[:, c0:c1], in1=xc[:, c0:c1], op=ALU.mult), reads=[ii_b[p], xc_b[p]], writes=[ii_b[p]])

            def sqrts(ft):
                for p in range(NP):
                    c0, c1 = cs(p)
                    k.op(ACT, lambda c0=c0, c1=c1: nc.scalar.activation(out=a2[:, c0:c1], in_=a2[:, c0:c1], func=AF.Sqrt, scale=-1.0, bias=1.0), reads=[a2_b[p]], writes=[a2_b[p]])

            def tail(ft, gg, gg_b):
                for p in range(NP):
                    c0, c1 = cs(p)
                    k.op(DVE, lambda c0=c0, c1=c1: nc.vector.tensor_tensor(out=a2[:, c0:c1], in0=a2[:, c0:c1], in1=ii[:, c0:c1], op=ALU.mult), reads=[a2_b[p], ii_b[p]], writes=[a2_b[p]])
                    init = 0.0 if p == 0 else hh_[:, c0 - 1:c0]
                    k.op(DVE, lambda c0=c0, c1=c1, init=init: nc.vector.tensor_tensor_scan(out=hh_[:, c0:c1], data0=rr[:, c0:c1], data1=a2[:, c0:c1], initial=init, op0=ALU.mult, op1=ALU.add),
                         reads=[rr_b[p], a2_b[p]] + ([hh_b[p - 1]] if p else []), writes=[hh_b[p]])
                    k.op(DVE, lambda c0=c0, c1=c1: nc.vector.tensor_tensor(out=rnb[:, c0:c1], in0=gg[:, c0:c1], in1=hh_[:, c0:c1], op=ALU.mult), reads=[gg_b[p], hh_b[p]], writes=[rnb_b[p]])
                    k.op(DVE, lambda c0=c0, c1=c1: nc.vector.tensor_tensor(out=sqb[:, c0:c1], in0=rnb[:, c0:c1], in1=rnb[:, c0:c1], op=ALU.mult), reads=[rnb_b[p]], writes=[sqb_b[p]])
                    for cc in range(c0, c1, 512):
                        ch = cc // 512
                        k.dma(POOL, rn_ds[ch % 2], rnn_d[ft * 128:(ft + 1) * 128, cc:cc + 512], rnb[:, cc:cc + 512], reads=[rnb_b[p]], writes=[DB(("rnn", ch))])
                    for tt in range(c0 // 128, c1 // 128):
                        k.op(PE, lambda tt=tt: nc.tensor.matmul(pstat[:, tt * 4 + ft: tt * 4 + ft + 1], lhsT=sqb[:, tt * 128:(tt + 1) * 128], rhs=ones_bf[:, 0:1], start=True, stop=True),
                             reads=[sqb_b[p], ones_b], writes=[pstat_b])

            cur_ft = load_ft(0)
            front(0, *cur_ft)
            for ft in range(4):
                xr, xr_b, gg, gg_b = cur_ft
                if ft + 1 < 4:
                    nxt_ft = load_ft(ft + 1)
                rnn_gates(ft)
                exps_m(ft)
                sqrts(ft)
                if ft + 1 < 4:
                    front(ft + 1, *nxt_ft)
                tail(ft, gg, gg_b)
                if ft + 1 < 4:
                    cur_ft = nxt_ft
            k.op(DVE, lambda: nc.vector.tensor_reduce(out=ssr[:], in_=pstat[:, 0:NT * 4].rearrange("p (t f) -> p t f", f=4), axis=mybir.AxisListType.X, op=ALU.add),
                 reads=[pstat_b], writes=[ssr_b])
            k.op(ACT, lambda: nc.scalar.activation(out=ssq[:], in_=ssr[:], func=AF.Sqrt, scale=1.0 / 512, bias=EPS), reads=[ssr_b], writes=[ssr_b])
            k.op(DVE, lambda: nc.vector.reciprocal(out=ssr[:], in_=ssq[:]), reads=[ssr_b], writes=[ssr_b])

        if upto <= 2:
            k.drain()
            return nc
        k.barrier()
        with ExitStack() as p4:
            kcT = [sb(f"kcT{g}", [128, 256], BF16, p4) for g in range(2)]
            cllo = sb("cllo", [128, 2, 8], F32, p4)
            vc_aug = [sb(f"vca{g}", [128, 2, 128], BF16, p4) for g in range(2)]
            cmp_b = Buf("cmp")
            cds = k.dsem("cmpc")
            k.dma(SP, cds, cllo[:], cllo_d, writes=[cmp_b])
            for g in range(2):
                k.dma(SP, cds, kcT[g][64:128, :], ce01_d, writes=[cmp_b])
                k.dma(SP, cds, vc_aug[g][:, :, 65:128], ovl_d, writes=[cmp_b])
                k.op(DVE, lambda g=g: nc.vector.memset(vc_aug[g][:, :, 64:65], 1.0), writes=[cmp_b])
                k.op(DVE, lambda g=g: nc.vector.memset(vc_aug[g][:, :, 0:64], 0.0), writes=[cmp_b])
            kTs = [sb(f"kTs{g}", [128, S], BF16, p4) for g in range(2)]
            kTw = [sb(f"kTw{g}", [128, S], BF16, p4) for g in range(2)]
            kds_shared = k.dsem("kT")
            cds_shared = k.dsem("cst")
            kTs_b = [Buf(f"kTs{g}") for g in range(2)]
            kTw_b = [Buf(f"kTw{g}") for g in range(2)]
            cmask = sb("cmask", [128, 2, S], BF16, p4)
            tab = sb("tab", [128, S], BF16, p4)
            sllo = sb("sllo", [128, 8], F32, p4)
            wm01 = sb("wm01", [128, 8, 512], BF16, p4)
            addm = sb("addm", [128, NT, 64], F32, p4)
            cmask_b, tab_b, sllo_b, wm01_b, addm_b = (Buf(n) for n in ("cmask", "tab", "sllo", "wm01", "addm"))
            wout = sb("wout", [128, 8, D], BF16, p4)
            wout_b = Buf("wout")
            gcol = sb("gcol", [128, 8], F32, p4)
            gcol_b = Buf("gcol")
            gds = k.dsem("gcol")
            pw = ExitStack()
            wst = Rot(k, pw, nc, "wst", 2, [128, D], F32)

            def issue_resident_loads():
                for g in range(2):
                    for (t, tb_, src) in ((kTs[g], kTs_b[g], ks_d), (kTw[g], kTw_b[g], kw_d)):
                        k.dma(SP, kds_shared, t[0:64, :], src[g * 64:(g + 1) * 64, :], reads=[DB((id(src), 0, ch)) for ch in range(NCH)], writes=[tb_])
                        k.dma(SP, kds_shared, t[64:128, :], e01_d, writes=[tb_])
                k.dma(SP, cds_shared, cmask[:], cmask_d, writes=[cmask_b])
                k.dma(SP, cds_shared, tab[64:128, :], tab_d, writes=[tab_b])
                k.dma(SP, cds_shared, sllo[:], sllo_d, writes=[sllo_b])
                k.dma(SP, cds_shared, wm01[:], wm01_d, writes=[wm01_b])
                k.dma(SP, cds_shared, addm[:], addm_d, writes=[addm_b])
                k.dma(SP, gds, gcol[:, 0:4], grnn_d, writes=[gcol_b])
                k.dma(SP, gds, gcol[:, 4:8], gatt_d, writes=[gcol_b])
                for kk in range(8):
                    t, tb, ds = wst.next()
                    k.dma(SP, ds, t[:], wout_d[kk * 128:(kk + 1) * 128, :], writes=[tb])
                    k.op(DVE, lambda kk=kk, t=t: nc.vector.tensor_scalar(out=wout[:, kk, :], in0=t[:], scalar1=gcol[:, kk:kk + 1], scalar2=None, op0=ALU.mult),
                         reads=[tb, gcol_b], writes=[wout_b])

            with ExitStack() as p3:
                w1 = [sb(f"w1_{i}", [64, 32, 256], BF16, p3) for i in range(2)]
                w2 = [sb(f"w2_{i}", [128, 2, 64], BF16, p3) for i in range(2)]
                pos = [sb(f"pos_{i}", [64, 32], BF16, p3) for i in range(2)]
                posf = [sb(f"posf_{i}", [64, 32], F32, p3) for i in range(2)]
                cw_b = Buf("cmpw")
                cw1_b = [[Buf(f"cw1_{i}_{q}") for q in range(4)] for i in range(2)]
                cwd = k.dsem("cmpw")
                cwd2 = k.dsem("cmpp")
                for i, (w1d, w2d, pd) in enumerate(((w1k_d, w2k_d, posk_d), (w1v_d, w2v_d, posv_d))):
                    w1v = w1d.rearrange("(l d) h -> d l h", d=64)
                    for l0 in range(0, 32, 8):
                        k.dma(POOL, k.dsem("w1"), w1[i][:, l0:l0 + 8, :], w1v[:, l0:l0 + 8, :], writes=[cw1_b[i][l0 // 8]])
                    k.dma(POOL, cwd, w2[i][:], w2d.rearrange("(t p) d -> p t d", p=128), writes=[cw_b])
                    k.dma(SP, cwd2, posf[i][:], pd, writes=[cw_b])
                    k.op(DVE, lambda i=i: nc.vector.tensor_copy(out=pos[i][:], in_=posf[i][:]), reads=[cw_b], writes=[cw_b])
                cin = [[sb(f"cin{i}{g}", [64, S], BF16, p3) for g in range(2)] for i in range(2)]
                cin_bs = [[Buf(f"cin{i}{g}") for g in range(2)] for i in range(2)]
                for i, src in enumerate((kc_d, vc_d)):
                    cin_ds = k.dsem("cin")
                    for g in range(2):
                        k.dma(SP, cin_ds, cin[i][g][:], src[g * 64:(g + 1) * 64, :], reads=[DB((id(src), 0, ch)) for ch in range(NCH)], writes=[cin_bs[i][g]])
                issue_resident_loads()
                hid = sb("hid", [128, 2, 256], BF16, p3)
                hid_b = Buf("hid")
                cbias = sb("cbias", [128, 2, 2], F32, p3)
                cbias_b = Buf("cbias")
                pc_ = [pst(f"pc{i}", [128, 512], F32, p3) for i in range(3)]
                pc_b = [Buf(f"pc{i}") for i in range(3)]
                ci = 0
                for i in range(2):
                    for ht in range(2):
                        pcc = pc_[ci % 3]; pcb = pc_b[ci % 3]; ci += 1
                        for l in range(32):
                            k.op(PE, lambda i=i, ht=ht, l=l, pcc=pcc: nc.tensor.matmul(pcc[:, 0:1], lhsT=w1[i][:, l, ht * 128:(ht + 1) * 128], rhs=pos[i][:, l:l + 1],
                                                                                      start=(l == 0), stop=(l == 31)), reads=[cw_b, cw1_b[i][l // 8]], writes=[pcb])
                        k.op(DVE, lambda i=i, ht=ht, pcc=pcc: nc.vector.tensor_copy(out=cbias[:, i, ht:ht + 1], in_=pcc[:, 0:1]), reads=[pcb], writes=[cbias_b])
                for i in range(2):
                    for g in range(2):
                        cv = cin[i][g][:].rearrange("p (c r) -> p c r", r=16)
                        for ht in range(2):
                            pcc = pc_[ci % 3]; pcb = pc_b[ci % 3]; ci += 1
                            for l in range(32):
                                rhs = cv[:, 0:255, l] if l < 16 else cv[:, 1:256, l - 16]
                                k.op(PE, lambda i=i, ht=ht, l=l, pcc=pcc, rhs=rhs: nc.tensor.matmul(pcc[:, 0:255], lhsT=w1[i][:, l, ht * 128:(ht + 1) * 128], rhs=rhs,
                                                                                                   start=(l == 0), stop=(l == 31)), reads=[cw_b, cw1_b[i][l // 8], cin_bs[i][g]], writes=[pcb])
                            k.op(ACT, lambda i=i, ht=ht, pcc=pcc: nc.scalar.activation(out=hid[:, ht, 0:255], in_=pcc[:, 0:255], func=AF.Gelu_apprx_tanh, bias=cbias[:, i, ht:ht + 1]),
                                 reads=[pcb, cbias_b], writes=[hid_b])
                        if i == 0:
                            pcc = pc_[ci % 3]; pcb = pc_b[ci % 3]; ci += 1
                            for ht in range(2):
                                k.op(PE, lambda ht=ht, pcc=pcc: nc.tensor.matmul(pcc[0:64, 0:255], lhsT=w2[0][:, ht, :], rhs=hid[:, ht, 0:255], start=(ht == 0), stop=(ht == 1)),
                                     reads=[cw_b, hid_b], writes=[pcb])
                            k.op(DVE, lambda g=g, pcc=pcc: nc.vector.tensor_copy(out=kcT[g][0:64, 0:255], in_=pcc[0:64, 0:255]), reads=[pcb], writes=[cmp_b])
                        else:
                            for ct in range(2):
                                nr = 128 if ct == 0 else 127
                                pcc = pc_[ci % 3]; pcb = pc_b[ci % 3]; ci += 1
                                for ht in range(2):
                                    k.op(PE, lambda ht=ht, ct=ct, nr=nr, pcc=pcc: nc.tensor.matmul(pcc[0:nr, 0:64], lhsT=hid[:, ht, ct * 128:ct * 128 + nr], rhs=w2[1][:, ht, :],
                                                                                                  start=(ht == 0), stop=(ht == 1)), reads=[cw_b, hid_b], writes=[pcb])
                                k.op(DVE, lambda g=g, ct=ct, nr=nr, pcc=pcc: nc.vector.tensor_copy(out=vc_aug[g][0:nr, ct, 0:64], in_=pcc[0:nr, 0:64]), reads=[pcb], writes=[cmp_b])

            if upto <= 3:
                pw.close()
                k.drain()
                return nc
            k.barrier()
            pw.close()
            qs = Rot(k, p4, nc, "qs", 1, [128, 8, 512], BF16)
            qw = Rot(k, p4, nc, "qw", 2, [128, 8, 512], BF16)
            SLOPES = [2.0 ** (-(h + 1)) for h in range(8)]
            PT = Rot(k, p4, nc, "PT", 6, [128, 512], BF16, with_dsem=False)
            PTc = Rot(k, p4, nc, "PTc", 2, [128, 512], BF16, with_dsem=False)
            att = sb("att", [128, 4, 512], F32, p4)
            att_b = [Buf(f"att{i}") for i in range(4)]
            attb = sb("attb", [128, 4, 512], BF16, p4)
            attb_b = Buf("attb")
            attT = sb("attT", [128, 4, 512], BF16, p4)
            attT_b = Buf("attT")
            rnT = Rot(k, p4, nc, "rnT", 3, [128, 4, 512], BF16)
            imp = sb("imp", [128, 4, 2, 64], F32, p4)
            imp_b = [Buf(f"imp{i}") for i in range(4)]
            for i_ in range(4):
                k.op(DVE, lambda i_=i_: nc.vector.memset(imp[:, i_, :, :], 0.0), writes=[imp_b[i_]])
            selbT = [sb(f"selbT{g}", [128, 512], BF16, p4) for g in range(2)]
            selbT_b = [Buf(f"selbT{g}") for g in range(2)]
            sm = sb("sm", [128, 256], F32, p4)
            sm_b = [Buf(f"sm{i}") for i in range(16)]
            smi = [0]
            tk_a = sb("tk_a", [128, 8, 64], F32, p4); tk_b_ = sb("tk_b", [128, 8, 64], F32, p4)
            tk8 = sb("tk8", [128, 8, 16], F32, p4); tksel = sb("tksel", [128, 8, 64], BF16, p4)
            tk_bufs = [Buf(f"tk{u}") for u in range(8)]
            ssa = sb("ssa", [128, 8], F32, p4)
            ssa_b = Buf("ssa")
            ysb = Rot(k, p4, nc, "ysb", 2, [128, D], F32)
            xres = Rot(k, p4, nc, "xres", 2, [128, D], F32)
            junk4 = sb("junk4", [128, D], BF16, p4)
            junk4_b = Buf("junk4")
            print("P4 sbuf bytes remaining:", nc.sbuf_bytes_remaining)
            k.barrier()
            pS = [pst(f"pS{i}", [128, 512], F32, p4) for i in range(3)]
            pS_b = [Buf(f"pS{i}") for i in range(3)]
            pA = [pst(f"pA{i}", [128, 512], F32, p4) for i in range(2)]
            pA_b = [Buf(f"pA{i}") for i in range(2)]
            pTT = pst("pTT", [128, 1024], BF16, p4)
            pTT_b = Buf("pTT")
            pW = [pst(f"pW{i}", [128, 512], F32, p4) for i in range(1)]
            pW_b = [Buf(f"pW{i}") for i in range(1)]
            pC = pst("pC", [128, 512], F32, p4)
            pC_b = Buf("pC")
            cnt = {"s": 0, "a": 0}

            def nextS():
                i = cnt["s"] % 3; cnt["s"] += 1
                return pS[i], pS_b[i]

            def nextA():
                i = cnt["a"] % 2; cnt["a"] += 1
                return pA[i], pA_b[i]

            def small():
                i = smi[0] % 16; smi[0] += 1
                return sm[:, 16 * i:16 * i + 16], sm_b[i]

            att_written = set()

            def evac_group(pa, pab, stride, n, h, tl0, tt0, gate_idx, first, with_imp=False, g=None, first_in_group=False):
                s4, s4b = small()
                sums = pa[:, 64:64 + (n - 1) * stride + 1:stride]
                k.op(DVE, lambda: nc.vector.tensor_scalar(out=s4[:, 0:n], in0=sums, scalar1=1e-30, scalar2=None, op0=ALU.max), reads=[pab], writes=[s4b])
                k.op(DVE, lambda: nc.vector.reciprocal(out=s4[:, 4:4 + n], in_=s4[:, 0:n]), reads=[s4b], writes=[s4b])
                k.op(DVE, lambda: nc.vector.tensor_tensor(out=s4[:, 8:8 + n], in0=s4[:, 4:4 + n], in1=gates[:, tt0:tt0 + n, gate_idx], op=ALU.mult),
                     reads=[s4b] + [vtok_b[tt0 + i] for i in range(n)], writes=[s4b])
                for i in range(n):
                    tl = tl0 + i
                    dst = att[:, tl, h * 64:(h + 1) * 64]
                    src = pa[:, i * stride:i * stride + 64]
                    first = (h, tl) not in att_written
                    att_written.add((h, tl))
                    if first:
                        k.op(DVE, lambda dst=dst, src=src, i=i: nc.vector.tensor_scalar(out=dst, in0=src, scalar1=s4[:, 8 + i:9 + i], scalar2=None, op0=ALU.mult),
                             reads=[pab, s4b], writes=[att_b[tl]])
                    else:
                        k.op(DVE, lambda dst=dst, src=src, i=i: nc.vector.scalar_tensor_tensor(out=dst, in0=src, scalar=s4[:, 8 + i:9 + i], in1=dst, op0=ALU.mult, op1=ALU.add),
                             reads=[pab, s4b, att_b[tl]], writes=[att_b[tl]])
                    if with_imp:
                        idst = imp[:, tl, g, 1:64]
                        isrc = pa[:, i * stride + 65:i * stride + 128]
                        if first_in_group:
                            k.op(DVE, lambda idst=idst, isrc=isrc, i=i: nc.vector.tensor_scalar(out=idst, in0=isrc, scalar1=s4[:, 4 + i:5 + i], scalar2=None, op0=ALU.mult),
                                 reads=[pab, s4b], writes=[imp_b[tl]])
                        else:
                            k.op(DVE, lambda idst=idst, isrc=isrc, i=i: nc.vector.scalar_tensor_tensor(out=idst, in0=isrc, scalar=s4[:, 4 + i:5 + i], in1=idst, op0=ALU.mult, op1=ALU.add),
                                 reads=[pab, s4b, imp_b[tl]], writes=[imp_b[tl]])

            dbg_ds = k.dsem("dbg")

            qview = q_d.rearrange("(h d) t -> d h t", d=64)

            def load_chunk(jn):
                tn = jn * 512
                rt_, rtb_, rds_ = rnT.next()
                k.dma(SP, rds_, rt_[:], rnn_d[:, tn:tn + 512].rearrange("(f p) t -> p f t", p=128), reads=[DB(("rnn", jn))], writes=[rtb_])
                qw_, qwb_, qwd_ = qw.next()
                k.dma(SP, qwd_, qw_[0:64, :, :], qview[:, :, tn:tn + 512], reads=[DB((id(q_d), c4 * 128, jn)) for c4 in range(4)], writes=[qwb_])
                for h_ in range(8):
                    k.op(DVE, lambda h_=h_, qw_=qw_: nc.vector.tensor_scalar(out=qw_[64:128, h_, :], in0=tab[64:128, tn:tn + 512], scalar1=SLOPES[h_], scalar2=None, op0=ALU.mult),
                         reads=[tab_b], writes=[qwb_])
                return rt_, rtb_, qw_, qwb_

            def load_qs(jn):
                tn = jn * 512
                qs_, qsb_, qsd_ = qs.next()
                k.dma(SP, qsd_, qs_[0:64, :, :], qview[:, :, tn:tn + 512], reads=[DB((id(q_d), c4 * 128, jn)) for c4 in range(4)], writes=[qsb_])
                return qs_, qsb_

            nxt = load_chunk(0)
            nxt_qs = load_qs(0)
            wcnt = [0]

            def nextW():
                return pW[0], pW_b[0]

            def make_cback(jc, rt, rtb):
                units = []

                def u_tr(half):
                    for tl2 in range(2):
                        tl = half * 2 + tl2
                        for f in range(4):
                            k.op(PE, lambda tl=tl, tl2=tl2, f=f: nc.tensor.transpose(out=pTT[:, (tl2 * 4 + f) * 128:(tl2 * 4 + f + 1) * 128], in_=attb[:, tl, f * 128:(f + 1) * 128], identity=ident[:]),
                                 reads=[attb_b, ident_b], writes=[pTT_b])
                    for tl2 in range(2):
                        tl = half * 2 + tl2
                        k.op(DVE, lambda tl=tl, tl2=tl2: nc.vector.tensor_copy(out=attT[:, :, tl * 128:(tl + 1) * 128],
                                                                             in_=pTT[:, tl2 * 512:(tl2 + 1) * 512].rearrange("p (f t) -> p f t", f=4)),
                             reads=[pTT_b], writes=[attT_b])
                units.append(lambda: u_tr(0))
                units.append(lambda: u_tr(1))
                st = {}

                def u_mm(tl, half, part):
                    tt = jc * 4 + tl
                    if half == 0 and part == 0:
                        yt, ytb, yds = ysb.next()
                        xr_, xrb, xds = xres.next()
                        k.dma(SP, xds, xr_[:], x_d[tt * 128:(tt + 1) * 128, :], writes=[xrb])
                        st[tl] = (yt, ytb, yds, xr_, xrb)
                    yt, ytb, yds, xr_, xrb = st[tl]
                    if part == 0:
                        st[(tl, half)] = nextW()
                    pw, pwb = st[(tl, half)]
                    if part == 0:
                        for f in range(4):
                            k.op(PE, lambda f=f: nc.tensor.matmul(pw[:, :], lhsT=rt[:, f, tl * 128:(tl + 1) * 128], rhs=wout[:, f, half * 512:(half + 1) * 512],
                                                                  start=(f == 0), stop=False), reads=[rtb, wout_b], writes=[pwb])
                    else:
                        for f in range(4):
                            k.op(PE, lambda f=f: nc.tensor.matmul(pw[:, :], lhsT=attT[:, f, tl * 128:(tl + 1) * 128], rhs=wout[:, 4 + f, half * 512:(half + 1) * 512],
                                                                  start=False, stop=(f == 3)), reads=[attT_b, wout_b], writes=[pwb])
                        k.op(DVE, lambda: nc.vector.tensor_scalar(out=yt[:, half * 512:(half + 1) * 512], in0=pw[:, :], scalar1=ssr[:, tt:tt + 1], scalar2=None, op0=ALU.mult),
                             reads=[pwb, ssr_b], writes=[ytb])

                def u_epi(tl):
                    tt = jc * 4 + tl
                    yt, ytb, yds, xr_, xrb = st[tl]
                    if debug:
                        k.dma(POOL, dbg_ds, dbg["d_y"][tt * 128:(tt + 1) * 128, :], yt[:], reads=[ytb], writes=[DB(("dy", tt))])
                    s4, s4b = small()
                    k.op(ACT, lambda: nc.scalar.activation(out=junk4[:], in_=yt[:], func=AF.Square, accum_out=s4[:, 0:1]), reads=[ytb], writes=[junk4_b, s4b])
                    k.op(ACT, lambda: nc.scalar.activation(out=s4[:, 1:2], in_=s4[:, 0:1], func=AF.Ln, scale=1.0 / D, bias=EPS), reads=[s4b], writes=[s4b])
                    k.op(ACT, lambda: nc.scalar.activation(out=s4[:, 2:3], in_=s4[:, 1:2], func=AF.Exp, scale=-0.5), reads=[s4b], writes=[s4b])
                    k.op(DVE, lambda: nc.vector.scalar_tensor_tensor(out=yt[:], in0=yt[:], scalar=s4[:, 2:3], in1=C1row[:], op0=ALU.mult, op1=ALU.mult),
                         reads=[ytb, s4b, C1_b], writes=[ytb])
                    k.op(DVE, lambda: nc.vector.tensor_tensor(out=yt[:], in0=yt[:], in1=xr_[:], op=ALU.add), reads=[ytb, xrb], writes=[ytb])
                    k.dma(POOL, yds, x1_d[tt * 128:(tt + 1) * 128, :], yt[:], reads=[ytb], writes=[DB(("x1", tt))])

                for tl in range(4):
                    for half in range(2):
                        for part in range(2):
                            units.append(lambda tl=tl, half=half, part=part: u_mm(tl, half, part))
                    units.append(lambda tl=tl: u_epi(tl))
                return units

            cback = []
            for j in range(NCH):
                t0 = j * 512
                att_written.clear()
                rt, rtb, qwt, qwb = nxt
                qst, qsb = nxt_qs
                if j + 1 < NCH:
                    nxt = load_chunk(j + 1)
                ncts = [ct for ct in range(2) if 16 * (ct * 128) + 31 <= t0 + 511]

                def cmp_S(h):
                    g = h // 4
                    pts = []
                    for ct in ncts:
                        nr = 128 if ct == 0 else 127
                        ps, psb = nextS()
                        k.op(PE, lambda ps=ps, ct=ct, nr=nr, g=g, h=h: nc.tensor.matmul(ps[0:nr, :], lhsT=kcT[g][:, ct * 128:ct * 128 + nr], rhs=qwt[:, h, :], start=True, stop=False),
                             reads=[cmp_b, qwb], writes=[psb])
                        k.op(PE, lambda ps=ps, ct=ct, nr=nr: nc.tensor.matmul(ps[0:nr, :], lhsT=ident[:, 0:nr], rhs=cmask[:, ct, t0:t0 + 512], start=False, stop=True),
                             reads=[ident_b, cmask_b], writes=[psb])
                        pt, ptb, _ = PTc.next()
                        k.op(ACT, lambda ps=ps, pt=pt, nr=nr, ct=ct, h=h: nc.scalar.activation(out=pt[0:nr, :], in_=ps[0:nr, :], func=AF.Exp, bias=cllo[0:nr, ct, h:h + 1]),
                             reads=[psb, cmp_b], writes=[ptb])
                        pts.append((pt, ptb, ct, nr))
                    return pts

                def cmp_PV(h, pts):
                    g = h // 4
                    nmm = 4 * len(pts)
                    mi = 0
                    for tl in range(4):
                        for (pt, ptb, ct, nr) in pts:
                            k.op(PE, lambda pt=pt, tl=tl, nr=nr, ct=ct, g=g, mi=mi, nmm=nmm: nc.tensor.matmul(
                                pC[:, tl * 128:(tl + 1) * 128], lhsT=pt[0:nr, tl * 128:(tl + 1) * 128], rhs=vc_aug[g][0:nr, ct, :],
                                start=(mi == 0), stop=(mi == nmm - 1)),
                                reads=[ptb, cmp_b], writes=[pC_b])
                            mi += 1
                    evac_group(pC, pC_b, 128, 4, h, 0, j * 4, 0 * 8 + h, True, with_imp=True, g=g, first_in_group=(h % 4 == 0))

                pre = []
                cst = {}

                def u_cS(h):
                    cst[h] = cmp_S(h)

                def u_cPV(h):
                    cmp_PV(h, cst[h])
                for h in range(8):
                    pre.append(lambda h=h: u_cS(h))
                    pre.append(lambda h=h: u_cPV(h))
                units8 = [(tl, g) for tl in range(4) for g in range(2)]

                def tk_stage(sidx):
                    for u, (tl, g) in enumerate(units8):
                        tt = j * 4 + tl
                        tb = tk_bufs[u]
                        if sidx == 0:
                            k.op(DVE, lambda u=u, tl=tl, g=g, tt=tt: nc.vector.tensor_tensor(out=tk_a[:, u, :], in0=imp[:, tl, g, :], in1=addm[:, tt, :], op=ALU.add),
                                 reads=[imp_b[tl], addm_b], writes=[tb])
                        elif sidx == 1:
                            k.op(DVE, lambda u=u: nc.vector.max(out=tk8[:, u, 0:8], in_=tk_a[:, u, :]), reads=[tb], writes=[tb])
                        elif sidx == 2:
                            k.op(DVE, lambda u=u: nc.vector.match_replace(out=tk_b_[:, u, :], in_to_replace=tk8[:, u, 0:8], in_values=tk_a[:, u, :], imm_value=-3.0e38), reads=[tb], writes=[tb])
                        elif sidx == 3:
                            k.op(DVE, lambda u=u: nc.vector.max(out=tk8[:, u, 8:16], in_=tk_b_[:, u, :]), reads=[tb], writes=[tb])
                        elif sidx == 4:
                            k.op(DVE, lambda u=u: nc.vector.tensor_scalar(out=tksel[:, u, :], in0=tk_a[:, u, :], scalar1=tk8[:, u, 15:16], scalar2=NEGM, op0=ALU.is_lt, op1=ALU.mult),
                                 reads=[tb], writes=[tb])
                        elif sidx == 5:
                            k.op(PE, lambda u=u, g=g, tl=tl: nc.tensor.transpose(out=pTT[0:64, (g * 4 + tl) * 128:(g * 4 + tl + 1) * 128], in_=tksel[:, u, :], identity=ident[:]),
                                 reads=[tb, ident_b], writes=[pTT_b])
                    if sidx == 6:
                        for g in range(2):
                            if os.environ.get("KDBG_S6") == "act":
                                k.op(ACT, lambda g=g: nc.scalar.activation(out=selbT[g][64:128, :], in_=pTT[0:64, g * 512:(g + 1) * 512], func=AF.Copy), reads=[pTT_b], writes=[selbT_b[g]])
                            else:
                                k.op(DVE, lambda g=g: nc.vector.tensor_copy(out=selbT[g][64:128, :], in_=pTT[0:64, g * 512:(g + 1) * 512]), reads=[pTT_b], writes=[selbT_b[g]])
                    if sidx == 7:
                        for h_ in range(8):
                            k.op(DVE, lambda h_=h_: nc.vector.tensor_tensor(out=qst[64:128, h_, :], in0=qwt[64:128, h_, :], in1=selbT[h_ // 4][64:128, :], op=ALU.add),
                                 reads=[qwb, selbT_b[h_ // 4]], writes=[qsb])
                for sidx in range(8):
                    pre.append(lambda sidx=sidx: tk_stage(sidx))

                tasks = []
                for br in (2, 1):
                    for h in range(8):
                        kts = list(range(0, 4 * j + 4)) if br == 1 else list(range(max(0, 4 * j - 4), 4 * j + 4))
                        grp = {"h": h, "br": br, "g": h // 4, "pa": None, "npv": 0, "done": 0}
                        for kt in kts:
                            tls = [tl for tl in range(4) if kt <= 4 * j + tl and (br == 1 or kt >= 4 * j + tl - 4)]
                            grp["npv"] += len(tls)
                            tasks.append({"grp": grp, "kt": kt, "tls": tls})
                n_win = sum(1 for tk in tasks if tk["grp"]["br"] == 2)

                def emit_S(tk):
                    grp = tk["grp"]; h = grp["h"]; br = grp["br"]; g = grp["g"]; kt = tk["kt"]
                    kT = kTs[g] if br == 1 else kTw[g]
                    qq, qqb = (qst, qsb) if br == 1 else (qwt, qwb)
                    ps, psb = nextS()
                    m = (kt - (4 * j - 4)) if br == 2 else (4 + kt - 4 * j)
                    use_msk = (br == 2) or (m >= 4)
                    c0, c1 = 0, 512
                    if use_msk:
                        if m < 4:
                            c1 = 128 * (m + 1)
                        else:
                            c0 = 128 * (m - 4)
                    k.op(PE, lambda: nc.tensor.matmul(ps[:, c0:c1], lhsT=kT[:, kt * 128:(kt + 1) * 128], rhs=qq[:, h, c0:c1], start=True, stop=True),
                         reads=[(kTs_b[g] if br == 1 else kTw_b[g]), qqb], writes=[psb])
                    pt, ptb, _ = PT.next()
                    k.op(ACT, lambda: nc.scalar.activation(out=pt[:, c0:c1], in_=ps[:, c0:c1], func=AF.Exp, bias=sllo[:, h:h + 1]), reads=[psb, sllo_b], writes=[ptb])
                    if use_msk:
                        k.op(DVE, lambda: nc.vector.tensor_tensor(out=pt[:, c0:c1], in0=pt[:, c0:c1], in1=wm01[:, m, c0:c1], op=ALU.mult), reads=[ptb, wm01_b], writes=[ptb])
                    tk["pt"] = pt; tk["ptb"] = ptb

                def emit_PV(tk):
                    grp = tk["grp"]; h = grp["h"]; br = grp["br"]; g = grp["g"]; kt = tk["kt"]
                    vA = vs_aug if br == 1 else vw_aug
                    if grp["pa"] is None:
                        grp["pa"] = nextA()
                    pa, pab = grp["pa"]
                    pt = tk["pt"]; ptb = tk["ptb"]
                    for tl in tk["tls"]:
                        fm = (grp["done"] == 0)
                        grp["done"] += 1
                        last = (grp["done"] == grp["npv"])
                        k.op(PE, lambda tl=tl, fm=fm, last=last: nc.tensor.matmul(pa[:, tl * 65:(tl + 1) * 65], lhsT=pt[:, tl * 128:(tl + 1) * 128], rhs=vA[:, kt, g, :], start=fm, stop=last),
                             reads=[ptb, vtok_b[kt], vones_b], writes=[pab])
                    if grp["done"] == grp["npv"]:
                        evac_group(pa, pab, 65, 4, h, 0, j * 4, br * 8 + h, False)

                LOOK = 3
                n_sel = len(tasks) - n_win
                pre_every = max(1, n_win // (len(pre) + 1))
                cb_every = max(1, (n_sel - 2) // (len(cback) + 1)) if cback else 1
                for i in range(len(tasks) + LOOK):
                    if i < len(tasks):
                        if i == n_win:
                            while pre:
                                pre.pop(0)()
                        emit_S(tasks[i])
                        if i < n_win:
                            if pre and (i % pre_every == pre_every - 1):
                                pre.pop(0)()
                        else:
                            if cback and ((i - n_win) % cb_every == cb_every - 1):
                                cback.pop(0)()
                    if i - LOOK >= 0:
                        emit_PV(tasks[i - LOOK])
                while cback:
                    cback.pop(0)()
                if debug:
                    for tl in range(4):
                        k.dma(POOL, dbg_ds, dbg["d_att"][(j * 4 + tl) * 128:(j * 4 + tl + 1) * 128, :], att[:, tl, :], reads=[att_b[tl]], writes=[DB(("datt", j, tl))])
                for tl in range(4):
                    k.op(ACT, lambda tl=tl: nc.scalar.activation(out=junk4[:, 0:512], in_=att[:, tl, :], func=AF.Square, accum_out=ssa[:, tl:tl + 1]),
                         reads=[att_b[tl]], writes=[junk4_b, ssa_b])
                k.op(ACT, lambda: nc.scalar.activation(out=ssa[:, 4:8], in_=ssa[:, 0:4], func=AF.Ln, scale=1.0 / 512, bias=EPS), reads=[ssa_b], writes=[ssa_b])
                k.op(ACT, lambda: nc.scalar.activation(out=ssa[:, 4:8], in_=ssa[:, 4:8], func=AF.Exp, scale=-0.5), reads=[ssa_b], writes=[ssa_b])
                k.op(DVE, lambda: nc.vector.tensor_tensor(out=ssa[:, 4:8], in0=ssa[:, 4:8], in1=ssq[:, j * 4:(j + 1) * 4], op=ALU.mult), reads=[ssa_b, ssr_b], writes=[ssa_b])
                for tl in range(4):
                    k.op(DVE, lambda tl=tl: nc.vector.tensor_scalar(out=attb[:, tl, :], in0=att[:, tl, :], scalar1=ssa[:, 4 + tl:5 + tl], scalar2=None, op0=ALU.mult),
                         reads=[att_b[tl], ssa_b], writes=[attb_b])
                cback = make_cback(j, rt, rtb)
                if os.environ.get("KDBG_CB") == "now":
                    while cback:
                        cback.pop(0)()
                if j + 1 < NCH:
                    nxt_qs = load_qs(j + 1)
            while cback:
                cback.pop(0)()

        mid.close()
        if upto <= 4:
            k.drain()
            return nc
        k.barrier()
        with ExitStack() as p5:
            wf1 = sb("wf1", [128, 8, 4 * D], BF16, p5)
            wf2 = sb("wf2", [128, 32, D], BF16, p5)
            wf1_b = [Buf(f"wf1_{i}") for i in range(8)]; wf2_b = [Buf(f"wf2_{i}") for i in range(4)]
            wf1v = wff1_d.rearrange("(k p) n -> p k n", p=128)
            wf2v = wff2_d.rearrange("(k p) n -> p k n", p=128)
            for cb in range(8):
                k.dma(POOL, k.dsem("wf1"), wf1[:, :, cb * 512:(cb + 1) * 512], wf1v[:, :, cb * 512:(cb + 1) * 512], writes=[wf1_b[cb]])
            for hh in range(4):
                k.dma(POOL, k.dsem("wf2"), wf2[:, hh * 8:(hh + 1) * 8, :], wf2v[:, hh * 8:(hh + 1) * 8, :], writes=[wf2_b[hh]])
            CH = 256
            xin = Rot(k, p5, nc, "xin", 4, [128, D], F32)
            xnb = Rot(k, p5, nc, "xnb", 2, [128, D], BF16, with_dsem=False)
            hT2 = Rot(k, p5, nc, "hT2", 2, [128, 8, CH], BF16, with_dsem=False)
            aT = Rot(k, p5, nc, "aT", 1, [128, 32, CH], BF16, with_dsem=False)
            r32 = Rot(k, p5, nc, "r32", 3, [128, CH], F32, with_dsem=False)
            y2 = Rot(k, p5, nc, "y2", 2, [128, D], F32, with_dsem=False)
            ot = Rot(k, p5, nc, "ot", 2, [128, D], F32)
            junk5 = sb("junk5", [128, D], BF16, p5)
            junk5_b = Buf("junk5")
            sm5 = sb("sm5", [128, 64], F32, p5)
            sm5_b = [Buf(f"sm5_{i}") for i in range(16)]
            s5i = [0]
            pT5 = [pst(f"pT5_{i}", [128, 1024], BF16, p5) for i in range(2)]
            pT5_b = [Buf(f"pT5_{i}") for i in range(2)]
            pF = [pst(f"pF{i}", [128, 512], F32, p5) for i in range(2)]
            pF_b = [Buf(f"pF{i}") for i in range(2)]
            pY = [pst(f"pY{i}", [128, 512], F32, p5) for i in range(4)]
            pY_b = [Buf(f"pY{i}") for i in range(4)]
            fi = [0]
            out_bufs = []

            def prologue_a(cj):
                xtiles = []
                nts = []
                for tl in range(2):
                    tt = cj * 2 + tl
                    xt_, xtb, xds = xin.next()
                    k.dma(SP, xds, xt_[:], x1_d[tt * 128:(tt + 1) * 128, :], reads=[DB(("x1", tt))], writes=[xtb])
                    xtiles.append((xt_, xtb))
                    i5 = s5i[0] % 16; s5i[0] += 1
                    s4 = sm5[:, 4 * i5:4 * i5 + 4]; s4b = sm5_b[i5]
                    k.op(ACT, lambda xt_=xt_, s4=s4: nc.scalar.activation(out=junk5[:], in_=xt_[:], func=AF.Square, accum_out=s4[:, 0:1]), reads=[xtb], writes=[junk5_b, s4b])
                    k.op(ACT, lambda s4=s4: nc.scalar.activation(out=s4[:, 1:2], in_=s4[:, 0:1], func=AF.Sqrt, scale=1.0 / D, bias=EPS), reads=[s4b], writes=[s4b])
                    k.op(DVE, lambda s4=s4: nc.vector.reciprocal(out=s4[:, 2:3], in_=s4[:, 1:2]), reads=[s4b], writes=[s4b])
                    n, nb, _ = xnb.next()
                    k.op(DVE, lambda xt_=xt_, n=n, s4=s4: nc.vector.tensor_scalar(out=n[:], in0=xt_[:], scalar1=s4[:, 2:3], scalar2=None, op0=ALU.mult), reads=[xtb, s4b], writes=[nb])
                    nts.append((n, nb))
                return xtiles, nts

            def prologue_b(cj, nts):
                ht2, ht2b, _ = hT2.next()
                for tl in range(2):
                    tt = cj * 2 + tl
                    n, nb = nts[tl]
                    pp = pT5[tt % 2]; ppb = pT5_b[tt % 2]
                    for jj in range(8):
                        k.op(PE, lambda jj=jj, n=n, pp=pp: nc.tensor.transpose(out=pp[:, jj * 128:(jj + 1) * 128], in_=n[:, jj * 128:(jj + 1) * 128], identity=ident[:]),
                             reads=[nb, ident_b], writes=[ppb])
                    for jj in range(8):
                        if tt % 2 == 0:
                            k.op(DVE, lambda jj=jj, pp=pp, tl=tl, ht2=ht2: nc.vector.tensor_scalar(out=ht2[:, jj, tl * 128:(tl + 1) * 128], in0=pp[:, jj * 128:(jj + 1) * 128],
                                                                                                   scalar1=A2[:, jj:jj + 1], scalar2=B2[:, jj:jj + 1], op0=ALU.mult, op1=ALU.add),
                                 reads=[ppb, A2_b, B2_b], writes=[ht2b])
                        else:
                            k.op(ACT, lambda jj=jj, pp=pp, tl=tl, ht2=ht2: nc.scalar.activation(out=ht2[:, jj, tl * 128:(tl + 1) * 128], in_=pp[:, jj * 128:(jj + 1) * 128],
                                                                                                func=AF.Identity, scale=A2[:, jj:jj + 1], bias=B2[:, jj:jj + 1]),
                                 reads=[ppb, A2_b, B2_b], writes=[ht2b])
                return ht2, ht2b

            def ff1(ht2, ht2b, mid_cb=None):
                at, atb, _ = aT.next()
                res = None
                for f in range(32):
                    if f == 10 and mid_cb is not None:
                        res = mid_cb()
                    pf = pF[fi[0] % 2]; pfb = pF_b[fi[0] % 2]; fi[0] += 1
                    for kk in range(8):
                        k.op(PE, lambda kk=kk, f=f, pf=pf: nc.tensor.matmul(pf[:, 0:CH], lhsT=wf1[:, kk, f * 128:(f + 1) * 128], rhs=ht2[:, kk, :], start=(kk == 0), stop=(kk == 7)),
                             reads=[wf1_b[f // 4], ht2b], writes=[pfb])
                    r, rb, _ = r32.next()
                    k.op(ACT, lambda pf=pf, r=r: nc.scalar.activation(out=r[:], in_=pf[:, 0:CH], func=AF.Relu), reads=[pfb], writes=[rb])
                    k.op(DVE, lambda r=r, f=f: nc.vector.tensor_tensor(out=at[:, f, :], in0=r[:], in1=r[:], op=ALU.mult), reads=[rb], writes=[atb])
                return at, atb, res

            def ff2(cj, at, atb, xtiles):
                for tl in range(2):
                    tt = cj * 2 + tl
                    yy, yyb, _ = y2.next()
                    for half in range(2):
                        py = pY[(tl * 2 + half) % 4]; pyb = pY_b[(tl * 2 + half) % 4]
                        for f in range(32):
                            k.op(PE, lambda f=f, tl=tl, half=half, py=py: nc.tensor.matmul(py[:, :], lhsT=at[:, f, tl * 128:(tl + 1) * 128], rhs=wf2[:, f, half * 512:(half + 1) * 512],
                                                                                          start=(f == 0), stop=(f == 31)), reads=[atb, wf2_b[f // 8]], writes=[pyb])
                        k.op(ACT, lambda half=half, yy=yy, py=py: nc.scalar.activation(out=yy[:, half * 512:(half + 1) * 512], in_=py[:, :], func=AF.Copy), reads=[pyb], writes=[yyb])
                    i5 = s5i[0] % 16; s5i[0] += 1
                    s4 = sm5[:, 4 * i5:4 * i5 + 4]; s4b = sm5_b[i5]
                    k.op(ACT, lambda yy=yy, s4=s4: nc.scalar.activation(out=junk5[:], in_=yy[:], func=AF.Square, accum_out=s4[:, 0:1]), reads=[yyb], writes=[junk5_b, s4b])
                    k.op(ACT, lambda s4=s4: nc.scalar.activation(out=s4[:, 1:2], in_=s4[:, 0:1], func=AF.Sqrt, scale=1.0 / D, bias=EPS), reads=[s4b], writes=[s4b])
                    k.op(DVE, lambda s4=s4: nc.vector.reciprocal(out=s4[:, 2:3], in_=s4[:, 1:2]), reads=[s4b], writes=[s4b])
                    k.op(DVE, lambda yy=yy, s4=s4: nc.vector.scalar_tensor_tensor(out=yy[:], in0=yy[:], scalar=s4[:, 2:3], in1=C2row[:], op0=ALU.mult, op1=ALU.mult),
                         reads=[yyb, s4b, C2_b], writes=[yyb])
                    o, ob, ods = ot.next()
                    xt_, xtb = xtiles[tl]
                    k.op(DVE, lambda yy=yy, o=o, xt_=xt_: nc.vector.tensor_tensor(out=o[:], in0=yy[:], in1=xt_[:], op=ALU.add), reads=[yyb, xtb], writes=[ob])
                    db = DB(("out", tt))
                    k.dma(POOL, ods, out_d[tt * 128:(tt + 1) * 128, :], o[:], reads=[ob], writes=[db])
                    out_bufs.append(db)

            NCJ = S // CH
            xtiles, nts = prologue_a(0)
            ht2, ht2b = prologue_b(0, nts)
            for cj in range(NCJ):
                at, atb, res = ff1(ht2, ht2b, (lambda cj=cj: prologue_a(cj + 1)) if cj + 1 < NCJ else None)
                if cj + 1 < NCJ:
                    xtiles_n, nts_n = res
                    ht2_n, ht2b_n = prologue_b(cj + 1, nts_n)
                ff2(cj, at, atb, xtiles)
                if cj + 1 < NCJ:
                    xtiles, ht2, ht2b = xtiles_n, ht2_n, ht2b_n
            k.drain()
            k.finish(out_bufs + [b for kk_, b in dbuf.items() if isinstance(kk_, tuple) and kk_ and kk_[0] in ("datt", "dy")] + ([DB("d_mod")] if debug else []))
        print("bass instructions:", k.ninst, "semaphores:", k.nsem)
    return nc


def _consts():
    bf = ml_dtypes.bfloat16
    t = np.arange(S)
    slopes = 2.0 ** (-np.arange(1, 9, dtype=np.float64))
    c = np.arange(256)
    ce = 16 * c + 31
    ce01 = ((ce[None, :] // 64) == np.arange(64)[:, None]).astype(np.float32)
    ce01[:, 255] = 0.0
    cidx = 16 * (np.arange(2)[None, :] * 128 + np.arange(128)[:, None]) + 31
    cllo = (slopes[None, None, :] * (cidx[:, :, None] % 64)).astype(np.float32)
    cend = (16 * (np.arange(2)[None, :, None] * 128 + np.arange(128)[:, None, None]) + 31)
    cmask = np.where(cend <= t[None, None, :], 0.0, NEGM).astype(np.float32)
    e01 = ((t[None, :] // 64) == np.arange(64)[:, None]).astype(np.float32)
    tab = (64.0 * (np.arange(64)[:, None] - (t[None, :] // 64))).astype(np.float32)
    sllo = (slopes[None, :] * (np.arange(128)[:, None] % 64)).astype(np.float32)
    m = np.arange(8)[None, :, None]; kk = np.arange(128)[:, None, None]; tl = np.arange(512)[None, None, :]
    dd = (512 + tl) - (128 * m + kk)
    wm01 = ((dd >= 0) & (dd < 512)).astype(np.float32)
    tok = (np.arange(NT)[None, :, None] * 128 + np.arange(128)[:, None, None])
    cur = tok // 64
    jb = np.arange(64)[None, None, :]
    forced = (jb == 0) | (jb == cur) | (jb == cur - 1)
    addm = np.where(forced, 1.0e4, np.where(jb <= cur, 0.0, -1.0e30)).astype(np.float32)
    cc = (np.arange(2)[None, :, None] * 128 + np.arange(128)[:, None, None])
    ovl = ((cc >= 4 * jb - 1) & (cc <= 4 * jb + 3) & (cc < 255)).astype(np.float32)
    return {
        "k_ident": np.eye(128, dtype=np.float32).astype(bf),
        "k_ce01": ce01.astype(bf), "k_cllo": cllo,
        "k_cmask": cmask.astype(bf), "k_e01": e01.astype(bf), "k_tab": tab.astype(bf), "k_sllo": sllo, "k_wm01": wm01.astype(bf),
        "k_addm": addm, "k_ovl": np.ascontiguousarray(ovl[:, :, 1:]).astype(bf),
    }


def _col(v, n):
    return np.ascontiguousarray(np.asarray(v, np.float32).reshape(n, 128).T)


def _shared_inputs(inp):
    L = 0
    f = lambda a: np.ascontiguousarray(np.asarray(a, np.float32))
    d = {
        "ada_w": f(inp["ada_w"][L]), "ada_b": f(inp["ada_b"][L]).reshape(1, -1),
        "g_pre1": _col(inp["pre_norm_mix"][L], 8), "g_pre2": _col(inp["pre_norm_mlp"][L], 8),
        "g_post1": np.ascontiguousarray(np.broadcast_to(f(inp["post_norm_mix"][L])[None, :], (128, D))),
        "g_post2": np.ascontiguousarray(np.broadcast_to(f(inp["post_norm_mlp"][L])[None, :], (128, D))),
        "w_in": f(inp["w_in"][L]),
        "conv_w": np.ascontiguousarray(f(inp["conv_w"][L]).T.reshape(4, 128, 4).transpose(1, 0, 2)),
        "conv_b": _col(inp["conv_b"][L], 4),
        "lru_wa": f(inp["lru_wa"][L]), "lru_wx": f(inp["lru_wx"][L]),
        "lru_ba": _col(inp["lru_ba"][L], 4), "lru_bx": _col(inp["lru_bx"][L], 4), "lru_lam": _col(inp["lru_lambda"][L], 4),
        "pos_k": np.ascontiguousarray(f(inp["cmp_pos_k"][L]).T), "pos_v": np.ascontiguousarray(f(inp["cmp_pos_v"][L]).T),
        "w1_k": f(inp["cmp_w1_k"][L]), "w1_v": f(inp["cmp_w1_v"][L]),
        "w2_k": f(inp["cmp_w2_k"][L]), "w2_v": f(inp["cmp_w2_v"][L]),
        "g_rnn": _col(inp["norm_rnn_out"][L], 4), "g_att": _col(inp["norm_att_out"][L], 4),
        "w_out": f(inp["w_out"][L]), "w_ff1": f(inp["w_ff1"][L]), "w_ff2": f(inp["w_ff2"][L]),
    }
    d.update(_consts())
    return d


def kernel(**inputs):
    debug = bool(inputs.pop("_debug", False))
    upto = inputs.pop("_upto", 99)
    cores = inputs.pop("_cores", None)
    x = np.asarray(inputs["x"], np.float32)
    c = np.asarray(inputs["c"], np.float32)
    B = x.shape[0]
    shared = _shared_inputs(inputs)
    nc = build(debug=debug, upto=upto)
    bs = list(range(B)) if cores is None else list(cores)
    in_maps = []
    for b in bs:
        m = dict(shared)
        m["x"] = np.ascontiguousarray(x[b])
        m["c"] = _col(c[b], 8)
        in_maps.append(m)
    res = run_bass_kernel_spmd(nc, in_maps, core_ids=list(range(len(bs))))
    if debug:
        return res.results
    return np.stack([np.asarray(r["out"], np.float32) for r in res.results], axis=0)
```

```python
import numpy as np
import ml_dtypes
from contextlib import ExitStack
import concourse.bass as bass
import concourse.mybir as mybir
from concourse.bass_utils import run_bass_kernel_spmd

F32 = mybir.dt.float32
BF16 = mybir.dt.bfloat16
AF = mybir.ActivationFunctionType
ALU = mybir.AluOpType

S = 4096
D = 1024
NT = S // 128
NCH = S // 512
DIN = 2328
NEGM = -30000.0
EPS = 1e-6
import os
EVAC = os.environ.get('KDBG_EVAC', '')


class Buf:
    __slots__ = ("name", "lw", "rd")

    def __init__(self, name):
        self.name = name
        self.lw = None
        self.rd = {}


class DSem:
    __slots__ = ("sem", "cnt")

    def __init__(self, sem):
        self.sem = sem
        self.cnt = 0


class Eng:
    def __init__(self, name, eng, self_sync):
        self.name = name
        self.eng = eng
        self.self_sync = self_sync
        self.cur = None
        self.cnt = 0
        self.own = set()
        self.waited = {}


class K:
    EPOCH = 4000

    def __init__(self, nc, es):
        self.nc = nc
        self.es = es
        self.nsem = 0
        self.pe = Eng("pe", nc.tensor, False)
        self.act = Eng("act", nc.scalar, True)
        self.dve = Eng("dve", nc.vector, True)
        self.pool = Eng("pool", nc.gpsimd, True)
        self.sp = Eng("sp", nc.sync, True)
        self.dsems = []
        self.ninst = 0

    def new_sem(self, name):
        self.nsem += 1
        return self.es.enter_context(self.nc.semaphore(f"{name}_{self.nsem}"))

    def dsem(self, name="d"):
        d = DSem(self.new_sem(name))
        self.dsems.append(d)
        return d

    def _wait(self, E, tok):
        sem, val = tok
        key = id(sem)
        if (not E.self_sync) and key in E.own:
            return
        if E.waited.get(key, 0) >= val:
            return
        E.eng.wait_ge(sem, val)
        E.waited[key] = val

    def _deps(self, E, reads, writes):
        for b in reads:
            if b.lw is not None:
                self._wait(E, b.lw)
        for b in writes:
            if b.lw is not None:
                self._wait(E, b.lw)
            for tok in b.rd.values():
                self._wait(E, tok)

    def _commit(self, tok, reads, writes):
        key = id(tok[0])
        for b in reads:
            old = b.rd.get(key)
            if old is None or old[1] < tok[1]:
                b.rd[key] = tok
        for b in writes:
            b.lw = tok
            b.rd = {}

    def op(self, E, fn, reads=(), writes=()):
        self._deps(E, reads, writes)
        inst = fn()
        if E.cur is None or E.cnt >= self.EPOCH:
            E.cur = self.new_sem(E.name)
            E.own.add(id(E.cur))
            E.cnt = 0
        E.cnt += 1
        inst.then_inc(E.cur, 1)
        tok = (E.cur, E.cnt)
        self._commit(tok, reads, writes)
        self.ninst += 1
        return tok

    def dma(self, Q, ds, out, in_, reads=(), writes=(), **kw):
        self._deps(Q, reads, writes)
        if ds.cnt:
            self._wait(Q, (ds.sem, ds.cnt))
        inst = Q.eng.dma_start(out=out, in_=in_, **kw)
        ds.cnt += 16
        inst.then_inc(ds.sem, 16)
        tok = (ds.sem, ds.cnt)
        self._commit(tok, reads, writes)
        self.ninst += 1
        return tok

    def drain(self):
        for E in (self.pe, self.act, self.dve, self.pool):
            if E.cur is not None:
                self._wait(self.sp, (E.cur, E.cnt))
        for d in self.dsems:
            if d.cnt:
                self._wait(self.sp, (d.sem, d.cnt))

    def barrier(self):
        engs = (self.pe, self.act, self.dve, self.pool, self.sp)
        for E in engs:
            for E2 in engs:
                if E2 is not E and E2.cur is not None and E2.cnt:
                    self._wait(E, (E2.cur, E2.cnt))
            for d in self.dsems:
                if d.cnt:
                    self._wait(E, (d.sem, d.cnt))

    def finish(self, bufs):
        for b in bufs:
            if b.lw is not None:
                self._wait(self.sp, b.lw)


class Rot:
    def __init__(self, k, es, nc, name, n, shape, dt, with_dsem=True):
        self.n = n
        self.i = 0
        self.slots = []
        for j in range(n):
            t = es.enter_context(nc.sbuf_tensor(f"{name}{j}", shape, dt))
            self.slots.append((t, Buf(f"{name}{j}"), k.dsem(name) if with_dsem else None))

    def next(self):
        s = self.slots[self.i % self.n]
        self.i += 1
        return s


class _Stop(Exception):
    pass


def build(debug=False, upto=99):
    nc = bass.Bass("TRN2", target_bir_lowering=False)

    def din(name, shape, dt=F32):
        return nc.dram_tensor(name, list(shape), dt, kind="ExternalInput").ap()

    def dscr(name, shape, dt):
        return nc.dram_tensor(name, list(shape), dt, kind="Internal").ap()

    x_d = din("x", [S, D])
    c_d = din("c", [128, 8])
    adaw_d = din("ada_w", [D, 6 * D])
    adab_d = din("ada_b", [1, 6 * D])
    gpre1_d = din("g_pre1", [128, 8])
    gpre2_d = din("g_pre2", [128, 8])
    gpost1_d = din("g_post1", [128, D])
    gpost2_d = din("g_post2", [128, D])
    win_d = din("w_in", [D, DIN])
    convw_d = din("conv_w", [128, 4, 4])
    convb_d = din("conv_b", [128, 4])
    wa_d = din("lru_wa", [8, 64, 64])
    wx_d = din("lru_wx", [8, 64, 64])
    ba_d = din("lru_ba", [128, 4])
    bx_d = din("lru_bx", [128, 4])
    lam_d = din("lru_lam", [128, 4])
    posk_d = din("pos_k", [64, 32])
    posv_d = din("pos_v", [64, 32])
    w1k_d = din("w1_k", [2048, 256])
    w1v_d = din("w1_v", [2048, 256])
    w2k_d = din("w2_k", [256, 64])
    w2v_d = din("w2_v", [256, 64])
    grnn_d = din("g_rnn", [128, 4])
    gatt_d = din("g_att", [128, 4])
    wout_d = din("w_out", [D, D])
    wff1_d = din("w_ff1", [D, 4 * D])
    wff2_d = din("w_ff2", [4 * D, D])
    ident_d = din("k_ident", [128, 128], BF16)
    ce01_d = din("k_ce01", [64, 256], BF16)
    cllo_d = din("k_cllo", [128, 2, 8])
    cmask_d = din("k_cmask", [128, 2, S], BF16)
    e01_d = din("k_e01", [64, S], BF16)
    tab_d = din("k_tab", [64, S], BF16)
    sllo_d = din("k_sllo", [128, 8])
    wm01_d = din("k_wm01", [128, 8, 512], BF16)
    addm_d = din("k_addm", [128, NT, 64])
    ovl_d = din("k_ovl", [128, 2, 63], BF16)
    out_d = nc.dram_tensor("out", [S, D], F32, kind="ExternalOutput").ap()
    zr_d = dscr("zr_s", [1024, S], F32)
    q_d = dscr("q_s", [512, S], BF16)
    kc_d = dscr("kc_s", [128, S], BF16)
    vc_d = dscr("vc_s", [128, S], BF16)
    ks_d = dscr("ks_s", [128, S], BF16)
    kw_d = dscr("kw_s", [128, S], BF16)
    rnn_d = dscr("rnn_s", [512, S], BF16)
    x1_d = dscr("x1_s", [S, D], F32)
    dbg = {}
    if debug:
        for nm, shp in [("d_mod", [1, 6 * D]), ("d_att", [S, 512]), ("d_y", [S, D])]:
            dbg[nm] = nc.dram_tensor(nm, shp, F32, kind="ExternalOutput").ap()

    with ExitStack() as es:
        k = K(nc, es)
        PE, ACT, DVE, POOL, SP = k.pe, k.act, k.dve, k.pool, k.sp

        def sb(name, shape, dt, stack=es):
            return stack.enter_context(nc.sbuf_tensor(name, list(shape), dt))

        def pst(name, shape, dt, stack=es):
            return stack.enter_context(nc.psum_tensor(name, list(shape), dt))

        dbuf = {}

        def DB(key):
            if key not in dbuf:
                dbuf[key] = Buf(str(key))
            return dbuf[key]

        ident = sb("ident", [128, 128], BF16)
        ident_b = Buf("ident")
        ld0 = k.dsem("ld0")
        k.dma(SP, ld0, ident[:], ident_d, writes=[ident_b])
        ones_bf = sb("ones_bf", [128, 128], BF16)
        ones_b = Buf("ones")
        k.op(DVE, lambda: nc.vector.memset(ones_bf[:], 1.0), writes=[ones_b])
        A1 = sb("A1", [128, 8], F32); B1 = sb("B1", [128, 8], F32)
        A2 = sb("A2", [128, 8], F32); B2 = sb("B2", [128, 8], F32)
        C1row = sb("C1row", [128, D], F32); C2row = sb("C2row", [128, D], F32)
        A1_b, B1_b, A2_b, B2_b, C1_b, C2_b = (Buf(n) for n in ("A1", "B1", "A2", "B2", "C1", "C2"))
        mid = es.enter_context(ExitStack())
        vs_aug = sb("vs_aug", [128, NT, 2, 65], BF16, mid)
        vw_aug = sb("vw_aug", [128, NT, 2, 65], BF16, mid)
        gates = sb("gates", [128, NT, 24], F32, mid)
        vtok_b = [Buf(f"vtok{t}") for t in range(NT)]
        vones_b = Buf("vones")
        k.op(DVE, lambda: nc.vector.memset(vs_aug[:, :, :, 64:65], 1.0), writes=[vones_b])
        k.op(DVE, lambda: nc.vector.memset(vw_aug[:, :, :, 64:65], 1.0), writes=[vones_b])
        ssr = sb("ssr", [128, NT], F32, mid)
        ssq = sb("ssq", [128, NT], F32, mid)
        ssr_b = Buf("ssr")

        with ExitStack() as p0:
            csb = sb("csb", [128, 8], F32, p0)
            scb = sb("scb", [128, 8], BF16, p0)
            c_b = Buf("c")
            k.dma(SP, k.dsem("c"), csb[:], c_d, writes=[c_b])
            k.op(ACT, lambda: nc.scalar.activation(out=scb[:], in_=csb[:], func=AF.Silu), reads=[c_b], writes=[c_b])
            adab = sb("adab", [1, 6 * D], F32, p0)
            adab_b = Buf("adab")
            k.dma(SP, k.dsem("adab"), adab[:], adab_d, writes=[adab_b])
            modrow = sb("modrow", [1, 6 * D], F32, p0)
            modrow_b = Buf("modrow")
            modcol = sb("modcol", [128, 48], F32, p0)
            modcol_b = Buf("modcol")
            gp1 = sb("gp1", [128, 8], F32, p0); gp2 = sb("gp2", [128, 8], F32, p0)
            gq1 = sb("gq1", [128, D], F32, p0); gq2 = sb("gq2", [128, D], F32, p0)
            g_b = Buf("gvecs")
            k.dma(SP, ld0, gp1[:], gpre1_d, writes=[g_b])
            k.dma(SP, ld0, gp2[:], gpre2_d, writes=[g_b])
            k.dma(SP, ld0, gq1[:], gpost1_d, writes=[g_b])
            k.dma(SP, ld0, gq2[:], gpost2_d, writes=[g_b])
            adaw = Rot(k, p0, nc, "adaw", 2, [128, 8, 512], BF16)
            ps_row = pst("ps_row", [128, 512], F32, p0)
            ps_row_b = Buf("ps_row")
            ps_col = pst("ps_col", [128, 512], F32, p0)
            ps_col_b = Buf("ps_col")
            one11 = sb("one11", [1, 128], F32, p0)
            one11_b = Buf("one11")
            k.op(DVE, lambda: nc.vector.memset(one11[:], 1.0), writes=[one11_b])
            for pc in range(12):
                t, tb, ds = adaw.next()
                k.dma(POOL, ds, t[:], adaw_d[:, pc * 512:(pc + 1) * 512].rearrange("(k p) n -> p k n", p=128), writes=[tb])
                for kk in range(8):
                    k.op(PE, lambda kk=kk, t=t: nc.tensor.matmul(ps_row[0:1, :], lhsT=scb[:, kk:kk + 1], rhs=t[:, kk, :],
                                                                start=(kk == 0), stop=(kk == 7)),
                         reads=[tb, c_b], writes=[ps_row_b])
                k.op(DVE, lambda pc=pc: nc.vector.tensor_tensor(out=modrow[0:1, pc * 512:(pc + 1) * 512], in0=ps_row[0:1, :],
                                                                in1=adab[0:1, pc * 512:(pc + 1) * 512], op=ALU.add),
                     reads=[ps_row_b, adab_b], writes=[modrow_b])
            if debug:
                k.dma(SP, ld0, dbg["d_mod"], modrow[:], reads=[modrow_b], writes=[DB("d_mod")])
            for j in range(48):
                k.op(PE, lambda j=j: nc.tensor.matmul(ps_col[:, j:j + 1], lhsT=modrow[0:1, j * 128:(j + 1) * 128], rhs=one11[0:1, 0:1],
                                                      start=True, stop=True),
                     reads=[modrow_b, one11_b], writes=[ps_col_b])
            k.op(DVE, lambda: nc.vector.tensor_copy(out=modcol[:], in_=ps_col[:, 0:48]), reads=[ps_col_b], writes=[modcol_b])
            k.op(DVE, lambda: nc.vector.scalar_tensor_tensor(out=A1[:], in0=modcol[:, 8:16], scalar=1.0, in1=gp1[:], op0=ALU.add, op1=ALU.mult),
                 reads=[modcol_b, g_b], writes=[A1_b])
            k.op(DVE, lambda: nc.vector.tensor_copy(out=B1[:], in_=modcol[:, 0:8]), reads=[modcol_b], writes=[B1_b])
            k.op(DVE, lambda: nc.vector.scalar_tensor_tensor(out=A2[:], in0=modcol[:, 32:40], scalar=1.0, in1=gp2[:], op0=ALU.add, op1=ALU.mult),
                 reads=[modcol_b, g_b], writes=[A2_b])
            k.op(DVE, lambda: nc.vector.tensor_copy(out=B2[:], in_=modcol[:, 24:32]), reads=[modcol_b], writes=[B2_b])
            for (base, crow, cb, gq) in ((2048, C1row, C1_b, gq1), (5120, C2row, C2_b, gq2)):
                for hh in range(2):
                    k.op(PE, lambda base=base, hh=hh: nc.tensor.matmul(ps_row[:, :], lhsT=one11[0:1, :],
                                                                      rhs=modrow[0:1, base + hh * 512: base + (hh + 1) * 512],
                                                                      start=True, stop=True),
                         reads=[modrow_b, one11_b], writes=[ps_row_b])
                    k.op(DVE, lambda hh=hh, crow=crow, gq=gq: nc.vector.scalar_tensor_tensor(
                        out=crow[:, hh * 512:(hh + 1) * 512], in0=ps_row[:, :], scalar=1.0, in1=gq[:, hh * 512:(hh + 1) * 512],
                        op0=ALU.add, op1=ALU.mult), reads=[ps_row_b, g_b], writes=[cb])

        if upto <= 0:
            k.drain()
            return nc
        k.barrier()
        with ExitStack() as p1:
            hT = sb("hT", [128, 8, S], BF16, p1)
            hT_b = [[Buf(f"hT{t}_{j}") for j in range(8)] for t in range(NT)]
            win = sb("win", [128, 8, DIN], BF16, p1)
            win_b = Buf("win")
            wds = k.dsem("win")
            k.dma(POOL, wds, win[:], win_d.rearrange("(k p) n -> p k n", p=128), writes=[win_b])
            xt = Rot(k, p1, nc, "xt", 3, [128, D], F32)
            junk = sb("junk", [128, D], BF16, p1)
            junk_b = Buf("junk")
            xn = Rot(k, p1, nc, "xn", 2, [128, D], BF16, with_dsem=False)
            pT = [pst(f"pT{i}", [128, 1024], BF16, p1) for i in range(2)]
            pT_b = [Buf(f"pT{i}") for i in range(2)]
            sm1 = sb("sm1", [128, NT * 4], F32, p1)
            sm1_b = [Buf(f"sm1_{t}") for t in range(NT)]
            pz = [pst(f"pz{i}", [128, 512], F32, p1) for i in range(4)]
            pz_b = [Buf(f"pz{i}") for i in range(4)]
            st32 = Rot(k, p1, nc, "st32", 3, [128, 512], F32)
            st16 = Rot(k, p1, nc, "st16", 3, [128, 512], BF16)
            zi = 0
            fm_tiles = []
            for ct in range(8):
                fm_tiles.append((ct * 128, zr_d, ct * 128, "f32", 1.0))
            for ct in range(4):
                fm_tiles.append((1024 + ct * 128, q_d, ct * 128, "bf", 0.125))
            fm_tiles.append((1536, kc_d, 0, "bf", 1.0))
            fm_tiles.append((1664, vc_d, 0, "bf", 1.0))
            fm_tiles.append((1792, ks_d, 0, "bf", 1.0))
            fm_tiles.append((2048, kw_d, 0, "bf", 1.0))

            def norm_a(tt):
                t, tb, ds = xt.next()
                k.dma(SP, ds, t[:], x_d[tt * 128:(tt + 1) * 128, :], writes=[tb])
                s4 = sm1[:, tt * 4:tt * 4 + 4]; s4b = sm1_b[tt]
                k.op(ACT, lambda: nc.scalar.activation(out=junk[:], in_=t[:], func=AF.Square, accum_out=s4[:, 0:1]), reads=[tb], writes=[junk_b, s4b])
                k.op(ACT, lambda: nc.scalar.activation(out=s4[:, 1:2], in_=s4[:, 0:1], func=AF.Ln, scale=1.0 / D, bias=EPS), reads=[s4b], writes=[s4b])
                k.op(ACT, lambda: nc.scalar.activation(out=s4[:, 2:3], in_=s4[:, 1:2], func=AF.Exp, scale=-0.5), reads=[s4b], writes=[s4b])
                n, nb, _ = xn.next()
                k.op(DVE, lambda: nc.vector.tensor_scalar(out=n[:], in0=t[:], scalar1=s4[:, 2:3], scalar2=None, op0=ALU.mult), reads=[tb, s4b], writes=[nb])
                return tt, n, nb

            def norm_b(tt, n, nb):
                pp = pT[tt % 2]; ppb = pT_b[tt % 2]
                for j in range(8):
                    k.op(PE, lambda j=j: nc.tensor.transpose(out=pp[:, j * 128:(j + 1) * 128], in_=n[:, j * 128:(j + 1) * 128], identity=ident[:]),
                         reads=[nb, ident_b], writes=[ppb])
                for j in range(8):
                    if tt % 2 == 0:
                        k.op(DVE, lambda j=j: nc.vector.tensor_scalar(out=hT[:, j, tt * 128:(tt + 1) * 128], in0=pp[:, j * 128:(j + 1) * 128],
                                                                      scalar1=A1[:, j:j + 1], scalar2=B1[:, j:j + 1], op0=ALU.mult, op1=ALU.add),
                             reads=[ppb, A1_b, B1_b], writes=[hT_b[tt][j]])
                    else:
                        k.op(ACT, lambda j=j: nc.scalar.activation(out=hT[:, j, tt * 128:(tt + 1) * 128], in_=pp[:, j * 128:(j + 1) * 128],
                                                                   func=AF.Identity, scale=A1[:, j:j + 1], bias=B1[:, j:j + 1]),
                             reads=[ppb, A1_b, B1_b], writes=[hT_b[tt][j]])

            npend = [norm_a(0)]

            def norm_tile(_tt_unused=None):
                tt, n, nb = npend[0]
                norm_b(tt, n, nb)
                if tt + 1 < NT:
                    npend[0] = norm_a(tt + 1)

            for tl in range(4):
                norm_tile(tl)
            for ch in range(NCH):
                for ti, (c0, dst, r0, kind, scl) in enumerate(fm_tiles):
                    if ch + 1 < NCH and ti in (2, 6, 10, 14):
                        norm_tile((ch + 1) * 4 + (ti - 2) // 4)
                    pzz = pz[zi % 4]; pzb = pz_b[zi % 4]
                    for kk in range(8):
                        k.op(PE, lambda kk=kk, c0=c0, ch=ch, pzz=pzz: nc.tensor.matmul(pzz[:, :], lhsT=win[:, kk, c0:c0 + 128], rhs=hT[:, kk, ch * 512:(ch + 1) * 512],
                                                                                      start=(kk == 0), stop=(kk == 7)),
                             reads=[win_b] + [hT_b[t4][kk] for t4 in range(ch * 4, ch * 4 + 4)], writes=[pzb])
                    t, tb, ds = (st32 if kind == "f32" else st16).next()
                    if zi % 2 == 0:
                        k.op(ACT, lambda t=t, pzz=pzz, scl=scl: nc.scalar.activation(out=t[:], in_=pzz[:, :], func=AF.Copy, scale=scl), reads=[pzb], writes=[tb])
                    else:
                        k.op(DVE, lambda t=t, pzz=pzz, scl=scl: nc.vector.tensor_scalar(out=t[:], in0=pzz[:, :], scalar1=scl, scalar2=None, op0=ALU.mult),
                             reads=[pzb], writes=[tb])
                    k.dma(POOL, ds, dst[r0:r0 + 128, ch * 512:(ch + 1) * 512], t[:], reads=[tb], writes=[DB((id(dst), r0, ch))])
                    zi += 1
                for tl in range(4):
                    tt = ch * 4 + tl
                    pzz = pz[zi % 4]; pzb = pz_b[zi % 4]
                    use_dve = (zi % 2 == 1)
                    zi += 1
                    for (c0, n, o0) in ((1920, 128, 0), (2176, 152, 128)):
                        for kk in range(8):
                            k.op(PE, lambda kk=kk, c0=c0, n=n, o0=o0, tt=tt, pzz=pzz: nc.tensor.matmul(
                                pzz[:, o0:o0 + n], lhsT=hT[:, kk, tt * 128:(tt + 1) * 128], rhs=win[:, kk, c0:c0 + n],
                                start=(kk == 0 and o0 == 0), stop=(kk == 7 and o0 == 128)),
                                reads=[win_b, hT_b[tt][kk]], writes=[pzb])
                    if use_dve:
                        k.op(DVE, lambda tt=tt, pzz=pzz: nc.vector.tensor_copy(out=vs_aug[:, tt, :, 0:64], in_=pzz[:, 0:128].rearrange("p (g d) -> p g d", g=2)), reads=[pzb], writes=[vtok_b[tt]])
                        k.op(DVE, lambda tt=tt, pzz=pzz: nc.vector.tensor_copy(out=vw_aug[:, tt, :, 0:64], in_=pzz[:, 128:256].rearrange("p (g d) -> p g d", g=2)), reads=[pzb], writes=[vtok_b[tt]])
                        k.op(DVE, lambda tt=tt, pzz=pzz: nc.vector.tensor_copy(out=gates[:, tt, :], in_=pzz[:, 256:280]), reads=[pzb], writes=[vtok_b[tt]])
                    else:
                        k.op(ACT, lambda tt=tt, pzz=pzz: nc.scalar.activation(out=vs_aug[:, tt, :, 0:64], in_=pzz[:, 0:128].rearrange("p (g d) -> p g d", g=2), func=AF.Copy), reads=[pzb], writes=[vtok_b[tt]])
                        k.op(ACT, lambda tt=tt, pzz=pzz: nc.scalar.activation(out=vw_aug[:, tt, :, 0:64], in_=pzz[:, 128:256].rearrange("p (g d) -> p g d", g=2), func=AF.Copy), reads=[pzb], writes=[vtok_b[tt]])
                        k.op(ACT, lambda tt=tt, pzz=pzz: nc.scalar.activation(out=gates[:, tt, :], in_=pzz[:, 256:280], func=AF.Copy), reads=[pzb], writes=[vtok_b[tt]])
            k.op(ACT, lambda: nc.scalar.activation(out=gates[:], in_=gates[:], func=AF.Sigmoid), reads=vtok_b, writes=vtok_b)

        if upto <= 1:
            k.drain()
            return nc
        k.barrier()
        with ExitStack() as p2:
            cw = sb("cw", [128, 4, 4], F32, p2); cb_ = sb("cb", [128, 4], F32, p2)
            bat = sb("bat", [128, 4], F32, p2); bxt = sb("bxt", [128, 4], F32, p2)
            lam = sb("lam", [128, 4], F32, p2); cc1 = sb("cc1", [128, 4], F32, p2); cc2 = sb("cc2", [128, 4], F32, p2)
            prm_b = Buf("rnnprm")
            pds = k.dsem("rp")
            for (t, d) in ((cw, convw_d), (cb_, convb_d), (bat, ba_d), (bxt, bx_d), (lam, lam_d)):
                k.dma(SP, pds, t[:], d, writes=[prm_b])
            k.op(ACT, lambda: nc.scalar.activation(out=cc1[:], in_=lam[:], func=AF.Exp, scale=-1.0), reads=[prm_b], writes=[prm_b])
            k.op(ACT, lambda: nc.scalar.activation(out=cc1[:], in_=cc1[:], func=AF.Ln, bias=1.0), reads=[prm_b], writes=[prm_b])
            k.op(DVE, lambda: nc.vector.tensor_scalar(out=cc2[:], in0=cc1[:], scalar1=-16.0, scalar2=None, op0=ALU.mult), reads=[prm_b], writes=[prm_b])
            k.op(DVE, lambda: nc.vector.tensor_scalar(out=cc1[:], in0=cc1[:], scalar1=-8.0, scalar2=None, op0=ALU.mult), reads=[prm_b], writes=[prm_b])
            wbd_a = sb("wbd_a", [128, 4, 128], BF16, p2); wbd_x = sb("wbd_x", [128, 4, 128], BF16, p2)
            wbd_b = Buf("wbd")
            k.op(DVE, lambda: nc.vector.memset(wbd_a[:], 0.0), writes=[wbd_b])
            k.op(DVE, lambda: nc.vector.memset(wbd_x[:], 0.0), writes=[wbd_b])
            for (t, d) in ((wbd_a, wa_d), (wbd_x, wx_d)):
                dv = d.rearrange("(f two) i j -> two i f j", two=2)
                for hh in range(2):
                    k.dma(POOL, k.dsem("wbd"), t[hh * 64:(hh + 1) * 64, :, hh * 64:(hh + 1) * 64], dv[hh], writes=[wbd_b])
            NP = 4
            PW = S // NP
            xrR = Rot(k, p2, nc, "xr", 2, [128, S], F32)
            ggR = Rot(k, p2, nc, "gg", 2, [128, S], F32)
            xc = sb("xc", [128, S], F32, p2); xcb = sb("xcb", [128, S], BF16, p2)
            rr = sb("rr", [128, S], F32, p2); ii = sb("ii", [128, S], F32, p2); a2 = sb("a2", [128, S], F32, p2)
            hh_ = sb("hh", [128, S], F32, p2)
            rnb = sb("rnb", [128, S], BF16, p2); sqb = sb("sqb", [128, S], BF16, p2)
            xc_b, xcb_b, rr_b, ii_b, a2_b, hh_b, rnb_b, sqb_b = ([Buf(f"{n}{p}") for p in range(NP)] for n in ("xc", "xcb", "rr", "ii", "a2", "hh", "rnb", "sqb"))
            rn_ds = [k.dsem("rn") for _ in range(2)]
            pg = [pst(f"pg{i}", [128, 512], F32, p2) for i in range(3)]
            pg_b = [Buf(f"pg{i}") for i in range(3)]
            pstat = pst("pstat", [128, 512], F32, p2)
            pstat_b = Buf("pstat")
            gi = [0]
            gg_parts = [[Buf(f"ggs{sl}_{p}") for p in range(NP)] for sl in range(2)]

            def load_ft(ft):
                xr, xr_b, xr_ds = xrR.next()
                gg, _, gg_ds = ggR.next()
                gg_bp = gg_parts[ft % 2]
                k.dma(SP, xr_ds, xr[:], zr_d[512 + ft * 128:512 + (ft + 1) * 128, :], reads=[DB((id(zr_d), 512 + ft * 128, ch)) for ch in range(NCH)], writes=[xr_b])
                k.dma(SP, gg_ds, gg[:], zr_d[ft * 128:(ft + 1) * 128, :], reads=[DB((id(zr_d), ft * 128, ch)) for ch in range(NCH)], writes=gg_bp)
                return xr, xr_b, gg, gg_bp

            def cs(p):
                return p * PW, (p + 1) * PW

            def front(ft, xr, xr_b, gg, gg_b):
                for p in range(NP):
                    c0, c1 = cs(p)
                    k.op(DVE, lambda c0=c0, c1=c1: nc.vector.tensor_scalar(out=xc[:, c0:c1], in0=xr[:, c0:c1], scalar1=cw[:, ft, 3:4], scalar2=cb_[:, ft:ft + 1], op0=ALU.mult, op1=ALU.add),
                         reads=[xr_b, prm_b], writes=[xc_b[p]])
                    for sh in (1, 2, 3):
                        lo = max(c0, sh)
                        k.op(DVE, lambda lo=lo, c1=c1, sh=sh: nc.vector.scalar_tensor_tensor(out=xc[:, lo:c1], in0=xr[:, lo - sh:c1 - sh], scalar=cw[:, ft, 3 - sh:4 - sh],
                                                                                            in1=xc[:, lo:c1], op0=ALU.mult, op1=ALU.add),
                             reads=[xr_b, prm_b, xc_b[p]], writes=[xc_b[p]])
                    k.op(ACT, lambda c0=c0, c1=c1: nc.scalar.activation(out=xcb[:, c0:c1], in_=xc[:, c0:c1], func=AF.Copy), reads=[xc_b[p]], writes=[xcb_b[p]])
                    k.op(ACT, lambda c0=c0, c1=c1: nc.scalar.activation(out=gg[:, c0:c1], in_=gg[:, c0:c1], func=AF.Gelu_apprx_tanh), reads=[gg_b[p]], writes=[gg_b[p]])

            def rnn_gates(ft):
                for p in range(NP):
                    c0, c1 = cs(p)
                    for (wt, bt, dst, dstb) in ((wbd_a, bat, rr, rr_b), (wbd_x, bxt, ii, ii_b)):
                        for cc in range(c0, c1, 512):
                            pgg = pg[gi[0] % 3]; pgb = pg_b[gi[0] % 3]; gi[0] += 1
                            k.op(PE, lambda wt=wt, cc=cc, pgg=pgg: nc.tensor.matmul(pgg[:, :], lhsT=wt[:, ft, :], rhs=xcb[:, cc:cc + 512], start=True, stop=True),
                                 reads=[wbd_b, xcb_b[p]], writes=[pgb])
                            k.op(ACT, lambda bt=bt, cc=cc, pgg=pgg, dst=dst: nc.scalar.activation(out=dst[:, cc:cc + 512], in_=pgg[:, :], func=AF.Sigmoid, bias=bt[:, ft:ft + 1]),
                                 reads=[pgb, prm_b], writes=[dstb[p]])

            def exps_m(ft):
                for p in range(NP):
                    c0, c1 = cs(p)
                    k.op(ACT, lambda c0=c0, c1=c1: nc.scalar.activation(out=a2[:, c0:c1], in_=rr[:, c0:c1], func=AF.Exp, scale=cc2[:, ft:ft + 1]), reads=[rr_b[p], prm_b], writes=[a2_b[p]])
                    k.op(ACT, lambda c0=c0, c1=c1: nc.scalar.activation(out=rr[:, c0:c1], in_=rr[:, c0:c1], func=AF.Exp, scale=cc1[:, ft:ft + 1]), reads=[rr_b[p], prm_b], writes=[rr_b[p]])
                    k.op(DVE, lambda c0=c0, c1=c1: nc.vector.tensor_tensor(out=ii[:, c0:c1], in0=ii[:, c0:c1], in1=xc[:, c0:c1], op=ALU.mult), reads=[ii_b[p], xc_b[p]], writes=[ii_b[p]])

            def sqrts(ft):
                for p in range(NP):
                    c0, c1 = cs(p)
                    k.op(ACT, lambda c0=c0, c1=c1: nc.scalar.activation(out=a2[:, c0:c1], in_=a2[:, c0:c1], func=AF.Sqrt, scale=-1.0, bias=1.0), reads=[a2_b[p]], writes=[a2_b[p]])

            def tail(ft, gg, gg_b):
                for p in range(NP):
                    c0, c1 = cs(p)
                    k.op(DVE, lambda c0=c0, c1=c1: nc.vector.tensor_tensor(out=a2[:, c0:c1], in0=a2[:, c0:c1], in1=ii[:, c0:c1], op=ALU.mult), reads=[a2_b[p], ii_b[p]], writes=[a2_b[p]])
                    init = 0.0 if p == 0 else hh_[:, c0 - 1:c0]
                    k.op(DVE, lambda c0=c0, c1=c1, init=init: nc.vector.tensor_tensor_scan(out=hh_[:, c0:c1], data0=rr[:, c0:c1], data1=a2[:, c0:c1], initial=init, op0=ALU.mult, op1=ALU.add),
                         reads=[rr_b[p], a2_b[p]] + ([hh_b[p - 1]] if p else []), writes=[hh_b[p]])
                    k.op(DVE, lambda c0=c0, c1=c1: nc.vector.tensor_tensor(out=rnb[:, c0:c1], in0=gg[:, c0:c1], in1=hh_[:, c0:c1], op=ALU.mult), reads=[gg_b[p], hh_b[p]], writes=[rnb_b[p]])
                    k.op(DVE, lambda c0=c0, c1=c1: nc.vector.tensor_tensor(out=sqb[:, c0:c1], in0=rnb[:, c0:c1], in1=rnb[:, c0:c1], op=ALU.mult), reads=[rnb_b[p]], writes=[sqb_b[p]])
                    for cc in range(c0, c1, 512):
                        ch = cc // 512
                        k.dma(POOL, rn_ds[ch % 2], rnn_d[ft * 128:(ft + 1) * 128, cc:cc + 512], rnb[:, cc:cc + 512], reads=[rnb_b[p]], writes=[DB(("rnn", ch))])
                    for tt in range(c0 // 128, c1 // 128):
                        k.op(PE, lambda tt=tt: nc.tensor.matmul(pstat[:, tt * 4 + ft: tt * 4 + ft + 1], lhsT=sqb[:, tt * 128:(tt + 1) * 128], rhs=ones_bf[:, 0:1], start=True, stop=True),
                             reads=[sqb_b[p], ones_b], writes=[pstat_b])

            cur_ft = load_ft(0)
            front(0, *cur_ft)
            for ft in range(4):
                xr, xr_b, gg, gg_b = cur_ft
                if ft + 1 < 4:
                    nxt_ft = load_ft(ft + 1)
                rnn_gates(ft)
                exps_m(ft)
                sqrts(ft)
                if ft + 1 < 4:
                    front(ft + 1, *nxt_ft)
                tail(ft, gg, gg_b)
                if ft + 1 < 4:
                    cur_ft = nxt_ft
            k.op(DVE, lambda: nc.vector.tensor_reduce(out=ssr[:], in_=pstat[:, 0:NT * 4].rearrange("p (t f) -> p t f", f=4), axis=mybir.AxisListType.X, op=ALU.add),
                 reads=[pstat_b], writes=[ssr_b])
            k.op(ACT, lambda: nc.scalar.activation(out=ssq[:], in_=ssr[:], func=AF.Sqrt, scale=1.0 / 512, bias=EPS), reads=[ssr_b], writes=[ssr_b])
            k.op(DVE, lambda: nc.vector.reciprocal(out=ssr[:], in_=ssq[:]), reads=[ssr_b], writes=[ssr_b])

        if upto <= 2:
            k.drain()
            return nc
        k.barrier()
        with ExitStack() as p4:
            kcT = [sb(f"kcT{g}", [128, 256], BF16, p4) for g in range(2)]
            cllo = sb("cllo", [128, 2, 8], F32, p4)
            vc_aug = [sb(f"vca{g}", [128, 2, 128], BF16, p4) for g in range(2)]
            cmp_b = Buf("cmp")
            cds = k.dsem("cmpc")
            k.dma(SP, cds, cllo[:], cllo_d, writes=[cmp_b])
            for g in range(2):
                k.dma(SP, cds, kcT[g][64:128, :], ce01_d, writes=[cmp_b])
                k.dma(SP, cds, vc_aug[g][:, :, 65:128], ovl_d, writes=[cmp_b])
                k.op(DVE, lambda g=g: nc.vector.memset(vc_aug[g][:, :, 64:65], 1.0), writes=[cmp_b])
                k.op(DVE, lambda g=g: nc.vector.memset(vc_aug[g][:, :, 0:64], 0.0), writes=[cmp_b])
            kTs = [sb(f"kTs{g}", [128, S], BF16, p4) for g in range(2)]
            kTw = [sb(f"kTw{g}", [128, S], BF16, p4) for g in range(2)]
            kds_shared = k.dsem("kT")
            cds_shared = k.dsem("cst")
            kTs_b = [Buf(f"kTs{g}") for g in range(2)]
            kTw_b = [Buf(f"kTw{g}") for g in range(2)]
            cmask = sb("cmask", [128, 2, S], BF16, p4)
            tab = sb("tab", [128, S], BF16, p4)
            sllo = sb("sllo", [128, 8], F32, p4)
            wm01 = sb("wm01", [128, 8, 512], BF16, p4)
            addm = sb("addm", [128, NT, 64], F32, p4)
            cmask_b, tab_b, sllo_b, wm01_b, addm_b = (Buf(n) for n in ("cmask", "tab", "sllo", "wm01", "addm"))
            wout = sb("wout", [128, 8, D], BF16, p4)
            wout_b = Buf("wout")
            gcol = sb("gcol", [128, 8], F32, p4)
            gcol_b = Buf("gcol")
            gds = k.dsem("gcol")
            pw = ExitStack()
            wst = Rot(k, pw, nc, "wst", 2, [128, D], F32)

            def issue_resident_loads(dep):
                for g in range(2):
                    for (t, tb_, src) in ((kTs[g], kTs_b[g], ks_d), (kTw[g], kTw_b[g], kw_d)):
                        k.dma(SP, kds_shared, t[0:64, :], src[g * 64:(g + 1) * 64, :], reads=[DB((id(src), 0, ch)) for ch in range(NCH)] + dep, writes=[tb_])
                        k.dma(SP, kds_shared, t[64:128, :], e01_d, writes=[tb_])
                k.dma(SP, cds_shared, cmask[:], cmask_d, writes=[cmask_b])
                k.dma(SP, cds_shared, tab[64:128, :], tab_d, writes=[tab_b])
                k.dma(SP, cds_shared, sllo[:], sllo_d, writes=[sllo_b])
                k.dma(SP, cds_shared, wm01[:], wm01_d, writes=[wm01_b])
                k.dma(SP, cds_shared, addm[:], addm_d, writes=[addm_b])
                k.dma(SP, gds, gcol[:, 0:4], grnn_d, writes=[gcol_b])
                k.dma(SP, gds, gcol[:, 4:8], gatt_d, writes=[gcol_b])
                for kk in range(8):
                    t, tb, ds = wst.next()
                    k.dma(SP, ds, t[:], wout_d[kk * 128:(kk + 1) * 128, :], writes=[tb])
                    k.op(DVE, lambda kk=kk, t=t: nc.vector.tensor_scalar(out=wout[:, kk, :], in0=t[:], scalar1=gcol[:, kk:kk + 1], scalar2=None, op0=ALU.mult),
                         reads=[tb, gcol_b], writes=[wout_b])

            with ExitStack() as p3:
                w1 = [sb(f"w1_{i}", [64, 32, 256], BF16, p3) for i in range(2)]
                w2 = [sb(f"w2_{i}", [128, 2, 64], BF16, p3) for i in range(2)]
                pos = [sb(f"pos_{i}", [64, 32], BF16, p3) for i in range(2)]
                posf = [sb(f"posf_{i}", [64, 32], F32, p3) for i in range(2)]
                cw_b = Buf("cmpw")
                cw1_b = [[Buf(f"cw1_{i}_{q}") for q in range(4)] for i in range(2)]
                cwd = k.dsem("cmpw")
                cwd2 = k.dsem("cmpp")
                for i, (w1d, w2d, pd) in enumerate(((w1k_d, w2k_d, posk_d), (w1v_d, w2v_d, posv_d))):
                    w1v = w1d.rearrange("(l d) h -> d l h", d=64)
                    for l0 in range(0, 32, 8):
                        k.dma(POOL, k.dsem("w1"), w1[i][:, l0:l0 + 8, :], w1v[:, l0:l0 + 8, :], writes=[cw1_b[i][l0 // 8]])
                    k.dma(POOL, cwd, w2[i][:], w2d.rearrange("(t p) d -> p t d", p=128), writes=[cw_b])
                    k.dma(SP, cwd2, posf[i][:], pd, writes=[cw_b])
                    k.op(DVE, lambda i=i: nc.vector.tensor_copy(out=pos[i][:], in_=posf[i][:]), reads=[cw_b], writes=[cw_b])
                cin = [[sb(f"cin{i}{g}", [64, S], BF16, p3) for g in range(2)] for i in range(2)]
                cin_bs = [[Buf(f"cin{i}{g}") for g in range(2)] for i in range(2)]
                for i, src in enumerate((kc_d, vc_d)):
                    cin_ds = k.dsem("cin")
                    for g in range(2):
                        k.dma(SP, cin_ds, cin[i][g][:], src[g * 64:(g + 1) * 64, :], reads=[DB((id(src), 0, ch)) for ch in range(NCH)], writes=[cin_bs[i][g]])
                hid = sb("hid", [128, 2, 256], BF16, p3)
                hid_b = Buf("hid")
                cbias = sb("cbias", [128, 2, 2], F32, p3)
                cbias_b = Buf("cbias")
                pc_ = [pst(f"pc{i}", [128, 512], F32, p3) for i in range(3)]
                pc_b = [Buf(f"pc{i}") for i in range(3)]
                ci = 0
                for i in range(2):
                    for ht in range(2):
                        pcc = pc_[ci % 3]; pcb = pc_b[ci % 3]; ci += 1
                        for l in range(32):
                            k.op(PE, lambda i=i, ht=ht, l=l, pcc=pcc: nc.tensor.matmul(pcc[:, 0:1], lhsT=w1[i][:, l, ht * 128:(ht + 1) * 128], rhs=pos[i][:, l:l + 1],
                                                                                      start=(l == 0), stop=(l == 31)), reads=[cw_b, cw1_b[i][l // 8]], writes=[pcb])
                        k.op(DVE, lambda i=i, ht=ht, pcc=pcc: nc.vector.tensor_copy(out=cbias[:, i, ht:ht + 1], in_=pcc[:, 0:1]), reads=[pcb], writes=[cbias_b])
                for i in range(2):
                    for g in range(2):
                        cv = cin[i][g][:].rearrange("p (c r) -> p c r", r=16)
                        for ht in range(2):
                            pcc = pc_[ci % 3]; pcb = pc_b[ci % 3]; ci += 1
                            for l in range(32):
                                rhs = cv[:, 0:255, l] if l < 16 else cv[:, 1:256, l - 16]
                                k.op(PE, lambda i=i, ht=ht, l=l, pcc=pcc, rhs=rhs: nc.tensor.matmul(pcc[:, 0:255], lhsT=w1[i][:, l, ht * 128:(ht + 1) * 128], rhs=rhs,
                                                                                                   start=(l == 0), stop=(l == 31)), reads=[cw_b, cw1_b[i][l // 8], cin_bs[i][g]], writes=[pcb])
                            k.op(ACT, lambda i=i, ht=ht, pcc=pcc: nc.scalar.activation(out=hid[:, ht, 0:255], in_=pcc[:, 0:255], func=AF.Gelu_apprx_tanh, bias=cbias[:, i, ht:ht + 1]),
                                 reads=[pcb, cbias_b], writes=[hid_b])
                            if i == 0 and g == 0 and ht == 0:
                                issue_resident_loads([hid_b])
                        if i == 0:
                            pcc = pc_[ci % 3]; pcb = pc_b[ci % 3]; ci += 1
                            for ht in range(2):
                                k.op(PE, lambda ht=ht, pcc=pcc: nc.tensor.matmul(pcc[0:64, 0:255], lhsT=w2[0][:, ht, :], rhs=hid[:, ht, 0:255], start=(ht == 0), stop=(ht == 1)),
                                     reads=[cw_b, hid_b], writes=[pcb])
                            k.op(DVE, lambda g=g, pcc=pcc: nc.vector.tensor_copy(out=kcT[g][0:64, 0:255], in_=pcc[0:64, 0:255]), reads=[pcb], writes=[cmp_b])
                        else:
                            for ct in range(2):
                                nr = 128 if ct == 0 else 127
                                pcc = pc_[ci % 3]; pcb = pc_b[ci % 3]; ci += 1
                                for ht in range(2):
                                    k.op(PE, lambda ht=ht, ct=ct, nr=nr, pcc=pcc: nc.tensor.matmul(pcc[0:nr, 0:64], lhsT=hid[:, ht, ct * 128:ct * 128 + nr], rhs=w2[1][:, ht, :],
                                                                                                  start=(ht == 0), stop=(ht == 1)), reads=[cw_b, hid_b], writes=[pcb])
                                k.op(DVE, lambda g=g, ct=ct, nr=nr, pcc=pcc: nc.vector.tensor_copy(out=vc_aug[g][0:nr, ct, 0:64], in_=pcc[0:nr, 0:64]), reads=[pcb], writes=[cmp_b])

            if upto <= 3:
                pw.close()
                k.drain()
                return nc
            k.barrier()
            pw.close()
            qs = Rot(k, p4, nc, "qs", 1, [128, 8, 512], BF16)
            qw = Rot(k, p4, nc, "qw", 2, [128, 8, 512], BF16)
            SLOPES = [2.0 ** (-(h + 1)) for h in range(8)]
            PT = Rot(k, p4, nc, "PT", 6, [128, 512], BF16, with_dsem=False)
            PTc = Rot(k, p4, nc, "PTc", 2, [128, 512], BF16, with_dsem=False)
            att = sb("att", [128, 4, 512], F32, p4)
            att_b = [Buf(f"att{i}") for i in range(4)]
            attb = sb("attb", [128, 4, 512], BF16, p4)
            attb_b = Buf("attb")
            attT = sb("attT", [128, 4, 512], BF16, p4)
            attT_b = Buf("attT")
            rnT = Rot(k, p4, nc, "rnT", 3, [128, 4, 512], BF16)
            imp = sb("imp", [128, 4, 2, 64], F32, p4)
            imp_b = [Buf(f"imp{i}") for i in range(4)]
            for i_ in range(4):
                k.op(DVE, lambda i_=i_: nc.vector.memset(imp[:, i_, :, :], 0.0), writes=[imp_b[i_]])
            selbT = [sb(f"selbT{g}", [128, 512], BF16, p4) for g in range(2)]
            selbT_b = [Buf(f"selbT{g}") for g in range(2)]
            sm = sb("sm", [128, 256], F32, p4)
            sm_b = [Buf(f"sm{i}") for i in range(16)]
            smi = [0]
            tk_a = sb("tk_a", [128, 8, 64], F32, p4); tk_b_ = sb("tk_b", [128, 8, 64], F32, p4)
            tk8 = sb("tk8", [128, 8, 16], F32, p4); tksel = sb("tksel", [128, 8, 64], BF16, p4)
            tk_bufs = [Buf(f"tk{u}") for u in range(8)]
            ssa = sb("ssa", [128, 8], F32, p4)
            ssa_b = Buf("ssa")
            ysb = Rot(k, p4, nc, "ysb", 2, [128, D], F32)
            xres = Rot(k, p4, nc, "xres", 2, [128, D], F32)
            junk4 = sb("junk4", [128, D], BF16, p4)
            junk4_b = Buf("junk4")
            print("P4 sbuf bytes remaining:", nc.sbuf_bytes_remaining)
            k.barrier()
            pS = [pst(f"pS{i}", [128, 512], F32, p4) for i in range(3)]
            pS_b = [Buf(f"pS{i}") for i in range(3)]
            pA = [pst(f"pA{i}", [128, 512], F32, p4) for i in range(2)]
            pA_b = [Buf(f"pA{i}") for i in range(2)]
            pTT = pst("pTT", [128, 1024], BF16, p4)
            pTT_b = Buf("pTT")
            pW = [pst(f"pW{i}", [128, 512], F32, p4) for i in range(1)]
            pW_b = [Buf(f"pW{i}") for i in range(1)]
            pC = pst("pC", [128, 512], F32, p4)
            pC_b = Buf("pC")
            cnt = {"s": 0, "a": 0}

            def nextS():
                i = cnt["s"] % 3; cnt["s"] += 1
                return pS[i], pS_b[i]

            def nextA():
                i = cnt["a"] % 2; cnt["a"] += 1
                return pA[i], pA_b[i]

            def small():
                i = smi[0] % 16; smi[0] += 1
                return sm[:, 16 * i:16 * i + 16], sm_b[i]

            att_written = set()

            def evac_group(pa, pab, stride, n, h, tl0, tt0, gate_idx, first, with_imp=False, g=None, first_in_group=False):
                s4, s4b = small()
                sums = pa[:, 64:64 + (n - 1) * stride + 1:stride]
                k.op(DVE, lambda: nc.vector.tensor_scalar(out=s4[:, 0:n], in0=sums, scalar1=1e-30, scalar2=None, op0=ALU.max), reads=[pab], writes=[s4b])
                k.op(DVE, lambda: nc.vector.reciprocal(out=s4[:, 4:4 + n], in_=s4[:, 0:n]), reads=[s4b], writes=[s4b])
                k.op(DVE, lambda: nc.vector.tensor_tensor(out=s4[:, 8:8 + n], in0=s4[:, 4:4 + n], in1=gates[:, tt0:tt0 + n, gate_idx], op=ALU.mult),
                     reads=[s4b] + [vtok_b[tt0 + i] for i in range(n)], writes=[s4b])
                for i in range(n):
                    tl = tl0 + i
                    dst = att[:, tl, h * 64:(h + 1) * 64]
                    src = pa[:, i * stride:i * stride + 64]
                    first = (h, tl) not in att_written
                    att_written.add((h, tl))
                    if first:
                        k.op(DVE, lambda dst=dst, src=src, i=i: nc.vector.tensor_scalar(out=dst, in0=src, scalar1=s4[:, 8 + i:9 + i], scalar2=None, op0=ALU.mult),
                             reads=[pab, s4b], writes=[att_b[tl]])
                    else:
                        k.op(DVE, lambda dst=dst, src=src, i=i: nc.vector.scalar_tensor_tensor(out=dst, in0=src, scalar=s4[:, 8 + i:9 + i], in1=dst, op0=ALU.mult, op1=ALU.add),
                             reads=[pab, s4b, att_b[tl]], writes=[att_b[tl]])
                    if with_imp:
                        idst = imp[:, tl, g, 1:64]
                        isrc = pa[:, i * stride + 65:i * stride + 128]
                        if first_in_group:
                            k.op(DVE, lambda idst=idst, isrc=isrc, i=i: nc.vector.tensor_scalar(out=idst, in0=isrc, scalar1=s4[:, 4 + i:5 + i], scalar2=None, op0=ALU.mult),
                                 reads=[pab, s4b], writes=[imp_b[tl]])
                        else:
                            k.op(DVE, lambda idst=idst, isrc=isrc, i=i: nc.vector.scalar_tensor_tensor(out=idst, in0=isrc, scalar=s4[:, 4 + i:5 + i], in1=idst, op0=ALU.mult, op1=ALU.add),
                                 reads=[pab, s4b, imp_b[tl]], writes=[imp_b[tl]])

            dbg_ds = k.dsem("dbg")

            qview = q_d.rearrange("(h d) t -> d h t", d=64)

            def load_chunk(jn):
                tn = jn * 512
                rt_, rtb_, rds_ = rnT.next()
                k.dma(SP, rds_, rt_[:], rnn_d[:, tn:tn + 512].rearrange("(f p) t -> p f t", p=128), reads=[DB(("rnn", jn))], writes=[rtb_])
                qw_, qwb_, qwd_ = qw.next()
                k.dma(SP, qwd_, qw_[0:64, :, :], qview[:, :, tn:tn + 512], reads=[DB((id(q_d), c4 * 128, jn)) for c4 in range(4)], writes=[qwb_])
                for h_ in range(8):
                    k.op(DVE, lambda h_=h_, qw_=qw_: nc.vector.tensor_scalar(out=qw_[64:128, h_, :], in0=tab[64:128, tn:tn + 512], scalar1=SLOPES[h_], scalar2=None, op0=ALU.mult),
                         reads=[tab_b], writes=[qwb_])
                return rt_, rtb_, qw_, qwb_

            def load_qs(jn):
                tn = jn * 512
                qs_, qsb_, qsd_ = qs.next()
                k.dma(SP, qsd_, qs_[0:64, :, :], qview[:, :, tn:tn + 512], reads=[DB((id(q_d), c4 * 128, jn)) for c4 in range(4)], writes=[qsb_])
                return qs_, qsb_

            nxt = load_chunk(0)
            nxt_qs = load_qs(0)
            wcnt = [0]

            def nextW():
                return pW[0], pW_b[0]

            def make_cback(jc, rt, rtb):
                units = []

                def u_tr(half):
                    for tl2 in range(2):
                        tl = half * 2 + tl2
                        for f in range(4):
                            k.op(PE, lambda tl=tl, tl2=tl2, f=f: nc.tensor.transpose(out=pTT[:, (tl2 * 4 + f) * 128:(tl2 * 4 + f + 1) * 128], in_=attb[:, tl, f * 128:(f + 1) * 128], identity=ident[:]),
                                 reads=[attb_b, ident_b], writes=[pTT_b])
                    for tl2 in range(2):
                        tl = half * 2 + tl2
                        k.op(DVE, lambda tl=tl, tl2=tl2: nc.vector.tensor_copy(out=attT[:, :, tl * 128:(tl + 1) * 128],
                                                                             in_=pTT[:, tl2 * 512:(tl2 + 1) * 512].rearrange("p (f t) -> p f t", f=4)),
                             reads=[pTT_b], writes=[attT_b])
                units.append(lambda: u_tr(0))
                units.append(lambda: u_tr(1))
                st = {}

                def u_mm(tl, half, part):
                    tt = jc * 4 + tl
                    if half == 0 and part == 0:
                        yt, ytb, yds = ysb.next()
                        xr_, xrb, xds = xres.next()
                        k.dma(SP, xds, xr_[:], x_d[tt * 128:(tt + 1) * 128, :], writes=[xrb])
                        st[tl] = (yt, ytb, yds, xr_, xrb)
                    yt, ytb, yds, xr_, xrb = st[tl]
                    if part == 0:
                        st[(tl, half)] = nextW()
                    pw, pwb = st[(tl, half)]
                    if part == 0:
                        for f in range(4):
                            k.op(PE, lambda f=f: nc.tensor.matmul(pw[:, :], lhsT=rt[:, f, tl * 128:(tl + 1) * 128], rhs=wout[:, f, half * 512:(half + 1) * 512],
                                                                  start=(f == 0), stop=False), reads=[rtb, wout_b], writes=[pwb])
                    else:
                        for f in range(4):
                            k.op(PE, lambda f=f: nc.tensor.matmul(pw[:, :], lhsT=attT[:, f, tl * 128:(tl + 1) * 128], rhs=wout[:, 4 + f, half * 512:(half + 1) * 512],
                                                                  start=False, stop=(f == 3)), reads=[attT_b, wout_b], writes=[pwb])
                        k.op(DVE, lambda: nc.vector.tensor_scalar(out=yt[:, half * 512:(half + 1) * 512], in0=pw[:, :], scalar1=ssr[:, tt:tt + 1], scalar2=None, op0=ALU.mult),
                             reads=[pwb, ssr_b], writes=[ytb])

                def u_epi(tl):
                    tt = jc * 4 + tl
                    yt, ytb, yds, xr_, xrb = st[tl]
                    if debug:
                        k.dma(POOL, dbg_ds, dbg["d_y"][tt * 128:(tt + 1) * 128, :], yt[:], reads=[ytb], writes=[DB(("dy", tt))])
                    s4, s4b = small()
                    k.op(ACT, lambda: nc.scalar.activation(out=junk4[:], in_=yt[:], func=AF.Square, accum_out=s4[:, 0:1]), reads=[ytb], writes=[junk4_b, s4b])
                    k.op(ACT, lambda: nc.scalar.activation(out=s4[:, 1:2], in_=s4[:, 0:1], func=AF.Ln, scale=1.0 / D, bias=EPS), reads=[s4b], writes=[s4b])
                    k.op(ACT, lambda: nc.scalar.activation(out=s4[:, 2:3], in_=s4[:, 1:2], func=AF.Exp, scale=-0.5), reads=[s4b], writes=[s4b])
                    k.op(DVE, lambda: nc.vector.scalar_tensor_tensor(out=yt[:], in0=yt[:], scalar=s4[:, 2:3], in1=C1row[:], op0=ALU.mult, op1=ALU.mult),
                         reads=[ytb, s4b, C1_b], writes=[ytb])
                    k.op(DVE, lambda: nc.vector.tensor_tensor(out=yt[:], in0=yt[:], in1=xr_[:], op=ALU.add), reads=[ytb, xrb], writes=[ytb])
                    k.dma(POOL, yds, x1_d[tt * 128:(tt + 1) * 128, :], yt[:], reads=[ytb], writes=[DB(("x1", tt))])

                for tl in range(4):
                    for half in range(2):
                        for part in range(2):
                            units.append(lambda tl=tl, half=half, part=part: u_mm(tl, half, part))
                    units.append(lambda tl=tl: u_epi(tl))
                return units

            cback = []
            for j in range(NCH):
                t0 = j * 512
                att_written.clear()
                rt, rtb, qwt, qwb = nxt
                qst, qsb = nxt_qs
                if j + 1 < NCH:
                    nxt = load_chunk(j + 1)
                ncts = [ct for ct in range(2) if 16 * (ct * 128) + 31 <= t0 + 511]

                def cmp_S(h):
                    g = h // 4
                    pts = []
                    for ct in ncts:
                        nr = 128 if ct == 0 else 127
                        ps, psb = nextS()
                        k.op(PE, lambda ps=ps, ct=ct, nr=nr, g=g, h=h: nc.tensor.matmul(ps[0:nr, :], lhsT=kcT[g][:, ct * 128:ct * 128 + nr], rhs=qwt[:, h, :], start=True, stop=False),
                             reads=[cmp_b, qwb], writes=[psb])
                        k.op(PE, lambda ps=ps, ct=ct, nr=nr: nc.tensor.matmul(ps[0:nr, :], lhsT=ident[:, 0:nr], rhs=cmask[:, ct, t0:t0 + 512], start=False, stop=True),
                             reads=[ident_b, cmask_b], writes=[psb])
                        pt, ptb, _ = PTc.next()
                        k.op(ACT, lambda ps=ps, pt=pt, nr=nr, ct=ct, h=h: nc.scalar.activation(out=pt[0:nr, :], in_=ps[0:nr, :], func=AF.Exp, bias=cllo[0:nr, ct, h:h + 1]),
                             reads=[psb, cmp_b], writes=[ptb])
                        pts.append((pt, ptb, ct, nr))
                    return pts

                def cmp_PV(h, pts):
                    g = h // 4
                    nmm = 4 * len(pts)
                    mi = 0
                    for tl in range(4):
                        for (pt, ptb, ct, nr) in pts:
                            k.op(PE, lambda pt=pt, tl=tl, nr=nr, ct=ct, g=g, mi=mi, nmm=nmm: nc.tensor.matmul(
                                pC[:, tl * 128:(tl + 1) * 128], lhsT=pt[0:nr, tl * 128:(tl + 1) * 128], rhs=vc_aug[g][0:nr, ct, :],
                                start=(mi == 0), stop=(mi == nmm - 1)),
                                reads=[ptb, cmp_b], writes=[pC_b])
                            mi += 1
                    evac_group(pC, pC_b, 128, 4, h, 0, j * 4, 0 * 8 + h, True, with_imp=True, g=g, first_in_group=(h % 4 == 0))

                pre = []
                cst = {}

                def u_cS(h):
                    cst[h] = cmp_S(h)

                def u_cPV(h):
                    cmp_PV(h, cst[h])
                for h in range(8):
                    pre.append(lambda h=h: u_cS(h))
                    pre.append(lambda h=h: u_cPV(h))
                units8 = [(tl, g) for tl in range(4) for g in range(2)]

                def tk_stage(sidx):
                    for u, (tl, g) in enumerate(units8):
                        tt = j * 4 + tl
                        tb = tk_bufs[u]
                        if sidx == 0:
                            k.op(DVE, lambda u=u, tl=tl, g=g, tt=tt: nc.vector.tensor_tensor(out=tk_a[:, u, :], in0=imp[:, tl, g, :], in1=addm[:, tt, :], op=ALU.add),
                                 reads=[imp_b[tl], addm_b], writes=[tb])
                        elif sidx == 1:
                            k.op(DVE, lambda u=u: nc.vector.max(out=tk8[:, u, 0:8], in_=tk_a[:, u, :]), reads=[tb], writes=[tb])
                        elif sidx == 2:
                            k.op(DVE, lambda u=u: nc.vector.match_replace(out=tk_b_[:, u, :], in_to_replace=tk8[:, u, 0:8], in_values=tk_a[:, u, :], imm_value=-3.0e38), reads=[tb], writes=[tb])
                        elif sidx == 3:
                            k.op(DVE, lambda u=u: nc.vector.max(out=tk8[:, u, 8:16], in_=tk_b_[:, u, :]), reads=[tb], writes=[tb])
                        elif sidx == 4:
                            k.op(DVE, lambda u=u: nc.vector.tensor_scalar(out=tksel[:, u, :], in0=tk_a[:, u, :], scalar1=tk8[:, u, 15:16], scalar2=NEGM, op0=ALU.is_lt, op1=ALU.mult),
                                 reads=[tb], writes=[tb])
                        elif sidx == 5:
                            k.op(PE, lambda u=u, g=g, tl=tl: nc.tensor.transpose(out=pTT[0:64, (g * 4 + tl) * 128:(g * 4 + tl + 1) * 128], in_=tksel[:, u, :], identity=ident[:]),
                                 reads=[tb, ident_b], writes=[pTT_b])
                    if sidx == 6:
                        for g in range(2):
                            if os.environ.get("KDBG_S6") == "act":
                                k.op(ACT, lambda g=g: nc.scalar.activation(out=selbT[g][64:128, :], in_=pTT[0:64, g * 512:(g + 1) * 512], func=AF.Copy), reads=[pTT_b], writes=[selbT_b[g]])
                            else:
                                k.op(DVE, lambda g=g: nc.vector.tensor_copy(out=selbT[g][64:128, :], in_=pTT[0:64, g * 512:(g + 1) * 512]), reads=[pTT_b], writes=[selbT_b[g]])
                    if sidx == 7:
                        for h_ in range(8):
                            k.op(DVE, lambda h_=h_: nc.vector.tensor_tensor(out=qst[64:128, h_, :], in0=qwt[64:128, h_, :], in1=selbT[h_ // 4][64:128, :], op=ALU.add),
                                 reads=[qwb, selbT_b[h_ // 4]], writes=[qsb])
                for sidx in range(8):
                    pre.append(lambda sidx=sidx: tk_stage(sidx))

                tasks = []
                for br in (2, 1):
                    for h in range(8):
                        kts = list(range(0, 4 * j + 4)) if br == 1 else list(range(max(0, 4 * j - 4), 4 * j + 4))
                        grp = {"h": h, "br": br, "g": h // 4, "pa": None, "npv": 0, "done": 0}
                        for kt in kts:
                            tls = [tl for tl in range(4) if kt <= 4 * j + tl and (br == 1 or kt >= 4 * j + tl - 4)]
                            grp["npv"] += len(tls)
                            tasks.append({"grp": grp, "kt": kt, "tls": tls})
                n_win = sum(1 for tk in tasks if tk["grp"]["br"] == 2)

                def emit_S(tk):
                    grp = tk["grp"]; h = grp["h"]; br = grp["br"]; g = grp["g"]; kt = tk["kt"]
                    kT = kTs[g] if br == 1 else kTw[g]
                    qq, qqb = (qst, qsb) if br == 1 else (qwt, qwb)
                    ps, psb = nextS()
                    m = (kt - (4 * j - 4)) if br == 2 else (4 + kt - 4 * j)
                    use_msk = (br == 2) or (m >= 4)
                    c0, c1 = 0, 512
                    if use_msk:
                        if m < 4:
                            c1 = 128 * (m + 1)
                        else:
                            c0 = 128 * (m - 4)
                    k.op(PE, lambda: nc.tensor.matmul(ps[:, c0:c1], lhsT=kT[:, kt * 128:(kt + 1) * 128], rhs=qq[:, h, c0:c1], start=True, stop=True),
                         reads=[(kTs_b[g] if br == 1 else kTw_b[g]), qqb], writes=[psb])
                    pt, ptb, _ = PT.next()
                    k.op(ACT, lambda: nc.scalar.activation(out=pt[:, c0:c1], in_=ps[:, c0:c1], func=AF.Exp, bias=sllo[:, h:h + 1]), reads=[psb, sllo_b], writes=[ptb])
                    if use_msk:
                        k.op(DVE, lambda: nc.vector.tensor_tensor(out=pt[:, c0:c1], in0=pt[:, c0:c1], in1=wm01[:, m, c0:c1], op=ALU.mult), reads=[ptb, wm01_b], writes=[ptb])
                    tk["pt"] = pt; tk["ptb"] = ptb

                def emit_PV(tk):
                    grp = tk["grp"]; h = grp["h"]; br = grp["br"]; g = grp["g"]; kt = tk["kt"]
                    vA = vs_aug if br == 1 else vw_aug
                    if grp["pa"] is None:
                        grp["pa"] = nextA()
                    pa, pab = grp["pa"]
                    pt = tk["pt"]; ptb = tk["ptb"]
                    for tl in tk["tls"]:
                        fm = (grp["done"] == 0)
                        grp["done"] += 1
                        last = (grp["done"] == grp["npv"])
                        k.op(PE, lambda tl=tl, fm=fm, last=last: nc.tensor.matmul(pa[:, tl * 65:(tl + 1) * 65], lhsT=pt[:, tl * 128:(tl + 1) * 128], rhs=vA[:, kt, g, :], start=fm, stop=last),
                             reads=[ptb, vtok_b[kt], vones_b], writes=[pab])
                    if grp["done"] == grp["npv"]:
                        evac_group(pa, pab, 65, 4, h, 0, j * 4, br * 8 + h, False)

                LOOK = 3
                n_sel = len(tasks) - n_win
                pre_every = max(1, n_win // (len(pre) + 1))
                cb_every = max(1, (n_sel - 2) // (len(cback) + 1)) if cback else 1
                for i in range(len(tasks) + LOOK):
                    if i < len(tasks):
                        if i == n_win:
                            while pre:
                                pre.pop(0)()
                        emit_S(tasks[i])
                        if i < n_win:
                            if pre and (i % pre_every == pre_every - 1):
                                pre.pop(0)()
                        else:
                            if cback and ((i - n_win) % cb_every == cb_every - 1):
                                cback.pop(0)()
                    if i - LOOK >= 0:
                        emit_PV(tasks[i - LOOK])
                while cback:
                    cback.pop(0)()
                if debug:
                    for tl in range(4):
                        k.dma(POOL, dbg_ds, dbg["d_att"][(j * 4 + tl) * 128:(j * 4 + tl + 1) * 128, :], att[:, tl, :], reads=[att_b[tl]], writes=[DB(("datt", j, tl))])
                for tl in range(4):
                    k.op(ACT, lambda tl=tl: nc.scalar.activation(out=junk4[:, 0:512], in_=att[:, tl, :], func=AF.Square, accum_out=ssa[:, tl:tl + 1]),
                         reads=[att_b[tl]], writes=[junk4_b, ssa_b])
                k.op(ACT, lambda: nc.scalar.activation(out=ssa[:, 4:8], in_=ssa[:, 0:4], func=AF.Ln, scale=1.0 / 512, bias=EPS), reads=[ssa_b], writes=[ssa_b])
                k.op(ACT, lambda: nc.scalar.activation(out=ssa[:, 4:8], in_=ssa[:, 4:8], func=AF.Exp, scale=-0.5), reads=[ssa_b], writes=[ssa_b])
                k.op(DVE, lambda: nc.vector.tensor_tensor(out=ssa[:, 4:8], in0=ssa[:, 4:8], in1=ssq[:, j * 4:(j + 1) * 4], op=ALU.mult), reads=[ssa_b, ssr_b], writes=[ssa_b])
                for tl in range(4):
                    k.op(DVE, lambda tl=tl: nc.vector.tensor_scalar(out=attb[:, tl, :], in0=att[:, tl, :], scalar1=ssa[:, 4 + tl:5 + tl], scalar2=None, op0=ALU.mult),
                         reads=[att_b[tl], ssa_b], writes=[attb_b])
                cback = make_cback(j, rt, rtb)
                if os.environ.get("KDBG_CB") == "now":
                    while cback:
                        cback.pop(0)()
                if j + 1 < NCH:
                    nxt_qs = load_qs(j + 1)
            while cback:
                cback.pop(0)()

        mid.close()
        if upto <= 4:
            k.drain()
            return nc
        k.barrier()
        with ExitStack() as p5:
            wf1 = sb("wf1", [128, 8, 4 * D], BF16, p5)
            wf2 = sb("wf2", [128, 32, D], BF16, p5)
            wf1_b = [Buf(f"wf1_{i}") for i in range(8)]; wf2_b = [Buf(f"wf2_{i}") for i in range(4)]
            wf1v = wff1_d.rearrange("(k p) n -> p k n", p=128)
            wf2v = wff2_d.rearrange("(k p) n -> p k n", p=128)
            for cb in range(8):
                k.dma(POOL, k.dsem("wf1"), wf1[:, :, cb * 512:(cb + 1) * 512], wf1v[:, :, cb * 512:(cb + 1) * 512], writes=[wf1_b[cb]])
            for hh in range(4):
                k.dma(POOL, k.dsem("wf2"), wf2[:, hh * 8:(hh + 1) * 8, :], wf2v[:, hh * 8:(hh + 1) * 8, :], writes=[wf2_b[hh]])
            CH = 256
            xin = Rot(k, p5, nc, "xin", 4, [128, D], F32)
            xnb = Rot(k, p5, nc, "xnb", 2, [128, D], BF16, with_dsem=False)
            hT2 = Rot(k, p5, nc, "hT2", 2, [128, 8, CH], BF16, with_dsem=False)
            aT = Rot(k, p5, nc, "aT", 1, [128, 32, CH], BF16, with_dsem=False)
            r32 = Rot(k, p5, nc, "r32", 3, [128, CH], F32, with_dsem=False)
            y2 = Rot(k, p5, nc, "y2", 2, [128, D], F32, with_dsem=False)
            ot = Rot(k, p5, nc, "ot", 2, [128, D], F32)
            junk5 = sb("junk5", [128, D], BF16, p5)
            junk5_b = Buf("junk5")
            sm5 = sb("sm5", [128, 64], F32, p5)
            sm5_b = [Buf(f"sm5_{i}") for i in range(16)]
            s5i = [0]
            pT5 = [pst(f"pT5_{i}", [128, 1024], BF16, p5) for i in range(2)]
            pT5_b = [Buf(f"pT5_{i}") for i in range(2)]
            pF = [pst(f"pF{i}", [128, 512], F32, p5) for i in range(2)]
            pF_b = [Buf(f"pF{i}") for i in range(2)]
            pY = [pst(f"pY{i}", [128, 512], F32, p5) for i in range(4)]
            pY_b = [Buf(f"pY{i}") for i in range(4)]
            fi = [0]
            out_bufs = []

            def prologue_a(cj):
                xtiles = []
                nts = []
                for tl in range(2):
                    tt = cj * 2 + tl
                    xt_, xtb, xds = xin.next()
                    k.dma(SP, xds, xt_[:], x1_d[tt * 128:(tt + 1) * 128, :], reads=[DB(("x1", tt))], writes=[xtb])
                    xtiles.append((xt_, xtb))
                    i5 = s5i[0] % 16; s5i[0] += 1
                    s4 = sm5[:, 4 * i5:4 * i5 + 4]; s4b = sm5_b[i5]
                    k.op(ACT, lambda xt_=xt_, s4=s4: nc.scalar.activation(out=junk5[:], in_=xt_[:], func=AF.Square, accum_out=s4[:, 0:1]), reads=[xtb], writes=[junk5_b, s4b])
                    k.op(ACT, lambda s4=s4: nc.scalar.activation(out=s4[:, 1:2], in_=s4[:, 0:1], func=AF.Sqrt, scale=1.0 / D, bias=EPS), reads=[s4b], writes=[s4b])
                    k.op(DVE, lambda s4=s4: nc.vector.reciprocal(out=s4[:, 2:3], in_=s4[:, 1:2]), reads=[s4b], writes=[s4b])
                    n, nb, _ = xnb.next()
                    k.op(DVE, lambda xt_=xt_, n=n, s4=s4: nc.vector.tensor_scalar(out=n[:], in0=xt_[:], scalar1=s4[:, 2:3], scalar2=None, op0=ALU.mult), reads=[xtb, s4b], writes=[nb])
                    nts.append((n, nb))
                return xtiles, nts

            def prologue_b(cj, nts):
                ht2, ht2b, _ = hT2.next()
                for tl in range(2):
                    tt = cj * 2 + tl
                    n, nb = nts[tl]
                    pp = pT5[tt % 2]; ppb = pT5_b[tt % 2]
                    for jj in range(8):
                        k.op(PE, lambda jj=jj, n=n, pp=pp: nc.tensor.transpose(out=pp[:, jj * 128:(jj + 1) * 128], in_=n[:, jj * 128:(jj + 1) * 128], identity=ident[:]),
                             reads=[nb, ident_b], writes=[ppb])
                    for jj in range(8):
                        if tt % 2 == 0:
                            k.op(DVE, lambda jj=jj, pp=pp, tl=tl, ht2=ht2: nc.vector.tensor_scalar(out=ht2[:, jj, tl * 128:(tl + 1) * 128], in0=pp[:, jj * 128:(jj + 1) * 128],
                                                                                                   scalar1=A2[:, jj:jj + 1], scalar2=B2[:, jj:jj + 1], op0=ALU.mult, op1=ALU.add),
                                 reads=[ppb, A2_b, B2_b], writes=[ht2b])
                        else:
                            k.op(ACT, lambda jj=jj, pp=pp, tl=tl, ht2=ht2: nc.scalar.activation(out=ht2[:, jj, tl * 128:(tl + 1) * 128], in_=pp[:, jj * 128:(jj + 1) * 128],
                                                                                                func=AF.Identity, scale=A2[:, jj:jj + 1], bias=B2[:, jj:jj + 1]),
                                 reads=[ppb, A2_b, B2_b], writes=[ht2b])
                return ht2, ht2b

            def ff1(ht2, ht2b, mid_cb=None):
                at, atb, _ = aT.next()
                res = None
                for f in range(32):
                    if f == 10 and mid_cb is not None:
                        res = mid_cb()
                    pf = pF[fi[0] % 2]; pfb = pF_b[fi[0] % 2]; fi[0] += 1
                    for kk in range(8):
                        k.op(PE, lambda kk=kk, f=f, pf=pf: nc.tensor.matmul(pf[:, 0:CH], lhsT=wf1[:, kk, f * 128:(f + 1) * 128], rhs=ht2[:, kk, :], start=(kk == 0), stop=(kk == 7)),
                             reads=[wf1_b[f // 4], ht2b], writes=[pfb])
                    r, rb, _ = r32.next()
                    k.op(ACT, lambda pf=pf, r=r: nc.scalar.activation(out=r[:], in_=pf[:, 0:CH], func=AF.Relu), reads=[pfb], writes=[rb])
                    k.op(DVE, lambda r=r, f=f: nc.vector.tensor_tensor(out=at[:, f, :], in0=r[:], in1=r[:], op=ALU.mult), reads=[rb], writes=[atb])
                return at, atb, res

            def ff2(cj, at, atb, xtiles):
                for tl in range(2):
                    tt = cj * 2 + tl
                    yy, yyb, _ = y2.next()
                    for half in range(2):
                        py = pY[(tl * 2 + half) % 4]; pyb = pY_b[(tl * 2 + half) % 4]
                        for f in range(32):
                            k.op(PE, lambda f=f, tl=tl, half=half, py=py: nc.tensor.matmul(py[:, :], lhsT=at[:, f, tl * 128:(tl + 1) * 128], rhs=wf2[:, f, half * 512:(half + 1) * 512],
                                                                                          start=(f == 0), stop=(f == 31)), reads=[atb, wf2_b[f // 8]], writes=[pyb])
                        k.op(ACT, lambda half=half, yy=yy, py=py: nc.scalar.activation(out=yy[:, half * 512:(half + 1) * 512], in_=py[:, :], func=AF.Copy), reads=[pyb], writes=[yyb])
                    i5 = s5i[0] % 16; s5i[0] += 1
                    s4 = sm5[:, 4 * i5:4 * i5 + 4]; s4b = sm5_b[i5]
                    k.op(ACT, lambda yy=yy, s4=s4: nc.scalar.activation(out=junk5[:], in_=yy[:], func=AF.Square, accum_out=s4[:, 0:1]), reads=[yyb], writes=[junk5_b, s4b])
                    k.op(ACT, lambda s4=s4: nc.scalar.activation(out=s4[:, 1:2], in_=s4[:, 0:1], func=AF.Sqrt, scale=1.0 / D, bias=EPS), reads=[s4b], writes=[s4b])
                    k.op(DVE, lambda s4=s4: nc.vector.reciprocal(out=s4[:, 2:3], in_=s4[:, 1:2]), reads=[s4b], writes=[s4b])
                    k.op(DVE, lambda yy=yy, s4=s4: nc.vector.scalar_tensor_tensor(out=yy[:], in0=yy[:], scalar=s4[:, 2:3], in1=C2row[:], op0=ALU.mult, op1=ALU.mult),
                         reads=[yyb, s4b, C2_b], writes=[yyb])
                    o, ob, ods = ot.next()
                    xt_, xtb = xtiles[tl]
                    k.op(DVE, lambda yy=yy, o=o, xt_=xt_: nc.vector.tensor_tensor(out=o[:], in0=yy[:], in1=xt_[:], op=ALU.add), reads=[yyb, xtb], writes=[ob])
                    db = DB(("out", tt))
                    k.dma(POOL, ods, out_d[tt * 128:(tt + 1) * 128, :], o[:], reads=[ob], writes=[db])
                    out_bufs.append(db)

            NCJ = S // CH
            xtiles, nts = prologue_a(0)
            ht2, ht2b = prologue_b(0, nts)
            for cj in range(NCJ):
                at, atb, res = ff1(ht2, ht2b, (lambda cj=cj: prologue_a(cj + 1)) if cj + 1 < NCJ else None)
                if cj + 1 < NCJ:
                    xtiles_n, nts_n = res
                    ht2_n, ht2b_n = prologue_b(cj + 1, nts_n)
                ff2(cj, at, atb, xtiles)
                if cj + 1 < NCJ:
                    xtiles, ht2, ht2b = xtiles_n, ht2_n, ht2b_n
            k.drain()
            k.finish(out_bufs + [b for kk_, b in dbuf.items() if isinstance(kk_, tuple) and kk_ and kk_[0] in ("datt", "dy")] + ([DB("d_mod")] if debug else []))
        print("bass instructions:", k.ninst, "semaphores:", k.nsem)
    return nc


def _consts():
    bf = ml_dtypes.bfloat16
    t = np.arange(S)
    slopes = 2.0 ** (-np.arange(1, 9, dtype=np.float64))
    c = np.arange(256)
    ce = 16 * c + 31
    ce01 = ((ce[None, :] // 64) == np.arange(64)[:, None]).astype(np.float32)
    ce01[:, 255] = 0.0
    cidx = 16 * (np.arange(2)[None, :] * 128 + np.arange(128)[:, None]) + 31
    cllo = (slopes[None, None, :] * (cidx[:, :, None] % 64)).astype(np.float32)
    cend = (16 * (np.arange(2)[None, :, None] * 128 + np.arange(128)[:, None, None]) + 31)
    cmask = np.where(cend <= t[None, None, :], 0.0, NEGM).astype(np.float32)
    e01 = ((t[None, :] // 64) == np.arange(64)[:, None]).astype(np.float32)
    tab = (64.0 * (np.arange(64)[:, None] - (t[None, :] // 64))).astype(np.float32)
    sllo = (slopes[None, :] * (np.arange(128)[:, None] % 64)).astype(np.float32)
    m = np.arange(8)[None, :, None]; kk = np.arange(128)[:, None, None]; tl = np.arange(512)[None, None, :]
    dd = (512 + tl) - (128 * m + kk)
    wm01 = ((dd >= 0) & (dd < 512)).astype(np.float32)
    tok = (np.arange(NT)[None, :, None] * 128 + np.arange(128)[:, None, None])
    cur = tok // 64
    jb = np.arange(64)[None, None, :]
    forced = (jb == 0) | (jb == cur) | (jb == cur - 1)
    addm = np.where(forced, 1.0e4, np.where(jb <= cur, 0.0, -1.0e30)).astype(np.float32)
    cc = (np.arange(2)[None, :, None] * 128 + np.arange(128)[:, None, None])
    ovl = ((cc >= 4 * jb - 1) & (cc <= 4 * jb + 3) & (cc < 255)).astype(np.float32)
    return {
        "k_ident": np.eye(128, dtype=np.float32).astype(bf),
        "k_ce01": ce01.astype(bf), "k_cllo": cllo,
        "k_cmask": cmask.astype(bf), "k_e01": e01.astype(bf), "k_tab": tab.astype(bf), "k_sllo": sllo, "k_wm01": wm01.astype(bf),
        "k_addm": addm, "k_ovl": np.ascontiguousarray(ovl[:, :, 1:]).astype(bf),
    }


def _col(v, n):
    return np.ascontiguousarray(np.asarray(v, np.float32).reshape(n, 128).T)


def _shared_inputs(inp):
    L = 0
    f = lambda a: np.ascontiguousarray(np.asarray(a, np.float32))
    d = {
        "ada_w": f(inp["ada_w"][L]), "ada_b": f(inp["ada_b"][L]).reshape(1, -1),
        "g_pre1": _col(inp["pre_norm_mix"][L], 8), "g_pre2": _col(inp["pre_norm_mlp"][L], 8),
        "g_post1": np.ascontiguousarray(np.broadcast_to(f(inp["post_norm_mix"][L])[None, :], (128, D))),
        "g_post2": np.ascontiguousarray(np.broadcast_to(f(inp["post_norm_mlp"][L])[None, :], (128, D))),
        "w_in": f(inp["w_in"][L]),
        "conv_w": np.ascontiguousarray(f(inp["conv_w"][L]).T.reshape(4, 128, 4).transpose(1, 0, 2)),
        "conv_b": _col(inp["conv_b"][L], 4),
        "lru_wa": f(inp["lru_wa"][L]), "lru_wx": f(inp["lru_wx"][L]),
        "lru_ba": _col(inp["lru_ba"][L], 4), "lru_bx": _col(inp["lru_bx"][L], 4), "lru_lam": _col(inp["lru_lambda"][L], 4),
        "pos_k": np.ascontiguousarray(f(inp["cmp_pos_k"][L]).T), "pos_v": np.ascontiguousarray(f(inp["cmp_pos_v"][L]).T),
        "w1_k": f(inp["cmp_w1_k"][L]), "w1_v": f(inp["cmp_w1_v"][L]),
        "w2_k": f(inp["cmp_w2_k"][L]), "w2_v": f(inp["cmp_w2_v"][L]),
        "g_rnn": _col(inp["norm_rnn_out"][L], 4), "g_att": _col(inp["norm_att_out"][L], 4),
        "w_out": f(inp["w_out"][L]), "w_ff1": f(inp["w_ff1"][L]), "w_ff2": f(inp["w_ff2"][L]),
    }
    d.update(_consts())
    return d


def kernel(**inputs):
    debug = bool(inputs.pop("_debug", False))
    upto = inputs.pop("_upto", 99)
    cores = inputs.pop("_cores", None)
    x = np.asarray(inputs["x"], np.float32)
    c = np.asarray(inputs["c"], np.float32)
    B = x.shape[0]
    shared = _shared_inputs(inputs)
    nc = build(debug=debug, upto=upto)
    bs = list(range(B)) if cores is None else list(cores)
    in_maps = []
    for b in bs:
        m = dict(shared)
        m["x"] = np.ascontiguousarray(x[b])
        m["c"] = _col(c[b], 8)
        in_maps.append(m)
    res = run_bass_kernel_spmd(nc, in_maps, core_ids=list(range(len(bs))))
    if debug:
        return res.results
    return np.stack([np.asarray(r["out"], np.float32) for r in res.results], axis=0)
```

```python
import numpy as np
import ml_dtypes
from contextlib import ExitStack
import concourse.bass as bass
import concourse.mybir as mybir
from concourse.bass_utils import run_bass_kernel_spmd

F32 = mybir.dt.float32
BF16 = mybir.dt.bfloat16
AF = mybir.ActivationFunctionType
ALU = mybir.AluOpType

S = 4096
D = 1024
NT = S // 128
NCH = S // 512
DIN = 2328
NEGM = -30000.0
EPS = 1e-6
import os
EVAC = os.environ.get('KDBG_EVAC', '')


class Buf:
    __slots__ = ("name", "lw", "rd")

    def __init__(self, name):
        self.name = name
        self.lw = None
        self.rd = {}


class DSem:
    __slots__ = ("sem", "cnt")

    def __init__(self, sem):
        self.sem = sem
        self.cnt = 0


class Eng:
    def __init__(self, name, eng, self_sync):
        self.name = name
        self.eng = eng
        self.self_sync = self_sync
        self.cur = None
        self.cnt = 0
        self.gidx = 0
        self.tokmap = {}
        self.waited = {}


class K:
    EPOCH = 4000

    def __init__(self, nc, es, needed=None, record=None):
        self.nc = nc
        self.es = es
        self.needed = needed
        self.record = record
        self.nsem = 0
        self.pe = Eng("pe", nc.tensor, False)
        self.act = Eng("act", nc.scalar, True)
        self.dve = Eng("dve", nc.vector, True)
        self.pool = Eng("pool", nc.gpsimd, True)
        self.sp = Eng("sp", nc.sync, True)
        self.dsems = []
        self.ninst = 0
        self.nsig = 0

    def new_sem(self, name):
        self.nsem += 1
        return self.es.enter_context(self.nc.semaphore(f"{name}_{self.nsem}"))

    def dsem(self, name="d"):
        d = DSem(self.new_sem(name))
        self.dsems.append(d)
        return d

    def _wait(self, E, tok):
        if tok[0] == "E":
            P, g = tok[1], tok[2]
            if (not E.self_sync) and P is E:
                return
            if E.waited.get(P.name, 0) >= g:
                return
            if self.record is not None:
                self.record.add((P.name, g))
            sem, val = P.tokmap[g]
            E.eng.wait_ge(sem, val)
            E.waited[P.name] = g
        else:
            ds, val = tok[1], tok[2]
            key = id(ds)
            if E.waited.get(key, 0) >= val:
                return
            E.eng.wait_ge(ds.sem, val)
            E.waited[key] = val

    def _deps(self, E, reads, writes):
        for b in reads:
            if b.lw is not None:
                self._wait(E, b.lw)
        for b in writes:
            if b.lw is not None:
                self._wait(E, b.lw)
            for tok in b.rd.values():
                self._wait(E, tok)

    def _commit(self, tok, reads, writes):
        key = tok[1].name if tok[0] == "E" else id(tok[1])
        for b in reads:
            old = b.rd.get(key)
            if old is None or old[2] < tok[2]:
                b.rd[key] = tok
        for b in writes:
            b.lw = tok
            b.rd = {}

    def op(self, E, fn, reads=(), writes=()):
        self._deps(E, reads, writes)
        inst = fn()
        E.gidx += 1
        g = E.gidx
        if self.needed is None or (E.name, g) in self.needed:
            if E.cur is None or E.cnt >= self.EPOCH:
                E.cur = self.new_sem(E.name)
                E.cnt = 0
            E.cnt += 1
            inst.then_inc(E.cur, 1)
            E.tokmap[g] = (E.cur, E.cnt)
            self.nsig += 1
        tok = ("E", E, g)
        self._commit(tok, reads, writes)
        self.ninst += 1
        return tok

    def dma(self, Q, ds, out, in_, reads=(), writes=(), **kw):
        self._deps(Q, reads, writes)
        if ds.cnt:
            self._wait(Q, ("D", ds, ds.cnt))
        inst = Q.eng.dma_start(out=out, in_=in_, **kw)
        ds.cnt += 16
        inst.then_inc(ds.sem, 16)
        tok = ("D", ds, ds.cnt)
        self._commit(tok, reads, writes)
        self.ninst += 1
        return tok

    def drain(self):
        for E in (self.pe, self.act, self.dve, self.pool):
            if E.gidx:
                self._wait(self.sp, ("E", E, E.gidx))
        for d in self.dsems:
            if d.cnt:
                self._wait(self.sp, ("D", d, d.cnt))

    def barrier(self):
        engs = (self.pe, self.act, self.dve, self.pool, self.sp)
        for E in engs:
            for E2 in engs:
                if E2 is not E and E2.gidx:
                    self._wait(E, ("E", E2, E2.gidx))
            for d in self.dsems:
                if d.cnt:
                    self._wait(E, ("D", d, d.cnt))

    def finish(self, bufs):
        for b in bufs:
            if b.lw is not None:
                self._wait(self.sp, b.lw)


class Rot:
    def __init__(self, k, es, nc, name, n, shape, dt, with_dsem=True):
        self.n = n
        self.i = 0
        self.slots = []
        for j in range(n):
            t = es.enter_context(nc.sbuf_tensor(f"{name}{j}", shape, dt))
            self.slots.append((t, Buf(f"{name}{j}"), k.dsem(name) if with_dsem else None))

    def next(self):
        s = self.slots[self.i % self.n]
        self.i += 1
        return s


class _Stop(Exception):
    pass


def build(debug=False, upto=99, needed=None, record=None):
    nc = bass.Bass("TRN2", target_bir_lowering=False)

    def din(name, shape, dt=F32):
        return nc.dram_tensor(name, list(shape), dt, kind="ExternalInput").ap()

    def dscr(name, shape, dt):
        return nc.dram_tensor(name, list(shape), dt, kind="Internal").ap()

    x_d = din("x", [S, D])
    c_d = din("c", [128, 8])
    adaw_d = din("ada_w", [D, 6 * D])
    adab_d = din("ada_b", [1, 6 * D])
    gpre1_d = din("g_pre1", [128, 8])
    gpre2_d = din("g_pre2", [128, 8])
    gpost1_d = din("g_post1", [128, D])
    gpost2_d = din("g_post2", [128, D])
    win_d = din("w_in", [D, DIN])
    convw_d = din("conv_w", [128, 4, 4])
    convb_d = din("conv_b", [128, 4])
    wa_d = din("lru_wa", [8, 64, 64])
    wx_d = din("lru_wx", [8, 64, 64])
    ba_d = din("lru_ba", [128, 4])
    bx_d = din("lru_bx", [128, 4])
    lam_d = din("lru_lam", [128, 4])
    posk_d = din("pos_k", [64, 32])
    posv_d = din("pos_v", [64, 32])
    w1k_d = din("w1_k", [2048, 256])
    w1v_d = din("w1_v", [2048, 256])
    w2k_d = din("w2_k", [256, 64])
    w2v_d = din("w2_v", [256, 64])
    grnn_d = din("g_rnn", [128, 4])
    gatt_d = din("g_att", [128, 4])
    wout_d = din("w_out", [D, D])
    wff1_d = din("w_ff1", [D, 4 * D])
    wff2_d = din("w_ff2", [4 * D, D])
    ident_d = din("k_ident", [128, 128], BF16)
    ce01_d = din("k_ce01", [64, 256], BF16)
    cllo_d = din("k_cllo", [128, 2, 8])
    cmask_d = din("k_cmask", [128, 2, S], BF16)
    e01_d = din("k_e01", [64, S], BF16)
    tab_d = din("k_tab", [64, S], BF16)
    sllo_d = din("k_sllo", [128, 8])
    wm01_d = din("k_wm01", [128, 8, 512], BF16)
    addm_d = din("k_addm", [128, NT, 64])
    ovl_d = din("k_ovl", [128, 2, 63], BF16)
    out_d = nc.dram_tensor("out", [S, D], F32, kind="ExternalOutput").ap()
    zr_d = dscr("zr_s", [1024, S], F32)
    q_d = dscr("q_s", [512, S], BF16)
    kc_d = dscr("kc_s", [128, S], BF16)
    vc_d = dscr("vc_s", [128, S], BF16)
    ks_d = dscr("ks_s", [128, S], BF16)
    kw_d = dscr("kw_s", [128, S], BF16)
    rnn_d = dscr("rnn_s", [512, S], BF16)
    x1_d = dscr("x1_s", [S, D], F32)
    dbg = {}
    if debug:
        for nm, shp in [("d_mod", [1, 6 * D]), ("d_att", [S, 512]), ("d_y", [S, D])]:
            dbg[nm] = nc.dram_tensor(nm, shp, F32, kind="ExternalOutput").ap()

    with ExitStack() as es:
        k = K(nc, es, needed=needed, record=record)
        PE, ACT, DVE, POOL, SP = k.pe, k.act, k.dve, k.pool, k.sp

        def sb(name, shape, dt, stack=es):
            return stack.enter_context(nc.sbuf_tensor(name, list(shape), dt))

        def pst(name, shape, dt, stack=es):
            return stack.enter_context(nc.psum_tensor(name, list(shape), dt))

        dbuf = {}

        def DB(key):
            if key not in dbuf:
                dbuf[key] = Buf(str(key))
            return dbuf[key]

        ident = sb("ident", [128, 128], BF16)
        ident_b = Buf("ident")
        ld0 = k.dsem("ld0")
        k.dma(SP, ld0, ident[:], ident_d, writes=[ident_b])
        ones_bf = sb("ones_bf", [128, 128], BF16)
        ones_b = Buf("ones")
        k.op(DVE, lambda: nc.vector.memset(ones_bf[:], 1.0), writes=[ones_b])
        A1 = sb("A1", [128, 8], F32); B1 = sb("B1", [128, 8], F32)
        A2 = sb("A2", [128, 8], F32); B2 = sb("B2", [128, 8], F32)
        C1row = sb("C1row", [128, D], F32); C2row = sb("C2row", [128, D], F32)
        A1_b, B1_b, A2_b, B2_b, C1_b, C2_b = (Buf(n) for n in ("A1", "B1", "A2", "B2", "C1", "C2"))
        mid = es.enter_context(ExitStack())
        vs_aug = sb("vs_aug", [128, NT, 2, 65], BF16, mid)
        vw_aug = sb("vw_aug", [128, NT, 2, 65], BF16, mid)
        gates = sb("gates", [128, NT, 24], F32, mid)
        vtok_b = [Buf(f"vtok{t}") for t in range(NT)]
        vones_b = Buf("vones")
        k.op(DVE, lambda: nc.vector.memset(vs_aug[:, :, :, 64:65], 1.0), writes=[vones_b])
        k.op(DVE, lambda: nc.vector.memset(vw_aug[:, :, :, 64:65], 1.0), writes=[vones_b])
        ssr = sb("ssr", [128, NT], F32, mid)
        ssq = sb("ssq", [128, NT], F32, mid)
        ssr_b = Buf("ssr")

        with ExitStack() as p0:
            csb = sb("csb", [128, 8], F32, p0)
            scb = sb("scb", [128, 8], BF16, p0)
            c_b = Buf("c")
            k.dma(SP, k.dsem("c"), csb[:], c_d, writes=[c_b])
            k.op(ACT, lambda: nc.scalar.activation(out=scb[:], in_=csb[:], func=AF.Silu), reads=[c_b], writes=[c_b])
            adab = sb("adab", [1, 6 * D], F32, p0)
            adab_b = Buf("adab")
            k.dma(SP, k.dsem("adab"), adab[:], adab_d, writes=[adab_b])
            modrow = sb("modrow", [1, 6 * D], F32, p0)
            modrow_b = Buf("modrow")
            modcol = sb("modcol", [128, 48], F32, p0)
            modcol_b = Buf("modcol")
            gp1 = sb("gp1", [128, 8], F32, p0); gp2 = sb("gp2", [128, 8], F32, p0)
            gq1 = sb("gq1", [128, D], F32, p0); gq2 = sb("gq2", [128, D], F32, p0)
            g_b = Buf("gvecs")
            k.dma(SP, ld0, gp1[:], gpre1_d, writes=[g_b])
            k.dma(SP, ld0, gp2[:], gpre2_d, writes=[g_b])
            k.dma(SP, ld0, gq1[:], gpost1_d, writes=[g_b])
            k.dma(SP, ld0, gq2[:], gpost2_d, writes=[g_b])
            adaw = Rot(k, p0, nc, "adaw", 2, [128, 8, 512], BF16)
            ps_row = pst("ps_row", [128, 512], F32, p0)
            ps_row_b = Buf("ps_row")
            ps_col = pst("ps_col", [128, 512], F32, p0)
            ps_col_b = Buf("ps_col")
            one11 = sb("one11", [1, 128], F32, p0)
            one11_b = Buf("one11")
            k.op(DVE, lambda: nc.vector.memset(one11[:], 1.0), writes=[one11_b])
            for pc in range(12):
                t, tb, ds = adaw.next()
                k.dma(POOL, ds, t[:], adaw_d[:, pc * 512:(pc + 1) * 512].rearrange("(k p) n -> p k n", p=128), writes=[tb])
                for kk in range(8):
                    k.op(PE, lambda kk=kk, t=t: nc.tensor.matmul(ps_row[0:1, :], lhsT=scb[:, kk:kk + 1], rhs=t[:, kk, :],
                                                                start=(kk == 0), stop=(kk == 7)),
                         reads=[tb, c_b], writes=[ps_row_b])
                k.op(DVE, lambda pc=pc: nc.vector.tensor_tensor(out=modrow[0:1, pc * 512:(pc + 1) * 512], in0=ps_row[0:1, :],
                                                                in1=adab[0:1, pc * 512:(pc + 1) * 512], op=ALU.add),
                     reads=[ps_row_b, adab_b], writes=[modrow_b])
            if debug:
                k.dma(SP, ld0, dbg["d_mod"], modrow[:], reads=[modrow_b], writes=[DB("d_mod")])
            for j in range(48):
                k.op(PE, lambda j=j: nc.tensor.matmul(ps_col[:, j:j + 1], lhsT=modrow[0:1, j * 128:(j + 1) * 128], rhs=one11[0:1, 0:1],
                                                      start=True, stop=True),
                     reads=[modrow_b, one11_b], writes=[ps_col_b])
            k.op(DVE, lambda: nc.vector.tensor_copy(out=modcol[:], in_=ps_col[:, 0:48]), reads=[ps_col_b], writes=[modcol_b])
            k.op(DVE, lambda: nc.vector.scalar_tensor_tensor(out=A1[:], in0=modcol[:, 8:16], scalar=1.0, in1=gp1[:], op0=ALU.add, op1=ALU.mult),
                 reads=[modcol_b, g_b], writes=[A1_b])
            k.op(DVE, lambda: nc.vector.tensor_copy(out=B1[:], in_=modcol[:, 0:8]), reads=[modcol_b], writes=[B1_b])
            k.op(DVE, lambda: nc.vector.scalar_tensor_tensor(out=A2[:], in0=modcol[:, 32:40], scalar=1.0, in1=gp2[:], op0=ALU.add, op1=ALU.mult),
                 reads=[modcol_b, g_b], writes=[A2_b])
            k.op(DVE, lambda: nc.vector.tensor_copy(out=B2[:], in_=modcol[:, 24:32]), reads=[modcol_b], writes=[B2_b])
            for (base, crow, cb, gq) in ((2048, C1row, C1_b, gq1), (5120, C2row, C2_b, gq2)):
                for hh in range(2):
                    k.op(PE, lambda base=base, hh=hh: nc.tensor.matmul(ps_row[:, :], lhsT=one11[0:1, :],
                                                                      rhs=modrow[0:1, base + hh * 512: base + (hh + 1) * 512],
                                                                      start=True, stop=True),
                         reads=[modrow_b, one11_b], writes=[ps_row_b])
                    k.op(DVE, lambda hh=hh, crow=crow, gq=gq: nc.vector.scalar_tensor_tensor(
                        out=crow[:, hh * 512:(hh + 1) * 512], in0=ps_row[:, :], scalar=1.0, in1=gq[:, hh * 512:(hh + 1) * 512],
                        op0=ALU.add, op1=ALU.mult), reads=[ps_row_b, g_b], writes=[cb])

        if upto <= 0:
            k.drain()
            return nc
        k.barrier()
        with ExitStack() as p1:
            hT = sb("hT", [128, 8, S], BF16, p1)
            hT_b = [[Buf(f"hT{t}_{j}") for j in range(8)] for t in range(NT)]
            win = sb("win", [128, 8, DIN], BF16, p1)
            win_b = Buf("win")
            wds = k.dsem("win")
            k.dma(POOL, wds, win[:], win_d.rearrange("(k p) n -> p k n", p=128), writes=[win_b])
            xt = Rot(k, p1, nc, "xt", 3, [128, D], F32)
            junk = sb("junk", [128, D], BF16, p1)
            junk_b = Buf("junk")
            xn = Rot(k, p1, nc, "xn", 2, [128, D], BF16, with_dsem=False)
            pT = [pst(f"pT{i}", [128, 1024], BF16, p1) for i in range(2)]
            pT_b = [Buf(f"pT{i}") for i in range(2)]
            sm1 = sb("sm1", [128, NT * 4], F32, p1)
            sm1_b = [Buf(f"sm1_{t}") for t in range(NT)]
            pz = [pst(f"pz{i}", [128, 512], F32, p1) for i in range(4)]
            pz_b = [Buf(f"pz{i}") for i in range(4)]
            st32 = Rot(k, p1, nc, "st32", 3, [128, 512], F32)
            st16 = Rot(k, p1, nc, "st16", 3, [128, 512], BF16)
            zi = 0
            fm_tiles = []
            for ct in range(8):
                fm_tiles.append((ct * 128, zr_d, ct * 128, "f32", 1.0))
            for ct in range(4):
                fm_tiles.append((1024 + ct * 128, q_d, ct * 128, "bf", 0.125))
            fm_tiles.append((1536, kc_d, 0, "bf", 1.0))
            fm_tiles.append((1664, vc_d, 0, "bf", 1.0))
            fm_tiles.append((1792, ks_d, 0, "bf", 1.0))
            fm_tiles.append((2048, kw_d, 0, "bf", 1.0))

            def norm_a(tt):
                t, tb, ds = xt.next()
                k.dma(SP, ds, t[:], x_d[tt * 128:(tt + 1) * 128, :], writes=[tb])
                s4 = sm1[:, tt * 4:tt * 4 + 4]; s4b = sm1_b[tt]
                k.op(ACT, lambda: nc.scalar.activation(out=junk[:], in_=t[:], func=AF.Square, accum_out=s4[:, 0:1]), reads=[tb], writes=[junk_b, s4b])
                k.op(ACT, lambda: nc.scalar.activation(out=s4[:, 1:2], in_=s4[:, 0:1], func=AF.Ln, scale=1.0 / D, bias=EPS), reads=[s4b], writes=[s4b])
                k.op(ACT, lambda: nc.scalar.activation(out=s4[:, 2:3], in_=s4[:, 1:2], func=AF.Exp, scale=-0.5), reads=[s4b], writes=[s4b])
                n, nb, _ = xn.next()
                k.op(DVE, lambda: nc.vector.tensor_scalar(out=n[:], in0=t[:], scalar1=s4[:, 2:3], scalar2=None, op0=ALU.mult), reads=[tb, s4b], writes=[nb])
                return tt, n, nb

            def norm_b(tt, n, nb):
                pp = pT[tt % 2]; ppb = pT_b[tt % 2]
                for j in range(8):
                    k.op(PE, lambda j=j: nc.tensor.transpose(out=pp[:, j * 128:(j + 1) * 128], in_=n[:, j * 128:(j + 1) * 128], identity=ident[:]),
                         reads=[nb, ident_b], writes=[ppb])
                for j in range(8):
                    if tt % 2 == 0:
                        k.op(DVE, lambda j=j: nc.vector.tensor_scalar(out=hT[:, j, tt * 128:(tt + 1) * 128], in0=pp[:, j * 128:(j + 1) * 128],
                                                                      scalar1=A1[:, j:j + 1], scalar2=B1[:, j:j + 1], op0=ALU.mult, op1=ALU.add),
                             reads=[ppb, A1_b, B1_b], writes=[hT_b[tt][j]])
                    else:
                        k.op(ACT, lambda j=j: nc.scalar.activation(out=hT[:, j, tt * 128:(tt + 1) * 128], in_=pp[:, j * 128:(j + 1) * 128],
                                                                   func=AF.Identity, scale=A1[:, j:j + 1], bias=B1[:, j:j + 1]),
                             reads=[ppb, A1_b, B1_b], writes=[hT_b[tt][j]])

            npend = [norm_a(0)]

            def norm_tile(_tt_unused=None):
                tt, n, nb = npend[0]
                norm_b(tt, n, nb)
                if tt + 1 < NT:
                    npend[0] = norm_a(tt + 1)

            for tl in range(4):
                norm_tile(tl)
            for ch in range(NCH):
                for ti, (c0, dst, r0, kind, scl) in enumerate(fm_tiles):
                    if ch + 1 < NCH and ti in (2, 6, 10, 14):
                        norm_tile((ch + 1) * 4 + (ti - 2) // 4)
                    pzz = pz[zi % 4]; pzb = pz_b[zi % 4]
                    for kk in range(8):
                        k.op(PE, lambda kk=kk, c0=c0, ch=ch, pzz=pzz: nc.tensor.matmul(pzz[:, :], lhsT=win[:, kk, c0:c0 + 128], rhs=hT[:, kk, ch * 512:(ch + 1) * 512],
                                                                                      start=(kk == 0), stop=(kk == 7)),
                             reads=[win_b] + [hT_b[t4][kk] for t4 in range(ch * 4, ch * 4 + 4)], writes=[pzb])
                    t, tb, ds = (st32 if kind == "f32" else st16).next()
                    if zi % 2 == 0:
                        k.op(ACT, lambda t=t, pzz=pzz, scl=scl: nc.scalar.activation(out=t[:], in_=pzz[:, :], func=AF.Copy, scale=scl), reads=[pzb], writes=[tb])
                    else:
                        k.op(DVE, lambda t=t, pzz=pzz, scl=scl: nc.vector.tensor_scalar(out=t[:], in0=pzz[:, :], scalar1=scl, scalar2=None, op0=ALU.mult),
                             reads=[pzb], writes=[tb])
                    k.dma(POOL, ds, dst[r0:r0 + 128, ch * 512:(ch + 1) * 512], t[:], reads=[tb], writes=[DB((id(dst), r0, ch))])
                    zi += 1
                for tl in range(4):
                    tt = ch * 4 + tl
                    pzz = pz[zi % 4]; pzb = pz_b[zi % 4]
                    use_dve = (zi % 2 == 1)
                    zi += 1
                    for (c0, n, o0) in ((1920, 128, 0), (2176, 152, 128)):
                        for kk in range(8):
                            k.op(PE, lambda kk=kk, c0=c0, n=n, o0=o0, tt=tt, pzz=pzz: nc.tensor.matmul(
                                pzz[:, o0:o0 + n], lhsT=hT[:, kk, tt * 128:(tt + 1) * 128], rhs=win[:, kk, c0:c0 + n],
                                start=(kk == 0 and o0 == 0), stop=(kk == 7 and o0 == 128)),
                                reads=[win_b, hT_b[tt][kk]], writes=[pzb])
                    if use_dve:
                        k.op(DVE, lambda tt=tt, pzz=pzz: nc.vector.tensor_copy(out=vs_aug[:, tt, :, 0:64], in_=pzz[:, 0:128].rearrange("p (g d) -> p g d", g=2)), reads=[pzb], writes=[vtok_b[tt]])
                        k.op(DVE, lambda tt=tt, pzz=pzz: nc.vector.tensor_copy(out=vw_aug[:, tt, :, 0:64], in_=pzz[:, 128:256].rearrange("p (g d) -> p g d", g=2)), reads=[pzb], writes=[vtok_b[tt]])
                        k.op(DVE, lambda tt=tt, pzz=pzz: nc.vector.tensor_copy(out=gates[:, tt, :], in_=pzz[:, 256:280]), reads=[pzb], writes=[vtok_b[tt]])
                    else:
                        k.op(ACT, lambda tt=tt, pzz=pzz: nc.scalar.activation(out=vs_aug[:, tt, :, 0:64], in_=pzz[:, 0:128].rearrange("p (g d) -> p g d", g=2), func=AF.Copy), reads=[pzb], writes=[vtok_b[tt]])
                        k.op(ACT, lambda tt=tt, pzz=pzz: nc.scalar.activation(out=vw_aug[:, tt, :, 0:64], in_=pzz[:, 128:256].rearrange("p (g d) -> p g d", g=2), func=AF.Copy), reads=[pzb], writes=[vtok_b[tt]])
                        k.op(ACT, lambda tt=tt, pzz=pzz: nc.scalar.activation(out=gates[:, tt, :], in_=pzz[:, 256:280], func=AF.Copy), reads=[pzb], writes=[vtok_b[tt]])
            k.op(ACT, lambda: nc.scalar.activation(out=gates[:], in_=gates[:], func=AF.Sigmoid), reads=vtok_b, writes=vtok_b)

        if upto <= 1:
            k.drain()
            return nc
        k.barrier()
        with ExitStack() as p2:
            cw = sb("cw", [128, 4, 4], F32, p2); cb_ = sb("cb", [128, 4], F32, p2)
            bat = sb("bat", [128, 4], F32, p2); bxt = sb("bxt", [128, 4], F32, p2)
            lam = sb("lam", [128, 4], F32, p2); cc1 = sb("cc1", [128, 4], F32, p2); cc2 = sb("cc2", [128, 4], F32, p2)
            prm_b = Buf("rnnprm")
            pds = k.dsem("rp")
            for (t, d) in ((cw, convw_d), (cb_, convb_d), (bat, ba_d), (bxt, bx_d), (lam, lam_d)):
                k.dma(SP, pds, t[:], d, writes=[prm_b])
            k.op(ACT, lambda: nc.scalar.activation(out=cc1[:], in_=lam[:], func=AF.Exp, scale=-1.0), reads=[prm_b], writes=[prm_b])
            k.op(ACT, lambda: nc.scalar.activation(out=cc1[:], in_=cc1[:], func=AF.Ln, bias=1.0), reads=[prm_b], writes=[prm_b])
            k.op(DVE, lambda: nc.vector.tensor_scalar(out=cc2[:], in0=cc1[:], scalar1=-16.0, scalar2=None, op0=ALU.mult), reads=[prm_b], writes=[prm_b])
            k.op(DVE, lambda: nc.vector.tensor_scalar(out=cc1[:], in0=cc1[:], scalar1=-8.0, scalar2=None, op0=ALU.mult), reads=[prm_b], writes=[prm_b])
            wbd_a = sb("wbd_a", [128, 4, 128], BF16, p2); wbd_x = sb("wbd_x", [128, 4, 128], BF16, p2)
            wbd_b = Buf("wbd")
            k.op(DVE, lambda: nc.vector.memset(wbd_a[:], 0.0), writes=[wbd_b])
            k.op(DVE, lambda: nc.vector.memset(wbd_x[:], 0.0), writes=[wbd_b])
            for (t, d) in ((wbd_a, wa_d), (wbd_x, wx_d)):
                dv = d.rearrange("(f two) i j -> two i f j", two=2)
                for hh in range(2):
                    k.dma(POOL, k.dsem("wbd"), t[hh * 64:(hh + 1) * 64, :, hh * 64:(hh + 1) * 64], dv[hh], writes=[wbd_b])
            NP = 4
            PW = S // NP
            xrR = Rot(k, p2, nc, "xr", 2, [128, S], F32)
            ggR = Rot(k, p2, nc, "gg", 2, [128, S], F32)
            xc = sb("xc", [128, S], F32, p2); xcb = sb("xcb", [128, S], BF16, p2)
            rr = sb("rr", [128, S], F32, p2); ii = sb("ii", [128, S], F32, p2); a2 = sb("a2", [128, S], F32, p2)
            hh_ = sb("hh", [128, S], F32, p2)
            rnb = sb("rnb", [128, S], BF16, p2); sqb = sb("sqb", [128, S], BF16, p2)
            xc_b, xcb_b, rr_b, ii_b, a2_b, hh_b, rnb_b, sqb_b = ([Buf(f"{n}{p}") for p in range(NP)] for n in ("xc", "xcb", "rr", "ii", "a2", "hh", "rnb", "sqb"))
            rn_ds = [k.dsem("rn") for _ in range(2)]
            pg = [pst(f"pg{i}", [128, 512], F32, p2) for i in range(3)]
            pg_b = [Buf(f"pg{i}") for i in range(3)]
            pstat = pst("pstat", [128, 512], F32, p2)
            pstat_b = Buf("pstat")
            gi = [0]
            gg_parts = [[Buf(f"ggs{sl}_{p}") for p in range(NP)] for sl in range(2)]

            def load_ft(ft):
                xr, xr_b, xr_ds = xrR.next()
                gg, _, gg_ds = ggR.next()
                gg_bp = gg_parts[ft % 2]
                k.dma(SP, xr_ds, xr[:], zr_d[512 + ft * 128:512 + (ft + 1) * 128, :], reads=[DB((id(zr_d), 512 + ft * 128, ch)) for ch in range(NCH)], writes=[xr_b])
                k.dma(SP, gg_ds, gg[:], zr_d[ft * 128:(ft + 1) * 128, :], reads=[DB((id(zr_d), ft * 128, ch)) for ch in range(NCH)], writes=gg_bp)
                return xr, xr_b, gg, gg_bp

            def cs(p):
                return p * PW, (p + 1) * PW

            def front(ft, xr, xr_b, gg, gg_b):
                for p in range(NP):
                    c0, c1 = cs(p)
                    k.op(DVE, lambda c0=c0, c1=c1: nc.vector.tensor_scalar(out=xc[:, c0:c1], in0=xr[:, c0:c1], scalar1=cw[:, ft, 3:4], scalar2=cb_[:, ft:ft + 1], op0=ALU.mult, op1=ALU.add),
                         reads=[xr_b, prm_b], writes=[xc_b[p]])
                    for sh in (1, 2, 3):
                        lo = max(c0, sh)
                        k.op(DVE, lambda lo=lo, c1=c1, sh=sh: nc.vector.scalar_tensor_tensor(out=xc[:, lo:c1], in0=xr[:, lo - sh:c1 - sh], scalar=cw[:, ft, 3 - sh:4 - sh],
                                                                                            in1=xc[:, lo:c1], op0=ALU.mult, op1=ALU.add),
                             reads=[xr_b, prm_b, xc_b[p]], writes=[xc_b[p]])
                    k.op(ACT, lambda c0=c0, c1=c1: nc.scalar.activation(out=xcb[:, c0:c1], in_=xc[:, c0:c1], func=AF.Copy), reads=[xc_b[p]], writes=[xcb_b[p]])
                    k.op(ACT, lambda c0=c0, c1=c1: nc.scalar.activation(out=gg[:, c0:c1], in_=gg[:, c0:c1], func=AF.Gelu_apprx_tanh), reads=[gg_b[p]], writes=[gg_b[p]])

            def rnn_gates(ft):
                for p in range(NP):
                    c0, c1 = cs(p)
                    for (wt, bt, dst, dstb) in ((wbd_a, bat, rr, rr_b), (wbd_x, bxt, ii, ii_b)):
                        for cc in range(c0, c1, 512):
                            pgg = pg[gi[0] % 3]; pgb = pg_b[gi[0] % 3]; gi[0] += 1
                            k.op(PE, lambda wt=wt, cc=cc, pgg=pgg: nc.tensor.matmul(pgg[:, :], lhsT=wt[:, ft, :], rhs=xcb[:, cc:cc + 512], start=True, stop=True),
                                 reads=[wbd_b, xcb_b[p]], writes=[pgb])
                            k.op(ACT, lambda bt=bt, cc=cc, pgg=pgg, dst=dst: nc.scalar.activation(out=dst[:, cc:cc + 512], in_=pgg[:, :], func=AF.Sigmoid, bias=bt[:, ft:ft + 1]),
                                 reads=[pgb, prm_b], writes=[dstb[p]])

            def exps_m(ft):
                for p in range(NP):
                    c0, c1 = cs(p)
                    k.op(ACT, lambda c0=c0, c1=c1: nc.scalar.activation(out=a2[:, c0:c1], in_=rr[:, c0:c1], func=AF.Exp, scale=cc2[:, ft:ft + 1]), reads=[rr_b[p], prm_b], writes=[a2_b[p]])
                    k.op(ACT, lambda c0=c0, c1=c1: nc.scalar.activation(out=rr[:, c0:c1], in_=rr[:, c0:c1], func=AF.Exp, scale=cc1[:, ft:ft + 1]), reads=[rr_b[p], prm_b], writes=[rr_b[p]])
                    k.op(DVE, lambda c0=c0, c1=c1: nc.vector.tensor_tensor(out=ii[:, c0:c1], in0=ii[:, c0:c1], in1=xc[:, c0:c1], op=ALU.mult), reads=[ii_b[p], xc_b[p]], writes=[ii_b[p]])

            def sqrts(ft):
                for p in range(NP):
                    c0, c1 = cs(p)
                    k.op(ACT, lambda c0=c0, c1=c1: nc.scalar.activation(out=a2[:, c0:c1], in_=a2[:, c0:c1], func=AF.Sqrt, scale=-1.0, bias=1.0), reads=[a2_b[p]], writes=[a2_b[p]])

            def tail(ft, gg, gg_b):
                for p in range(NP):
                    c0, c1 = cs(p)
                    k.op(DVE, lambda c0=c0, c1=c1: nc.vector.tensor_tensor(out=a2[:, c0:c1], in0=a2[:, c0:c1], in1=ii[:, c0:c1], op=ALU.mult), reads=[a2_b[p], ii_b[p]], writes=[a2_b[p]])
                    init = 0.0 if p == 0 else hh_[:, c0 - 1:c0]
                    k.op(DVE, lambda c0=c0, c1=c1, init=init: nc.vector.tensor_tensor_scan(out=hh_[:, c0:c1], data0=rr[:, c0:c1], data1=a2[:, c0:c1], initial=init, op0=ALU.mult, op1=ALU.add),
                         reads=[rr_b[p], a2_b[p]] + ([hh_b[p - 1]] if p else []), writes=[hh_b[p]])
                    k.op(DVE, lambda c0=c0, c1=c1: nc.vector.tensor_tensor(out=rnb[:, c0:c1], in0=gg[:, c0:c1], in1=hh_[:, c0:c1], op=ALU.mult), reads=[gg_b[p], hh_b[p]], writes=[rnb_b[p]])
                    k.op(DVE, lambda c0=c0, c1=c1: nc.vector.tensor_tensor(out=sqb[:, c0:c1], in0=rnb[:, c0:c1], in1=rnb[:, c0:c1], op=ALU.mult), reads=[rnb_b[p]], writes=[sqb_b[p]])
                    for cc in range(c0, c1, 512):
                        ch = cc // 512
                        k.dma(POOL, rn_ds[ch % 2], rnn_d[ft * 128:(ft + 1) * 128, cc:cc + 512], rnb[:, cc:cc + 512], reads=[rnb_b[p]], writes=[DB(("rnn", ch))])
                    for tt in range(c0 // 128, c1 // 128):
                        k.op(PE, lambda tt=tt: nc.tensor.matmul(pstat[:, tt * 4 + ft: tt * 4 + ft + 1], lhsT=sqb[:, tt * 128:(tt + 1) * 128], rhs=ones_bf[:, 0:1], start=True, stop=True),
                             reads=[sqb_b[p], ones_b], writes=[pstat_b])

            cur_ft = load_ft(0)
            front(0, *cur_ft)
            for ft in range(4):
                xr, xr_b, gg, gg_b = cur_ft
                if ft + 1 < 4:
                    nxt_ft = load_ft(ft + 1)
                rnn_gates(ft)
                exps_m(ft)
                sqrts(ft)
                if ft + 1 < 4:
                    front(ft + 1, *nxt_ft)
                tail(ft, gg, gg_b)
                if ft + 1 < 4:
                    cur_ft = nxt_ft
            k.op(DVE, lambda: nc.vector.tensor_reduce(out=ssr[:], in_=pstat[:, 0:NT * 4].rearrange("p (t f) -> p t f", f=4), axis=mybir.AxisListType.X, op=ALU.add),
                 reads=[pstat_b], writes=[ssr_b])
            k.op(ACT, lambda: nc.scalar.activation(out=ssq[:], in_=ssr[:], func=AF.Sqrt, scale=1.0 / 512, bias=EPS), reads=[ssr_b], writes=[ssr_b])
            k.op(DVE, lambda: nc.vector.reciprocal(out=ssr[:], in_=ssq[:]), reads=[ssr_b], writes=[ssr_b])

        if upto <= 2:
            k.drain()
            return nc
        k.barrier()
        with ExitStack() as p4:
            kcT = [sb(f"kcT{g}", [128, 256], BF16, p4) for g in range(2)]
            cllo = sb("cllo", [128, 2, 8], F32, p4)
            vc_aug = [sb(f"vca{g}", [128, 2, 128], BF16, p4) for g in range(2)]
            cmp_b = Buf("cmp")
            cds = k.dsem("cmpc")
            k.dma(SP, cds, cllo[:], cllo_d, writes=[cmp_b])
            for g in range(2):
                k.dma(SP, cds, kcT[g][64:128, :], ce01_d, writes=[cmp_b])
                k.dma(SP, cds, vc_aug[g][:, :, 65:128], ovl_d, writes=[cmp_b])
                k.op(DVE, lambda g=g: nc.vector.memset(vc_aug[g][:, :, 64:65], 1.0), writes=[cmp_b])
                k.op(DVE, lambda g=g: nc.vector.memset(vc_aug[g][:, :, 0:64], 0.0), writes=[cmp_b])
            kTs = [sb(f"kTs{g}", [128, S], BF16, p4) for g in range(2)]
            kTw = [sb(f"kTw{g}", [128, S], BF16, p4) for g in range(2)]
            kds_shared = k.dsem("kT")
            cds_shared = k.dsem("cst")
            kTs_b = [Buf(f"kTs{g}") for g in range(2)]
            kTw_b = [Buf(f"kTw{g}") for g in range(2)]
            cmask = sb("cmask", [128, 2, S], BF16, p4)
            tab = sb("tab", [128, S], BF16, p4)
            sllo = sb("sllo", [128, 8], F32, p4)
            wm01 = sb("wm01", [128, 8, 512], BF16, p4)
            addm = sb("addm", [128, NT, 64], F32, p4)
            cmask_b, tab_b, sllo_b, wm01_b, addm_b = (Buf(n) for n in ("cmask", "tab", "sllo", "wm01", "addm"))
            wout = sb("wout", [128, 8, D], BF16, p4)
            wout_b = Buf("wout")
            gcol = sb("gcol", [128, 8], F32, p4)
            gcol_b = Buf("gcol")
            gds = k.dsem("gcol")
            pw = ExitStack()
            wst = Rot(k, pw, nc, "wst", 2, [128, D], F32)

            def issue_resident_loads(dep):
                for g in range(2):
                    for (t, tb_, src) in ((kTs[g], kTs_b[g], ks_d), (kTw[g], kTw_b[g], kw_d)):
                        k.dma(SP, kds_shared, t[0:64, :], src[g * 64:(g + 1) * 64, :], reads=[DB((id(src), 0, ch)) for ch in range(NCH)] + dep, writes=[tb_])
                        k.dma(SP, kds_shared, t[64:128, :], e01_d, writes=[tb_])
                k.dma(SP, cds_shared, cmask[:], cmask_d, writes=[cmask_b])
                k.dma(SP, cds_shared, tab[64:128, :], tab_d, writes=[tab_b])
                k.dma(SP, cds_shared, sllo[:], sllo_d, writes=[sllo_b])
                k.dma(SP, cds_shared, wm01[:], wm01_d, writes=[wm01_b])
                k.dma(SP, cds_shared, addm[:], addm_d, writes=[addm_b])
                k.dma(SP, gds, gcol[:, 0:4], grnn_d, writes=[gcol_b])
                k.dma(SP, gds, gcol[:, 4:8], gatt_d, writes=[gcol_b])
                for kk in range(8):
                    t, tb, ds = wst.next()
                    k.dma(SP, ds, t[:], wout_d[kk * 128:(kk + 1) * 128, :], writes=[tb])
                    k.op(DVE, lambda kk=kk, t=t: nc.vector.tensor_scalar(out=wout[:, kk, :], in0=t[:], scalar1=gcol[:, kk:kk + 1], scalar2=None, op0=ALU.mult),
                         reads=[tb, gcol_b], writes=[wout_b])

            with ExitStack() as p3:
                w1 = [sb(f"w1_{i}", [64, 32, 256], BF16, p3) for i in range(2)]
                w2 = [sb(f"w2_{i}", [128, 2, 64], BF16, p3) for i in range(2)]
                pos = [sb(f"pos_{i}", [64, 32], BF16, p3) for i in range(2)]
                posf = [sb(f"posf_{i}", [64, 32], F32, p3) for i in range(2)]
                cw_b = Buf("cmpw")
                cw1_b = [[Buf(f"cw1_{i}_{q}") for q in range(4)] for i in range(2)]
                cwd = k.dsem("cmpw")
                cwd2 = k.dsem("cmpp")
                for i, (w1d, w2d, pd) in enumerate(((w1k_d, w2k_d, posk_d), (w1v_d, w2v_d, posv_d))):
                    w1v = w1d.rearrange("(l d) h -> d l h", d=64)
                    for l0 in range(0, 32, 8):
                        k.dma(POOL, k.dsem("w1"), w1[i][:, l0:l0 + 8, :], w1v[:, l0:l0 + 8, :], writes=[cw1_b[i][l0 // 8]])
                    k.dma(POOL, cwd, w2[i][:], w2d.rearrange("(t p) d -> p t d", p=128), writes=[cw_b])
                    k.dma(SP, cwd2, posf[i][:], pd, writes=[cw_b])
                    k.op(DVE, lambda i=i: nc.vector.tensor_copy(out=pos[i][:], in_=posf[i][:]), reads=[cw_b], writes=[cw_b])
                cin = [[sb(f"cin{i}{g}", [64, S], BF16, p3) for g in range(2)] for i in range(2)]
                cin_bs = [[Buf(f"cin{i}{g}") for g in range(2)] for i in range(2)]
                for i, src in enumerate((kc_d, vc_d)):
                    cin_ds = k.dsem("cin")
                    for g in range(2):
                        k.dma(SP, cin_ds, cin[i][g][:], src[g * 64:(g + 1) * 64, :], reads=[DB((id(src), 0, ch)) for ch in range(NCH)], writes=[cin_bs[i][g]])
                hid = sb("hid", [128, 2, 256], BF16, p3)
                hid_b = Buf("hid")
                cbias = sb("cbias", [128, 2, 2], F32, p3)
                cbias_b = Buf("cbias")
                pc_ = [pst(f"pc{i}", [128, 512], F32, p3) for i in range(3)]
                pc_b = [Buf(f"pc{i}") for i in range(3)]
                ci = 0
                for i in range(2):
                    for ht in range(2):
                        pcc = pc_[ci % 3]; pcb = pc_b[ci % 3]; ci += 1
                        for l in range(32):
                            k.op(PE, lambda i=i, ht=ht, l=l, pcc=pcc: nc.tensor.matmul(pcc[:, 0:1], lhsT=w1[i][:, l, ht * 128:(ht + 1) * 128], rhs=pos[i][:, l:l + 1],
                                                                                      start=(l == 0), stop=(l == 31)), reads=[cw_b, cw1_b[i][l // 8]], writes=[pcb])
                        k.op(DVE, lambda i=i, ht=ht, pcc=pcc: nc.vector.tensor_copy(out=cbias[:, i, ht:ht + 1], in_=pcc[:, 0:1]), reads=[pcb], writes=[cbias_b])
                for i in range(2):
                    for g in range(2):
                        cv = cin[i][g][:].rearrange("p (c r) -> p c r", r=16)
                        for ht in range(2):
                            pcc = pc_[ci % 3]; pcb = pc_b[ci % 3]; ci += 1
                            for l in range(32):
                                rhs = cv[:, 0:255, l] if l < 16 else cv[:, 1:256, l - 16]
                                k.op(PE, lambda i=i, ht=ht, l=l, pcc=pcc, rhs=rhs: nc.tensor.matmul(pcc[:, 0:255], lhsT=w1[i][:, l, ht * 128:(ht + 1) * 128], rhs=rhs,
                                                                                                   start=(l == 0), stop=(l == 31)), reads=[cw_b, cw1_b[i][l // 8], cin_bs[i][g]], writes=[pcb])
                            k.op(ACT, lambda i=i, ht=ht, pcc=pcc: nc.scalar.activation(out=hid[:, ht, 0:255], in_=pcc[:, 0:255], func=AF.Gelu_apprx_tanh, bias=cbias[:, i, ht:ht + 1]),
                                 reads=[pcb, cbias_b], writes=[hid_b])
                            if i == 0 and g == 0 and ht == 0:
                                issue_resident_loads([hid_b])
                        if i == 0:
                            pcc = pc_[ci % 3]; pcb = pc_b[ci % 3]; ci += 1
                            for ht in range(2):
                                k.op(PE, lambda ht=ht, pcc=pcc: nc.tensor.matmul(pcc[0:64, 0:255], lhsT=w2[0][:, ht, :], rhs=hid[:, ht, 0:255], start=(ht == 0), stop=(ht == 1)),
                                     reads=[cw_b, hid_b], writes=[pcb])
                            k.op(DVE, lambda g=g, pcc=pcc: nc.vector.tensor_copy(out=kcT[g][0:64, 0:255], in_=pcc[0:64, 0:255]), reads=[pcb], writes=[cmp_b])
                        else:
                            for ct in range(2):
                                nr = 128 if ct == 0 else 127
                                pcc = pc_[ci % 3]; pcb = pc_b[ci % 3]; ci += 1
                                for ht in range(2):
                                    k.op(PE, lambda ht=ht, ct=ct, nr=nr, pcc=pcc: nc.tensor.matmul(pcc[0:nr, 0:64], lhsT=hid[:, ht, ct * 128:ct * 128 + nr], rhs=w2[1][:, ht, :],
                                                                                                  start=(ht == 0), stop=(ht == 1)), reads=[cw_b, hid_b], writes=[pcb])
                                k.op(DVE, lambda g=g, ct=ct, nr=nr, pcc=pcc: nc.vector.tensor_copy(out=vc_aug[g][0:nr, ct, 0:64], in_=pcc[0:nr, 0:64]), reads=[pcb], writes=[cmp_b])

            if upto <= 3:
                pw.close()
                k.drain()
                return nc
            k.barrier()
            pw.close()
            qs = Rot(k, p4, nc, "qs", 1, [128, 8, 512], BF16)
            qw = Rot(k, p4, nc, "qw", 2, [128, 8, 512], BF16)
            SLOPES = [2.0 ** (-(h + 1)) for h in range(8)]
            PT = Rot(k, p4, nc, "PT", 6, [128, 512], BF16, with_dsem=False)
            PTc = Rot(k, p4, nc, "PTc", 2, [128, 512], BF16, with_dsem=False)
            att = sb("att", [128, 4, 512], F32, p4)
            att_b = [Buf(f"att{i}") for i in range(4)]
            attb = sb("attb", [128, 4, 512], BF16, p4)
            attb_b = Buf("attb")
            attT = sb("attT", [128, 4, 512], BF16, p4)
            attT_b = Buf("attT")
            rnT = Rot(k, p4, nc, "rnT", 3, [128, 4, 512], BF16)
            imp = sb("imp", [128, 4, 2, 64], F32, p4)
            imp_b = [Buf(f"imp{i}") for i in range(4)]
            for i_ in range(4):
                k.op(DVE, lambda i_=i_: nc.vector.memset(imp[:, i_, :, :], 0.0), writes=[imp_b[i_]])
            selbT = [sb(f"selbT{g}", [128, 512], BF16, p4) for g in range(2)]
            selbT_b = [Buf(f"selbT{g}") for g in range(2)]
            sm = sb("sm", [128, 256], F32, p4)
            sm_b = [Buf(f"sm{i}") for i in range(16)]
            smi = [0]
            tk_a = sb("tk_a", [128, 8, 64], F32, p4); tk_b_ = sb("tk_b", [128, 8, 64], F32, p4)
            tk8 = sb("tk8", [128, 8, 16], F32, p4); tksel = sb("tksel", [128, 8, 64], BF16, p4)
            tk_bufs = [Buf(f"tk{u}") for u in range(8)]
            ssa = sb("ssa", [128, 8], F32, p4)
            ssa_b = Buf("ssa")
            ysb = Rot(k, p4, nc, "ysb", 2, [128, D], F32)
            xres = Rot(k, p4, nc, "xres", 2, [128, D], F32)
            junk4 = sb("junk4", [128, D], BF16, p4)
            junk4_b = Buf("junk4")
            print("P4 sbuf bytes remaining:", nc.sbuf_bytes_remaining)
            k.barrier()
            pS = [pst(f"pS{i}", [128, 512], F32, p4) for i in range(3)]
            pS_b = [Buf(f"pS{i}") for i in range(3)]
            pA = [pst(f"pA{i}", [128, 512], F32, p4) for i in range(2)]
            pA_b = [Buf(f"pA{i}") for i in range(2)]
            pTT = pst("pTT", [128, 1024], BF16, p4)
            pTT_b = Buf("pTT")
            pW = [pst(f"pW{i}", [128, 512], F32, p4) for i in range(1)]
            pW_b = [Buf(f"pW{i}") for i in range(1)]
            pC = pst("pC", [128, 512], F32, p4)
            pC_b = Buf("pC")
            cnt = {"s": 0, "a": 0}

            def nextS():
                i = cnt["s"] % 3; cnt["s"] += 1
                return pS[i], pS_b[i]

            def nextA():
                i = cnt["a"] % 2; cnt["a"] += 1
                return pA[i], pA_b[i]

            def small():
                i = smi[0] % 16; smi[0] += 1
                return sm[:, 16 * i:16 * i + 16], sm_b[i]

            att_written = set()

            def evac_group(pa, pab, stride, n, h, tl0, tt0, gate_idx, first, with_imp=False, g=None, first_in_group=False):
                s4, s4b = small()
                sums = pa[:, 64:64 + (n - 1) * stride + 1:stride]
                k.op(DVE, lambda: nc.vector.tensor_scalar(out=s4[:, 0:n], in0=sums, scalar1=1e-30, scalar2=None, op0=ALU.max), reads=[pab], writes=[s4b])
                k.op(DVE, lambda: nc.vector.reciprocal(out=s4[:, 4:4 + n], in_=s4[:, 0:n]), reads=[s4b], writes=[s4b])
                k.op(DVE, lambda: nc.vector.tensor_tensor(out=s4[:, 8:8 + n], in0=s4[:, 4:4 + n], in1=gates[:, tt0:tt0 + n, gate_idx], op=ALU.mult),
                     reads=[s4b] + [vtok_b[tt0 + i] for i in range(n)], writes=[s4b])
                for i in range(n):
                    tl = tl0 + i
                    dst = att[:, tl, h * 64:(h + 1) * 64]
                    src = pa[:, i * stride:i * stride + 64]
                    first = (h, tl) not in att_written
                    att_written.add((h, tl))
                    if first:
                        k.op(DVE, lambda dst=dst, src=src, i=i: nc.vector.tensor_scalar(out=dst, in0=src, scalar1=s4[:, 8 + i:9 + i], scalar2=None, op0=ALU.mult),
                             reads=[pab, s4b], writes=[att_b[tl]])
                    else:
                        k.op(DVE, lambda dst=dst, src=src, i=i: nc.vector.scalar_tensor_tensor(out=dst, in0=src, scalar=s4[:, 8 + i:9 + i], in1=dst, op0=ALU.mult, op1=ALU.add),
                             reads=[pab, s4b, att_b[tl]], writes=[att_b[tl]])
                    if with_imp:
                        idst = imp[:, tl, g, 1:64]
                        isrc = pa[:, i * stride + 65:i * stride + 128]
                        if first_in_group:
                            k.op(DVE, lambda idst=idst, isrc=isrc, i=i: nc.vector.tensor_scalar(out=idst, in0=isrc, scalar1=s4[:, 4 + i:5 + i], scalar2=None, op0=ALU.mult),
                                 reads=[pab, s4b], writes=[imp_b[tl]])
                        else:
                            k.op(DVE, lambda idst=idst, isrc=isrc, i=i: nc.vector.scalar_tensor_tensor(out=idst, in0=isrc, scalar=s4[:, 4 + i:5 + i], in1=idst, op0=ALU.mult, op1=ALU.add),
                                 reads=[pab, s4b, imp_b[tl]], writes=[imp_b[tl]])

            dbg_ds = k.dsem("dbg")

            qview = q_d.rearrange("(h d) t -> d h t", d=64)

            def load_chunk(jn):
                tn = jn * 512
                rt_, rtb_, rds_ = rnT.next()
                k.dma(SP, rds_, rt_[:], rnn_d[:, tn:tn + 512].rearrange("(f p) t -> p f t", p=128), reads=[DB(("rnn", jn))], writes=[rtb_])
                qw_, qwb_, qwd_ = qw.next()
                k.dma(SP, qwd_, qw_[0:64, :, :], qview[:, :, tn:tn + 512], reads=[DB((id(q_d), c4 * 128, jn)) for c4 in range(4)], writes=[qwb_])
                for h_ in range(8):
                    k.op(DVE, lambda h_=h_, qw_=qw_: nc.vector.tensor_scalar(out=qw_[64:128, h_, :], in0=tab[64:128, tn:tn + 512], scalar1=SLOPES[h_], scalar2=None, op0=ALU.mult),
                         reads=[tab_b], writes=[qwb_])
                return rt_, rtb_, qw_, qwb_

            def load_qs(jn):
                tn = jn * 512
                qs_, qsb_, qsd_ = qs.next()
                k.dma(SP, qsd_, qs_[0:64, :, :], qview[:, :, tn:tn + 512], reads=[DB((id(q_d), c4 * 128, jn)) for c4 in range(4)], writes=[qsb_])
                return qs_, qsb_

            nxt = load_chunk(0)
            nxt_qs = load_qs(0)
            wcnt = [0]

            def nextW():
                return pW[0], pW_b[0]

            def make_cback(jc, rt, rtb):
                units = []

                def u_tr(half):
                    for tl2 in range(2):
                        tl = half * 2 + tl2
                        for f in range(4):
                            k.op(PE, lambda tl=tl, tl2=tl2, f=f: nc.tensor.transpose(out=pTT[:, (tl2 * 4 + f) * 128:(tl2 * 4 + f + 1) * 128], in_=attb[:, tl, f * 128:(f + 1) * 128], identity=ident[:]),
                                 reads=[attb_b, ident_b], writes=[pTT_b])
                    for tl2 in range(2):
                        tl = half * 2 + tl2
                        k.op(DVE, lambda tl=tl, tl2=tl2: nc.vector.tensor_copy(out=attT[:, :, tl * 128:(tl + 1) * 128],
                                                                             in_=pTT[:, tl2 * 512:(tl2 + 1) * 512].rearrange("p (f t) -> p f t", f=4)),
                             reads=[pTT_b], writes=[attT_b])
                units.append(lambda: u_tr(0))
                units.append(lambda: u_tr(1))
                st = {}

                def u_mm(tl, half, part):
                    tt = jc * 4 + tl
                    if half == 0 and part == 0:
                        yt, ytb, yds = ysb.next()
                        xr_, xrb, xds = xres.next()
                        k.dma(SP, xds, xr_[:], x_d[tt * 128:(tt + 1) * 128, :], writes=[xrb])
                        st[tl] = (yt, ytb, yds, xr_, xrb)
                    yt, ytb, yds, xr_, xrb = st[tl]
                    if part == 0:
                        st[(tl, half)] = nextW()
                    pw, pwb = st[(tl, half)]
                    if part == 0:
                        for f in range(4):
                            k.op(PE, lambda f=f: nc.tensor.matmul(pw[:, :], lhsT=rt[:, f, tl * 128:(tl + 1) * 128], rhs=wout[:, f, half * 512:(half + 1) * 512],
                                                                  start=(f == 0), stop=False), reads=[rtb, wout_b], writes=[pwb])
                    else:
                        for f in range(4):
                            k.op(PE, lambda f=f: nc.tensor.matmul(pw[:, :], lhsT=attT[:, f, tl * 128:(tl + 1) * 128], rhs=wout[:, 4 + f, half * 512:(half + 1) * 512],
                                                                  start=False, stop=(f == 3)), reads=[attT_b, wout_b], writes=[pwb])
                        k.op(DVE, lambda: nc.vector.tensor_scalar(out=yt[:, half * 512:(half + 1) * 512], in0=pw[:, :], scalar1=ssr[:, tt:tt + 1], scalar2=None, op0=ALU.mult),
                             reads=[pwb, ssr_b], writes=[ytb])

                def u_epi(tl):
                    tt = jc * 4 + tl
                    yt, ytb, yds, xr_, xrb = st[tl]
                    if debug:
                        k.dma(POOL, dbg_ds, dbg["d_y"][tt * 128:(tt + 1) * 128, :], yt[:], reads=[ytb], writes=[DB(("dy", tt))])
                    s4, s4b = small()
                    k.op(ACT, lambda: nc.scalar.activation(out=junk4[:], in_=yt[:], func=AF.Square, accum_out=s4[:, 0:1]), reads=[ytb], writes=[junk4_b, s4b])
                    k.op(ACT, lambda: nc.scalar.activation(out=s4[:, 1:2], in_=s4[:, 0:1], func=AF.Ln, scale=1.0 / D, bias=EPS), reads=[s4b], writes=[s4b])
                    k.op(ACT, lambda: nc.scalar.activation(out=s4[:, 2:3], in_=s4[:, 1:2], func=AF.Exp, scale=-0.5), reads=[s4b], writes=[s4b])
                    k.op(DVE, lambda: nc.vector.scalar_tensor_tensor(out=yt[:], in0=yt[:], scalar=s4[:, 2:3], in1=C1row[:], op0=ALU.mult, op1=ALU.mult),
                         reads=[ytb, s4b, C1_b], writes=[ytb])
                    k.op(DVE, lambda: nc.vector.tensor_tensor(out=yt[:], in0=yt[:], in1=xr_[:], op=ALU.add), reads=[ytb, xrb], writes=[ytb])
                    k.dma(POOL, yds, x1_d[tt * 128:(tt + 1) * 128, :], yt[:], reads=[ytb], writes=[DB(("x1", tt))])

                for tl in range(4):
                    for half in range(2):
                        for part in range(2):
                            units.append(lambda tl=tl, half=half, part=part: u_mm(tl, half, part))
                    units.append(lambda tl=tl: u_epi(tl))
                return units

            cback = []
            for j in range(NCH):
                t0 = j * 512
                att_written.clear()
                rt, rtb, qwt, qwb = nxt
                qst, qsb = nxt_qs
                if j + 1 < NCH:
                    nxt = load_chunk(j + 1)
                ncts = [ct for ct in range(2) if 16 * (ct * 128) + 31 <= t0 + 511]

                def cmp_S(h):
                    g = h // 4
                    pts = []
                    for ct in ncts:
                        nr = 128 if ct == 0 else 127
                        ps, psb = nextS()
                        k.op(PE, lambda ps=ps, ct=ct, nr=nr, g=g, h=h: nc.tensor.matmul(ps[0:nr, :], lhsT=kcT[g][:, ct * 128:ct * 128 + nr], rhs=qwt[:, h, :], start=True, stop=False),
                             reads=[cmp_b, qwb], writes=[psb])
                        k.op(PE, lambda ps=ps, ct=ct, nr=nr: nc.tensor.matmul(ps[0:nr, :], lhsT=ident[:, 0:nr], rhs=cmask[:, ct, t0:t0 + 512], start=False, stop=True),
                             reads=[ident_b, cmask_b], writes=[psb])
                        pt, ptb, _ = PTc.next()
                        k.op(ACT, lambda ps=ps, pt=pt, nr=nr, ct=ct, h=h: nc.scalar.activation(out=pt[0:nr, :], in_=ps[0:nr, :], func=AF.Exp, bias=cllo[0:nr, ct, h:h + 1]),
                             reads=[psb, cmp_b], writes=[ptb])
                        pts.append((pt, ptb, ct, nr))
                    return pts

                def cmp_PV(h, pts):
                    g = h // 4
                    nmm = 4 * len(pts)
                    mi = 0
                    for tl in range(4):
                        for (pt, ptb, ct, nr) in pts:
                            k.op(PE, lambda pt=pt, tl=tl, nr=nr, ct=ct, g=g, mi=mi, nmm=nmm: nc.tensor.matmul(
                                pC[:, tl * 128:(tl + 1) * 128], lhsT=pt[0:nr, tl * 128:(tl + 1) * 128], rhs=vc_aug[g][0:nr, ct, :],
                                start=(mi == 0), stop=(mi == nmm - 1)),
                                reads=[ptb, cmp_b], writes=[pC_b])
                            mi += 1
                    evac_group(pC, pC_b, 128, 4, h, 0, j * 4, 0 * 8 + h, True, with_imp=True, g=g, first_in_group=(h % 4 == 0))

                pre = []
                cst = {}

                def u_cS(h):
                    cst[h] = cmp_S(h)

                def u_cPV(h):
                    cmp_PV(h, cst[h])
                for h in range(8):
                    pre.append(lambda h=h: u_cS(h))
                    pre.append(lambda h=h: u_cPV(h))
                units8 = [(tl, g) for tl in range(4) for g in range(2)]

                def tk_stage(sidx):
                    for u, (tl, g) in enumerate(units8):
                        tt = j * 4 + tl
                        tb = tk_bufs[u]
                        if sidx == 0:
                            k.op(DVE, lambda u=u, tl=tl, g=g, tt=tt: nc.vector.tensor_tensor(out=tk_a[:, u, :], in0=imp[:, tl, g, :], in1=addm[:, tt, :], op=ALU.add),
                                 reads=[imp_b[tl], addm_b], writes=[tb])
                        elif sidx == 1:
                            k.op(DVE, lambda u=u: nc.vector.max(out=tk8[:, u, 0:8], in_=tk_a[:, u, :]), reads=[tb], writes=[tb])
                        elif sidx == 2:
                            k.op(DVE, lambda u=u: nc.vector.match_replace(out=tk_b_[:, u, :], in_to_replace=tk8[:, u, 0:8], in_values=tk_a[:, u, :], imm_value=-3.0e38), reads=[tb], writes=[tb])
                        elif sidx == 3:
                            k.op(DVE, lambda u=u: nc.vector.max(out=tk8[:, u, 8:16], in_=tk_b_[:, u, :]), reads=[tb], writes=[tb])
                        elif sidx == 4:
                            k.op(DVE, lambda u=u: nc.vector.tensor_scalar(out=tksel[:, u, :], in0=tk_a[:, u, :], scalar1=tk8[:, u, 15:16], scalar2=NEGM, op0=ALU.is_lt, op1=ALU.mult),
                                 reads=[tb], writes=[tb])
                        elif sidx == 5:
                            k.op(PE, lambda u=u, g=g, tl=tl: nc.tensor.transpose(out=pTT[0:64, (g * 4 + tl) * 128:(g * 4 + tl + 1) * 128], in_=tksel[:, u, :], identity=ident[:]),
                                 reads=[tb, ident_b], writes=[pTT_b])
                    if sidx == 6:
                        for g in range(2):
                            if os.environ.get("KDBG_S6") == "act":
                                k.op(ACT, lambda g=g: nc.scalar.activation(out=selbT[g][64:128, :], in_=pTT[0:64, g * 512:(g + 1) * 512], func=AF.Copy), reads=[pTT_b], writes=[selbT_b[g]])
                            else:
                                k.op(DVE, lambda g=g: nc.vector.tensor_copy(out=selbT[g][64:128, :], in_=pTT[0:64, g * 512:(g + 1) * 512]), reads=[pTT_b], writes=[selbT_b[g]])
                    if sidx == 7:
                        for h_ in range(8):
                            k.op(DVE, lambda h_=h_: nc.vector.tensor_tensor(out=qst[64:128, h_, :], in0=qwt[64:128, h_, :], in1=selbT[h_ // 4][64:128, :], op=ALU.add),
                                 reads=[qwb, selbT_b[h_ // 4]], writes=[qsb])
                for sidx in range(8):
                    pre.append(lambda sidx=sidx: tk_stage(sidx))

                tasks = []
                for br in (2, 1):
                    for h in range(8):
                        kts = list(range(0, 4 * j + 4)) if br == 1 else list(range(max(0, 4 * j - 4), 4 * j + 4))
                        grp = {"h": h, "br": br, "g": h // 4, "pa": None, "npv": 0, "done": 0}
                        for kt in kts:
                            tls = [tl for tl in range(4) if kt <= 4 * j + tl and (br == 1 or kt >= 4 * j + tl - 4)]
                            grp["npv"] += len(tls)
                            tasks.append({"grp": grp, "kt": kt, "tls": tls})
                n_win = sum(1 for tk in tasks if tk["grp"]["br"] == 2)

                def emit_S(tk):
                    grp = tk["grp"]; h = grp["h"]; br = grp["br"]; g = grp["g"]; kt = tk["kt"]
                    kT = kTs[g] if br == 1 else kTw[g]
                    qq, qqb = (qst, qsb) if br == 1 else (qwt, qwb)
                    ps, psb = nextS()
                    m = (kt - (4 * j - 4)) if br == 2 else (4 + kt - 4 * j)
                    use_msk = (br == 2) or (m >= 4)
                    c0, c1 = 0, 512
                    if use_msk:
                        if m < 4:
                            c1 = 128 * (m + 1)
                        else:
                            c0 = 128 * (m - 4)
                    k.op(PE, lambda: nc.tensor.matmul(ps[:, c0:c1], lhsT=kT[:, kt * 128:(kt + 1) * 128], rhs=qq[:, h, c0:c1], start=True, stop=True),
                         reads=[(kTs_b[g] if br == 1 else kTw_b[g]), qqb], writes=[psb])
                    pt, ptb, _ = PT.next()
                    k.op(ACT, lambda: nc.scalar.activation(out=pt[:, c0:c1], in_=ps[:, c0:c1], func=AF.Exp, bias=sllo[:, h:h + 1]), reads=[psb, sllo_b], writes=[ptb])
                    if use_msk:
                        k.op(DVE, lambda: nc.vector.tensor_tensor(out=pt[:, c0:c1], in0=pt[:, c0:c1], in1=wm01[:, m, c0:c1], op=ALU.mult), reads=[ptb, wm01_b], writes=[ptb])
                    tk["pt"] = pt; tk["ptb"] = ptb

                def emit_PV(tk):
                    grp = tk["grp"]; h = grp["h"]; br = grp["br"]; g = grp["g"]; kt = tk["kt"]
                    vA = vs_aug if br == 1 else vw_aug
                    if grp["pa"] is None:
                        grp["pa"] = nextA()
                    pa, pab = grp["pa"]
                    pt = tk["pt"]; ptb = tk["ptb"]
                    for tl in tk["tls"]:
                        fm = (grp["done"] == 0)
                        grp["done"] += 1
                        last = (grp["done"] == grp["npv"])
                        k.op(PE, lambda tl=tl, fm=fm, last=last: nc.tensor.matmul(pa[:, tl * 65:(tl + 1) * 65], lhsT=pt[:, tl * 128:(tl + 1) * 128], rhs=vA[:, kt, g, :], start=fm, stop=last),
                             reads=[ptb, vtok_b[kt], vones_b], writes=[pab])
                    if grp["done"] == grp["npv"]:
                        evac_group(pa, pab, 65, 4, h, 0, j * 4, br * 8 + h, False)

                LOOK = 3
                n_sel = len(tasks) - n_win
                pre_every = max(1, n_win // (len(pre) + 1))
                cb_every = max(1, (n_sel - 2) // (len(cback) + 1)) if cback else 1
                for i in range(len(tasks) + LOOK):
                    if i < len(tasks):
                        if i == n_win:
                            while pre:
                                pre.pop(0)()
                        emit_S(tasks[i])
                        if i < n_win:
                            if pre and (i % pre_every == pre_every - 1):
                                pre.pop(0)()
                        else:
                            if cback and ((i - n_win) % cb_every == cb_every - 1):
                                cback.pop(0)()
                    if i - LOOK >= 0:
                        emit_PV(tasks[i - LOOK])
                while cback:
                    cback.pop(0)()
                if debug:
                    for tl in range(4):
                        k.dma(POOL, dbg_ds, dbg["d_att"][(j * 4 + tl) * 128:(j * 4 + tl + 1) * 128, :], att[:, tl, :], reads=[att_b[tl]], writes=[DB(("datt", j, tl))])
                for tl in range(4):
                    k.op(ACT, lambda tl=tl: nc.scalar.activation(out=junk4[:, 0:512], in_=att[:, tl, :], func=AF.Square, accum_out=ssa[:, tl:tl + 1]),
                         reads=[att_b[tl]], writes=[junk4_b, ssa_b])
                k.op(ACT, lambda: nc.scalar.activation(out=ssa[:, 4:8], in_=ssa[:, 0:4], func=AF.Ln, scale=1.0 / 512, bias=EPS), reads=[ssa_b], writes=[ssa_b])
                k.op(ACT, lambda: nc.scalar.activation(out=ssa[:, 4:8], in_=ssa[:, 4:8], func=AF.Exp, scale=-0.5), reads=[ssa_b], writes=[ssa_b])
                k.op(DVE, lambda: nc.vector.tensor_tensor(out=ssa[:, 4:8], in0=ssa[:, 4:8], in1=ssq[:, j * 4:(j + 1) * 4], op=ALU.mult), reads=[ssa_b, ssr_b], writes=[ssa_b])
                for tl in range(4):
                    k.op(DVE, lambda tl=tl: nc.vector.tensor_scalar(out=attb[:, tl, :], in0=att[:, tl, :], scalar1=ssa[:, 4 + tl:5 + tl], scalar2=None, op0=ALU.mult),
                         reads=[att_b[tl], ssa_b], writes=[attb_b])
                cback = make_cback(j, rt, rtb)
                if os.environ.get("KDBG_CB") == "now":
                    while cback:
                        cback.pop(0)()
                if j + 1 < NCH:
                    nxt_qs = load_qs(j + 1)
            while cback:
                cback.pop(0)()

        mid.close()
        if upto <= 4:
            k.drain()
            return nc
        k.barrier()
        with ExitStack() as p5:
            wf1 = sb("wf1", [128, 8, 4 * D], BF16, p5)
            wf2 = sb("wf2", [128, 32, D], BF16, p5)
            wf1_b = [Buf(f"wf1_{i}") for i in range(8)]; wf2_b = [Buf(f"wf2_{i}") for i in range(4)]
            wf1v = wff1_d.rearrange("(k p) n -> p k n", p=128)
            wf2v = wff2_d.rearrange("(k p) n -> p k n", p=128)
            for cb in range(8):
                k.dma(POOL, k.dsem("wf1"), wf1[:, :, cb * 512:(cb + 1) * 512], wf1v[:, :, cb * 512:(cb + 1) * 512], writes=[wf1_b[cb]])
            for hh in range(4):
                k.dma(POOL, k.dsem("wf2"), wf2[:, hh * 8:(hh + 1) * 8, :], wf2v[:, hh * 8:(hh + 1) * 8, :], writes=[wf2_b[hh]])
            CH = 256
            xin = Rot(k, p5, nc, "xin", 4, [128, D], F32)
            xnb = Rot(k, p5, nc, "xnb", 2, [128, D], BF16, with_dsem=False)
            hT2 = Rot(k, p5, nc, "hT2", 2, [128, 8, CH], BF16, with_dsem=False)
            aT = Rot(k, p5, nc, "aT", 1, [128, 32, CH], BF16, with_dsem=False)
            r32 = Rot(k, p5, nc, "r32", 3, [128, CH], F32, with_dsem=False)
            y2 = Rot(k, p5, nc, "y2", 2, [128, D], F32, with_dsem=False)
            ot = Rot(k, p5, nc, "ot", 2, [128, D], F32)
            junk5 = sb("junk5", [128, D], BF16, p5)
            junk5_b = Buf("junk5")
            sm5 = sb("sm5", [128, 64], F32, p5)
            sm5_b = [Buf(f"sm5_{i}") for i in range(16)]
            s5i = [0]
            pT5 = [pst(f"pT5_{i}", [128, 1024], BF16, p5) for i in range(2)]
            pT5_b = [Buf(f"pT5_{i}") for i in range(2)]
            pF = [pst(f"pF{i}", [128, 512], F32, p5) for i in range(2)]
            pF_b = [Buf(f"pF{i}") for i in range(2)]
            pY = [pst(f"pY{i}", [128, 512], F32, p5) for i in range(4)]
            pY_b = [Buf(f"pY{i}") for i in range(4)]
            fi = [0]
            out_bufs = []

            def prologue_a(cj):
                xtiles = []
                nts = []
                for tl in range(2):
                    tt = cj * 2 + tl
                    xt_, xtb, xds = xin.next()
                    k.dma(SP, xds, xt_[:], x1_d[tt * 128:(tt + 1) * 128, :], reads=[DB(("x1", tt))], writes=[xtb])
                    xtiles.append((xt_, xtb))
                    i5 = s5i[0] % 16; s5i[0] += 1
                    s4 = sm5[:, 4 * i5:4 * i5 + 4]; s4b = sm5_b[i5]
                    k.op(ACT, lambda xt_=xt_, s4=s4: nc.scalar.activation(out=junk5[:], in_=xt_[:], func=AF.Square, accum_out=s4[:, 0:1]), reads=[xtb], writes=[junk5_b, s4b])
                    k.op(ACT, lambda s4=s4: nc.scalar.activation(out=s4[:, 1:2], in_=s4[:, 0:1], func=AF.Sqrt, scale=1.0 / D, bias=EPS), reads=[s4b], writes=[s4b])
                    k.op(DVE, lambda s4=s4: nc.vector.reciprocal(out=s4[:, 2:3], in_=s4[:, 1:2]), reads=[s4b], writes=[s4b])
                    n, nb, _ = xnb.next()
                    k.op(DVE, lambda xt_=xt_, n=n, s4=s4: nc.vector.tensor_scalar(out=n[:], in0=xt_[:], scalar1=s4[:, 2:3], scalar2=None, op0=ALU.mult), reads=[xtb, s4b], writes=[nb])
                    nts.append((n, nb))
                return xtiles, nts

            def prologue_b(cj, nts):
                ht2, ht2b, _ = hT2.next()
                for tl in range(2):
                    tt = cj * 2 + tl
                    n, nb = nts[tl]
                    pp = pT5[tt % 2]; ppb = pT5_b[tt % 2]
                    for jj in range(8):
                        k.op(PE, lambda jj=jj, n=n, pp=pp: nc.tensor.transpose(out=pp[:, jj * 128:(jj + 1) * 128], in_=n[:, jj * 128:(jj + 1) * 128], identity=ident[:]),
                             reads=[nb, ident_b], writes=[ppb])
                    for jj in range(8):
                        if tt % 2 == 0:
                            k.op(DVE, lambda jj=jj, pp=pp, tl=tl, ht2=ht2: nc.vector.tensor_scalar(out=ht2[:, jj, tl * 128:(tl + 1) * 128], in0=pp[:, jj * 128:(jj + 1) * 128],
                                                                                                   scalar1=A2[:, jj:jj + 1], scalar2=B2[:, jj:jj + 1], op0=ALU.mult, op1=ALU.add),
                                 reads=[ppb, A2_b, B2_b], writes=[ht2b])
                        else:
                            k.op(ACT, lambda jj=jj, pp=pp, tl=tl, ht2=ht2: nc.scalar.activation(out=ht2[:, jj, tl * 128:(tl + 1) * 128], in_=pp[:, jj * 128:(jj + 1) * 128],
                                                                                                func=AF.Identity, scale=A2[:, jj:jj + 1], bias=B2[:, jj:jj + 1]),
                                 reads=[ppb, A2_b, B2_b], writes=[ht2b])
                return ht2, ht2b

            def ff1(ht2, ht2b, mid_cb=None):
                at, atb, _ = aT.next()
                res = None
                for f in range(32):
                    if f == 10 and mid_cb is not None:
                        res = mid_cb()
                    pf = pF[fi[0] % 2]; pfb = pF_b[fi[0] % 2]; fi[0] += 1
                    for kk in range(8):
                        k.op(PE, lambda kk=kk, f=f, pf=pf: nc.tensor.matmul(pf[:, 0:CH], lhsT=wf1[:, kk, f * 128:(f + 1) * 128], rhs=ht2[:, kk, :], start=(kk == 0), stop=(kk == 7)),
                             reads=[wf1_b[f // 4], ht2b], writes=[pfb])
                    r, rb, _ = r32.next()
                    k.op(ACT, lambda pf=pf, r=r: nc.scalar.activation(out=r[:], in_=pf[:, 0:CH], func=AF.Relu), reads=[pfb], writes=[rb])
                    k.op(DVE, lambda r=r, f=f: nc.vector.tensor_tensor(out=at[:, f, :], in0=r[:], in1=r[:], op=ALU.mult), reads=[rb], writes=[atb])
                return at, atb, res

            def ff2(cj, at, atb, xtiles):
                for tl in range(2):
                    tt = cj * 2 + tl
                    yy, yyb, _ = y2.next()
                    for half in range(2):
                        py = pY[(tl * 2 + half) % 4]; pyb = pY_b[(tl * 2 + half) % 4]
                        for f in range(32):
                            k.op(PE, lambda f=f, tl=tl, half=half, py=py: nc.tensor.matmul(py[:, :], lhsT=at[:, f, tl * 128:(tl + 1) * 128], rhs=wf2[:, f, half * 512:(half + 1) * 512],
                                                                                          start=(f == 0), stop=(f == 31)), reads=[atb, wf2_b[f // 8]], writes=[pyb])
                        k.op(ACT, lambda half=half, yy=yy, py=py: nc.scalar.activation(out=yy[:, half * 512:(half + 1) * 512], in_=py[:, :], func=AF.Copy), reads=[pyb], writes=[yyb])
                    i5 = s5i[0] % 16; s5i[0] += 1
                    s4 = sm5[:, 4 * i5:4 * i5 + 4]; s4b = sm5_b[i5]
                    k.op(ACT, lambda yy=yy, s4=s4: nc.scalar.activation(out=junk5[:], in_=yy[:], func=AF.Square, accum_out=s4[:, 0:1]), reads=[yyb], writes=[junk5_b, s4b])
                    k.op(ACT, lambda s4=s4: nc.scalar.activation(out=s4[:, 1:2], in_=s4[:, 0:1], func=AF.Sqrt, scale=1.0 / D, bias=EPS), reads=[s4b], writes=[s4b])
                    k.op(DVE, lambda s4=s4: nc.vector.reciprocal(out=s4[:, 2:3], in_=s4[:, 1:2]), reads=[s4b], writes=[s4b])
                    k.op(DVE, lambda yy=yy, s4=s4: nc.vector.scalar_tensor_tensor(out=yy[:], in0=yy[:], scalar=s4[:, 2:3], in1=C2row[:], op0=ALU.mult, op1=ALU.mult),
                         reads=[yyb, s4b, C2_b], writes=[yyb])
                    o, ob, ods = ot.next()
                    xt_, xtb = xtiles[tl]
                    k.op(DVE, lambda yy=yy, o=o, xt_=xt_: nc.vector.tensor_tensor(out=o[:], in0=yy[:], in1=xt_[:], op=ALU.add), reads=[yyb, xtb], writes=[ob])
                    db = DB(("out", tt))
                    k.dma(POOL, ods, out_d[tt * 128:(tt + 1) * 128, :], o[:], reads=[ob], writes=[db])
                    out_bufs.append(db)

            NCJ = S // CH
            xtiles, nts = prologue_a(0)
            ht2, ht2b = prologue_b(0, nts)
            for cj in range(NCJ):
                at, atb, res = ff1(ht2, ht2b, (lambda cj=cj: prologue_a(cj + 1)) if cj + 1 < NCJ else None)
                if cj + 1 < NCJ:
                    xtiles_n, nts_n = res
                    ht2_n, ht2b_n = prologue_b(cj + 1, nts_n)
                ff2(cj, at, atb, xtiles)
                if cj + 1 < NCJ:
                    xtiles, ht2, ht2b = xtiles_n, ht2_n, ht2b_n
            k.drain()
            k.finish(out_bufs + [b for kk_, b in dbuf.items() if isinstance(kk_, tuple) and kk_ and kk_[0] in ("datt", "dy")] + ([DB("d_mod")] if debug else []))
        print("bass instructions:", k.ninst, "signalling:", k.nsig, "semaphores:", k.nsem)
    return nc


def _consts():
    bf = ml_dtypes.bfloat16
    t = np.arange(S)
    slopes = 2.0 ** (-np.arange(1, 9, dtype=np.float64))
    c = np.arange(256)
    ce = 16 * c + 31
    ce01 = ((ce[None, :] // 64) == np.arange(64)[:, None]).astype(np.float32)
    ce01[:, 255] = 0.0
    cidx = 16 * (np.arange(2)[None, :] * 128 + np.arange(128)[:, None]) + 31
    cllo = (slopes[None, None, :] * (cidx[:, :, None] % 64)).astype(np.float32)
    cend = (16 * (np.arange(2)[None, :, None] * 128 + np.arange(128)[:, None, None]) + 31)
    cmask = np.where(cend <= t[None, None, :], 0.0, NEGM).astype(np.float32)
    e01 = ((t[None, :] // 64) == np.arange(64)[:, None]).astype(np.float32)
    tab = (64.0 * (np.arange(64)[:, None] - (t[None, :] // 64))).astype(np.float32)
    sllo = (slopes[None, :] * (np.arange(128)[:, None] % 64)).astype(np.float32)
    m = np.arange(8)[None, :, None]; kk = np.arange(128)[:, None, None]; tl = np.arange(512)[None, None, :]
    dd = (512 + tl) - (128 * m + kk)
    wm01 = ((dd >= 0) & (dd < 512)).astype(np.float32)
    tok = (np.arange(NT)[None, :, None] * 128 + np.arange(128)[:, None, None])
    cur = tok // 64
    jb = np.arange(64)[None, None, :]
    forced = (jb == 0) | (jb == cur) | (jb == cur - 1)
    addm = np.where(forced, 1.0e4, np.where(jb <= cur, 0.0, -1.0e30)).astype(np.float32)
    cc = (np.arange(2)[None, :, None] * 128 + np.arange(128)[:, None, None])
    ovl = ((cc >= 4 * jb - 1) & (cc <= 4 * jb + 3) & (cc < 255)).astype(np.float32)
    return {
        "k_ident": np.eye(128, dtype=np.float32).astype(bf),
        "k_ce01": ce01.astype(bf), "k_cllo": cllo,
        "k_cmask": cmask.astype(bf), "k_e01": e01.astype(bf), "k_tab": tab.astype(bf), "k_sllo": sllo, "k_wm01": wm01.astype(bf),
        "k_addm": addm, "k_ovl": np.ascontiguousarray(ovl[:, :, 1:]).astype(bf),
    }


def _col(v, n):
    return np.ascontiguousarray(np.asarray(v, np.float32).reshape(n, 128).T)


def _shared_inputs(inp):
    L = 0
    f = lambda a: np.ascontiguousarray(np.asarray(a, np.float32))
    d = {
        "ada_w": f(inp["ada_w"][L]), "ada_b": f(inp["ada_b"][L]).reshape(1, -1),
        "g_pre1": _col(inp["pre_norm_mix"][L], 8), "g_pre2": _col(inp["pre_norm_mlp"][L], 8),
        "g_post1": np.ascontiguousarray(np.broadcast_to(f(inp["post_norm_mix"][L])[None, :], (128, D))),
        "g_post2": np.ascontiguousarray(np.broadcast_to(f(inp["post_norm_mlp"][L])[None, :], (128, D))),
        "w_in": f(inp["w_in"][L]),
        "conv_w": np.ascontiguousarray(f(inp["conv_w"][L]).T.reshape(4, 128, 4).transpose(1, 0, 2)),
        "conv_b": _col(inp["conv_b"][L], 4),
        "lru_wa": f(inp["lru_wa"][L]), "lru_wx": f(inp["lru_wx"][L]),
        "lru_ba": _col(inp["lru_ba"][L], 4), "lru_bx": _col(inp["lru_bx"][L], 4), "lru_lam": _col(inp["lru_lambda"][L], 4),
        "pos_k": np.ascontiguousarray(f(inp["cmp_pos_k"][L]).T), "pos_v": np.ascontiguousarray(f(inp["cmp_pos_v"][L]).T),
        "w1_k": f(inp["cmp_w1_k"][L]), "w1_v": f(inp["cmp_w1_v"][L]),
        "w2_k": f(inp["cmp_w2_k"][L]), "w2_v": f(inp["cmp_w2_v"][L]),
        "g_rnn": _col(inp["norm_rnn_out"][L], 4), "g_att": _col(inp["norm_att_out"][L], 4),
        "w_out": f(inp["w_out"][L]), "w_ff1": f(inp["w_ff1"][L]), "w_ff2": f(inp["w_ff2"][L]),
    }
    d.update(_consts())
    return d


def kernel(**inputs):
    debug = bool(inputs.pop("_debug", False))
    upto = inputs.pop("_upto", 99)
    cores = inputs.pop("_cores", None)
    x = np.asarray(inputs["x"], np.float32)
    c = np.asarray(inputs["c"], np.float32)
    B = x.shape[0]
    shared = _shared_inputs(inputs)
    rec = set()
    build(debug=debug, upto=upto, record=rec)
    nc = build(debug=debug, upto=upto, needed=rec)
    bs = list(range(B)) if cores is None else list(cores)
    in_maps = []
    for b in bs:
        m = dict(shared)
        m["x"] = np.ascontiguousarray(x[b])
        m["c"] = _col(c[b], 8)
        in_maps.append(m)
    res = run_bass_kernel_spmd(nc, in_maps, core_ids=list(range(len(bs))))
    if debug:
        return res.results
    return np.stack([np.asarray(r["out"], np.float32) for r in res.results], axis=0)
```

```python
import numpy as np
import ml_dtypes
from contextlib import ExitStack
import concourse.bass as bass
import concourse.mybir as mybir
from concourse.bass_utils import run_bass_kernel_spmd

F32 = mybir.dt.float32
BF16 = mybir.dt.bfloat16
AF = mybir.ActivationFunctionType
ALU = mybir.AluOpType

S = 4096
D = 1024
NT = S // 128
NCH = S // 512
DIN = 2328
NEGM = -30000.0
EPS = 1e-6
import os
EVAC = os.environ.get('KDBG_EVAC', '')


class Buf:
    __slots__ = ("name", "lw", "rd")

    def __init__(self, name):
        self.name = name
        self.lw = None
        self.rd = {}


class DSem:
    __slots__ = ("sem", "cnt")

    def __init__(self, sem):
        self.sem = sem
        self.cnt = 0


class Eng:
    def __init__(self, name, eng, self_sync):
        self.name = name
        self.eng = eng
        self.self_sync = self_sync
        self.cur = None
        self.cnt = 0
        self.gidx = 0
        self.tokmap = {}
        self.waited = {}


class K:
    EPOCH = 4000

    def __init__(self, nc, es, needed=None, record=None):
        self.nc = nc
        self.es = es
        self.needed = needed
        self.record = record
        self.nsem = 0
        self.pe = Eng("pe", nc.tensor, False)
        self.act = Eng("act", nc.scalar, True)
        self.dve = Eng("dve", nc.vector, True)
        self.pool = Eng("pool", nc.gpsimd, True)
        self.sp = Eng("sp", nc.sync, True)
        self.dsems = []
        self.ninst = 0
        self.nsig = 0

    def new_sem(self, name):
        self.nsem += 1
        return self.es.enter_context(self.nc.semaphore(f"{name}_{self.nsem}"))

    def dsem(self, name="d"):
        d = DSem(self.new_sem(name))
        self.dsems.append(d)
        return d

    def _wait(self, E, tok):
        if tok[0] == "E":
            P, g = tok[1], tok[2]
            if (not E.self_sync) and P is E:
                return
            if E.waited.get(P.name, 0) >= g:
                return
            if self.record is not None:
                self.record.add((P.name, g))
            sem, val = P.tokmap[g]
            E.eng.wait_ge(sem, val)
            E.waited[P.name] = g
        else:
            ds, val = tok[1], tok[2]
            key = id(ds)
            if E.waited.get(key, 0) >= val:
                return
            E.eng.wait_ge(ds.sem, val)
            E.waited[key] = val

    def _deps(self, E, reads, writes):
        for b in reads:
            if b.lw is not None:
                self._wait(E, b.lw)
        for b in writes:
            if b.lw is not None:
                self._wait(E, b.lw)
            for tok in b.rd.values():
                self._wait(E, tok)

    def _commit(self, tok, reads, writes):
        key = tok[1].name if tok[0] == "E" else id(tok[1])
        for b in reads:
            old = b.rd.get(key)
            if old is None or old[2] < tok[2]:
                b.rd[key] = tok
        for b in writes:
            b.lw = tok
            b.rd = {}

    def op(self, E, fn, reads=(), writes=()):
        self._deps(E, reads, writes)
        inst = fn()
        E.gidx += 1
        g = E.gidx
        if self.needed is None or (E.name, g) in self.needed:
            if E.cur is None or E.cnt >= self.EPOCH:
                E.cur = self.new_sem(E.name)
                E.cnt = 0
            E.cnt += 1
            inst.then_inc(E.cur, 1)
            E.tokmap[g] = (E.cur, E.cnt)
            self.nsig += 1
        tok = ("E", E, g)
        self._commit(tok, reads, writes)
        self.ninst += 1
        return tok

    def dma(self, Q, ds, out, in_, reads=(), writes=(), **kw):
        self._deps(Q, reads, writes)
        if ds.cnt:
            self._wait(Q, ("D", ds, ds.cnt))
        inst = Q.eng.dma_start(out=out, in_=in_, **kw)
        ds.cnt += 16
        inst.then_inc(ds.sem, 16)
        tok = ("D", ds, ds.cnt)
        self._commit(tok, reads, writes)
        self.ninst += 1
        return tok

    def drain(self):
        for E in (self.pe, self.act, self.dve, self.pool):
            if E.gidx:
                self._wait(self.sp, ("E", E, E.gidx))
        for d in self.dsems:
            if d.cnt:
                self._wait(self.sp, ("D", d, d.cnt))

    def barrier(self):
        engs = (self.pe, self.act, self.dve, self.pool, self.sp)
        for E in engs:
            for E2 in engs:
                if E2 is not E and E2.gidx:
                    self._wait(E, ("E", E2, E2.gidx))
            for d in self.dsems:
                if d.cnt:
                    self._wait(E, ("D", d, d.cnt))

    def finish(self, bufs):
        for b in bufs:
            if b.lw is not None:
                self._wait(self.sp, b.lw)


class Rot:
    def __init__(self, k, es, nc, name, n, shape, dt, with_dsem=True):
        self.n = n
        self.i = 0
        self.slots = []
        for j in range(n):
            t = es.enter_context(nc.sbuf_tensor(f"{name}{j}", shape, dt))
            self.slots.append((t, Buf(f"{name}{j}"), k.dsem(name) if with_dsem else None))

    def next(self):
        s = self.slots[self.i % self.n]
        self.i += 1
        return s


class _Stop(Exception):
    pass


def build(debug=False, upto=99, needed=None, record=None):
    nc = bass.Bass("TRN2", target_bir_lowering=False)

    def din(name, shape, dt=F32):
        return nc.dram_tensor(name, list(shape), dt, kind="ExternalInput").ap()

    def dscr(name, shape, dt):
        return nc.dram_tensor(name, list(shape), dt, kind="Internal").ap()

    x_d = din("x", [S, D])
    c_d = din("c", [128, 8])
    adaw_d = din("ada_w", [D, 6 * D])
    adab_d = din("ada_b", [1, 6 * D])
    gpre1_d = din("g_pre1", [128, 8])
    gpre2_d = din("g_pre2", [128, 8])
    gpost1_d = din("g_post1", [128, D])
    gpost2_d = din("g_post2", [128, D])
    win_d = din("w_in", [D, DIN])
    convw_d = din("conv_w", [128, 4, 4])
    convb_d = din("conv_b", [128, 4])
    wa_d = din("lru_wa", [8, 64, 64])
    wx_d = din("lru_wx", [8, 64, 64])
    ba_d = din("lru_ba", [128, 4])
    bx_d = din("lru_bx", [128, 4])
    lam_d = din("lru_lam", [128, 4])
    posk_d = din("pos_k", [64, 32])
    posv_d = din("pos_v", [64, 32])
    w1k_d = din("w1_k", [2048, 256])
    w1v_d = din("w1_v", [2048, 256])
    w2k_d = din("w2_k", [256, 64])
    w2v_d = din("w2_v", [256, 64])
    grnn_d = din("g_rnn", [128, 4])
    gatt_d = din("g_att", [128, 4])
    wout_d = din("w_out", [D, D])
    wff1_d = din("w_ff1", [D, 4 * D])
    wff2_d = din("w_ff2", [4 * D, D])
    ident_d = din("k_ident", [128, 128], BF16)
    ce01_d = din("k_ce01", [64, 256], BF16)
    cllo_d = din("k_cllo", [128, 2, 8])
    cmask_d = din("k_cmask", [128, 2, S], BF16)
    e01_d = din("k_e01", [64, S], BF16)
    tab_d = din("k_tab", [64, S], BF16)
    sllo_d = din("k_sllo", [128, 8])
    wm01_d = din("k_wm01", [128, 8, 512], BF16)
    addm_d = din("k_addm", [128, NT, 64])
    ovl_d = din("k_ovl", [128, 2, 63], BF16)
    out_d = nc.dram_tensor("out", [S, D], F32, kind="ExternalOutput").ap()
    zr_d = dscr("zr_s", [1024, S], F32)
    q_d = dscr("q_s", [512, S], BF16)
    kc_d = dscr("kc_s", [128, S], BF16)
    vc_d = dscr("vc_s", [128, S], BF16)
    ks_d = dscr("ks_s", [128, S], BF16)
    kw_d = dscr("kw_s", [128, S], BF16)
    rnn_d = dscr("rnn_s", [512, S], BF16)
    x1_d = dscr("x1_s", [S, D], F32)
    dbg = {}
    if debug:
        for nm, shp in [("d_mod", [1, 6 * D]), ("d_att", [S, 512]), ("d_y", [S, D])]:
            dbg[nm] = nc.dram_tensor(nm, shp, F32, kind="ExternalOutput").ap()

    with ExitStack() as es:
        k = K(nc, es, needed=needed, record=record)
        PE, ACT, DVE, POOL, SP = k.pe, k.act, k.dve, k.pool, k.sp

        def sb(name, shape, dt, stack=es):
            return stack.enter_context(nc.sbuf_tensor(name, list(shape), dt))

        def pst(name, shape, dt, stack=es):
            return stack.enter_context(nc.psum_tensor(name, list(shape), dt))

        dbuf = {}

        def DB(key):
            if key not in dbuf:
                dbuf[key] = Buf(str(key))
            return dbuf[key]

        ident = sb("ident", [128, 128], BF16)
        ident_b = Buf("ident")
        ld0 = k.dsem("ld0")
        k.dma(SP, ld0, ident[:], ident_d, writes=[ident_b])
        ones_bf = sb("ones_bf", [128, 128], BF16)
        ones_b = Buf("ones")
        k.op(DVE, lambda: nc.vector.memset(ones_bf[:], 1.0), writes=[ones_b])
        A1 = sb("A1", [128, 8], F32); B1 = sb("B1", [128, 8], F32)
        A2 = sb("A2", [128, 8], F32); B2 = sb("B2", [128, 8], F32)
        C1row = sb("C1row", [128, D], F32); C2row = sb("C2row", [128, D], F32)
        A1_b, B1_b, A2_b, B2_b, C1_b, C2_b = (Buf(n) for n in ("A1", "B1", "A2", "B2", "C1", "C2"))
        mid = es.enter_context(ExitStack())
        vs_aug = sb("vs_aug", [128, NT, 2, 65], BF16, mid)
        vw_aug = sb("vw_aug", [128, NT, 2, 65], BF16, mid)
        gates = sb("gates", [128, NT, 24], F32, mid)
        vtok_b = [Buf(f"vtok{t}") for t in range(NT)]
        vones_b = Buf("vones")
        k.op(DVE, lambda: nc.vector.memset(vs_aug[:, :, :, 64:65], 1.0), writes=[vones_b])
        k.op(DVE, lambda: nc.vector.memset(vw_aug[:, :, :, 64:65], 1.0), writes=[vones_b])
        ssr = sb("ssr", [128, NT], F32, mid)
        ssq = sb("ssq", [128, NT], F32, mid)
        ssr_b = Buf("ssr")

        with ExitStack() as p0:
            csb = sb("csb", [128, 8], F32, p0)
            scb = sb("scb", [128, 8], BF16, p0)
            c_b = Buf("c")
            k.dma(SP, k.dsem("c"), csb[:], c_d, writes=[c_b])
            k.op(ACT, lambda: nc.scalar.activation(out=scb[:], in_=csb[:], func=AF.Silu), reads=[c_b], writes=[c_b])
            adab = sb("adab", [1, 6 * D], F32, p0)
            adab_b = Buf("adab")
            k.dma(SP, k.dsem("adab"), adab[:], adab_d, writes=[adab_b])
            modrow = sb("modrow", [1, 6 * D], F32, p0)
            modrow_b = Buf("modrow")
            modcol = sb("modcol", [128, 48], F32, p0)
            modcol_b = Buf("modcol")
            gp1 = sb("gp1", [128, 8], F32, p0); gp2 = sb("gp2", [128, 8], F32, p0)
            gq1 = sb("gq1", [128, D], F32, p0); gq2 = sb("gq2", [128, D], F32, p0)
            g_b = Buf("gvecs")
            k.dma(SP, ld0, gp1[:], gpre1_d, writes=[g_b])
            k.dma(SP, ld0, gp2[:], gpre2_d, writes=[g_b])
            k.dma(SP, ld0, gq1[:], gpost1_d, writes=[g_b])
            k.dma(SP, ld0, gq2[:], gpost2_d, writes=[g_b])
            adaw = Rot(k, p0, nc, "adaw", 2, [128, 8, 512], BF16)
            ps_row = pst("ps_row", [128, 512], F32, p0)
            ps_row_b = Buf("ps_row")
            ps_col = pst("ps_col", [128, 512], F32, p0)
            ps_col_b = Buf("ps_col")
            one11 = sb("one11", [1, 128], F32, p0)
            one11_b = Buf("one11")
            k.op(DVE, lambda: nc.vector.memset(one11[:], 1.0), writes=[one11_b])
            for pc in range(12):
                t, tb, ds = adaw.next()
                k.dma(POOL, ds, t[:], adaw_d[:, pc * 512:(pc + 1) * 512].rearrange("(k p) n -> p k n", p=128), writes=[tb])
                for kk in range(8):
                    k.op(PE, lambda kk=kk, t=t: nc.tensor.matmul(ps_row[0:1, :], lhsT=scb[:, kk:kk + 1], rhs=t[:, kk, :],
                                                                start=(kk == 0), stop=(kk == 7)),
                         reads=[tb, c_b], writes=[ps_row_b])
                k.op(DVE, lambda pc=pc: nc.vector.tensor_tensor(out=modrow[0:1, pc * 512:(pc + 1) * 512], in0=ps_row[0:1, :],
                                                                in1=adab[0:1, pc * 512:(pc + 1) * 512], op=ALU.add),
                     reads=[ps_row_b, adab_b], writes=[modrow_b])
            if debug:
                k.dma(SP, ld0, dbg["d_mod"], modrow[:], reads=[modrow_b], writes=[DB("d_mod")])
            for j in range(48):
                k.op(PE, lambda j=j: nc.tensor.matmul(ps_col[:, j:j + 1], lhsT=modrow[0:1, j * 128:(j + 1) * 128], rhs=one11[0:1, 0:1],
                                                      start=True, stop=True),
                     reads=[modrow_b, one11_b], writes=[ps_col_b])
            k.op(DVE, lambda: nc.vector.tensor_copy(out=modcol[:], in_=ps_col[:, 0:48]), reads=[ps_col_b], writes=[modcol_b])
            k.op(DVE, lambda: nc.vector.scalar_tensor_tensor(out=A1[:], in0=modcol[:, 8:16], scalar=1.0, in1=gp1[:], op0=ALU.add, op1=ALU.mult),
                 reads=[modcol_b, g_b], writes=[A1_b])
            k.op(DVE, lambda: nc.vector.tensor_copy(out=B1[:], in_=modcol[:, 0:8]), reads=[modcol_b], writes=[B1_b])
            k.op(DVE, lambda: nc.vector.scalar_tensor_tensor(out=A2[:], in0=modcol[:, 32:40], scalar=1.0, in1=gp2[:], op0=ALU.add, op1=ALU.mult),
                 reads=[modcol_b, g_b], writes=[A2_b])
            k.op(DVE, lambda: nc.vector.tensor_copy(out=B2[:], in_=modcol[:, 24:32]), reads=[modcol_b], writes=[B2_b])
            for (base, crow, cb, gq) in ((2048, C1row, C1_b, gq1), (5120, C2row, C2_b, gq2)):
                for hh in range(2):
                    k.op(PE, lambda base=base, hh=hh: nc.tensor.matmul(ps_row[:, :], lhsT=one11[0:1, :],
                                                                      rhs=modrow[0:1, base + hh * 512: base + (hh + 1) * 512],
                                                                      start=True, stop=True),
                         reads=[modrow_b, one11_b], writes=[ps_row_b])
                    k.op(DVE, lambda hh=hh, crow=crow, gq=gq: nc.vector.scalar_tensor_tensor(
                        out=crow[:, hh * 512:(hh + 1) * 512], in0=ps_row[:, :], scalar=1.0, in1=gq[:, hh * 512:(hh + 1) * 512],
                        op0=ALU.add, op1=ALU.mult), reads=[ps_row_b, g_b], writes=[cb])

        if upto <= 0:
            k.drain()
            return nc
        k.barrier()
        with ExitStack() as p1:
            hT = sb("hT", [128, 8, S], BF16, p1)
            hT_b = [[Buf(f"hT{t}_{j}") for j in range(8)] for t in range(NT)]
            win = sb("win", [128, 8, DIN], BF16, p1)
            win_b = Buf("win")
            wds = k.dsem("win")
            k.dma(POOL, wds, win[:], win_d.rearrange("(k p) n -> p k n", p=128), writes=[win_b])
            xt = Rot(k, p1, nc, "xt", 3, [128, D], F32)
            junk = sb("junk", [128, D], BF16, p1)
            junk_b = Buf("junk")
            xn = Rot(k, p1, nc, "xn", 2, [128, D], BF16, with_dsem=False)
            pT = [pst(f"pT{i}", [128, 1024], BF16, p1) for i in range(2)]
            pT_b = [Buf(f"pT{i}") for i in range(2)]
            sm1 = sb("sm1", [128, NT * 4], F32, p1)
            sm1_b = [Buf(f"sm1_{t}") for t in range(NT)]
            pz = [pst(f"pz{i}", [128, 512], F32, p1) for i in range(4)]
            pz_b = [Buf(f"pz{i}") for i in range(4)]
            st32 = Rot(k, p1, nc, "st32", 3, [128, 512], F32)
            st16 = Rot(k, p1, nc, "st16", 3, [128, 512], BF16)
            zi = 0
            fm_tiles = []
            for ct in range(8):
                fm_tiles.append((ct * 128, zr_d, ct * 128, "f32", 1.0))
            for ct in range(4):
                fm_tiles.append((1024 + ct * 128, q_d, ct * 128, "bf", 0.125))
            fm_tiles.append((1536, kc_d, 0, "bf", 1.0))
            fm_tiles.append((1664, vc_d, 0, "bf", 1.0))
            fm_tiles.append((1792, ks_d, 0, "bf", 1.0))
            fm_tiles.append((2048, kw_d, 0, "bf", 1.0))

            def norm_a(tt):
                t, tb, ds = xt.next()
                k.dma(SP, ds, t[:], x_d[tt * 128:(tt + 1) * 128, :], writes=[tb])
                s4 = sm1[:, tt * 4:tt * 4 + 4]; s4b = sm1_b[tt]
                k.op(ACT, lambda: nc.scalar.activation(out=junk[:], in_=t[:], func=AF.Square, accum_out=s4[:, 0:1]), reads=[tb], writes=[junk_b, s4b])
                k.op(ACT, lambda: nc.scalar.activation(out=s4[:, 1:2], in_=s4[:, 0:1], func=AF.Ln, scale=1.0 / D, bias=EPS), reads=[s4b], writes=[s4b])
                k.op(ACT, lambda: nc.scalar.activation(out=s4[:, 2:3], in_=s4[:, 1:2], func=AF.Exp, scale=-0.5), reads=[s4b], writes=[s4b])
                n, nb, _ = xn.next()
                k.op(DVE, lambda: nc.vector.tensor_scalar(out=n[:], in0=t[:], scalar1=s4[:, 2:3], scalar2=None, op0=ALU.mult), reads=[tb, s4b], writes=[nb])
                return tt, n, nb

            def norm_b(tt, n, nb):
                pp = pT[tt % 2]; ppb = pT_b[tt % 2]
                for j in range(8):
                    k.op(PE, lambda j=j: nc.tensor.transpose(out=pp[:, j * 128:(j + 1) * 128], in_=n[:, j * 128:(j + 1) * 128], identity=ident[:]),
                         reads=[nb, ident_b], writes=[ppb])
                for j in range(8):
                    if tt % 2 == 0:
                        k.op(DVE, lambda j=j: nc.vector.tensor_scalar(out=hT[:, j, tt * 128:(tt + 1) * 128], in0=pp[:, j * 128:(j + 1) * 128],
                                                                      scalar1=A1[:, j:j + 1], scalar2=B1[:, j:j + 1], op0=ALU.mult, op1=ALU.add),
                             reads=[ppb, A1_b, B1_b], writes=[hT_b[tt][j]])
                    else:
                        k.op(ACT, lambda j=j: nc.scalar.activation(out=hT[:, j, tt * 128:(tt + 1) * 128], in_=pp[:, j * 128:(j + 1) * 128],
                                                                   func=AF.Identity, scale=A1[:, j:j + 1], bias=B1[:, j:j + 1]),
                             reads=[ppb, A1_b, B1_b], writes=[hT_b[tt][j]])

            npend = [norm_a(0)]

            def norm_tile(_tt_unused=None):
                tt, n, nb = npend[0]
                norm_b(tt, n, nb)
                if tt + 1 < NT:
                    npend[0] = norm_a(tt + 1)

            for tl in range(4):
                norm_tile(tl)
            for ch in range(NCH):
                for ti, (c0, dst, r0, kind, scl) in enumerate(fm_tiles):
                    if ch + 1 < NCH and ti in (2, 6, 10, 14):
                        norm_tile((ch + 1) * 4 + (ti - 2) // 4)
                    pzz = pz[zi % 4]; pzb = pz_b[zi % 4]
                    for kk in range(8):
                        k.op(PE, lambda kk=kk, c0=c0, ch=ch, pzz=pzz: nc.tensor.matmul(pzz[:, :], lhsT=win[:, kk, c0:c0 + 128], rhs=hT[:, kk, ch * 512:(ch + 1) * 512],
                                                                                      start=(kk == 0), stop=(kk == 7)),
                             reads=[win_b] + [hT_b[t4][kk] for t4 in range(ch * 4, ch * 4 + 4)], writes=[pzb])
                    t, tb, ds = (st32 if kind == "f32" else st16).next()
                    if zi % 2 == 0:
                        k.op(ACT, lambda t=t, pzz=pzz, scl=scl: nc.scalar.activation(out=t[:], in_=pzz[:, :], func=AF.Copy, scale=scl), reads=[pzb], writes=[tb])
                    else:
                        k.op(DVE, lambda t=t, pzz=pzz, scl=scl: nc.vector.tensor_scalar(out=t[:], in0=pzz[:, :], scalar1=scl, scalar2=None, op0=ALU.mult),
                             reads=[pzb], writes=[tb])
                    k.dma(POOL, ds, dst[r0:r0 + 128, ch * 512:(ch + 1) * 512], t[:], reads=[tb], writes=[DB((id(dst), r0, ch))])
                    zi += 1
                for tl in range(4):
                    tt = ch * 4 + tl
                    pzz = pz[zi % 4]; pzb = pz_b[zi % 4]
                    use_dve = (zi % 2 == 1)
                    zi += 1
                    for (c0, n, o0) in ((1920, 128, 0), (2176, 152, 128)):
                        for kk in range(8):
                            k.op(PE, lambda kk=kk, c0=c0, n=n, o0=o0, tt=tt, pzz=pzz: nc.tensor.matmul(
                                pzz[:, o0:o0 + n], lhsT=hT[:, kk, tt * 128:(tt + 1) * 128], rhs=win[:, kk, c0:c0 + n],
                                start=(kk == 0 and o0 == 0), stop=(kk == 7 and o0 == 128)),
                                reads=[win_b, hT_b[tt][kk]], writes=[pzb])
                    if use_dve:
                        k.op(DVE, lambda tt=tt, pzz=pzz: nc.vector.tensor_copy(out=vs_aug[:, tt, :, 0:64], in_=pzz[:, 0:128].rearrange("p (g d) -> p g d", g=2)), reads=[pzb], writes=[vtok_b[tt]])
                        k.op(DVE, lambda tt=tt, pzz=pzz: nc.vector.tensor_copy(out=vw_aug[:, tt, :, 0:64], in_=pzz[:, 128:256].rearrange("p (g d) -> p g d", g=2)), reads=[pzb], writes=[vtok_b[tt]])
                        k.op(DVE, lambda tt=tt, pzz=pzz: nc.vector.tensor_copy(out=gates[:, tt, :], in_=pzz[:, 256:280]), reads=[pzb], writes=[vtok_b[tt]])
                    else:
                        k.op(ACT, lambda tt=tt, pzz=pzz: nc.scalar.activation(out=vs_aug[:, tt, :, 0:64], in_=pzz[:, 0:128].rearrange("p (g d) -> p g d", g=2), func=AF.Copy), reads=[pzb], writes=[vtok_b[tt]])
                        k.op(ACT, lambda tt=tt, pzz=pzz: nc.scalar.activation(out=vw_aug[:, tt, :, 0:64], in_=pzz[:, 128:256].rearrange("p (g d) -> p g d", g=2), func=AF.Copy), reads=[pzb], writes=[vtok_b[tt]])
                        k.op(ACT, lambda tt=tt, pzz=pzz: nc.scalar.activation(out=gates[:, tt, :], in_=pzz[:, 256:280], func=AF.Copy), reads=[pzb], writes=[vtok_b[tt]])
            k.op(ACT, lambda: nc.scalar.activation(out=gates[:], in_=gates[:], func=AF.Sigmoid), reads=vtok_b, writes=vtok_b)

        if upto <= 1:
            k.drain()
            return nc
        k.barrier()
        with ExitStack() as p2:
            cw = sb("cw", [128, 4, 4], F32, p2); cb_ = sb("cb", [128, 4], F32, p2)
            bat = sb("bat", [128, 4], F32, p2); bxt = sb("bxt", [128, 4], F32, p2)
            lam = sb("lam", [128, 4], F32, p2); cc1 = sb("cc1", [128, 4], F32, p2); cc2 = sb("cc2", [128, 4], F32, p2)
            prm_b = Buf("rnnprm")
            pds = k.dsem("rp")
            for (t, d) in ((cw, convw_d), (cb_, convb_d), (bat, ba_d), (bxt, bx_d), (lam, lam_d)):
                k.dma(SP, pds, t[:], d, writes=[prm_b])
            k.op(ACT, lambda: nc.scalar.activation(out=cc1[:], in_=lam[:], func=AF.Exp, scale=-1.0), reads=[prm_b], writes=[prm_b])
            k.op(ACT, lambda: nc.scalar.activation(out=cc1[:], in_=cc1[:], func=AF.Ln, bias=1.0), reads=[prm_b], writes=[prm_b])
            k.op(DVE, lambda: nc.vector.tensor_scalar(out=cc2[:], in0=cc1[:], scalar1=-16.0, scalar2=None, op0=ALU.mult), reads=[prm_b], writes=[prm_b])
            k.op(DVE, lambda: nc.vector.tensor_scalar(out=cc1[:], in0=cc1[:], scalar1=-8.0, scalar2=None, op0=ALU.mult), reads=[prm_b], writes=[prm_b])
            wbd_a = sb("wbd_a", [128, 4, 128], BF16, p2); wbd_x = sb("wbd_x", [128, 4, 128], BF16, p2)
            wbd_b = Buf("wbd")
            k.op(DVE, lambda: nc.vector.memset(wbd_a[:], 0.0), writes=[wbd_b])
            k.op(DVE, lambda: nc.vector.memset(wbd_x[:], 0.0), writes=[wbd_b])
            for (t, d) in ((wbd_a, wa_d), (wbd_x, wx_d)):
                dv = d.rearrange("(f two) i j -> two i f j", two=2)
                for hh in range(2):
                    k.dma(POOL, k.dsem("wbd"), t[hh * 64:(hh + 1) * 64, :, hh * 64:(hh + 1) * 64], dv[hh], writes=[wbd_b])
            NP = 4
            PW = S // NP
            xrR = Rot(k, p2, nc, "xr", 2, [128, S], F32)
            ggR = Rot(k, p2, nc, "gg", 2, [128, S], F32)
            xc = sb("xc", [128, S], F32, p2); xcb = sb("xcb", [128, S], BF16, p2)
            rr = sb("rr", [128, S], F32, p2); ii = sb("ii", [128, S], F32, p2); a2 = sb("a2", [128, S], F32, p2)
            hh_ = sb("hh", [128, S], F32, p2)
            rnb = sb("rnb", [128, S], BF16, p2); sqb = sb("sqb", [128, S], BF16, p2)
            xc_b, xcb_b, rr_b, ii_b, a2_b, hh_b, rnb_b, sqb_b = ([Buf(f"{n}{p}") for p in range(NP)] for n in ("xc", "xcb", "rr", "ii", "a2", "hh", "rnb", "sqb"))
            rn_ds = [k.dsem("rn") for _ in range(2)]
            pg = [pst(f"pg{i}", [128, 512], F32, p2) for i in range(3)]
            pg_b = [Buf(f"pg{i}") for i in range(3)]
            pstat = pst("pstat", [128, 512], F32, p2)
            pstat_b = Buf("pstat")
            gi = [0]
            gg_parts = [[Buf(f"ggs{sl}_{p}") for p in range(NP)] for sl in range(2)]

            def load_ft(ft):
                xr, xr_b, xr_ds = xrR.next()
                gg, _, gg_ds = ggR.next()
                gg_bp = gg_parts[ft % 2]
                k.dma(SP, xr_ds, xr[:], zr_d[512 + ft * 128:512 + (ft + 1) * 128, :], reads=[DB((id(zr_d), 512 + ft * 128, ch)) for ch in range(NCH)], writes=[xr_b])
                k.dma(SP, gg_ds, gg[:], zr_d[ft * 128:(ft + 1) * 128, :], reads=[DB((id(zr_d), ft * 128, ch)) for ch in range(NCH)], writes=gg_bp)
                return xr, xr_b, gg, gg_bp

            def cs(p):
                return p * PW, (p + 1) * PW

            def front(ft, xr, xr_b, gg, gg_b):
                for p in range(NP):
                    c0, c1 = cs(p)
                    k.op(DVE, lambda c0=c0, c1=c1: nc.vector.tensor_scalar(out=xc[:, c0:c1], in0=xr[:, c0:c1], scalar1=cw[:, ft, 3:4], scalar2=cb_[:, ft:ft + 1], op0=ALU.mult, op1=ALU.add),
                         reads=[xr_b, prm_b], writes=[xc_b[p]])
                    for sh in (1, 2, 3):
                        lo = max(c0, sh)
                        k.op(DVE, lambda lo=lo, c1=c1, sh=sh: nc.vector.scalar_tensor_tensor(out=xc[:, lo:c1], in0=xr[:, lo - sh:c1 - sh], scalar=cw[:, ft, 3 - sh:4 - sh],
                                                                                            in1=xc[:, lo:c1], op0=ALU.mult, op1=ALU.add),
                             reads=[xr_b, prm_b, xc_b[p]], writes=[xc_b[p]])
                    k.op(ACT, lambda c0=c0, c1=c1: nc.scalar.activation(out=xcb[:, c0:c1], in_=xc[:, c0:c1], func=AF.Copy), reads=[xc_b[p]], writes=[xcb_b[p]])
                    k.op(ACT, lambda c0=c0, c1=c1: nc.scalar.activation(out=gg[:, c0:c1], in_=gg[:, c0:c1], func=AF.Gelu_apprx_tanh), reads=[gg_b[p]], writes=[gg_b[p]])

            def rnn_gates(ft):
                for p in range(NP):
                    c0, c1 = cs(p)
                    for (wt, bt, dst, dstb) in ((wbd_a, bat, rr, rr_b), (wbd_x, bxt, ii, ii_b)):
                        for cc in range(c0, c1, 512):
                            pgg = pg[gi[0] % 3]; pgb = pg_b[gi[0] % 3]; gi[0] += 1
                            k.op(PE, lambda wt=wt, cc=cc, pgg=pgg: nc.tensor.matmul(pgg[:, :], lhsT=wt[:, ft, :], rhs=xcb[:, cc:cc + 512], start=True, stop=True),
                                 reads=[wbd_b, xcb_b[p]], writes=[pgb])
                            k.op(ACT, lambda bt=bt, cc=cc, pgg=pgg, dst=dst: nc.scalar.activation(out=dst[:, cc:cc + 512], in_=pgg[:, :], func=AF.Sigmoid, bias=bt[:, ft:ft + 1]),
                                 reads=[pgb, prm_b], writes=[dstb[p]])

            def exps_m(ft):
                for p in range(NP):
                    c0, c1 = cs(p)
                    k.op(ACT, lambda c0=c0, c1=c1: nc.scalar.activation(out=a2[:, c0:c1], in_=rr[:, c0:c1], func=AF.Exp, scale=cc2[:, ft:ft + 1]), reads=[rr_b[p], prm_b], writes=[a2_b[p]])
                    k.op(ACT, lambda c0=c0, c1=c1: nc.scalar.activation(out=rr[:, c0:c1], in_=rr[:, c0:c1], func=AF.Exp, scale=cc1[:, ft:ft + 1]), reads=[rr_b[p], prm_b], writes=[rr_b[p]])
                    k.op(DVE, lambda c0=c0, c1=c1: nc.vector.tensor_tensor(out=ii[:, c0:c1], in0=ii[:, c0:c1], in1=xc[:, c0:c1], op=ALU.mult), reads=[ii_b[p], xc_b[p]], writes=[ii_b[p]])

            def sqrts(ft):
                for p in range(NP):
                    c0, c1 = cs(p)
                    k.op(ACT, lambda c0=c0, c1=c1: nc.scalar.activation(out=a2[:, c0:c1], in_=a2[:, c0:c1], func=AF.Sqrt, scale=-1.0, bias=1.0), reads=[a2_b[p]], writes=[a2_b[p]])

            def tail(ft, gg, gg_b):
                for p in range(NP):
                    c0, c1 = cs(p)
                    k.op(DVE, lambda c0=c0, c1=c1: nc.vector.tensor_tensor(out=a2[:, c0:c1], in0=a2[:, c0:c1], in1=ii[:, c0:c1], op=ALU.mult), reads=[a2_b[p], ii_b[p]], writes=[a2_b[p]])
                    init = 0.0 if p == 0 else hh_[:, c0 - 1:c0]
                    k.op(DVE, lambda c0=c0, c1=c1, init=init: nc.vector.tensor_tensor_scan(out=hh_[:, c0:c1], data0=rr[:, c0:c1], data1=a2[:, c0:c1], initial=init, op0=ALU.mult, op1=ALU.add),
                         reads=[rr_b[p], a2_b[p]] + ([hh_b[p - 1]] if p else []), writes=[hh_b[p]])
                    k.op(DVE, lambda c0=c0, c1=c1: nc.vector.tensor_tensor(out=rnb[:, c0:c1], in0=gg[:, c0:c1], in1=hh_[:, c0:c1], op=ALU.mult), reads=[gg_b[p], hh_b[p]], writes=[rnb_b[p]])
                    k.op(DVE, lambda c0=c0, c1=c1: nc.vector.tensor_tensor(out=sqb[:, c0:c1], in0=rnb[:, c0:c1], in1=rnb[:, c0:c1], op=ALU.mult), reads=[rnb_b[p]], writes=[sqb_b[p]])
                    for cc in range(c0, c1, 512):
                        ch = cc // 512
                        k.dma(POOL, rn_ds[ch % 2], rnn_d[ft * 128:(ft + 1) * 128, cc:cc + 512], rnb[:, cc:cc + 512], reads=[rnb_b[p]], writes=[DB(("rnn", ch))])
                    for tt in range(c0 // 128, c1 // 128):
                        k.op(PE, lambda tt=tt: nc.tensor.matmul(pstat[:, tt * 4 + ft: tt * 4 + ft + 1], lhsT=sqb[:, tt * 128:(tt + 1) * 128], rhs=ones_bf[:, 0:1], start=True, stop=True),
                             reads=[sqb_b[p], ones_b], writes=[pstat_b])

            cur_ft = load_ft(0)
            front(0, *cur_ft)
            for ft in range(4):
                xr, xr_b, gg, gg_b = cur_ft
                if ft + 1 < 4:
                    nxt_ft = load_ft(ft + 1)
                rnn_gates(ft)
                exps_m(ft)
                sqrts(ft)
                if ft + 1 < 4:
                    front(ft + 1, *nxt_ft)
                tail(ft, gg, gg_b)
                if ft + 1 < 4:
                    cur_ft = nxt_ft
            k.op(DVE, lambda: nc.vector.tensor_reduce(out=ssr[:], in_=pstat[:, 0:NT * 4].rearrange("p (t f) -> p t f", f=4), axis=mybir.AxisListType.X, op=ALU.add),
                 reads=[pstat_b], writes=[ssr_b])
            k.op(ACT, lambda: nc.scalar.activation(out=ssq[:], in_=ssr[:], func=AF.Sqrt, scale=1.0 / 512, bias=EPS), reads=[ssr_b], writes=[ssr_b])
            k.op(DVE, lambda: nc.vector.reciprocal(out=ssr[:], in_=ssq[:]), reads=[ssr_b], writes=[ssr_b])

        if upto <= 2:
            k.drain()
            return nc
        k.barrier()
        with ExitStack() as p4:
            kcT = [sb(f"kcT{g}", [128, 256], BF16, p4) for g in range(2)]
            cllo = sb("cllo", [128, 2, 8], F32, p4)
            vc_aug = [sb(f"vca{g}", [128, 2, 128], BF16, p4) for g in range(2)]
            cmp_b = Buf("cmp")
            cds = k.dsem("cmpc")
            k.dma(SP, cds, cllo[:], cllo_d, writes=[cmp_b])
            for g in range(2):
                k.dma(SP, cds, kcT[g][64:128, :], ce01_d, writes=[cmp_b])
                k.dma(SP, cds, vc_aug[g][:, :, 65:128], ovl_d, writes=[cmp_b])
                k.op(DVE, lambda g=g: nc.vector.memset(vc_aug[g][:, :, 64:65], 1.0), writes=[cmp_b])
                k.op(DVE, lambda g=g: nc.vector.memset(vc_aug[g][:, :, 0:64], 0.0), writes=[cmp_b])
            kTs = [sb(f"kTs{g}", [128, S], BF16, p4) for g in range(2)]
            kTw = [sb(f"kTw{g}", [128, S], BF16, p4) for g in range(2)]
            kds_shared = k.dsem("kT")
            cds_shared = k.dsem("cst")
            kTs_b = [Buf(f"kTs{g}") for g in range(2)]
            kTw_b = [Buf(f"kTw{g}") for g in range(2)]
            cmask = sb("cmask", [128, 2, S], BF16, p4)
            tab = sb("tab", [128, S], BF16, p4)
            sllo = sb("sllo", [128, 8], F32, p4)
            wm01 = sb("wm01", [128, 8, 512], BF16, p4)
            addm = sb("addm", [128, NT, 64], F32, p4)
            cmask_b, tab_b, sllo_b, wm01_b, addm_b = (Buf(n) for n in ("cmask", "tab", "sllo", "wm01", "addm"))
            wout = sb("wout", [128, 8, D], BF16, p4)
            wout_b = Buf("wout")
            gcol = sb("gcol", [128, 8], F32, p4)
            gcol_b = Buf("gcol")
            gds = k.dsem("gcol")
            pw = ExitStack()
            wst = Rot(k, pw, nc, "wst", 2, [128, D], F32)

            def issue_resident_loads(dep):
                for g in range(2):
                    for (t, tb_, src) in ((kTs[g], kTs_b[g], ks_d), (kTw[g], kTw_b[g], kw_d)):
                        k.dma(SP, kds_shared, t[0:64, :], src[g * 64:(g + 1) * 64, :], reads=[DB((id(src), 0, ch)) for ch in range(NCH)] + dep, writes=[tb_])
                        k.dma(SP, kds_shared, t[64:128, :], e01_d, writes=[tb_])
                k.dma(SP, cds_shared, cmask[:], cmask_d, writes=[cmask_b])
                k.dma(SP, cds_shared, tab[64:128, :], tab_d, writes=[tab_b])
                k.dma(SP, cds_shared, sllo[:], sllo_d, writes=[sllo_b])
                k.dma(SP, cds_shared, wm01[:], wm01_d, writes=[wm01_b])
                k.dma(SP, cds_shared, addm[:], addm_d, writes=[addm_b])
                k.dma(SP, gds, gcol[:, 0:4], grnn_d, writes=[gcol_b])
                k.dma(SP, gds, gcol[:, 4:8], gatt_d, writes=[gcol_b])
                for kk in range(8):
                    t, tb, ds = wst.next()
                    k.dma(SP, ds, t[:], wout_d[kk * 128:(kk + 1) * 128, :], writes=[tb])
                    k.op(DVE, lambda kk=kk, t=t: nc.vector.tensor_scalar(out=wout[:, kk, :], in0=t[:], scalar1=gcol[:, kk:kk + 1], scalar2=None, op0=ALU.mult),
                         reads=[tb, gcol_b], writes=[wout_b])

            with ExitStack() as p3:
                w1 = [sb(f"w1_{i}", [64, 32, 256], BF16, p3) for i in range(2)]
                w2 = [sb(f"w2_{i}", [128, 2, 64], BF16, p3) for i in range(2)]
                pos = [sb(f"pos_{i}", [64, 32], BF16, p3) for i in range(2)]
                posf = [sb(f"posf_{i}", [64, 32], F32, p3) for i in range(2)]
                cw_b = Buf("cmpw")
                cw1_b = [[Buf(f"cw1_{i}_{q}") for q in range(4)] for i in range(2)]
                cwd = k.dsem("cmpw")
                cwd2 = k.dsem("cmpp")
                for i, (w1d, w2d, pd) in enumerate(((w1k_d, w2k_d, posk_d), (w1v_d, w2v_d, posv_d))):
                    w1v = w1d.rearrange("(l d) h -> d l h", d=64)
                    for l0 in range(0, 32, 8):
                        k.dma(POOL, k.dsem("w1"), w1[i][:, l0:l0 + 8, :], w1v[:, l0:l0 + 8, :], writes=[cw1_b[i][l0 // 8]])
                    k.dma(POOL, cwd, w2[i][:], w2d.rearrange("(t p) d -> p t d", p=128), writes=[cw_b])
                    k.dma(SP, cwd2, posf[i][:], pd, writes=[cw_b])
                    k.op(DVE, lambda i=i: nc.vector.tensor_copy(out=pos[i][:], in_=posf[i][:]), reads=[cw_b], writes=[cw_b])
                cin = [[sb(f"cin{i}{g}", [64, S], BF16, p3) for g in range(2)] for i in range(2)]
                cin_bs = [[Buf(f"cin{i}{g}") for g in range(2)] for i in range(2)]
                for i, src in enumerate((kc_d, vc_d)):
                    cin_ds = k.dsem("cin")
                    for g in range(2):
                        k.dma(SP, cin_ds, cin[i][g][:], src[g * 64:(g + 1) * 64, :], reads=[DB((id(src), 0, ch)) for ch in range(NCH)], writes=[cin_bs[i][g]])
                hid = sb("hid", [128, 2, 256], BF16, p3)
                hid_b = Buf("hid")
                cbias = sb("cbias", [128, 2, 2], F32, p3)
                cbias_b = Buf("cbias")
                pc_ = [pst(f"pc{i}", [128, 512], F32, p3) for i in range(3)]
                pc_b = [Buf(f"pc{i}") for i in range(3)]
                ci = 0
                for i in range(2):
                    for ht in range(2):
                        pcc = pc_[ci % 3]; pcb = pc_b[ci % 3]; ci += 1
                        for l in range(32):
                            k.op(PE, lambda i=i, ht=ht, l=l, pcc=pcc: nc.tensor.matmul(pcc[:, 0:1], lhsT=w1[i][:, l, ht * 128:(ht + 1) * 128], rhs=pos[i][:, l:l + 1],
                                                                                      start=(l == 0), stop=(l == 31)), reads=[cw_b, cw1_b[i][l // 8]], writes=[pcb])
                        k.op(DVE, lambda i=i, ht=ht, pcc=pcc: nc.vector.tensor_copy(out=cbias[:, i, ht:ht + 1], in_=pcc[:, 0:1]), reads=[pcb], writes=[cbias_b])
                for i in range(2):
                    for g in range(2):
                        cv = cin[i][g][:].rearrange("p (c r) -> p c r", r=16)
                        for ht in range(2):
                            pcc = pc_[ci % 3]; pcb = pc_b[ci % 3]; ci += 1
                            for l in range(32):
                                rhs = cv[:, 0:255, l] if l < 16 else cv[:, 1:256, l - 16]
                                k.op(PE, lambda i=i, ht=ht, l=l, pcc=pcc, rhs=rhs: nc.tensor.matmul(pcc[:, 0:255], lhsT=w1[i][:, l, ht * 128:(ht + 1) * 128], rhs=rhs,
                                                                                                   start=(l == 0), stop=(l == 31)), reads=[cw_b, cw1_b[i][l // 8], cin_bs[i][g]], writes=[pcb])
                            k.op(ACT, lambda i=i, ht=ht, pcc=pcc: nc.scalar.activation(out=hid[:, ht, 0:255], in_=pcc[:, 0:255], func=AF.Gelu_apprx_tanh, bias=cbias[:, i, ht:ht + 1]),
                                 reads=[pcb, cbias_b], writes=[hid_b])
                            if i == 0 and g == 0 and ht == 0:
                                issue_resident_loads([hid_b])
                        if i == 0:
                            pcc = pc_[ci % 3]; pcb = pc_b[ci % 3]; ci += 1
                            for ht in range(2):
                                k.op(PE, lambda ht=ht, pcc=pcc: nc.tensor.matmul(pcc[0:64, 0:255], lhsT=w2[0][:, ht, :], rhs=hid[:, ht, 0:255], start=(ht == 0), stop=(ht == 1)),
                                     reads=[cw_b, hid_b], writes=[pcb])
                            k.op(DVE, lambda g=g, pcc=pcc: nc.vector.tensor_copy(out=kcT[g][0:64, 0:255], in_=pcc[0:64, 0:255]), reads=[pcb], writes=[cmp_b])
                        else:
                            for ct in range(2):
                                nr = 128 if ct == 0 else 127
                                pcc = pc_[ci % 3]; pcb = pc_b[ci % 3]; ci += 1
                                for ht in range(2):
                                    k.op(PE, lambda ht=ht, ct=ct, nr=nr, pcc=pcc: nc.tensor.matmul(pcc[0:nr, 0:64], lhsT=hid[:, ht, ct * 128:ct * 128 + nr], rhs=w2[1][:, ht, :],
                                                                                                  start=(ht == 0), stop=(ht == 1)), reads=[cw_b, hid_b], writes=[pcb])
                                k.op(DVE, lambda g=g, ct=ct, nr=nr, pcc=pcc: nc.vector.tensor_copy(out=vc_aug[g][0:nr, ct, 0:64], in_=pcc[0:nr, 0:64]), reads=[pcb], writes=[cmp_b])

            if upto <= 3:
                pw.close()
                k.drain()
                return nc
            k.barrier()
            pw.close()
            qs = Rot(k, p4, nc, "qs", 1, [128, 8, 512], BF16)
            qw = Rot(k, p4, nc, "qw", 2, [128, 8, 512], BF16)
            SLOPES = [2.0 ** (-(h + 1)) for h in range(8)]
            PT = Rot(k, p4, nc, "PT", 6, [128, 512], BF16, with_dsem=False)
            PTc = Rot(k, p4, nc, "PTc", 2, [128, 512], BF16, with_dsem=False)
            att = sb("att", [128, 4, 512], F32, p4)
            att_b = [Buf(f"att{i}") for i in range(4)]
            attb = sb("attb", [128, 4, 512], BF16, p4)
            attb_b = Buf("attb")
            attT = sb("attT", [128, 4, 512], BF16, p4)
            attT_b = Buf("attT")
            rnT = Rot(k, p4, nc, "rnT", 3, [128, 4, 512], BF16)
            imp = sb("imp", [128, 4, 2, 64], F32, p4)
            imp_b = [Buf(f"imp{i}") for i in range(4)]
            for i_ in range(4):
                k.op(DVE, lambda i_=i_: nc.vector.memset(imp[:, i_, :, :], 0.0), writes=[imp_b[i_]])
            selbT = [sb(f"selbT{g}", [128, 512], BF16, p4) for g in range(2)]
            selbT_b = [Buf(f"selbT{g}") for g in range(2)]
            sm = sb("sm", [128, 256], F32, p4)
            sm_b = [Buf(f"sm{i}") for i in range(16)]
            smi = [0]
            tk_a = sb("tk_a", [128, 8, 64], F32, p4); tk_b_ = sb("tk_b", [128, 8, 64], F32, p4)
            tk8 = sb("tk8", [128, 8, 16], F32, p4); tksel = sb("tksel", [128, 8, 64], BF16, p4)
            tk_bufs = [Buf(f"tk{u}") for u in range(8)]
            ssa = sb("ssa", [128, 8], F32, p4)
            ssa_b = Buf("ssa")
            ysb = Rot(k, p4, nc, "ysb", 2, [128, D], F32)
            xres = Rot(k, p4, nc, "xres", 2, [128, D], F32)
            junk4 = sb("junk4", [128, D], BF16, p4)
            junk4_b = Buf("junk4")
            print("P4 sbuf bytes remaining:", nc.sbuf_bytes_remaining)
            k.barrier()
            pS = [pst(f"pS{i}", [128, 512], F32, p4) for i in range(3)]
            pS_b = [Buf(f"pS{i}") for i in range(3)]
            pA = [pst(f"pA{i}", [128, 512], F32, p4) for i in range(2)]
            pA_b = [Buf(f"pA{i}") for i in range(2)]
            pTT = pst("pTT", [128, 1024], BF16, p4)
            pTT_b = Buf("pTT")
            pW = [pst(f"pW{i}", [128, 512], F32, p4) for i in range(1)]
            pW_b = [Buf(f"pW{i}") for i in range(1)]
            pC = pst("pC", [128, 512], F32, p4)
            pC_b = Buf("pC")
            cnt = {"s": 0, "a": 0}

            def nextS():
                i = cnt["s"] % 3; cnt["s"] += 1
                return pS[i], pS_b[i]

            def nextA():
                i = cnt["a"] % 2; cnt["a"] += 1
                return pA[i], pA_b[i]

            def small():
                i = smi[0] % 16; smi[0] += 1
                return sm[:, 16 * i:16 * i + 16], sm_b[i]

            att_written = set()

            def evac_group(pa, pab, stride, n, h, tl0, tt0, gate_idx, first, with_imp=False, g=None, first_in_group=False):
                s4, s4b = small()
                sums = pa[:, 64:64 + (n - 1) * stride + 1:stride]
                k.op(DVE, lambda: nc.vector.tensor_scalar(out=s4[:, 0:n], in0=sums, scalar1=1e-30, scalar2=None, op0=ALU.max), reads=[pab], writes=[s4b])
                k.op(DVE, lambda: nc.vector.reciprocal(out=s4[:, 4:4 + n], in_=s4[:, 0:n]), reads=[s4b], writes=[s4b])
                k.op(DVE, lambda: nc.vector.tensor_tensor(out=s4[:, 8:8 + n], in0=s4[:, 4:4 + n], in1=gates[:, tt0:tt0 + n, gate_idx], op=ALU.mult),
                     reads=[s4b] + [vtok_b[tt0 + i] for i in range(n)], writes=[s4b])
                for i in range(n):
                    tl = tl0 + i
                    dst = att[:, tl, h * 64:(h + 1) * 64]
                    src = pa[:, i * stride:i * stride + 64]
                    first = (h, tl) not in att_written
                    att_written.add((h, tl))
                    if first:
                        k.op(DVE, lambda dst=dst, src=src, i=i: nc.vector.tensor_scalar(out=dst, in0=src, scalar1=s4[:, 8 + i:9 + i], scalar2=None, op0=ALU.mult),
                             reads=[pab, s4b], writes=[att_b[tl]])
                    else:
                        k.op(DVE, lambda dst=dst, src=src, i=i: nc.vector.scalar_tensor_tensor(out=dst, in0=src, scalar=s4[:, 8 + i:9 + i], in1=dst, op0=ALU.mult, op1=ALU.add),
                             reads=[pab, s4b, att_b[tl]], writes=[att_b[tl]])
                    if with_imp:
                        idst = imp[:, tl, g, 1:64]
                        isrc = pa[:, i * stride + 65:i * stride + 128]
                        if first_in_group:
                            k.op(DVE, lambda idst=idst, isrc=isrc, i=i: nc.vector.tensor_scalar(out=idst, in0=isrc, scalar1=s4[:, 4 + i:5 + i], scalar2=None, op0=ALU.mult),
                                 reads=[pab, s4b], writes=[imp_b[tl]])
                        else:
                            k.op(DVE, lambda idst=idst, isrc=isrc, i=i: nc.vector.scalar_tensor_tensor(out=idst, in0=isrc, scalar=s4[:, 4 + i:5 + i], in1=idst, op0=ALU.mult, op1=ALU.add),
                                 reads=[pab, s4b, imp_b[tl]], writes=[imp_b[tl]])

            dbg_ds = k.dsem("dbg")

            qview = q_d.rearrange("(h d) t -> d h t", d=64)

            def load_chunk(jn):
                tn = jn * 512
                rt_, rtb_, rds_ = rnT.next()
                k.dma(SP, rds_, rt_[:], rnn_d[:, tn:tn + 512].rearrange("(f p) t -> p f t", p=128), reads=[DB(("rnn", jn))], writes=[rtb_])
                qw_, qwb_, qwd_ = qw.next()
                k.dma(SP, qwd_, qw_[0:64, :, :], qview[:, :, tn:tn + 512], reads=[DB((id(q_d), c4 * 128, jn)) for c4 in range(4)], writes=[qwb_])
                for h_ in range(8):
                    k.op(DVE, lambda h_=h_, qw_=qw_: nc.vector.tensor_scalar(out=qw_[64:128, h_, :], in0=tab[64:128, tn:tn + 512], scalar1=SLOPES[h_], scalar2=None, op0=ALU.mult),
                         reads=[tab_b], writes=[qwb_])
                return rt_, rtb_, qw_, qwb_

            def load_qs(jn):
                tn = jn * 512
                qs_, qsb_, qsd_ = qs.next()
                k.dma(SP, qsd_, qs_[0:64, :, :], qview[:, :, tn:tn + 512], reads=[DB((id(q_d), c4 * 128, jn)) for c4 in range(4)], writes=[qsb_])
                return qs_, qsb_

            nxt = load_chunk(0)
            nxt_qs = load_qs(0)
            wcnt = [0]

            def nextW():
                return pW[0], pW_b[0]

            def make_cback(jc, rt, rtb):
                units = []

                def u_tr(half):
                    for tl2 in range(2):
                        tl = half * 2 + tl2
                        for f in range(4):
                            k.op(PE, lambda tl=tl, tl2=tl2, f=f: nc.tensor.transpose(out=pTT[:, (tl2 * 4 + f) * 128:(tl2 * 4 + f + 1) * 128], in_=attb[:, tl, f * 128:(f + 1) * 128], identity=ident[:]),
                                 reads=[attb_b, ident_b], writes=[pTT_b])
                    for tl2 in range(2):
                        tl = half * 2 + tl2
                        k.op(DVE, lambda tl=tl, tl2=tl2: nc.vector.tensor_copy(out=attT[:, :, tl * 128:(tl + 1) * 128],
                                                                             in_=pTT[:, tl2 * 512:(tl2 + 1) * 512].rearrange("p (f t) -> p f t", f=4)),
                             reads=[pTT_b], writes=[attT_b])
                units.append(lambda: u_tr(0))
                units.append(lambda: u_tr(1))
                st = {}

                def u_mm(tl, half, part):
                    tt = jc * 4 + tl
                    if half == 0 and part == 0:
                        yt, ytb, yds = ysb.next()
                        xr_, xrb, xds = xres.next()
                        k.dma(SP, xds, xr_[:], x_d[tt * 128:(tt + 1) * 128, :], writes=[xrb])
                        st[tl] = (yt, ytb, yds, xr_, xrb)
                    yt, ytb, yds, xr_, xrb = st[tl]
                    if part == 0:
                        st[(tl, half)] = nextW()
                    pw, pwb = st[(tl, half)]
                    if part == 0:
                        for f in range(4):
                            k.op(PE, lambda f=f: nc.tensor.matmul(pw[:, :], lhsT=rt[:, f, tl * 128:(tl + 1) * 128], rhs=wout[:, f, half * 512:(half + 1) * 512],
                                                                  start=(f == 0), stop=False), reads=[rtb, wout_b], writes=[pwb])
                    else:
                        for f in range(4):
                            k.op(PE, lambda f=f: nc.tensor.matmul(pw[:, :], lhsT=attT[:, f, tl * 128:(tl + 1) * 128], rhs=wout[:, 4 + f, half * 512:(half + 1) * 512],
                                                                  start=False, stop=(f == 3)), reads=[attT_b, wout_b], writes=[pwb])
                        k.op(DVE, lambda: nc.vector.tensor_scalar(out=yt[:, half * 512:(half + 1) * 512], in0=pw[:, :], scalar1=ssr[:, tt:tt + 1], scalar2=None, op0=ALU.mult),
                             reads=[pwb, ssr_b], writes=[ytb])

                def u_epi(tl):
                    tt = jc * 4 + tl
                    yt, ytb, yds, xr_, xrb = st[tl]
                    if debug:
                        k.dma(POOL, dbg_ds, dbg["d_y"][tt * 128:(tt + 1) * 128, :], yt[:], reads=[ytb], writes=[DB(("dy", tt))])
                    s4, s4b = small()
                    k.op(ACT, lambda: nc.scalar.activation(out=junk4[:], in_=yt[:], func=AF.Square, accum_out=s4[:, 0:1]), reads=[ytb], writes=[junk4_b, s4b])
                    k.op(ACT, lambda: nc.scalar.activation(out=s4[:, 1:2], in_=s4[:, 0:1], func=AF.Ln, scale=1.0 / D, bias=EPS), reads=[s4b], writes=[s4b])
                    k.op(ACT, lambda: nc.scalar.activation(out=s4[:, 2:3], in_=s4[:, 1:2], func=AF.Exp, scale=-0.5), reads=[s4b], writes=[s4b])
                    k.op(DVE, lambda: nc.vector.scalar_tensor_tensor(out=yt[:], in0=yt[:], scalar=s4[:, 2:3], in1=C1row[:], op0=ALU.mult, op1=ALU.mult),
                         reads=[ytb, s4b, C1_b], writes=[ytb])
                    k.op(DVE, lambda: nc.vector.tensor_tensor(out=yt[:], in0=yt[:], in1=xr_[:], op=ALU.add), reads=[ytb, xrb], writes=[ytb])
                    k.dma(POOL, yds, x1_d[tt * 128:(tt + 1) * 128, :], yt[:], reads=[ytb], writes=[DB(("x1", tt))])

                for tl in range(4):
                    for half in range(2):
                        for part in range(2):
                            units.append(lambda tl=tl, half=half, part=part: u_mm(tl, half, part))
                    units.append(lambda tl=tl: u_epi(tl))
                return units

            cback = []
            for j in range(NCH):
                t0 = j * 512
                att_written.clear()
                rt, rtb, qwt, qwb = nxt
                qst, qsb = nxt_qs
                if j + 1 < NCH:
                    nxt = load_chunk(j + 1)
                ncts = [ct for ct in range(2) if 16 * (ct * 128) + 31 <= t0 + 511]

                def cmp_S(h):
                    g = h // 4
                    pts = []
                    for ct in ncts:
                        nr = 128 if ct == 0 else 127
                        ps, psb = nextS()
                        k.op(PE, lambda ps=ps, ct=ct, nr=nr, g=g, h=h: nc.tensor.matmul(ps[0:nr, :], lhsT=kcT[g][:, ct * 128:ct * 128 + nr], rhs=qwt[:, h, :], start=True, stop=False),
                             reads=[cmp_b, qwb], writes=[psb])
                        k.op(PE, lambda ps=ps, ct=ct, nr=nr: nc.tensor.matmul(ps[0:nr, :], lhsT=ident[:, 0:nr], rhs=cmask[:, ct, t0:t0 + 512], start=False, stop=True),
                             reads=[ident_b, cmask_b], writes=[psb])
                        pt, ptb, _ = PTc.next()
                        k.op(ACT, lambda ps=ps, pt=pt, nr=nr, ct=ct, h=h: nc.scalar.activation(out=pt[0:nr, :], in_=ps[0:nr, :], func=AF.Exp, bias=cllo[0:nr, ct, h:h + 1]),
                             reads=[psb, cmp_b], writes=[ptb])
                        pts.append((pt, ptb, ct, nr))
                    return pts

                def cmp_PV(h, pts):
                    g = h // 4
                    nmm = 4 * len(pts)
                    mi = 0
                    for tl in range(4):
                        for (pt, ptb, ct, nr) in pts:
                            k.op(PE, lambda pt=pt, tl=tl, nr=nr, ct=ct, g=g, mi=mi, nmm=nmm: nc.tensor.matmul(
                                pC[:, tl * 128:(tl + 1) * 128], lhsT=pt[0:nr, tl * 128:(tl + 1) * 128], rhs=vc_aug[g][0:nr, ct, :],
                                start=(mi == 0), stop=(mi == nmm - 1)),
                                reads=[ptb, cmp_b], writes=[pC_b])
                            mi += 1
                    evac_group(pC, pC_b, 128, 4, h, 0, j * 4, 0 * 8 + h, True, with_imp=True, g=g, first_in_group=(h % 4 == 0))

                pre = []
                cst = {}

                def u_cS(h):
                    cst[h] = cmp_S(h)

                def u_cPV(h):
                    cmp_PV(h, cst[h])
                for h in range(8):
                    pre.append(lambda h=h: u_cS(h))
                    pre.append(lambda h=h: u_cPV(h))
                units8 = [(tl, g) for tl in range(4) for g in range(2)]

                def tk_stage(sidx):
                    for u, (tl, g) in enumerate(units8):
                        tt = j * 4 + tl
                        tb = tk_bufs[u]
                        if sidx == 0:
                            k.op(DVE, lambda u=u, tl=tl, g=g, tt=tt: nc.vector.tensor_tensor(out=tk_a[:, u, :], in0=imp[:, tl, g, :], in1=addm[:, tt, :], op=ALU.add),
                                 reads=[imp_b[tl], addm_b], writes=[tb])
                        elif sidx == 1:
                            k.op(DVE, lambda u=u: nc.vector.max(out=tk8[:, u, 0:8], in_=tk_a[:, u, :]), reads=[tb], writes=[tb])
                        elif sidx == 2:
                            k.op(DVE, lambda u=u: nc.vector.match_replace(out=tk_b_[:, u, :], in_to_replace=tk8[:, u, 0:8], in_values=tk_a[:, u, :], imm_value=-3.0e38), reads=[tb], writes=[tb])
                        elif sidx == 3:
                            k.op(DVE, lambda u=u: nc.vector.max(out=tk8[:, u, 8:16], in_=tk_b_[:, u, :]), reads=[tb], writes=[tb])
                        elif sidx == 4:
                            k.op(DVE, lambda u=u: nc.vector.tensor_scalar(out=tksel[:, u, :], in0=tk_a[:, u, :], scalar1=tk8[:, u, 15:16], scalar2=NEGM, op0=ALU.is_lt, op1=ALU.mult),
                                 reads=[tb], writes=[tb])
                        elif sidx == 5:
                            k.op(PE, lambda u=u, g=g, tl=tl: nc.tensor.transpose(out=pTT[0:64, (g * 4 + tl) * 128:(g * 4 + tl + 1) * 128], in_=tksel[:, u, :], identity=ident[:]),
                                 reads=[tb, ident_b], writes=[pTT_b])
                    if sidx == 6:
                        for g in range(2):
                            if os.environ.get("KDBG_S6") == "act":
                                k.op(ACT, lambda g=g: nc.scalar.activation(out=selbT[g][64:128, :], in_=pTT[0:64, g * 512:(g + 1) * 512], func=AF.Copy), reads=[pTT_b], writes=[selbT_b[g]])
                            else:
                                k.op(DVE, lambda g=g: nc.vector.tensor_copy(out=selbT[g][64:128, :], in_=pTT[0:64, g * 512:(g + 1) * 512]), reads=[pTT_b], writes=[selbT_b[g]])
                    if sidx == 7:
                        for h_ in range(8):
                            k.op(DVE, lambda h_=h_: nc.vector.tensor_tensor(out=qst[64:128, h_, :], in0=qwt[64:128, h_, :], in1=selbT[h_ // 4][64:128, :], op=ALU.add),
                                 reads=[qwb, selbT_b[h_ // 4]], writes=[qsb])
                for sidx in range(8):
                    pre.append(lambda sidx=sidx: tk_stage(sidx))

                tasks = []
                for br in (2, 1):
                    for h in range(8):
                        kts = list(range(0, 4 * j + 4)) if br == 1 else list(range(max(0, 4 * j - 4), 4 * j + 4))
                        grp = {"h": h, "br": br, "g": h // 4, "pa": None, "npv": 0, "done": 0}
                        for kt in kts:
                            tls = [tl for tl in range(4) if kt <= 4 * j + tl and (br == 1 or kt >= 4 * j + tl - 4)]
                            grp["npv"] += len(tls)
                            tasks.append({"grp": grp, "kt": kt, "tls": tls})
                n_win = sum(1 for tk in tasks if tk["grp"]["br"] == 2)

                def emit_S(tk):
                    grp = tk["grp"]; h = grp["h"]; br = grp["br"]; g = grp["g"]; kt = tk["kt"]
                    kT = kTs[g] if br == 1 else kTw[g]
                    qq, qqb = (qst, qsb) if br == 1 else (qwt, qwb)
                    ps, psb = nextS()
                    m = (kt - (4 * j - 4)) if br == 2 else (4 + kt - 4 * j)
                    use_msk = (br == 2) or (m >= 4)
                    c0, c1 = 0, 512
                    if use_msk:
                        if m < 4:
                            c1 = 128 * (m + 1)
                        else:
                            c0 = 128 * (m - 4)
                    k.op(PE, lambda: nc.tensor.matmul(ps[:, c0:c1], lhsT=kT[:, kt * 128:(kt + 1) * 128], rhs=qq[:, h, c0:c1], start=True, stop=True),
                         reads=[(kTs_b[g] if br == 1 else kTw_b[g]), qqb], writes=[psb])
                    pt, ptb, _ = PT.next()
                    k.op(ACT, lambda: nc.scalar.activation(out=pt[:, c0:c1], in_=ps[:, c0:c1], func=AF.Exp, bias=sllo[:, h:h + 1]), reads=[psb, sllo_b], writes=[ptb])
                    if use_msk:
                        k.op(DVE, lambda: nc.vector.tensor_tensor(out=pt[:, c0:c1], in0=pt[:, c0:c1], in1=wm01[:, m, c0:c1], op=ALU.mult), reads=[ptb, wm01_b], writes=[ptb])
                    tk["pt"] = pt; tk["ptb"] = ptb

                def emit_PV(tk):
                    grp = tk["grp"]; h = grp["h"]; br = grp["br"]; g = grp["g"]; kt = tk["kt"]
                    vA = vs_aug if br == 1 else vw_aug
                    if grp["pa"] is None:
                        grp["pa"] = nextA()
                    pa, pab = grp["pa"]
                    pt = tk["pt"]; ptb = tk["ptb"]
                    for tl in tk["tls"]:
                        fm = (grp["done"] == 0)
                        grp["done"] += 1
                        last = (grp["done"] == grp["npv"])
                        k.op(PE, lambda tl=tl, fm=fm, last=last: nc.tensor.matmul(pa[:, tl * 65:(tl + 1) * 65], lhsT=pt[:, tl * 128:(tl + 1) * 128], rhs=vA[:, kt, g, :], start=fm, stop=last),
                             reads=[ptb, vtok_b[kt], vones_b], writes=[pab])
                    if grp["done"] == grp["npv"]:
                        evac_group(pa, pab, 65, 4, h, 0, j * 4, br * 8 + h, False)

                LOOK = 4
                n_sel = len(tasks) - n_win
                pre_every = max(1, n_win // (len(pre) + 1))
                cb_every = max(1, (n_sel - 2) // (len(cback) + 1)) if cback else 1
                for i in range(len(tasks) + LOOK):
                    if i < len(tasks):
                        if i == n_win:
                            while pre:
                                pre.pop(0)()
                        emit_S(tasks[i])
                        if i < n_win:
                            if pre and (i % pre_every == pre_every - 1):
                                pre.pop(0)()
                        else:
                            if cback and ((i - n_win) % cb_every == cb_every - 1):
                                cback.pop(0)()
                    if i - LOOK >= 0:
                        emit_PV(tasks[i - LOOK])
                while cback:
                    cback.pop(0)()
                if debug:
                    for tl in range(4):
                        k.dma(POOL, dbg_ds, dbg["d_att"][(j * 4 + tl) * 128:(j * 4 + tl + 1) * 128, :], att[:, tl, :], reads=[att_b[tl]], writes=[DB(("datt", j, tl))])
                for tl in range(4):
                    k.op(ACT, lambda tl=tl: nc.scalar.activation(out=junk4[:, 0:512], in_=att[:, tl, :], func=AF.Square, accum_out=ssa[:, tl:tl + 1]),
                         reads=[att_b[tl]], writes=[junk4_b, ssa_b])
                k.op(ACT, lambda: nc.scalar.activation(out=ssa[:, 4:8], in_=ssa[:, 0:4], func=AF.Ln, scale=1.0 / 512, bias=EPS), reads=[ssa_b], writes=[ssa_b])
                k.op(ACT, lambda: nc.scalar.activation(out=ssa[:, 4:8], in_=ssa[:, 4:8], func=AF.Exp, scale=-0.5), reads=[ssa_b], writes=[ssa_b])
                k.op(DVE, lambda: nc.vector.tensor_tensor(out=ssa[:, 4:8], in0=ssa[:, 4:8], in1=ssq[:, j * 4:(j + 1) * 4], op=ALU.mult), reads=[ssa_b, ssr_b], writes=[ssa_b])
                for tl in range(4):
                    k.op(DVE, lambda tl=tl: nc.vector.tensor_scalar(out=attb[:, tl, :], in0=att[:, tl, :], scalar1=ssa[:, 4 + tl:5 + tl], scalar2=None, op0=ALU.mult),
                         reads=[att_b[tl], ssa_b], writes=[attb_b])
                cback = make_cback(j, rt, rtb)
                if os.environ.get("KDBG_CB") == "now":
                    while cback:
                        cback.pop(0)()
                if j + 1 < NCH:
                    nxt_qs = load_qs(j + 1)
            while cback:
                cback.pop(0)()

        mid.close()
        if upto <= 4:
            k.drain()
            return nc
        k.barrier()
        with ExitStack() as p5:
            wf1 = sb("wf1", [128, 8, 4 * D], BF16, p5)
            wf2 = sb("wf2", [128, 32, D], BF16, p5)
            wf1_b = [Buf(f"wf1_{i}") for i in range(8)]; wf2_b = [Buf(f"wf2_{i}") for i in range(4)]
            wf1v = wff1_d.rearrange("(k p) n -> p k n", p=128)
            wf2v = wff2_d.rearrange("(k p) n -> p k n", p=128)
            for cb in range(8):
                k.dma(POOL, k.dsem("wf1"), wf1[:, :, cb * 512:(cb + 1) * 512], wf1v[:, :, cb * 512:(cb + 1) * 512], writes=[wf1_b[cb]])
            for hh in range(4):
                k.dma(POOL, k.dsem("wf2"), wf2[:, hh * 8:(hh + 1) * 8, :], wf2v[:, hh * 8:(hh + 1) * 8, :], writes=[wf2_b[hh]])
            CH = 256
            xin = Rot(k, p5, nc, "xin", 4, [128, D], F32)
            xnb = Rot(k, p5, nc, "xnb", 2, [128, D], BF16, with_dsem=False)
            hT2 = Rot(k, p5, nc, "hT2", 2, [128, 8, CH], BF16, with_dsem=False)
            aT = Rot(k, p5, nc, "aT", 1, [128, 32, CH], BF16, with_dsem=False)
            r32 = Rot(k, p5, nc, "r32", 3, [128, CH], F32, with_dsem=False)
            y2 = Rot(k, p5, nc, "y2", 2, [128, D], F32, with_dsem=False)
            ot = Rot(k, p5, nc, "ot", 2, [128, D], F32)
            junk5 = sb("junk5", [128, D], BF16, p5)
            junk5_b = Buf("junk5")
            sm5 = sb("sm5", [128, 64], F32, p5)
            sm5_b = [Buf(f"sm5_{i}") for i in range(16)]
            s5i = [0]
            pT5 = [pst(f"pT5_{i}", [128, 1024], BF16, p5) for i in range(2)]
            pT5_b = [Buf(f"pT5_{i}") for i in range(2)]
            pF = [pst(f"pF{i}", [128, 512], F32, p5) for i in range(2)]
            pF_b = [Buf(f"pF{i}") for i in range(2)]
            pY = [pst(f"pY{i}", [128, 512], F32, p5) for i in range(4)]
            pY_b = [Buf(f"pY{i}") for i in range(4)]
            fi = [0]
            out_bufs = []

            def prologue_a(cj):
                xtiles = []
                nts = []
                for tl in range(2):
                    tt = cj * 2 + tl
                    xt_, xtb, xds = xin.next()
                    k.dma(SP, xds, xt_[:], x1_d[tt * 128:(tt + 1) * 128, :], reads=[DB(("x1", tt))], writes=[xtb])
                    xtiles.append((xt_, xtb))
                    i5 = s5i[0] % 16; s5i[0] += 1
                    s4 = sm5[:, 4 * i5:4 * i5 + 4]; s4b = sm5_b[i5]
                    k.op(ACT, lambda xt_=xt_, s4=s4: nc.scalar.activation(out=junk5[:], in_=xt_[:], func=AF.Square, accum_out=s4[:, 0:1]), reads=[xtb], writes=[junk5_b, s4b])
                    k.op(ACT, lambda s4=s4: nc.scalar.activation(out=s4[:, 1:2], in_=s4[:, 0:1], func=AF.Sqrt, scale=1.0 / D, bias=EPS), reads=[s4b], writes=[s4b])
                    k.op(DVE, lambda s4=s4: nc.vector.reciprocal(out=s4[:, 2:3], in_=s4[:, 1:2]), reads=[s4b], writes=[s4b])
                    n, nb, _ = xnb.next()
                    k.op(DVE, lambda xt_=xt_, n=n, s4=s4: nc.vector.tensor_scalar(out=n[:], in0=xt_[:], scalar1=s4[:, 2:3], scalar2=None, op0=ALU.mult), reads=[xtb, s4b], writes=[nb])
                    nts.append((n, nb))
                return xtiles, nts

            def prologue_b(cj, nts):
                ht2, ht2b, _ = hT2.next()
                for tl in range(2):
                    tt = cj * 2 + tl
                    n, nb = nts[tl]
                    pp = pT5[tt % 2]; ppb = pT5_b[tt % 2]
                    for jj in range(8):
                        k.op(PE, lambda jj=jj, n=n, pp=pp: nc.tensor.transpose(out=pp[:, jj * 128:(jj + 1) * 128], in_=n[:, jj * 128:(jj + 1) * 128], identity=ident[:]),
                             reads=[nb, ident_b], writes=[ppb])
                    for jj in range(8):
                        if tt % 2 == 0:
                            k.op(DVE, lambda jj=jj, pp=pp, tl=tl, ht2=ht2: nc.vector.tensor_scalar(out=ht2[:, jj, tl * 128:(tl + 1) * 128], in0=pp[:, jj * 128:(jj + 1) * 128],
                                                                                                   scalar1=A2[:, jj:jj + 1], scalar2=B2[:, jj:jj + 1], op0=ALU.mult, op1=ALU.add),
                                 reads=[ppb, A2_b, B2_b], writes=[ht2b])
                        else:
                            k.op(ACT, lambda jj=jj, pp=pp, tl=tl, ht2=ht2: nc.scalar.activation(out=ht2[:, jj, tl * 128:(tl + 1) * 128], in_=pp[:, jj * 128:(jj + 1) * 128],
                                                                                                func=AF.Identity, scale=A2[:, jj:jj + 1], bias=B2[:, jj:jj + 1]),
                                 reads=[ppb, A2_b, B2_b], writes=[ht2b])
                return ht2, ht2b

            def ff1(ht2, ht2b, mid_cb=None):
                at, atb, _ = aT.next()
                res = None
                for f in range(32):
                    if f == 10 and mid_cb is not None:
                        res = mid_cb()
                    pf = pF[fi[0] % 2]; pfb = pF_b[fi[0] % 2]; fi[0] += 1
                    for kk in range(8):
                        k.op(PE, lambda kk=kk, f=f, pf=pf: nc.tensor.matmul(pf[:, 0:CH], lhsT=wf1[:, kk, f * 128:(f + 1) * 128], rhs=ht2[:, kk, :], start=(kk == 0), stop=(kk == 7)),
                             reads=[wf1_b[f // 4], ht2b], writes=[pfb])
                    r, rb, _ = r32.next()
                    k.op(ACT, lambda pf=pf, r=r: nc.scalar.activation(out=r[:], in_=pf[:, 0:CH], func=AF.Relu), reads=[pfb], writes=[rb])
                    k.op(DVE, lambda r=r, f=f: nc.vector.tensor_tensor(out=at[:, f, :], in0=r[:], in1=r[:], op=ALU.mult), reads=[rb], writes=[atb])
                return at, atb, res

            def ff2(cj, at, atb, xtiles):
                for tl in range(2):
                    tt = cj * 2 + tl
                    yy, yyb, _ = y2.next()
                    for half in range(2):
                        py = pY[(tl * 2 + half) % 4]; pyb = pY_b[(tl * 2 + half) % 4]
                        for f in range(32):
                            k.op(PE, lambda f=f, tl=tl, half=half, py=py: nc.tensor.matmul(py[:, :], lhsT=at[:, f, tl * 128:(tl + 1) * 128], rhs=wf2[:, f, half * 512:(half + 1) * 512],
                                                                                          start=(f == 0), stop=(f == 31)), reads=[atb, wf2_b[f // 8]], writes=[pyb])
                        k.op(ACT, lambda half=half, yy=yy, py=py: nc.scalar.activation(out=yy[:, half * 512:(half + 1) * 512], in_=py[:, :], func=AF.Copy), reads=[pyb], writes=[yyb])
                    i5 = s5i[0] % 16; s5i[0] += 1
                    s4 = sm5[:, 4 * i5:4 * i5 + 4]; s4b = sm5_b[i5]
                    k.op(ACT, lambda yy=yy, s4=s4: nc.scalar.activation(out=junk5[:], in_=yy[:], func=AF.Square, accum_out=s4[:, 0:1]), reads=[yyb], writes=[junk5_b, s4b])
                    k.op(ACT, lambda s4=s4: nc.scalar.activation(out=s4[:, 1:2], in_=s4[:, 0:1], func=AF.Sqrt, scale=1.0 / D, bias=EPS), reads=[s4b], writes=[s4b])
                    k.op(DVE, lambda s4=s4: nc.vector.reciprocal(out=s4[:, 2:3], in_=s4[:, 1:2]), reads=[s4b], writes=[s4b])
                    k.op(DVE, lambda yy=yy, s4=s4: nc.vector.scalar_tensor_tensor(out=yy[:], in0=yy[:], scalar=s4[:, 2:3], in1=C2row[:], op0=ALU.mult, op1=ALU.mult),
                         reads=[yyb, s4b, C2_b], writes=[yyb])
                    o, ob, ods = ot.next()
                    xt_, xtb = xtiles[tl]
                    k.op(DVE, lambda yy=yy, o=o, xt_=xt_: nc.vector.tensor_tensor(out=o[:], in0=yy[:], in1=xt_[:], op=ALU.add), reads=[yyb, xtb], writes=[ob])
                    db = DB(("out", tt))
                    k.dma(POOL, ods, out_d[tt * 128:(tt + 1) * 128, :], o[:], reads=[ob], writes=[db])
                    out_bufs.append(db)

            NCJ = S // CH
            xtiles, nts = prologue_a(0)
            ht2, ht2b = prologue_b(0, nts)
            for cj in range(NCJ):
                at, atb, res = ff1(ht2, ht2b, (lambda cj=cj: prologue_a(cj + 1)) if cj + 1 < NCJ else None)
                if cj + 1 < NCJ:
                    xtiles_n, nts_n = res
                    ht2_n, ht2b_n = prologue_b(cj + 1, nts_n)
                ff2(cj, at, atb, xtiles)
                if cj + 1 < NCJ:
                    xtiles, ht2, ht2b = xtiles_n, ht2_n, ht2b_n
            k.drain()
            k.finish(out_bufs + [b for kk_, b in dbuf.items() if isinstance(kk_, tuple) and kk_ and kk_[0] in ("datt", "dy")] + ([DB("d_mod")] if debug else []))
        print("bass instructions:", k.ninst, "signalling:", k.nsig, "semaphores:", k.nsem)
    return nc


def _consts():
    bf = ml_dtypes.bfloat16
    t = np.arange(S)
    slopes = 2.0 ** (-np.arange(1, 9, dtype=np.float64))
    c = np.arange(256)
    ce = 16 * c + 31
    ce01 = ((ce[None, :] // 64) == np.arange(64)[:, None]).astype(np.float32)
    ce01[:, 255] = 0.0
    cidx = 16 * (np.arange(2)[None, :] * 128 + np.arange(128)[:, None]) + 31
    cllo = (slopes[None, None, :] * (cidx[:, :, None] % 64)).astype(np.float32)
    cend = (16 * (np.arange(2)[None, :, None] * 128 + np.arange(128)[:, None, None]) + 31)
    cmask = np.where(cend <= t[None, None, :], 0.0, NEGM).astype(np.float32)
    e01 = ((t[None, :] // 64) == np.arange(64)[:, None]).astype(np.float32)
    tab = (64.0 * (np.arange(64)[:, None] - (t[None, :] // 64))).astype(np.float32)
    sllo = (slopes[None, :] * (np.arange(128)[:, None] % 64)).astype(np.float32)
    m = np.arange(8)[None, :, None]; kk = np.arange(128)[:, None, None]; tl = np.arange(512)[None, None, :]
    dd = (512 + tl) - (128 * m + kk)
    wm01 = ((dd >= 0) & (dd < 512)).astype(np.float32)
    tok = (np.arange(NT)[None, :, None] * 128 + np.arange(128)[:, None, None])
    cur = tok // 64
    jb = np.arange(64)[None, None, :]
    forced = (jb == 0) | (jb == cur) | (jb == cur - 1)
    addm = np.where(forced, 1.0e4, np.where(jb <= cur, 0.0, -1.0e30)).astype(np.float32)
    cc = (np.arange(2)[None, :, None] * 128 + np.arange(128)[:, None, None])
    ovl = ((cc >= 4 * jb - 1) & (cc <= 4 * jb + 3) & (cc < 255)).astype(np.float32)
    return {
        "k_ident": np.eye(128, dtype=np.float32).astype(bf),
        "k_ce01": ce01.astype(bf), "k_cllo": cllo,
        "k_cmask": cmask.astype(bf), "k_e01": e01.astype(bf), "k_tab": tab.astype(bf), "k_sllo": sllo, "k_wm01": wm01.astype(bf),
        "k_addm": addm, "k_ovl": np.ascontiguousarray(ovl[:, :, 1:]).astype(bf),
    }


def _col(v, n):
    return np.ascontiguousarray(np.asarray(v, np.float32).reshape(n, 128).T)


def _shared_inputs(inp):
    L = 0
    f = lambda a: np.ascontiguousarray(np.asarray(a, np.float32))
    d = {
        "ada_w": f(inp["ada_w"][L]), "ada_b": f(inp["ada_b"][L]).reshape(1, -1),
        "g_pre1": _col(inp["pre_norm_mix"][L], 8), "g_pre2": _col(inp["pre_norm_mlp"][L], 8),
        "g_post1": np.ascontiguousarray(np.broadcast_to(f(inp["post_norm_mix"][L])[None, :], (128, D))),
        "g_post2": np.ascontiguousarray(np.broadcast_to(f(inp["post_norm_mlp"][L])[None, :], (128, D))),
        "w_in": f(inp["w_in"][L]),
        "conv_w": np.ascontiguousarray(f(inp["conv_w"][L]).T.reshape(4, 128, 4).transpose(1, 0, 2)),
        "conv_b": _col(inp["conv_b"][L], 4),
        "lru_wa": f(inp["lru_wa"][L]), "lru_wx": f(inp["lru_wx"][L]),
        "lru_ba": _col(inp["lru_ba"][L], 4), "lru_bx": _col(inp["lru_bx"][L], 4), "lru_lam": _col(inp["lru_lambda"][L], 4),
        "pos_k": np.ascontiguousarray(f(inp["cmp_pos_k"][L]).T), "pos_v": np.ascontiguousarray(f(inp["cmp_pos_v"][L]).T),
        "w1_k": f(inp["cmp_w1_k"][L]), "w1_v": f(inp["cmp_w1_v"][L]),
        "w2_k": f(inp["cmp_w2_k"][L]), "w2_v": f(inp["cmp_w2_v"][L]),
        "g_rnn": _col(inp["norm_rnn_out"][L], 4), "g_att": _col(inp["norm_att_out"][L], 4),
        "w_out": f(inp["w_out"][L]), "w_ff1": f(inp["w_ff1"][L]), "w_ff2": f(inp["w_ff2"][L]),
    }
    d.update(_consts())
    return d


def kernel(**inputs):
    debug = bool(inputs.pop("_debug", False))
    upto = inputs.pop("_upto", 99)
    cores = inputs.pop("_cores", None)
    x = np.asarray(inputs["x"], np.float32)
    c = np.asarray(inputs["c"], np.float32)
    B = x.shape[0]
    shared = _shared_inputs(inputs)
    rec = set()
    build(debug=debug, upto=upto, record=rec)
    nc = build(debug=debug, upto=upto, needed=rec)
    bs = list(range(B)) if cores is None else list(cores)
    in_maps = []
    for b in bs:
        m = dict(shared)
        m["x"] = np.ascontiguousarray(x[b])
        m["c"] = _col(c[b], 8)
        in_maps.append(m)
    res = run_bass_kernel_spmd(nc, in_maps, core_ids=list(range(len(bs))))
    if debug:
        return res.results
    return np.stack([np.asarray(r["out"], np.float32) for r in res.results], axis=0)
```

```python
import numpy as np
import ml_dtypes
from contextlib import ExitStack
import concourse.bass as bass
import concourse.mybir as mybir
from concourse.bass_utils import run_bass_kernel_spmd

F32 = mybir.dt.float32
BF16 = mybir.dt.bfloat16
AF = mybir.ActivationFunctionType
ALU = mybir.AluOpType

S = 4096
D = 1024
NT = S // 128
NCH = S // 512
DIN = 2328
NEGM = -30000.0
EPS = 1e-6
import os
EVAC = os.environ.get('KDBG_EVAC', '')


class Buf:
    __slots__ = ("name", "lw", "rd")

    def __init__(self, name):
        self.name = name
        self.lw = None
        self.rd = {}


class DSem:
    __slots__ = ("sem", "cnt")

    def __init__(self, sem):
        self.sem = sem
        self.cnt = 0


class Eng:
    def __init__(self, name, eng, self_sync):
        self.name = name
        self.eng = eng
        self.self_sync = self_sync
        self.cur = None
        self.cnt = 0
        self.gidx = 0
        self.tokmap = {}
        self.waited = {}


class K:
    EPOCH = 4000

    def __init__(self, nc, es, needed=None, record=None):
        self.nc = nc
        self.es = es
        self.needed = needed
        self.record = record
        self.nsem = 0
        self.pe = Eng("pe", nc.tensor, False)
        self.act = Eng("act", nc.scalar, True)
        self.dve = Eng("dve", nc.vector, True)
        self.pool = Eng("pool", nc.gpsimd, True)
        self.sp = Eng("sp", nc.sync, True)
        self.dsems = []
        self.ninst = 0
        self.nsig = 0

    def new_sem(self, name):
        self.nsem += 1
        return self.es.enter_context(self.nc.semaphore(f"{name}_{self.nsem}"))

    def dsem(self, name="d"):
        d = DSem(self.new_sem(name))
        self.dsems.append(d)
        return d

    def _wait(self, E, tok):
        if tok[0] == "E":
            P, g = tok[1], tok[2]
            if (not E.self_sync) and P is E:
                return
            if E.waited.get(P.name, 0) >= g:
                return
            if self.record is not None:
                self.record.add((P.name, g))
            sem, val = P.tokmap[g]
            E.eng.wait_ge(sem, val)
            E.waited[P.name] = g
        else:
            ds, val = tok[1], tok[2]
            key = id(ds)
            if E.waited.get(key, 0) >= val:
                return
            E.eng.wait_ge(ds.sem, val)
            E.waited[key] = val

    def _deps(self, E, reads, writes):
        for b in reads:
            if b.lw is not None:
                self._wait(E, b.lw)
        for b in writes:
            if b.lw is not None:
                self._wait(E, b.lw)
            for tok in b.rd.values():
                self._wait(E, tok)

    def _commit(self, tok, reads, writes):
        key = tok[1].name if tok[0] == "E" else id(tok[1])
        for b in reads:
            old = b.rd.get(key)
            if old is None or old[2] < tok[2]:
                b.rd[key] = tok
        for b in writes:
            b.lw = tok
            b.rd = {}

    def op(self, E, fn, reads=(), writes=()):
        self._deps(E, reads, writes)
        inst = fn()
        E.gidx += 1
        g = E.gidx
        if self.needed is None or (E.name, g) in self.needed:
            if E.cur is None or E.cnt >= self.EPOCH:
                E.cur = self.new_sem(E.name)
                E.cnt = 0
            E.cnt += 1
            inst.then_inc(E.cur, 1)
            E.tokmap[g] = (E.cur, E.cnt)
            self.nsig += 1
        tok = ("E", E, g)
        self._commit(tok, reads, writes)
        self.ninst += 1
        return tok

    def dma(self, Q, ds, out, in_, reads=(), writes=(), **kw):
        self._deps(Q, reads, writes)
        if ds.cnt:
            self._wait(Q, ("D", ds, ds.cnt))
        inst = Q.eng.dma_start(out=out, in_=in_, **kw)
        ds.cnt += 16
        inst.then_inc(ds.sem, 16)
        tok = ("D", ds, ds.cnt)
        self._commit(tok, reads, writes)
        self.ninst += 1
        return tok

    def drain(self):
        for E in (self.pe, self.act, self.dve, self.pool):
            if E.gidx:
                self._wait(self.sp, ("E", E, E.gidx))
        for d in self.dsems:
            if d.cnt:
                self._wait(self.sp, ("D", d, d.cnt))

    def barrier(self):
        engs = (self.pe, self.act, self.dve, self.pool, self.sp)
        for E in engs:
            for E2 in engs:
                if E2 is not E and E2.gidx:
                    self._wait(E, ("E", E2, E2.gidx))
            for d in self.dsems:
                if d.cnt:
                    self._wait(E, ("D", d, d.cnt))

    def finish(self, bufs):
        for b in bufs:
            if b.lw is not None:
                self._wait(self.sp, b.lw)


class Rot:
    def __init__(self, k, es, nc, name, n, shape, dt, with_dsem=True):
        self.n = n
        self.i = 0
        self.slots = []
        for j in range(n):
            t = es.enter_context(nc.sbuf_tensor(f"{name}{j}", shape, dt))
            self.slots.append((t, Buf(f"{name}{j}"), k.dsem(name) if with_dsem else None))

    def next(self):
        s = self.slots[self.i % self.n]
        self.i += 1
        return s


class _Stop(Exception):
    pass


def build(debug=False, upto=99, needed=None, record=None):
    nc = bass.Bass("TRN2", target_bir_lowering=False)

    def din(name, shape, dt=F32):
        return nc.dram_tensor(name, list(shape), dt, kind="ExternalInput").ap()

    def dscr(name, shape, dt):
        return nc.dram_tensor(name, list(shape), dt, kind="Internal").ap()

    x_d = din("x", [S, D])
    c_d = din("c", [128, 8])
    adaw_d = din("ada_w", [D, 6 * D])
    adab_d = din("ada_b", [1, 6 * D])
    gpre1_d = din("g_pre1", [128, 8])
    gpre2_d = din("g_pre2", [128, 8])
    gpost1_d = din("g_post1", [128, D])
    gpost2_d = din("g_post2", [128, D])
    win_d = din("w_in", [D, DIN])
    convw_d = din("conv_w", [128, 4, 4])
    convb_d = din("conv_b", [128, 4])
    wa_d = din("lru_wa", [8, 64, 64])
    wx_d = din("lru_wx", [8, 64, 64])
    ba_d = din("lru_ba", [128, 4])
    bx_d = din("lru_bx", [128, 4])
    lam_d = din("lru_lam", [128, 4])
    posk_d = din("pos_k", [64, 32])
    posv_d = din("pos_v", [64, 32])
    w1k_d = din("w1_k", [2048, 256])
    w1v_d = din("w1_v", [2048, 256])
    w2k_d = din("w2_k", [256, 64])
    w2v_d = din("w2_v", [256, 64])
    grnn_d = din("g_rnn", [128, 4])
    gatt_d = din("g_att", [128, 4])
    wout_d = din("w_out", [D, D])
    wff1_d = din("w_ff1", [D, 4 * D])
    wff2_d = din("w_ff2", [4 * D, D])
    ident_d = din("k_ident", [128, 128], BF16)
    ce01_d = din("k_ce01", [64, 256], BF16)
    cllo_d = din("k_cllo", [128, 2, 8])
    cmask_d = din("k_cmask", [128, 2, S], BF16)
    e01_d = din("k_e01", [64, S], BF16)
    tab_d = din("k_tab", [64, S], BF16)
    sllo_d = din("k_sllo", [128, 8])
    wm01_d = din("k_wm01", [128, 8, 512], BF16)
    addm_d = din("k_addm", [128, NT, 64])
    ovl_d = din("k_ovl", [128, 2, 63], BF16)
    out_d = nc.dram_tensor("out", [S, D], F32, kind="ExternalOutput").ap()
    zr_d = dscr("zr_s", [1024, S], F32)
    q_d = dscr("q_s", [512, S], BF16)
    kc_d = dscr("kc_s", [128, S], BF16)
    vc_d = dscr("vc_s", [128, S], BF16)
    ks_d = dscr("ks_s", [128, S], BF16)
    kw_d = dscr("kw_s", [128, S], BF16)
    rnn_d = dscr("rnn_s", [512, S], BF16)
    x1_d = dscr("x1_s", [S, D], F32)
    dbg = {}
    if debug:
        for nm, shp in [("d_mod", [1, 6 * D]), ("d_att", [S, 512]), ("d_y", [S, D])]:
            dbg[nm] = nc.dram_tensor(nm, shp, F32, kind="ExternalOutput").ap()

    with ExitStack() as es:
        k = K(nc, es, needed=needed, record=record)
        PE, ACT, DVE, POOL, SP = k.pe, k.act, k.dve, k.pool, k.sp

        def sb(name, shape, dt, stack=es):
            return stack.enter_context(nc.sbuf_tensor(name, list(shape), dt))

        def pst(name, shape, dt, stack=es):
            return stack.enter_context(nc.psum_tensor(name, list(shape), dt))

        dbuf = {}

        def DB(key):
            if key not in dbuf:
                dbuf[key] = Buf(str(key))
            return dbuf[key]

        ident = sb("ident", [128, 128], BF16)
        ident_b = Buf("ident")
        ld0 = k.dsem("ld0")
        k.dma(SP, ld0, ident[:], ident_d, writes=[ident_b])
        ones_bf = sb("ones_bf", [128, 128], BF16)
        ones_b = Buf("ones")
        k.op(DVE, lambda: nc.vector.memset(ones_bf[:], 1.0), writes=[ones_b])
        A1 = sb("A1", [128, 8], F32); B1 = sb("B1", [128, 8], F32)
        A2 = sb("A2", [128, 8], F32); B2 = sb("B2", [128, 8], F32)
        C1row = sb("C1row", [128, D], F32); C2row = sb("C2row", [128, D], F32)
        A1_b, B1_b, A2_b, B2_b, C1_b, C2_b = (Buf(n) for n in ("A1", "B1", "A2", "B2", "C1", "C2"))
        mid = es.enter_context(ExitStack())
        vs_aug = sb("vs_aug", [128, NT, 2, 65], BF16, mid)
        vw_aug = sb("vw_aug", [128, NT, 2, 65], BF16, mid)
        gates = sb("gates", [128, NT, 24], F32, mid)
        vtok_b = [Buf(f"vtok{t}") for t in range(NT)]
        vones_b = Buf("vones")
        k.op(DVE, lambda: nc.vector.memset(vs_aug[:, :, :, 64:65], 1.0), writes=[vones_b])
        k.op(DVE, lambda: nc.vector.memset(vw_aug[:, :, :, 64:65], 1.0), writes=[vones_b])
        ssr = sb("ssr", [128, NT], F32, mid)
        ssq = sb("ssq", [128, NT], F32, mid)
        ssr_b = Buf("ssr")

        with ExitStack() as p0:
            csb = sb("csb", [128, 8], F32, p0)
            scb = sb("scb", [128, 8], BF16, p0)
            c_b = Buf("c")
            k.dma(SP, k.dsem("c"), csb[:], c_d, writes=[c_b])
            k.op(ACT, lambda: nc.scalar.activation(out=scb[:], in_=csb[:], func=AF.Silu), reads=[c_b], writes=[c_b])
            adab = sb("adab", [1, 6 * D], F32, p0)
            adab_b = Buf("adab")
            k.dma(SP, k.dsem("adab"), adab[:], adab_d, writes=[adab_b])
            modrow = sb("modrow", [1, 6 * D], F32, p0)
            modrow_b = Buf("modrow")
            modcol = sb("modcol", [128, 48], F32, p0)
            modcol_b = Buf("modcol")
            gp1 = sb("gp1", [128, 8], F32, p0); gp2 = sb("gp2", [128, 8], F32, p0)
            gq1 = sb("gq1", [128, D], F32, p0); gq2 = sb("gq2", [128, D], F32, p0)
            g_b = Buf("gvecs")
            k.dma(SP, ld0, gp1[:], gpre1_d, writes=[g_b])
            k.dma(SP, ld0, gp2[:], gpre2_d, writes=[g_b])
            k.dma(SP, ld0, gq1[:], gpost1_d, writes=[g_b])
            k.dma(SP, ld0, gq2[:], gpost2_d, writes=[g_b])
            adaw = Rot(k, p0, nc, "adaw", 2, [128, 8, 512], BF16)
            ps_row = pst("ps_row", [128, 512], F32, p0)
            ps_row_b = Buf("ps_row")
            ps_col = pst("ps_col", [128, 512], F32, p0)
            ps_col_b = Buf("ps_col")
            one11 = sb("one11", [1, 128], F32, p0)
            one11_b = Buf("one11")
            k.op(DVE, lambda: nc.vector.memset(one11[:], 1.0), writes=[one11_b])
            for pc in range(12):
                t, tb, ds = adaw.next()
                k.dma(POOL, ds, t[:], adaw_d[:, pc * 512:(pc + 1) * 512].rearrange("(k p) n -> p k n", p=128), writes=[tb])
                for kk in range(8):
                    k.op(PE, lambda kk=kk, t=t: nc.tensor.matmul(ps_row[0:1, :], lhsT=scb[:, kk:kk + 1], rhs=t[:, kk, :],
                                                                start=(kk == 0), stop=(kk == 7)),
                         reads=[tb, c_b], writes=[ps_row_b])
                k.op(DVE, lambda pc=pc: nc.vector.tensor_tensor(out=modrow[0:1, pc * 512:(pc + 1) * 512], in0=ps_row[0:1, :],
                                                                in1=adab[0:1, pc * 512:(pc + 1) * 512], op=ALU.add),
                     reads=[ps_row_b, adab_b], writes=[modrow_b])
            if debug:
                k.dma(SP, ld0, dbg["d_mod"], modrow[:], reads=[modrow_b], writes=[DB("d_mod")])
            for j in range(48):
                k.op(PE, lambda j=j: nc.tensor.matmul(ps_col[:, j:j + 1], lhsT=modrow[0:1, j * 128:(j + 1) * 128], rhs=one11[0:1, 0:1],
                                                      start=True, stop=True),
                     reads=[modrow_b, one11_b], writes=[ps_col_b])
            k.op(DVE, lambda: nc.vector.tensor_copy(out=modcol[:], in_=ps_col[:, 0:48]), reads=[ps_col_b], writes=[modcol_b])
            k.op(DVE, lambda: nc.vector.scalar_tensor_tensor(out=A1[:], in0=modcol[:, 8:16], scalar=1.0, in1=gp1[:], op0=ALU.add, op1=ALU.mult),
                 reads=[modcol_b, g_b], writes=[A1_b])
            k.op(DVE, lambda: nc.vector.tensor_copy(out=B1[:], in_=modcol[:, 0:8]), reads=[modcol_b], writes=[B1_b])
            k.op(DVE, lambda: nc.vector.scalar_tensor_tensor(out=A2[:], in0=modcol[:, 32:40], scalar=1.0, in1=gp2[:], op0=ALU.add, op1=ALU.mult),
                 reads=[modcol_b, g_b], writes=[A2_b])
            k.op(DVE, lambda: nc.vector.tensor_copy(out=B2[:], in_=modcol[:, 24:32]), reads=[modcol_b], writes=[B2_b])
            for (base, crow, cb, gq) in ((2048, C1row, C1_b, gq1), (5120, C2row, C2_b, gq2)):
                for hh in range(2):
                    k.op(PE, lambda base=base, hh=hh: nc.tensor.matmul(ps_row[:, :], lhsT=one11[0:1, :],
                                                                      rhs=modrow[0:1, base + hh * 512: base + (hh + 1) * 512],
                                                                      start=True, stop=True),
                         reads=[modrow_b, one11_b], writes=[ps_row_b])
                    k.op(DVE, lambda hh=hh, crow=crow, gq=gq: nc.vector.scalar_tensor_tensor(
                        out=crow[:, hh * 512:(hh + 1) * 512], in0=ps_row[:, :], scalar=1.0, in1=gq[:, hh * 512:(hh + 1) * 512],
                        op0=ALU.add, op1=ALU.mult), reads=[ps_row_b, g_b], writes=[cb])

        if upto <= 0:
            k.drain()
            return nc
        k.barrier()
        with ExitStack() as p1:
            hT = sb("hT", [128, 8, S], BF16, p1)
            hT_b = [[Buf(f"hT{t}_{j}") for j in range(8)] for t in range(NT)]
            win = sb("win", [128, 8, DIN], BF16, p1)
            win_b = Buf("win")
            wds = k.dsem("win")
            k.dma(POOL, wds, win[:], win_d.rearrange("(k p) n -> p k n", p=128), writes=[win_b])
            xt = Rot(k, p1, nc, "xt", 3, [128, D], F32)
            junk = sb("junk", [128, D], BF16, p1)
            junk_b = Buf("junk")
            xn = Rot(k, p1, nc, "xn", 2, [128, D], BF16, with_dsem=False)
            pT = [pst(f"pT{i}", [128, 1024], BF16, p1) for i in range(2)]
            pT_b = [Buf(f"pT{i}") for i in range(2)]
            sm1 = sb("sm1", [128, NT * 4], F32, p1)
            sm1_b = [Buf(f"sm1_{t}") for t in range(NT)]
            pz = [pst(f"pz{i}", [128, 512], F32, p1) for i in range(4)]
            pz_b = [Buf(f"pz{i}") for i in range(4)]
            st32 = Rot(k, p1, nc, "st32", 3, [128, 512], F32)
            st16 = Rot(k, p1, nc, "st16", 3, [128, 512], BF16)
            zi = 0
            fm_tiles = []
            for ct in range(8):
                fm_tiles.append((ct * 128, zr_d, ct * 128, "f32", 1.0))
            for ct in range(4):
                fm_tiles.append((1024 + ct * 128, q_d, ct * 128, "bf", 0.125))
            fm_tiles.append((1536, kc_d, 0, "bf", 1.0))
            fm_tiles.append((1664, vc_d, 0, "bf", 1.0))
            fm_tiles.append((1792, ks_d, 0, "bf", 1.0))
            fm_tiles.append((2048, kw_d, 0, "bf", 1.0))

            def norm_a(tt):
                t, tb, ds = xt.next()
                k.dma(SP, ds, t[:], x_d[tt * 128:(tt + 1) * 128, :], writes=[tb])
                s4 = sm1[:, tt * 4:tt * 4 + 4]; s4b = sm1_b[tt]
                k.op(ACT, lambda: nc.scalar.activation(out=junk[:], in_=t[:], func=AF.Square, accum_out=s4[:, 0:1]), reads=[tb], writes=[junk_b, s4b])
                k.op(ACT, lambda: nc.scalar.activation(out=s4[:, 1:2], in_=s4[:, 0:1], func=AF.Ln, scale=1.0 / D, bias=EPS), reads=[s4b], writes=[s4b])
                k.op(ACT, lambda: nc.scalar.activation(out=s4[:, 2:3], in_=s4[:, 1:2], func=AF.Exp, scale=-0.5), reads=[s4b], writes=[s4b])
                n, nb, _ = xn.next()
                k.op(DVE, lambda: nc.vector.tensor_scalar(out=n[:], in0=t[:], scalar1=s4[:, 2:3], scalar2=None, op0=ALU.mult), reads=[tb, s4b], writes=[nb])
                return tt, n, nb

            def norm_b(tt, n, nb):
                pp = pT[tt % 2]; ppb = pT_b[tt % 2]
                for j in range(8):
                    k.op(PE, lambda j=j: nc.tensor.transpose(out=pp[:, j * 128:(j + 1) * 128], in_=n[:, j * 128:(j + 1) * 128], identity=ident[:]),
                         reads=[nb, ident_b], writes=[ppb])
                for j in range(8):
                    if tt % 2 == 0:
                        k.op(DVE, lambda j=j: nc.vector.tensor_scalar(out=hT[:, j, tt * 128:(tt + 1) * 128], in0=pp[:, j * 128:(j + 1) * 128],
                                                                      scalar1=A1[:, j:j + 1], scalar2=B1[:, j:j + 1], op0=ALU.mult, op1=ALU.add),
                             reads=[ppb, A1_b, B1_b], writes=[hT_b[tt][j]])
                    else:
                        k.op(ACT, lambda j=j: nc.scalar.activation(out=hT[:, j, tt * 128:(tt + 1) * 128], in_=pp[:, j * 128:(j + 1) * 128],
                                                                   func=AF.Identity, scale=A1[:, j:j + 1], bias=B1[:, j:j + 1]),
                             reads=[ppb, A1_b, B1_b], writes=[hT_b[tt][j]])

            npend = [norm_a(0)]

            def norm_tile(_tt_unused=None):
                tt, n, nb = npend[0]
                norm_b(tt, n, nb)
                if tt + 1 < NT:
                    npend[0] = norm_a(tt + 1)

            for tl in range(4):
                norm_tile(tl)
            for ch in range(NCH):
                for ti, (c0, dst, r0, kind, scl) in enumerate(fm_tiles):
                    if ch + 1 < NCH and ti in (2, 6, 10, 14):
                        norm_tile((ch + 1) * 4 + (ti - 2) // 4)
                    pzz = pz[zi % 4]; pzb = pz_b[zi % 4]
                    for kk in range(8):
                        k.op(PE, lambda kk=kk, c0=c0, ch=ch, pzz=pzz: nc.tensor.matmul(pzz[:, :], lhsT=win[:, kk, c0:c0 + 128], rhs=hT[:, kk, ch * 512:(ch + 1) * 512],
                                                                                      start=(kk == 0), stop=(kk == 7)),
                             reads=[win_b] + [hT_b[t4][kk] for t4 in range(ch * 4, ch * 4 + 4)], writes=[pzb])
                    t, tb, ds = (st32 if kind == "f32" else st16).next()
                    if zi % 2 == 0:
                        k.op(ACT, lambda t=t, pzz=pzz, scl=scl: nc.scalar.activation(out=t[:], in_=pzz[:, :], func=AF.Copy, scale=scl), reads=[pzb], writes=[tb])
                    else:
                        k.op(DVE, lambda t=t, pzz=pzz, scl=scl: nc.vector.tensor_scalar(out=t[:], in0=pzz[:, :], scalar1=scl, scalar2=None, op0=ALU.mult),
                             reads=[pzb], writes=[tb])
                    k.dma(POOL, ds, dst[r0:r0 + 128, ch * 512:(ch + 1) * 512], t[:], reads=[tb], writes=[DB((id(dst), r0, ch))])
                    zi += 1
                for tl in range(4):
                    tt = ch * 4 + tl
                    pzz = pz[zi % 4]; pzb = pz_b[zi % 4]
                    use_dve = (zi % 2 == 1)
                    zi += 1
                    for (c0, n, o0) in ((1920, 128, 0), (2176, 152, 128)):
                        for kk in range(8):
                            k.op(PE, lambda kk=kk, c0=c0, n=n, o0=o0, tt=tt, pzz=pzz: nc.tensor.matmul(
                                pzz[:, o0:o0 + n], lhsT=hT[:, kk, tt * 128:(tt + 1) * 128], rhs=win[:, kk, c0:c0 + n],
                                start=(kk == 0 and o0 == 0), stop=(kk == 7 and o0 == 128)),
                                reads=[win_b, hT_b[tt][kk]], writes=[pzb])
                    if use_dve:
                        k.op(DVE, lambda tt=tt, pzz=pzz: nc.vector.tensor_copy(out=vs_aug[:, tt, :, 0:64], in_=pzz[:, 0:128].rearrange("p (g d) -> p g d", g=2)), reads=[pzb], writes=[vtok_b[tt]])
                        k.op(DVE, lambda tt=tt, pzz=pzz: nc.vector.tensor_copy(out=vw_aug[:, tt, :, 0:64], in_=pzz[:, 128:256].rearrange("p (g d) -> p g d", g=2)), reads=[pzb], writes=[vtok_b[tt]])
                        k.op(DVE, lambda tt=tt, pzz=pzz: nc.vector.tensor_copy(out=gates[:, tt, :], in_=pzz[:, 256:280]), reads=[pzb], writes=[vtok_b[tt]])
                    else:
                        k.op(ACT, lambda tt=tt, pzz=pzz: nc.scalar.activation(out=vs_aug[:, tt, :, 0:64], in_=pzz[:, 0:128].rearrange("p (g d) -> p g d", g=2), func=AF.Copy), reads=[pzb], writes=[vtok_b[tt]])
                        k.op(ACT, lambda tt=tt, pzz=pzz: nc.scalar.activation(out=vw_aug[:, tt, :, 0:64], in_=pzz[:, 128:256].rearrange("p (g d) -> p g d", g=2), func=AF.Copy), reads=[pzb], writes=[vtok_b[tt]])
                        k.op(ACT, lambda tt=tt, pzz=pzz: nc.scalar.activation(out=gates[:, tt, :], in_=pzz[:, 256:280], func=AF.Copy), reads=[pzb], writes=[vtok_b[tt]])
            k.op(ACT, lambda: nc.scalar.activation(out=gates[:], in_=gates[:], func=AF.Sigmoid), reads=vtok_b, writes=vtok_b)

        if upto <= 1:
            k.drain()
            return nc
        k.barrier()
        with ExitStack() as p2:
            cw = sb("cw", [128, 4, 4], F32, p2); cb_ = sb("cb", [128, 4], F32, p2)
            bat = sb("bat", [128, 4], F32, p2); bxt = sb("bxt", [128, 4], F32, p2)
            lam = sb("lam", [128, 4], F32, p2); cc1 = sb("cc1", [128, 4], F32, p2); cc2 = sb("cc2", [128, 4], F32, p2)
            prm_b = Buf("rnnprm")
            pds = k.dsem("rp")
            for (t, d) in ((cw, convw_d), (cb_, convb_d), (bat, ba_d), (bxt, bx_d), (lam, lam_d)):
                k.dma(SP, pds, t[:], d, writes=[prm_b])
            k.op(ACT, lambda: nc.scalar.activation(out=cc1[:], in_=lam[:], func=AF.Exp, scale=-1.0), reads=[prm_b], writes=[prm_b])
            k.op(ACT, lambda: nc.scalar.activation(out=cc1[:], in_=cc1[:], func=AF.Ln, bias=1.0), reads=[prm_b], writes=[prm_b])
            k.op(DVE, lambda: nc.vector.tensor_scalar(out=cc2[:], in0=cc1[:], scalar1=-16.0, scalar2=None, op0=ALU.mult), reads=[prm_b], writes=[prm_b])
            k.op(DVE, lambda: nc.vector.tensor_scalar(out=cc1[:], in0=cc1[:], scalar1=-8.0, scalar2=None, op0=ALU.mult), reads=[prm_b], writes=[prm_b])
            wbd_a = sb("wbd_a", [128, 4, 128], BF16, p2); wbd_x = sb("wbd_x", [128, 4, 128], BF16, p2)
            wbd_b = Buf("wbd")
            k.op(DVE, lambda: nc.vector.memset(wbd_a[:], 0.0), writes=[wbd_b])
            k.op(DVE, lambda: nc.vector.memset(wbd_x[:], 0.0), writes=[wbd_b])
            for (t, d) in ((wbd_a, wa_d), (wbd_x, wx_d)):
                dv = d.rearrange("(f two) i j -> two i f j", two=2)
                for hh in range(2):
                    k.dma(POOL, k.dsem("wbd"), t[hh * 64:(hh + 1) * 64, :, hh * 64:(hh + 1) * 64], dv[hh], writes=[wbd_b])
            NP = 4
            PW = S // NP
            xrR = Rot(k, p2, nc, "xr", 2, [128, S], F32)
            ggR = Rot(k, p2, nc, "gg", 2, [128, S], F32)
            xc = sb("xc", [128, S], F32, p2); xcb = sb("xcb", [128, S], BF16, p2)
            rr = sb("rr", [128, S], F32, p2); ii = sb("ii", [128, S], F32, p2); a2 = sb("a2", [128, S], F32, p2)
            hh_ = sb("hh", [128, S], F32, p2)
            rnb = sb("rnb", [128, S], BF16, p2); sqb = sb("sqb", [128, S], BF16, p2)
            xc_b, xcb_b, rr_b, ii_b, a2_b, hh_b, rnb_b, sqb_b = ([Buf(f"{n}{p}") for p in range(NP)] for n in ("xc", "xcb", "rr", "ii", "a2", "hh", "rnb", "sqb"))
            rn_ds = [k.dsem("rn") for _ in range(2)]
            pg = [pst(f"pg{i}", [128, 512], F32, p2) for i in range(3)]
            pg_b = [Buf(f"pg{i}") for i in range(3)]
            pstat = pst("pstat", [128, 512], F32, p2)
            pstat_b = Buf("pstat")
            gi = [0]
            gg_parts = [[Buf(f"ggs{sl}_{p}") for p in range(NP)] for sl in range(2)]

            def load_ft(ft):
                xr, xr_b, xr_ds = xrR.next()
                gg, _, gg_ds = ggR.next()
                gg_bp = gg_parts[ft % 2]
                k.dma(SP, xr_ds, xr[:], zr_d[512 + ft * 128:512 + (ft + 1) * 128, :], reads=[DB((id(zr_d), 512 + ft * 128, ch)) for ch in range(NCH)], writes=[xr_b])
                k.dma(SP, gg_ds, gg[:], zr_d[ft * 128:(ft + 1) * 128, :], reads=[DB((id(zr_d), ft * 128, ch)) for ch in range(NCH)], writes=gg_bp)
                return xr, xr_b, gg, gg_bp

            def cs(p):
                return p * PW, (p + 1) * PW

            def front(ft, xr, xr_b, gg, gg_b):
                for p in range(NP):
                    c0, c1 = cs(p)
                    k.op(DVE, lambda c0=c0, c1=c1: nc.vector.tensor_scalar(out=xc[:, c0:c1], in0=xr[:, c0:c1], scalar1=cw[:, ft, 3:4], scalar2=cb_[:, ft:ft + 1], op0=ALU.mult, op1=ALU.add),
                         reads=[xr_b, prm_b], writes=[xc_b[p]])
                    for sh in (1, 2, 3):
                        lo = max(c0, sh)
                        k.op(DVE, lambda lo=lo, c1=c1, sh=sh: nc.vector.scalar_tensor_tensor(out=xc[:, lo:c1], in0=xr[:, lo - sh:c1 - sh], scalar=cw[:, ft, 3 - sh:4 - sh],
                                                                                            in1=xc[:, lo:c1], op0=ALU.mult, op1=ALU.add),
                             reads=[xr_b, prm_b, xc_b[p]], writes=[xc_b[p]])
                    k.op(ACT, lambda c0=c0, c1=c1: nc.scalar.activation(out=xcb[:, c0:c1], in_=xc[:, c0:c1], func=AF.Copy), reads=[xc_b[p]], writes=[xcb_b[p]])
                    k.op(ACT, lambda c0=c0, c1=c1: nc.scalar.activation(out=gg[:, c0:c1], in_=gg[:, c0:c1], func=AF.Gelu_apprx_tanh), reads=[gg_b[p]], writes=[gg_b[p]])

            def rnn_gates(ft):
                for p in range(NP):
                    c0, c1 = cs(p)
                    for (wt, bt, dst, dstb) in ((wbd_a, bat, rr, rr_b), (wbd_x, bxt, ii, ii_b)):
                        for cc in range(c0, c1, 512):
                            pgg = pg[gi[0] % 3]; pgb = pg_b[gi[0] % 3]; gi[0] += 1
                            k.op(PE, lambda wt=wt, cc=cc, pgg=pgg: nc.tensor.matmul(pgg[:, :], lhsT=wt[:, ft, :], rhs=xcb[:, cc:cc + 512], start=True, stop=True),
                                 reads=[wbd_b, xcb_b[p]], writes=[pgb])
                            k.op(ACT, lambda bt=bt, cc=cc, pgg=pgg, dst=dst: nc.scalar.activation(out=dst[:, cc:cc + 512], in_=pgg[:, :], func=AF.Sigmoid, bias=bt[:, ft:ft + 1]),
                                 reads=[pgb, prm_b], writes=[dstb[p]])

            def exps_m(ft):
                for p in range(NP):
                    c0, c1 = cs(p)
                    k.op(ACT, lambda c0=c0, c1=c1: nc.scalar.activation(out=a2[:, c0:c1], in_=rr[:, c0:c1], func=AF.Exp, scale=cc2[:, ft:ft + 1]), reads=[rr_b[p], prm_b], writes=[a2_b[p]])
                    k.op(ACT, lambda c0=c0, c1=c1: nc.scalar.activation(out=rr[:, c0:c1], in_=rr[:, c0:c1], func=AF.Exp, scale=cc1[:, ft:ft + 1]), reads=[rr_b[p], prm_b], writes=[rr_b[p]])
                    k.op(DVE, lambda c0=c0, c1=c1: nc.vector.tensor_tensor(out=ii[:, c0:c1], in0=ii[:, c0:c1], in1=xc[:, c0:c1], op=ALU.mult), reads=[ii_b[p], xc_b[p]], writes=[ii_b[p]])

            def sqrts(ft):
                for p in range(NP):
                    c0, c1 = cs(p)
                    k.op(ACT, lambda c0=c0, c1=c1: nc.scalar.activation(out=a2[:, c0:c1], in_=a2[:, c0:c1], func=AF.Sqrt, scale=-1.0, bias=1.0), reads=[a2_b[p]], writes=[a2_b[p]])

            def tail(ft, gg, gg_b):
                for p in range(NP):
                    c0, c1 = cs(p)
                    k.op(DVE, lambda c0=c0, c1=c1: nc.vector.tensor_tensor(out=a2[:, c0:c1], in0=a2[:, c0:c1], in1=ii[:, c0:c1], op=ALU.mult), reads=[a2_b[p], ii_b[p]], writes=[a2_b[p]])
                    init = 0.0 if p == 0 else hh_[:, c0 - 1:c0]
                    k.op(DVE, lambda c0=c0, c1=c1, init=init: nc.vector.tensor_tensor_scan(out=hh_[:, c0:c1], data0=rr[:, c0:c1], data1=a2[:, c0:c1], initial=init, op0=ALU.mult, op1=ALU.add),
                         reads=[rr_b[p], a2_b[p]] + ([hh_b[p - 1]] if p else []), writes=[hh_b[p]])
                    k.op(DVE, lambda c0=c0, c1=c1: nc.vector.tensor_tensor(out=rnb[:, c0:c1], in0=gg[:, c0:c1], in1=hh_[:, c0:c1], op=ALU.mult), reads=[gg_b[p], hh_b[p]], writes=[rnb_b[p]])
                    k.op(DVE, lambda c0=c0, c1=c1: nc.vector.tensor_tensor(out=sqb[:, c0:c1], in0=rnb[:, c0:c1], in1=rnb[:, c0:c1], op=ALU.mult), reads=[rnb_b[p]], writes=[sqb_b[p]])
                    for cc in range(c0, c1, 512):
                        ch = cc // 512
                        k.dma(POOL, rn_ds[ch % 2], rnn_d[ft * 128:(ft + 1) * 128, cc:cc + 512], rnb[:, cc:cc + 512], reads=[rnb_b[p]], writes=[DB(("rnn", ch))])
                    for tt in range(c0 // 128, c1 // 128):
                        k.op(PE, lambda tt=tt: nc.tensor.matmul(pstat[:, tt * 4 + ft: tt * 4 + ft + 1], lhsT=sqb[:, tt * 128:(tt + 1) * 128], rhs=ones_bf[:, 0:1], start=True, stop=True),
                             reads=[sqb_b[p], ones_b], writes=[pstat_b])

            cur_ft = load_ft(0)
            front(0, *cur_ft)
            for ft in range(4):
                xr, xr_b, gg, gg_b = cur_ft
                if ft + 1 < 4:
                    nxt_ft = load_ft(ft + 1)
                rnn_gates(ft)
                exps_m(ft)
                sqrts(ft)
                if ft + 1 < 4:
                    front(ft + 1, *nxt_ft)
                tail(ft, gg, gg_b)
                if ft + 1 < 4:
                    cur_ft = nxt_ft
            k.op(DVE, lambda: nc.vector.tensor_reduce(out=ssr[:], in_=pstat[:, 0:NT * 4].rearrange("p (t f) -> p t f", f=4), axis=mybir.AxisListType.X, op=ALU.add),
                 reads=[pstat_b], writes=[ssr_b])
            k.op(ACT, lambda: nc.scalar.activation(out=ssq[:], in_=ssr[:], func=AF.Sqrt, scale=1.0 / 512, bias=EPS), reads=[ssr_b], writes=[ssr_b])
            k.op(DVE, lambda: nc.vector.reciprocal(out=ssr[:], in_=ssq[:]), reads=[ssr_b], writes=[ssr_b])

        if upto <= 2:
            k.drain()
            return nc
        k.barrier()
        with ExitStack() as p4:
            kcT = [sb(f"kcT{g}", [128, 256], BF16, p4) for g in range(2)]
            cllo = sb("cllo", [128, 2, 8], F32, p4)
            vc_aug = [sb(f"vca{g}", [128, 2, 128], BF16, p4) for g in range(2)]
            cmp_b = Buf("cmp")
            cds = k.dsem("cmpc")
            k.dma(SP, cds, cllo[:], cllo_d, writes=[cmp_b])
            for g in range(2):
                k.dma(SP, cds, kcT[g][64:128, :], ce01_d, writes=[cmp_b])
                k.dma(SP, cds, vc_aug[g][:, :, 65:128], ovl_d, writes=[cmp_b])
                k.op(DVE, lambda g=g: nc.vector.memset(vc_aug[g][:, :, 64:65], 1.0), writes=[cmp_b])
                k.op(DVE, lambda g=g: nc.vector.memset(vc_aug[g][:, :, 0:64], 0.0), writes=[cmp_b])
            kTs = [sb(f"kTs{g}", [128, S], BF16, p4) for g in range(2)]
            kTw = [sb(f"kTw{g}", [128, S], BF16, p4) for g in range(2)]
            kds_shared = k.dsem("kT")
            cds_shared = k.dsem("cst")
            kTs_b = [Buf(f"kTs{g}") for g in range(2)]
            kTw_b = [Buf(f"kTw{g}") for g in range(2)]
            cmask = sb("cmask", [128, 2, S], BF16, p4)
            tab = sb("tab", [128, S], BF16, p4)
            sllo = sb("sllo", [128, 8], F32, p4)
            wm01 = sb("wm01", [128, 8, 512], BF16, p4)
            addm = sb("addm", [128, NT, 64], F32, p4)
            cmask_b, tab_b, sllo_b, wm01_b, addm_b = (Buf(n) for n in ("cmask", "tab", "sllo", "wm01", "addm"))
            wout = sb("wout", [128, 8, D], BF16, p4)
            wout_b = Buf("wout")
            gcol = sb("gcol", [128, 8], F32, p4)
            gcol_b = Buf("gcol")
            gds = k.dsem("gcol")
            pw = ExitStack()
            wst = Rot(k, pw, nc, "wst", 2, [128, D], F32)

            def issue_resident_loads(dep):
                for g in range(2):
                    for (t, tb_, src) in ((kTs[g], kTs_b[g], ks_d), (kTw[g], kTw_b[g], kw_d)):
                        k.dma(SP, kds_shared, t[0:64, :], src[g * 64:(g + 1) * 64, :], reads=[DB((id(src), 0, ch)) for ch in range(NCH)] + dep, writes=[tb_])
                        k.dma(SP, kds_shared, t[64:128, :], e01_d, writes=[tb_])
                k.dma(SP, cds_shared, cmask[:], cmask_d, writes=[cmask_b])
                k.dma(SP, cds_shared, tab[64:128, :], tab_d, writes=[tab_b])
                k.dma(SP, cds_shared, sllo[:], sllo_d, writes=[sllo_b])
                k.dma(SP, cds_shared, wm01[:], wm01_d, writes=[wm01_b])
                k.dma(SP, cds_shared, addm[:], addm_d, writes=[addm_b])
                k.dma(SP, gds, gcol[:, 0:4], grnn_d, writes=[gcol_b])
                k.dma(SP, gds, gcol[:, 4:8], gatt_d, writes=[gcol_b])
                for kk in range(8):
                    t, tb, ds = wst.next()
                    k.dma(SP, ds, t[:], wout_d[kk * 128:(kk + 1) * 128, :], writes=[tb])
                    k.op(DVE, lambda kk=kk, t=t: nc.vector.tensor_scalar(out=wout[:, kk, :], in0=t[:], scalar1=gcol[:, kk:kk + 1], scalar2=None, op0=ALU.mult),
                         reads=[tb, gcol_b], writes=[wout_b])

            with ExitStack() as p3:
                w1 = [sb(f"w1_{i}", [64, 32, 256], BF16, p3) for i in range(2)]
                w2 = [sb(f"w2_{i}", [128, 2, 64], BF16, p3) for i in range(2)]
                pos = [sb(f"pos_{i}", [64, 32], BF16, p3) for i in range(2)]
                posf = [sb(f"posf_{i}", [64, 32], F32, p3) for i in range(2)]
                cw_b = Buf("cmpw")
                cw1_b = [[Buf(f"cw1_{i}_{q}") for q in range(4)] for i in range(2)]
                cwd = k.dsem("cmpw")
                cwd2 = k.dsem("cmpp")
                for i, (w1d, w2d, pd) in enumerate(((w1k_d, w2k_d, posk_d), (w1v_d, w2v_d, posv_d))):
                    w1v = w1d.rearrange("(l d) h -> d l h", d=64)
                    for l0 in range(0, 32, 8):
                        k.dma(POOL, k.dsem("w1"), w1[i][:, l0:l0 + 8, :], w1v[:, l0:l0 + 8, :], writes=[cw1_b[i][l0 // 8]])
                    k.dma(POOL, cwd, w2[i][:], w2d.rearrange("(t p) d -> p t d", p=128), writes=[cw_b])
                    k.dma(SP, cwd2, posf[i][:], pd, writes=[cw_b])
                    k.op(DVE, lambda i=i: nc.vector.tensor_copy(out=pos[i][:], in_=posf[i][:]), reads=[cw_b], writes=[cw_b])
                cin = [[sb(f"cin{i}{g}", [64, S], BF16, p3) for g in range(2)] for i in range(2)]
                cin_bs = [[Buf(f"cin{i}{g}") for g in range(2)] for i in range(2)]
                for i, src in enumerate((kc_d, vc_d)):
                    cin_ds = k.dsem("cin")
                    for g in range(2):
                        k.dma(SP, cin_ds, cin[i][g][:], src[g * 64:(g + 1) * 64, :], reads=[DB((id(src), 0, ch)) for ch in range(NCH)], writes=[cin_bs[i][g]])
                hid = sb("hid", [128, 2, 256], BF16, p3)
                hid_b = Buf("hid")
                cbias = sb("cbias", [128, 2, 2], F32, p3)
                cbias_b = Buf("cbias")
                pc_ = [pst(f"pc{i}", [128, 512], F32, p3) for i in range(3)]
                pc_b = [Buf(f"pc{i}") for i in range(3)]
                ci = 0
                for i in range(2):
                    for ht in range(2):
                        pcc = pc_[ci % 3]; pcb = pc_b[ci % 3]; ci += 1
                        for l in range(32):
                            k.op(PE, lambda i=i, ht=ht, l=l, pcc=pcc: nc.tensor.matmul(pcc[:, 0:1], lhsT=w1[i][:, l, ht * 128:(ht + 1) * 128], rhs=pos[i][:, l:l + 1],
                                                                                      start=(l == 0), stop=(l == 31)), reads=[cw_b, cw1_b[i][l // 8]], writes=[pcb])
                        k.op(DVE, lambda i=i, ht=ht, pcc=pcc: nc.vector.tensor_copy(out=cbias[:, i, ht:ht + 1], in_=pcc[:, 0:1]), reads=[pcb], writes=[cbias_b])
                for i in range(2):
                    for g in range(2):
                        cv = cin[i][g][:].rearrange("p (c r) -> p c r", r=16)
                        for ht in range(2):
                            pcc = pc_[ci % 3]; pcb = pc_b[ci % 3]; ci += 1
                            for l in range(32):
                                rhs = cv[:, 0:255, l] if l < 16 else cv[:, 1:256, l - 16]
                                k.op(PE, lambda i=i, ht=ht, l=l, pcc=pcc, rhs=rhs: nc.tensor.matmul(pcc[:, 0:255], lhsT=w1[i][:, l, ht * 128:(ht + 1) * 128], rhs=rhs,
                                                                                                   start=(l == 0), stop=(l == 31)), reads=[cw_b, cw1_b[i][l // 8], cin_bs[i][g]], writes=[pcb])
                            k.op(ACT, lambda i=i, ht=ht, pcc=pcc: nc.scalar.activation(out=hid[:, ht, 0:255], in_=pcc[:, 0:255], func=AF.Gelu_apprx_tanh, bias=cbias[:, i, ht:ht + 1]),
                                 reads=[pcb, cbias_b], writes=[hid_b])
                            if i == 0 and g == 0 and ht == 0:
                                issue_resident_loads([hid_b])
                        if i == 0:
                            pcc = pc_[ci % 3]; pcb = pc_b[ci % 3]; ci += 1
                            for ht in range(2):
                                k.op(PE, lambda ht=ht, pcc=pcc: nc.tensor.matmul(pcc[0:64, 0:255], lhsT=w2[0][:, ht, :], rhs=hid[:, ht, 0:255], start=(ht == 0), stop=(ht == 1)),
                                     reads=[cw_b, hid_b], writes=[pcb])
                            k.op(DVE, lambda g=g, pcc=pcc: nc.vector.tensor_copy(out=kcT[g][0:64, 0:255], in_=pcc[0:64, 0:255]), reads=[pcb], writes=[cmp_b])
                        else:
                            for ct in range(2):
                                nr = 128 if ct == 0 else 127
                                pcc = pc_[ci % 3]; pcb = pc_b[ci % 3]; ci += 1
                                for ht in range(2):
                                    k.op(PE, lambda ht=ht, ct=ct, nr=nr, pcc=pcc: nc.tensor.matmul(pcc[0:nr, 0:64], lhsT=hid[:, ht, ct * 128:ct * 128 + nr], rhs=w2[1][:, ht, :],
                                                                                                  start=(ht == 0), stop=(ht == 1)), reads=[cw_b, hid_b], writes=[pcb])
                                k.op(DVE, lambda g=g, ct=ct, nr=nr, pcc=pcc: nc.vector.tensor_copy(out=vc_aug[g][0:nr, ct, 0:64], in_=pcc[0:nr, 0:64]), reads=[pcb], writes=[cmp_b])

            if upto <= 3:
                pw.close()
                k.drain()
                return nc
            k.barrier()
            pw.close()
            qs = Rot(k, p4, nc, "qs", 1, [128, 8, 512], BF16)
            qw = Rot(k, p4, nc, "qw", 2, [128, 8, 512], BF16)
            SLOPES = [2.0 ** (-(h + 1)) for h in range(8)]
            PT = Rot(k, p4, nc, "PT", 6, [128, 512], BF16, with_dsem=False)
            PTc = Rot(k, p4, nc, "PTc", 2, [128, 512], BF16, with_dsem=False)
            att = sb("att", [128, 4, 512], F32, p4)
            att_b = [Buf(f"att{i}") for i in range(4)]
            attb = sb("attb", [128, 4, 512], BF16, p4)
            attb_b = Buf("attb")
            attT = sb("attT", [128, 4, 512], BF16, p4)
            attT_b = Buf("attT")
            rnT = Rot(k, p4, nc, "rnT", 3, [128, 4, 512], BF16)
            imp = sb("imp", [128, 4, 2, 64], F32, p4)
            imp_b = [Buf(f"imp{i}") for i in range(4)]
            for i_ in range(4):
                k.op(DVE, lambda i_=i_: nc.vector.memset(imp[:, i_, :, :], 0.0), writes=[imp_b[i_]])
            selbT = [sb(f"selbT{g}", [128, 512], BF16, p4) for g in range(2)]
            selbT_b = [Buf(f"selbT{g}") for g in range(2)]
            sm = sb("sm", [128, 256], F32, p4)
            sm_b = [Buf(f"sm{i}") for i in range(16)]
            smi = [0]
            tk_a = sb("tk_a", [128, 8, 64], F32, p4); tk_b_ = sb("tk_b", [128, 8, 64], F32, p4)
            tk8 = sb("tk8", [128, 8, 16], F32, p4); tksel = sb("tksel", [128, 8, 64], BF16, p4)
            tk_bufs = [Buf(f"tk{u}") for u in range(8)]
            ssa = sb("ssa", [128, 8], F32, p4)
            ssa_b = Buf("ssa")
            ysb = Rot(k, p4, nc, "ysb", 2, [128, D], F32)
            xres = Rot(k, p4, nc, "xres", 2, [128, D], F32)
            junk4 = sb("junk4", [128, D], BF16, p4)
            junk4_b = Buf("junk4")
            print("P4 sbuf bytes remaining:", nc.sbuf_bytes_remaining)
            k.barrier()
            pS = [pst(f"pS{i}", [128, 512], F32, p4) for i in range(3)]
            pS_b = [Buf(f"pS{i}") for i in range(3)]
            pA = [pst(f"pA{i}", [128, 512], F32, p4) for i in range(2)]
            pA_b = [Buf(f"pA{i}") for i in range(2)]
            pTT = pst("pTT", [128, 1024], BF16, p4)
            pTT_b = Buf("pTT")
            pW = [pst(f"pW{i}", [128, 512], F32, p4) for i in range(1)]
            pW_b = [Buf(f"pW{i}") for i in range(1)]
            pC = pst("pC", [128, 512], F32, p4)
            pC_b = Buf("pC")
            cnt = {"s": 0, "a": 0}

            def nextS():
                i = cnt["s"] % 3; cnt["s"] += 1
                return pS[i], pS_b[i]

            def nextA():
                i = cnt["a"] % 2; cnt["a"] += 1
                return pA[i], pA_b[i]

            def small():
                i = smi[0] % 16; smi[0] += 1
                return sm[:, 16 * i:16 * i + 16], sm_b[i]

            att_written = set()

            def evac_group(pa, pab, stride, n, h, tl0, tt0, gate_idx, first, with_imp=False, g=None, first_in_group=False):
                s4, s4b = small()
                sums = pa[:, 64:64 + (n - 1) * stride + 1:stride]
                k.op(DVE, lambda: nc.vector.tensor_scalar(out=s4[:, 0:n], in0=sums, scalar1=1e-30, scalar2=None, op0=ALU.max), reads=[pab], writes=[s4b])
                k.op(DVE, lambda: nc.vector.reciprocal(out=s4[:, 4:4 + n], in_=s4[:, 0:n]), reads=[s4b], writes=[s4b])
                k.op(DVE, lambda: nc.vector.tensor_tensor(out=s4[:, 8:8 + n], in0=s4[:, 4:4 + n], in1=gates[:, tt0:tt0 + n, gate_idx], op=ALU.mult),
                     reads=[s4b] + [vtok_b[tt0 + i] for i in range(n)], writes=[s4b])
                for i in range(n):
                    tl = tl0 + i
                    dst = att[:, tl, h * 64:(h + 1) * 64]
                    src = pa[:, i * stride:i * stride + 64]
                    first = (h, tl) not in att_written
                    att_written.add((h, tl))
                    if first:
                        k.op(DVE, lambda dst=dst, src=src, i=i: nc.vector.tensor_scalar(out=dst, in0=src, scalar1=s4[:, 8 + i:9 + i], scalar2=None, op0=ALU.mult),
                             reads=[pab, s4b], writes=[att_b[tl]])
                    else:
                        k.op(DVE, lambda dst=dst, src=src, i=i: nc.vector.scalar_tensor_tensor(out=dst, in0=src, scalar=s4[:, 8 + i:9 + i], in1=dst, op0=ALU.mult, op1=ALU.add),
                             reads=[pab, s4b, att_b[tl]], writes=[att_b[tl]])
                    if with_imp:
                        idst = imp[:, tl, g, 1:64]
                        isrc = pa[:, i * stride + 65:i * stride + 128]
                        if first_in_group:
                            k.op(DVE, lambda idst=idst, isrc=isrc, i=i: nc.vector.tensor_scalar(out=idst, in0=isrc, scalar1=s4[:, 4 + i:5 + i], scalar2=None, op0=ALU.mult),
                                 reads=[pab, s4b], writes=[imp_b[tl]])
                        else:
                            k.op(DVE, lambda idst=idst, isrc=isrc, i=i: nc.vector.scalar_tensor_tensor(out=idst, in0=isrc, scalar=s4[:, 4 + i:5 + i], in1=idst, op0=ALU.mult, op1=ALU.add),
                                 reads=[pab, s4b, imp_b[tl]], writes=[imp_b[tl]])

            dbg_ds = k.dsem("dbg")

            qview = q_d.rearrange("(h d) t -> d h t", d=64)

            def load_chunk(jn):
                tn = jn * 512
                rt_, rtb_, rds_ = rnT.next()
                k.dma(SP, rds_, rt_[:], rnn_d[:, tn:tn + 512].rearrange("(f p) t -> p f t", p=128), reads=[DB(("rnn", jn))], writes=[rtb_])
                qw_, qwb_, qwd_ = qw.next()
                k.dma(SP, qwd_, qw_[0:64, :, :], qview[:, :, tn:tn + 512], reads=[DB((id(q_d), c4 * 128, jn)) for c4 in range(4)], writes=[qwb_])
                for h_ in range(8):
                    k.op(DVE, lambda h_=h_, qw_=qw_: nc.vector.tensor_scalar(out=qw_[64:128, h_, :], in0=tab[64:128, tn:tn + 512], scalar1=SLOPES[h_], scalar2=None, op0=ALU.mult),
                         reads=[tab_b], writes=[qwb_])
                return rt_, rtb_, qw_, qwb_

            def load_qs(jn):
                tn = jn * 512
                qs_, qsb_, qsd_ = qs.next()
                k.dma(SP, qsd_, qs_[0:64, :, :], qview[:, :, tn:tn + 512], reads=[DB((id(q_d), c4 * 128, jn)) for c4 in range(4)], writes=[qsb_])
                return qs_, qsb_

            nxt = load_chunk(0)
            nxt_qs = load_qs(0)
            wcnt = [0]

            def nextW():
                return pW[0], pW_b[0]

            def make_cback(jc, rt, rtb):
                units = []

                def u_tr(half):
                    for tl2 in range(2):
                        tl = half * 2 + tl2
                        for f in range(4):
                            k.op(PE, lambda tl=tl, tl2=tl2, f=f: nc.tensor.transpose(out=pTT[:, (tl2 * 4 + f) * 128:(tl2 * 4 + f + 1) * 128], in_=attb[:, tl, f * 128:(f + 1) * 128], identity=ident[:]),
                                 reads=[attb_b, ident_b], writes=[pTT_b])
                    for tl2 in range(2):
                        tl = half * 2 + tl2
                        k.op(DVE, lambda tl=tl, tl2=tl2: nc.vector.tensor_copy(out=attT[:, :, tl * 128:(tl + 1) * 128],
                                                                             in_=pTT[:, tl2 * 512:(tl2 + 1) * 512].rearrange("p (f t) -> p f t", f=4)),
                             reads=[pTT_b], writes=[attT_b])
                units.append(lambda: u_tr(0))
                units.append(lambda: u_tr(1))
                st = {}

                def u_mm(tl, half, part):
                    tt = jc * 4 + tl
                    if half == 0 and part == 0:
                        yt, ytb, yds = ysb.next()
                        xr_, xrb, xds = xres.next()
                        k.dma(SP, xds, xr_[:], x_d[tt * 128:(tt + 1) * 128, :], writes=[xrb])
                        st[tl] = (yt, ytb, yds, xr_, xrb)
                    yt, ytb, yds, xr_, xrb = st[tl]
                    if part == 0:
                        st[(tl, half)] = nextW()
                    pw, pwb = st[(tl, half)]
                    if part == 0:
                        for f in range(4):
                            k.op(PE, lambda f=f: nc.tensor.matmul(pw[:, :], lhsT=rt[:, f, tl * 128:(tl + 1) * 128], rhs=wout[:, f, half * 512:(half + 1) * 512],
                                                                  start=(f == 0), stop=False), reads=[rtb, wout_b], writes=[pwb])
                    else:
                        for f in range(4):
                            k.op(PE, lambda f=f: nc.tensor.matmul(pw[:, :], lhsT=attT[:, f, tl * 128:(tl + 1) * 128], rhs=wout[:, 4 + f, half * 512:(half + 1) * 512],
                                                                  start=False, stop=(f == 3)), reads=[attT_b, wout_b], writes=[pwb])
                        k.op(DVE, lambda: nc.vector.tensor_scalar(out=yt[:, half * 512:(half + 1) * 512], in0=pw[:, :], scalar1=ssr[:, tt:tt + 1], scalar2=None, op0=ALU.mult),
                             reads=[pwb, ssr_b], writes=[ytb])

                def u_epi(tl):
                    tt = jc * 4 + tl
                    yt, ytb, yds, xr_, xrb = st[tl]
                    if debug:
                        k.dma(POOL, dbg_ds, dbg["d_y"][tt * 128:(tt + 1) * 128, :], yt[:], reads=[ytb], writes=[DB(("dy", tt))])
                    s4, s4b = small()
                    k.op(ACT, lambda: nc.scalar.activation(out=junk4[:], in_=yt[:], func=AF.Square, accum_out=s4[:, 0:1]), reads=[ytb], writes=[junk4_b, s4b])
                    k.op(ACT, lambda: nc.scalar.activation(out=s4[:, 1:2], in_=s4[:, 0:1], func=AF.Ln, scale=1.0 / D, bias=EPS), reads=[s4b], writes=[s4b])
                    k.op(ACT, lambda: nc.scalar.activation(out=s4[:, 2:3], in_=s4[:, 1:2], func=AF.Exp, scale=-0.5), reads=[s4b], writes=[s4b])
                    k.op(DVE, lambda: nc.vector.scalar_tensor_tensor(out=yt[:], in0=yt[:], scalar=s4[:, 2:3], in1=C1row[:], op0=ALU.mult, op1=ALU.mult),
                         reads=[ytb, s4b, C1_b], writes=[ytb])
                    k.op(DVE, lambda: nc.vector.tensor_tensor(out=yt[:], in0=yt[:], in1=xr_[:], op=ALU.add), reads=[ytb, xrb], writes=[ytb])
                    k.dma(POOL, yds, x1_d[tt * 128:(tt + 1) * 128, :], yt[:], reads=[ytb], writes=[DB(("x1", tt))])

                for tl in range(4):
                    for half in range(2):
                        for part in range(2):
                            units.append(lambda tl=tl, half=half, part=part: u_mm(tl, half, part))
                    units.append(lambda tl=tl: u_epi(tl))
                return units

            cback = []
            for j in range(NCH):
                t0 = j * 512
                att_written.clear()
                rt, rtb, qwt, qwb = nxt
                qst, qsb = nxt_qs
                if j + 1 < NCH:
                    nxt = load_chunk(j + 1)
                ncts = [ct for ct in range(2) if 16 * (ct * 128) + 31 <= t0 + 511]

                def cmp_S(h):
                    g = h // 4
                    pts = []
                    for ct in ncts:
                        nr = 128 if ct == 0 else 127
                        ps, psb = nextS()
                        k.op(PE, lambda ps=ps, ct=ct, nr=nr, g=g, h=h: nc.tensor.matmul(ps[0:nr, :], lhsT=kcT[g][:, ct * 128:ct * 128 + nr], rhs=qwt[:, h, :], start=True, stop=False),
                             reads=[cmp_b, qwb], writes=[psb])
                        k.op(PE, lambda ps=ps, ct=ct, nr=nr: nc.tensor.matmul(ps[0:nr, :], lhsT=ident[:, 0:nr], rhs=cmask[:, ct, t0:t0 + 512], start=False, stop=True),
                             reads=[ident_b, cmask_b], writes=[psb])
                        pt, ptb, _ = PTc.next()
                        k.op(ACT, lambda ps=ps, pt=pt, nr=nr, ct=ct, h=h: nc.scalar.activation(out=pt[0:nr, :], in_=ps[0:nr, :], func=AF.Exp, bias=cllo[0:nr, ct, h:h + 1]),
                             reads=[psb, cmp_b], writes=[ptb])
                        pts.append((pt, ptb, ct, nr))
                    return pts

                def cmp_PV(h, pts):
                    g = h // 4
                    nmm = 4 * len(pts)
                    mi = 0
                    for tl in range(4):
                        for (pt, ptb, ct, nr) in pts:
                            k.op(PE, lambda pt=pt, tl=tl, nr=nr, ct=ct, g=g, mi=mi, nmm=nmm: nc.tensor.matmul(
                                pC[:, tl * 128:(tl + 1) * 128], lhsT=pt[0:nr, tl * 128:(tl + 1) * 128], rhs=vc_aug[g][0:nr, ct, :],
                                start=(mi == 0), stop=(mi == nmm - 1)),
                                reads=[ptb, cmp_b], writes=[pC_b])
                            mi += 1
                    evac_group(pC, pC_b, 128, 4, h, 0, j * 4, 0 * 8 + h, True, with_imp=True, g=g, first_in_group=(h % 4 == 0))

                pre = []
                cst = {}

                def u_cS(h):
                    cst[h] = cmp_S(h)

                def u_cPV(h):
                    cmp_PV(h, cst[h])
                for h in range(8):
                    pre.append(lambda h=h: u_cS(h))
                    pre.append(lambda h=h: u_cPV(h))
                units8 = [(tl, g) for tl in range(4) for g in range(2)]

                def tk_stage(sidx):
                    for u, (tl, g) in enumerate(units8):
                        tt = j * 4 + tl
                        tb = tk_bufs[u]
                        if sidx == 0:
                            k.op(DVE, lambda u=u, tl=tl, g=g, tt=tt: nc.vector.tensor_tensor(out=tk_a[:, u, :], in0=imp[:, tl, g, :], in1=addm[:, tt, :], op=ALU.add),
                                 reads=[imp_b[tl], addm_b], writes=[tb])
                        elif sidx == 1:
                            k.op(DVE, lambda u=u: nc.vector.max(out=tk8[:, u, 0:8], in_=tk_a[:, u, :]), reads=[tb], writes=[tb])
                        elif sidx == 2:
                            k.op(DVE, lambda u=u: nc.vector.match_replace(out=tk_b_[:, u, :], in_to_replace=tk8[:, u, 0:8], in_values=tk_a[:, u, :], imm_value=-3.0e38), reads=[tb], writes=[tb])
                        elif sidx == 3:
                            k.op(DVE, lambda u=u: nc.vector.max(out=tk8[:, u, 8:16], in_=tk_b_[:, u, :]), reads=[tb], writes=[tb])
                        elif sidx == 4:
                            k.op(DVE, lambda u=u: nc.vector.tensor_scalar(out=tksel[:, u, :], in0=tk_a[:, u, :], scalar1=tk8[:, u, 15:16], scalar2=NEGM, op0=ALU.is_lt, op1=ALU.mult),
                                 reads=[tb], writes=[tb])
                        elif sidx == 5:
                            k.op(PE, lambda u=u, g=g, tl=tl: nc.tensor.transpose(out=pTT[0:64, (g * 4 + tl) * 128:(g * 4 + tl + 1) * 128], in_=tksel[:, u, :], identity=ident[:]),
                                 reads=[tb, ident_b], writes=[pTT_b])
                    if sidx == 6:
                        for g in range(2):
                            if os.environ.get("KDBG_S6") == "act":
                                k.op(ACT, lambda g=g: nc.scalar.activation(out=selbT[g][64:128, :], in_=pTT[0:64, g * 512:(g + 1) * 512], func=AF.Copy), reads=[pTT_b], writes=[selbT_b[g]])
                            else:
                                k.op(DVE, lambda g=g: nc.vector.tensor_copy(out=selbT[g][64:128, :], in_=pTT[0:64, g * 512:(g + 1) * 512]), reads=[pTT_b], writes=[selbT_b[g]])
                    if sidx == 7:
                        for h_ in range(8):
                            k.op(DVE, lambda h_=h_: nc.vector.tensor_tensor(out=qst[64:128, h_, :], in0=qwt[64:128, h_, :], in1=selbT[h_ // 4][64:128, :], op=ALU.add),
                                 reads=[qwb, selbT_b[h_ // 4]], writes=[qsb])
                for sidx in range(8):
                    pre.append(lambda sidx=sidx: tk_stage(sidx))

                tasks = []
                for br in (2, 1):
                    for h in range(8):
                        kts = list(range(0, 4 * j + 4)) if br == 1 else list(range(max(0, 4 * j - 4), 4 * j + 4))
                        grp = {"h": h, "br": br, "g": h // 4, "pa": None, "npv": 0, "done": 0}
                        for kt in kts:
                            tls = [tl for tl in range(4) if kt <= 4 * j + tl and (br == 1 or kt >= 4 * j + tl - 4)]
                            grp["npv"] += len(tls)
                            tasks.append({"grp": grp, "kt": kt, "tls": tls})
                n_win = sum(1 for tk in tasks if tk["grp"]["br"] == 2)

                def emit_S(tk):
                    grp = tk["grp"]; h = grp["h"]; br = grp["br"]; g = grp["g"]; kt = tk["kt"]
                    kT = kTs[g] if br == 1 else kTw[g]
                    qq, qqb = (qst, qsb) if br == 1 else (qwt, qwb)
                    ps, psb = nextS()
                    m = (kt - (4 * j - 4)) if br == 2 else (4 + kt - 4 * j)
                    use_msk = (br == 2) or (m >= 4)
                    c0, c1 = 0, 512
                    if use_msk:
                        if m < 4:
                            c1 = 128 * (m + 1)
                        else:
                            c0 = 128 * (m - 4)
                    k.op(PE, lambda: nc.tensor.matmul(ps[:, c0:c1], lhsT=kT[:, kt * 128:(kt + 1) * 128], rhs=qq[:, h, c0:c1], start=True, stop=True),
                         reads=[(kTs_b[g] if br == 1 else kTw_b[g]), qqb], writes=[psb])
                    pt, ptb, _ = PT.next()
                    k.op(ACT, lambda: nc.scalar.activation(out=pt[:, c0:c1], in_=ps[:, c0:c1], func=AF.Exp, bias=sllo[:, h:h + 1]), reads=[psb, sllo_b], writes=[ptb])
                    if use_msk:
                        k.op(DVE, lambda: nc.vector.tensor_tensor(out=pt[:, c0:c1], in0=pt[:, c0:c1], in1=wm01[:, m, c0:c1], op=ALU.mult), reads=[ptb, wm01_b], writes=[ptb])
                    tk["pt"] = pt; tk["ptb"] = ptb

                def emit_PV(tk):
                    grp = tk["grp"]; h = grp["h"]; br = grp["br"]; g = grp["g"]; kt = tk["kt"]
                    vA = vs_aug if br == 1 else vw_aug
                    if grp["pa"] is None:
                        grp["pa"] = nextA()
                    pa, pab = grp["pa"]
                    pt = tk["pt"]; ptb = tk["ptb"]
                    for tl in tk["tls"]:
                        fm = (grp["done"] == 0)
                        grp["done"] += 1
                        last = (grp["done"] == grp["npv"])
                        k.op(PE, lambda tl=tl, fm=fm, last=last: nc.tensor.matmul(pa[:, tl * 65:(tl + 1) * 65], lhsT=pt[:, tl * 128:(tl + 1) * 128], rhs=vA[:, kt, g, :], start=fm, stop=last),
                             reads=[ptb, vtok_b[kt], vones_b], writes=[pab])
                    if grp["done"] == grp["npv"]:
                        evac_group(pa, pab, 65, 4, h, 0, j * 4, br * 8 + h, False)

                LOOK = 5
                n_sel = len(tasks) - n_win
                pre_every = max(1, n_win // (len(pre) + 1))
                cb_every = max(1, (n_sel - 2) // (len(cback) + 1)) if cback else 1
                for i in range(len(tasks) + LOOK):
                    if i < len(tasks):
                        if i == n_win:
                            while pre:
                                pre.pop(0)()
                        emit_S(tasks[i])
                        if i < n_win:
                            if pre and (i % pre_every == pre_every - 1):
                                pre.pop(0)()
                        else:
                            if cback and ((i - n_win) % cb_every == cb_every - 1):
                                cback.pop(0)()
                    if i - LOOK >= 0:
                        emit_PV(tasks[i - LOOK])
                while cback:
                    cback.pop(0)()
                if debug:
                    for tl in range(4):
                        k.dma(POOL, dbg_ds, dbg["d_att"][(j * 4 + tl) * 128:(j * 4 + tl + 1) * 128, :], att[:, tl, :], reads=[att_b[tl]], writes=[DB(("datt", j, tl))])
                for tl in range(4):
                    k.op(ACT, lambda tl=tl: nc.scalar.activation(out=junk4[:, 0:512], in_=att[:, tl, :], func=AF.Square, accum_out=ssa[:, tl:tl + 1]),
                         reads=[att_b[tl]], writes=[junk4_b, ssa_b])
                k.op(ACT, lambda: nc.scalar.activation(out=ssa[:, 4:8], in_=ssa[:, 0:4], func=AF.Ln, scale=1.0 / 512, bias=EPS), reads=[ssa_b], writes=[ssa_b])
                k.op(ACT, lambda: nc.scalar.activation(out=ssa[:, 4:8], in_=ssa[:, 4:8], func=AF.Exp, scale=-0.5), reads=[ssa_b], writes=[ssa_b])
                k.op(DVE, lambda: nc.vector.tensor_tensor(out=ssa[:, 4:8], in0=ssa[:, 4:8], in1=ssq[:, j * 4:(j + 1) * 4], op=ALU.mult), reads=[ssa_b, ssr_b], writes=[ssa_b])
                for tl in range(4):
                    k.op(DVE, lambda tl=tl: nc.vector.tensor_scalar(out=attb[:, tl, :], in0=att[:, tl, :], scalar1=ssa[:, 4 + tl:5 + tl], scalar2=None, op0=ALU.mult),
                         reads=[att_b[tl], ssa_b], writes=[attb_b])
                cback = make_cback(j, rt, rtb)
                if os.environ.get("KDBG_CB") == "now":
                    while cback:
                        cback.pop(0)()
                if j + 1 < NCH:
                    nxt_qs = load_qs(j + 1)
            while cback:
                cback.pop(0)()

        mid.close()
        if upto <= 4:
            k.drain()
            return nc
        k.barrier()
        with ExitStack() as p5:
            wf1 = sb("wf1", [128, 8, 4 * D], BF16, p5)
            wf2 = sb("wf2", [128, 32, D], BF16, p5)
            wf1_b = [Buf(f"wf1_{i}") for i in range(8)]; wf2_b = [Buf(f"wf2_{i}") for i in range(4)]
            wf1v = wff1_d.rearrange("(k p) n -> p k n", p=128)
            wf2v = wff2_d.rearrange("(k p) n -> p k n", p=128)
            for cb in range(8):
                k.dma(POOL, k.dsem("wf1"), wf1[:, :, cb * 512:(cb + 1) * 512], wf1v[:, :, cb * 512:(cb + 1) * 512], writes=[wf1_b[cb]])
            for hh in range(4):
                k.dma(POOL, k.dsem("wf2"), wf2[:, hh * 8:(hh + 1) * 8, :], wf2v[:, hh * 8:(hh + 1) * 8, :], writes=[wf2_b[hh]])
            CH = 256
            xin = Rot(k, p5, nc, "xin", 4, [128, D], F32)
            xnb = Rot(k, p5, nc, "xnb", 2, [128, D], BF16, with_dsem=False)
            hT2 = Rot(k, p5, nc, "hT2", 2, [128, 8, CH], BF16, with_dsem=False)
            aT = Rot(k, p5, nc, "aT", 1, [128, 32, CH], BF16, with_dsem=False)
            r32 = Rot(k, p5, nc, "r32", 3, [128, CH], F32, with_dsem=False)
            y2 = Rot(k, p5, nc, "y2", 2, [128, D], F32, with_dsem=False)
            ot = Rot(k, p5, nc, "ot", 2, [128, D], F32)
            junk5 = sb("junk5", [128, D], BF16, p5)
            junk5_b = Buf("junk5")
            sm5 = sb("sm5", [128, 64], F32, p5)
            sm5_b = [Buf(f"sm5_{i}") for i in range(16)]
            s5i = [0]
            pT5 = [pst(f"pT5_{i}", [128, 1024], BF16, p5) for i in range(2)]
            pT5_b = [Buf(f"pT5_{i}") for i in range(2)]
            pF = [pst(f"pF{i}", [128, 512], F32, p5) for i in range(2)]
            pF_b = [Buf(f"pF{i}") for i in range(2)]
            pY = [pst(f"pY{i}", [128, 512], F32, p5) for i in range(4)]
            pY_b = [Buf(f"pY{i}") for i in range(4)]
            fi = [0]
            out_bufs = []

            def prologue_a(cj):
                xtiles = []
                nts = []
                for tl in range(2):
                    tt = cj * 2 + tl
                    xt_, xtb, xds = xin.next()
                    k.dma(SP, xds, xt_[:], x1_d[tt * 128:(tt + 1) * 128, :], reads=[DB(("x1", tt))], writes=[xtb])
                    xtiles.append((xt_, xtb))
                    i5 = s5i[0] % 16; s5i[0] += 1
                    s4 = sm5[:, 4 * i5:4 * i5 + 4]; s4b = sm5_b[i5]
                    k.op(ACT, lambda xt_=xt_, s4=s4: nc.scalar.activation(out=junk5[:], in_=xt_[:], func=AF.Square, accum_out=s4[:, 0:1]), reads=[xtb], writes=[junk5_b, s4b])
                    k.op(ACT, lambda s4=s4: nc.scalar.activation(out=s4[:, 1:2], in_=s4[:, 0:1], func=AF.Sqrt, scale=1.0 / D, bias=EPS), reads=[s4b], writes=[s4b])
                    k.op(DVE, lambda s4=s4: nc.vector.reciprocal(out=s4[:, 2:3], in_=s4[:, 1:2]), reads=[s4b], writes=[s4b])
                    n, nb, _ = xnb.next()
                    k.op(DVE, lambda xt_=xt_, n=n, s4=s4: nc.vector.tensor_scalar(out=n[:], in0=xt_[:], scalar1=s4[:, 2:3], scalar2=None, op0=ALU.mult), reads=[xtb, s4b], writes=[nb])
                    nts.append((n, nb))
                return xtiles, nts

            def prologue_b(cj, nts):
                ht2, ht2b, _ = hT2.next()
                for tl in range(2):
                    tt = cj * 2 + tl
                    n, nb = nts[tl]
                    pp = pT5[tt % 2]; ppb = pT5_b[tt % 2]
                    for jj in range(8):
                        k.op(PE, lambda jj=jj, n=n, pp=pp: nc.tensor.transpose(out=pp[:, jj * 128:(jj + 1) * 128], in_=n[:, jj * 128:(jj + 1) * 128], identity=ident[:]),
                             reads=[nb, ident_b], writes=[ppb])
                    for jj in range(8):
                        if tt % 2 == 0:
                            k.op(DVE, lambda jj=jj, pp=pp, tl=tl, ht2=ht2: nc.vector.tensor_scalar(out=ht2[:, jj, tl * 128:(tl + 1) * 128], in0=pp[:, jj * 128:(jj + 1) * 128],
                                                                                                   scalar1=A2[:, jj:jj + 1], scalar2=B2[:, jj:jj + 1], op0=ALU.mult, op1=ALU.add),
                                 reads=[ppb, A2_b, B2_b], writes=[ht2b])
                        else:
                            k.op(ACT, lambda jj=jj, pp=pp, tl=tl, ht2=ht2: nc.scalar.activation(out=ht2[:, jj, tl * 128:(tl + 1) * 128], in_=pp[:, jj * 128:(jj + 1) * 128],
                                                                                                func=AF.Identity, scale=A2[:, jj:jj + 1], bias=B2[:, jj:jj + 1]),
                                 reads=[ppb, A2_b, B2_b], writes=[ht2b])
                return ht2, ht2b

            def ff1(ht2, ht2b, mid_cb=None):
                at, atb, _ = aT.next()
                res = None
                for f in range(32):
                    if f == 10 and mid_cb is not None:
                        res = mid_cb()
                    pf = pF[fi[0] % 2]; pfb = pF_b[fi[0] % 2]; fi[0] += 1
                    for kk in range(8):
                        k.op(PE, lambda kk=kk, f=f, pf=pf: nc.tensor.matmul(pf[:, 0:CH], lhsT=wf1[:, kk, f * 128:(f + 1) * 128], rhs=ht2[:, kk, :], start=(kk == 0), stop=(kk == 7)),
                             reads=[wf1_b[f // 4], ht2b], writes=[pfb])
                    r, rb, _ = r32.next()
                    k.op(ACT, lambda pf=pf, r=r: nc.scalar.activation(out=r[:], in_=pf[:, 0:CH], func=AF.Relu), reads=[pfb], writes=[rb])
                    k.op(DVE, lambda r=r, f=f: nc.vector.tensor_tensor(out=at[:, f, :], in0=r[:], in1=r[:], op=ALU.mult), reads=[rb], writes=[atb])
                return at, atb, res

            def ff2(cj, at, atb, xtiles):
                for tl in range(2):
                    tt = cj * 2 + tl
                    yy, yyb, _ = y2.next()
                    for half in range(2):
                        py = pY[(tl * 2 + half) % 4]; pyb = pY_b[(tl * 2 + half) % 4]
                        for f in range(32):
                            k.op(PE, lambda f=f, tl=tl, half=half, py=py: nc.tensor.matmul(py[:, :], lhsT=at[:, f, tl * 128:(tl + 1) * 128], rhs=wf2[:, f, half * 512:(half + 1) * 512],
                                                                                          start=(f == 0), stop=(f == 31)), reads=[atb, wf2_b[f // 8]], writes=[pyb])
                        k.op(ACT, lambda half=half, yy=yy, py=py: nc.scalar.activation(out=yy[:, half * 512:(half + 1) * 512], in_=py[:, :], func=AF.Copy), reads=[pyb], writes=[yyb])
                    i5 = s5i[0] % 16; s5i[0] += 1
                    s4 = sm5[:, 4 * i5:4 * i5 + 4]; s4b = sm5_b[i5]
                    k.op(ACT, lambda yy=yy, s4=s4: nc.scalar.activation(out=junk5[:], in_=yy[:], func=AF.Square, accum_out=s4[:, 0:1]), reads=[yyb], writes=[junk5_b, s4b])
                    k.op(ACT, lambda s4=s4: nc.scalar.activation(out=s4[:, 1:2], in_=s4[:, 0:1], func=AF.Sqrt, scale=1.0 / D, bias=EPS), reads=[s4b], writes=[s4b])
                    k.op(DVE, lambda s4=s4: nc.vector.reciprocal(out=s4[:, 2:3], in_=s4[:, 1:2]), reads=[s4b], writes=[s4b])
                    k.op(DVE, lambda yy=yy, s4=s4: nc.vector.scalar_tensor_tensor(out=yy[:], in0=yy[:], scalar=s4[:, 2:3], in1=C2row[:], op0=ALU.mult, op1=ALU.mult),
                         reads=[yyb, s4b, C2_b], writes=[yyb])
                    o, ob, ods = ot.next()
                    xt_, xtb = xtiles[tl]
                    k.op(DVE, lambda yy=yy, o=o, xt_=xt_: nc.vector.tensor_tensor(out=o[:], in0=yy[:], in1=xt_[:], op=ALU.add), reads=[yyb, xtb], writes=[ob])
                    db = DB(("out", tt))
                    k.dma(POOL, ods, out_d[tt * 128:(tt + 1) * 128, :], o[:], reads=[ob], writes=[db])
                    out_bufs.append(db)

            NCJ = S // CH
            xtiles, nts = prologue_a(0)
            ht2, ht2b = prologue_b(0, nts)
            for cj in range(NCJ):
                at, atb, res = ff1(ht2, ht2b, (lambda cj=cj: prologue_a(cj + 1)) if cj + 1 < NCJ else None)
                if cj + 1 < NCJ:
                    xtiles_n, nts_n = res
                    ht2_n, ht2b_n = prologue_b(cj + 1, nts_n)
                ff2(cj, at, atb, xtiles)
                if cj + 1 < NCJ:
                    xtiles, ht2, ht2b = xtiles_n, ht2_n, ht2b_n
            k.drain()
            k.finish(out_bufs + [b for kk_, b in dbuf.items() if isinstance(kk_, tuple) and kk_ and kk_[0] in ("datt", "dy")] + ([DB("d_mod")] if debug else []))
        print("bass instructions:", k.ninst, "signalling:", k.nsig, "semaphores:", k.nsem)
    return nc


def _consts():
    bf = ml_dtypes.bfloat16
    t = np.arange(S)
    slopes = 2.0 ** (-np.arange(1, 9, dtype=np.float64))
    c = np.arange(256)
    ce = 16 * c + 31
    ce01 = ((ce[None, :] // 64) == np.arange(64)[:, None]).astype(np.float32)
    ce01[:, 255] = 0.0
    cidx = 16 * (np.arange(2)[None, :] * 128 + np.arange(128)[:, None]) + 31
    cllo = (slopes[None, None, :] * (cidx[:, :, None] % 64)).astype(np.float32)
    cend = (16 * (np.arange(2)[None, :, None] * 128 + np.arange(128)[:, None, None]) + 31)
    cmask = np.where(cend <= t[None, None, :], 0.0, NEGM).astype(np.float32)
    e01 = ((t[None, :] // 64) == np.arange(64)[:, None]).astype(np.float32)
    tab = (64.0 * (np.arange(64)[:, None] - (t[None, :] // 64))).astype(np.float32)
    sllo = (slopes[None, :] * (np.arange(128)[:, None] % 64)).astype(np.float32)
    m = np.arange(8)[None, :, None]; kk = np.arange(128)[:, None, None]; tl = np.arange(512)[None, None, :]
    dd = (512 + tl) - (128 * m + kk)
    wm01 = ((dd >= 0) & (dd < 512)).astype(np.float32)
    tok = (np.arange(NT)[None, :, None] * 128 + np.arange(128)[:, None, None])
    cur = tok // 64
    jb = np.arange(64)[None, None, :]
    forced = (jb == 0) | (jb == cur) | (jb == cur - 1)
    addm = np.where(forced, 1.0e4, np.where(jb <= cur, 0.0, -1.0e30)).astype(np.float32)
    cc = (np.arange(2)[None, :, None] * 128 + np.arange(128)[:, None, None])
    ovl = ((cc >= 4 * jb - 1) & (cc <= 4 * jb + 3) & (cc < 255)).astype(np.float32)
    return {
        "k_ident": np.eye(128, dtype=np.float32).astype(bf),
        "k_ce01": ce01.astype(bf), "k_cllo": cllo,
        "k_cmask": cmask.astype(bf), "k_e01": e01.astype(bf), "k_tab": tab.astype(bf), "k_sllo": sllo, "k_wm01": wm01.astype(bf),
        "k_addm": addm, "k_ovl": np.ascontiguousarray(ovl[:, :, 1:]).astype(bf),
    }


def _col(v, n):
    return np.ascontiguousarray(np.asarray(v, np.float32).reshape(n, 128).T)


def _shared_inputs(inp):
    L = 0
    f = lambda a: np.ascontiguousarray(np.asarray(a, np.float32))
    d = {
        "ada_w": f(inp["ada_w"][L]), "ada_b": f(inp["ada_b"][L]).reshape(1, -1),
        "g_pre1": _col(inp["pre_norm_mix"][L], 8), "g_pre2": _col(inp["pre_norm_mlp"][L], 8),
        "g_post1": np.ascontiguousarray(np.broadcast_to(f(inp["post_norm_mix"][L])[None, :], (128, D))),
        "g_post2": np.ascontiguousarray(np.broadcast_to(f(inp["post_norm_mlp"][L])[None, :], (128, D))),
        "w_in": f(inp["w_in"][L]),
        "conv_w": np.ascontiguousarray(f(inp["conv_w"][L]).T.reshape(4, 128, 4).transpose(1, 0, 2)),
        "conv_b": _col(inp["conv_b"][L], 4),
        "lru_wa": f(inp["lru_wa"][L]), "lru_wx": f(inp["lru_wx"][L]),
        "lru_ba": _col(inp["lru_ba"][L], 4), "lru_bx": _col(inp["lru_bx"][L], 4), "lru_lam": _col(inp["lru_lambda"][L], 4),
        "pos_k": np.ascontiguousarray(f(inp["cmp_pos_k"][L]).T), "pos_v": np.ascontiguousarray(f(inp["cmp_pos_v"][L]).T),
        "w1_k": f(inp["cmp_w1_k"][L]), "w1_v": f(inp["cmp_w1_v"][L]),
        "w2_k": f(inp["cmp_w2_k"][L]), "w2_v": f(inp["cmp_w2_v"][L]),
        "g_rnn": _col(inp["norm_rnn_out"][L], 4), "g_att": _col(inp["norm_att_out"][L], 4),
        "w_out": f(inp["w_out"][L]), "w_ff1": f(inp["w_ff1"][L]), "w_ff2": f(inp["w_ff2"][L]),
    }
    d.update(_consts())
    return d


def kernel(**inputs):
    debug = bool(inputs.pop("_debug", False))
    upto = inputs.pop("_upto", 99)
    cores = inputs.pop("_cores", None)
    x = np.asarray(inputs["x"], np.float32)
    c = np.asarray(inputs["c"], np.float32)
    B = x.shape[0]
    shared = _shared_inputs(inputs)
    rec = set()
    build(debug=debug, upto=upto, record=rec)
    nc = build(debug=debug, upto=upto, needed=rec)
    bs = list(range(B)) if cores is None else list(cores)
    in_maps = []
    for b in bs:
        m = dict(shared)
        m["x"] = np.ascontiguousarray(x[b])
        m["c"] = _col(c[b], 8)
        in_maps.append(m)
    res = run_bass_kernel_spmd(nc, in_maps, core_ids=list(range(len(bs))))
    if debug:
        return res.results
    return np.stack([np.asarray(r["out"], np.float32) for r in res.results], axis=0)
```
